# Optimizing a Trainium2 kernel written in Bass

```python
import jax, jax.numpy as jnp
from jax import lax
import numpy as np

D_MODEL = 1024
BATCH = 4
SEQ = 4096
DEPTH = 2

GRID_W = 64
CTX_LEN = 256
D_FF = 2816
CONV_WIDTH = 31
CONV_PAD = (CONV_WIDTH - 1) // 2
N_Q_HEADS = 16
N_KV_HEADS = 4
HEAD_DIM = 64
WINDOW = 128
BLOCK = 128
ROPE_BASE = 10000.0
NORM_EPS = 1e-6
N_MOD = 9
N_CONV_LAYERS = (DEPTH + 1) // 2
N_ATTN_LAYERS = DEPTH // 2
Q_DIM = N_Q_HEADS * HEAD_DIM
KV_DIM = N_KV_HEADS * HEAD_DIM

kernel_name = "hybrid_conv_swa_macaron_prefix_trunk"


def rmsnorm(x, g):
    xf = x.astype(jnp.float32)
    y = xf * lax.rsqrt(jnp.mean(xf * xf, axis=-1, keepdims=True) + NORM_EPS)
    return y.astype(x.dtype) * g


def modulate(x, g, shift, scale):
    return rmsnorm(x, g) * (1 + scale) + shift


def swiglu(h, w_in, w_out):
    gu = h @ w_in
    gate, up = jnp.split(gu, 2, axis=-1)
    return (jax.nn.silu(gate) * up) @ w_out


def conformer_conv(h, pw1_w, pw1_b, dw_w, dw_b, ln_g, ln_b, pw2_w, pw2_b):
    a = h @ pw1_w + pw1_b
    u, g = jnp.split(a, 2, axis=-1)
    u = u * jax.nn.sigmoid(g)
    u = lax.conv_general_dilated(
        u, dw_w[:, None, :].astype(u.dtype), window_strides=(1,),
        padding=[(CONV_PAD, CONV_PAD)], dimension_numbers=("NWC", "WIO", "NWC"),
        feature_group_count=D_MODEL) + dw_b
    uf = u.astype(jnp.float32)
    mu = jnp.mean(uf, axis=-1, keepdims=True)
    var = jnp.mean(jnp.square(uf - mu), axis=-1, keepdims=True)
    u = ((uf - mu) * lax.rsqrt(var + NORM_EPS)).astype(u.dtype) * ln_g + ln_b
    return jax.nn.silu(u) @ pw2_w + pw2_b


def axial_rope_tables(n_tok):
    n_rows = n_tok // GRID_W
    row = jnp.repeat(jnp.arange(n_rows), GRID_W).astype(jnp.float32)
    col = jnp.tile(jnp.arange(GRID_W), n_rows).astype(jnp.float32)
    n_freq = HEAD_DIM // 4
    inv = ROPE_BASE ** (-jnp.arange(n_freq, dtype=jnp.float32) / n_freq)
    ang = jnp.stack([row[:, None] * inv, col[:, None] * inv], axis=1)
    return jnp.cos(ang), jnp.sin(ang)


def apply_rope(x, cos, sin):
    n_freq = HEAD_DIM // 4
    xs = x.reshape(x.shape[:-1] + (2, 2, n_freq))
    x1 = xs[..., 0, :]
    x2 = xs[..., 1, :]
    bshape = (1, cos.shape[0]) + (1,) * (x.ndim - 3) + (2, n_freq)
    cs = cos.reshape(bshape).astype(x.dtype)
    sn = sin.reshape(bshape).astype(x.dtype)
    out = jnp.stack([x1 * cs - x2 * sn, x1 * sn + x2 * cs], axis=-2)
    return out.reshape(x.shape)


def windowed_gqa_sink(h_lat, h_ctx, w_qkv, w_o, sink, with_ctx_out):
    B, S, _ = h_lat.shape
    C = h_ctx.shape[1]
    G = N_Q_HEADS // N_KV_HEADS
    scale = HEAD_DIM ** -0.5

    def proj(h):
        qkv = h @ w_qkv
        q = qkv[..., :Q_DIM].reshape(h.shape[:2] + (N_KV_HEADS, G, HEAD_DIM))
        k = qkv[..., Q_DIM:Q_DIM + KV_DIM].reshape(h.shape[:2] + (N_KV_HEADS, HEAD_DIM))
        v = qkv[..., Q_DIM + KV_DIM:].reshape(h.shape[:2] + (N_KV_HEADS, HEAD_DIM))
        return q, k, v

    ql, kl, vl = proj(h_lat)
    qc, kc, vc = proj(h_ctx)
    cos, sin = axial_rope_tables(S)
    ql = apply_rope(ql, cos, sin) * scale
    kl = apply_rope(kl, cos, sin)
    pad = ((0, 0), (WINDOW, WINDOW), (0, 0), (0, 0))
    k_pad = jnp.pad(kl, pad)
    v_pad = jnp.pad(vl, pad)
    sink_col = sink.reshape(N_KV_HEADS, G, 1, 1).astype(jnp.float32)
    kw = BLOCK + 2 * WINDOW

    def one_block(n):
        start = n * BLOCK
        qb = lax.dynamic_slice_in_dim(ql, start, BLOCK, axis=1)
        kb = lax.dynamic_slice_in_dim(k_pad, start, kw, axis=1)
        vb = lax.dynamic_slice_in_dim(v_pad, start, kw, axis=1)
        i_pos = start + jnp.arange(BLOCK)
        j_pos = start - WINDOW + jnp.arange(kw)
        valid = ((jnp.abs(j_pos[None, :] - i_pos[:, None]) <= WINDOW)
                 & (j_pos[None, :] >= 0) & (j_pos[None, :] < S))
        s_win = jnp.einsum('bqhgd,bkhd->bhgqk', qb, kb).astype(jnp.float32)
        s_win = jnp.where(valid, s_win, -jnp.inf)
        s_ctx = jnp.einsum('bqhgd,bchd->bhgqc', qb, kc).astype(jnp.float32)
        sinks = jnp.broadcast_to(sink_col, s_win.shape[:-1] + (1,))
        p = jax.nn.softmax(jnp.concatenate([s_win, s_ctx, sinks], axis=-1), axis=-1).astype(vb.dtype)
        o = (jnp.einsum('bhgqk,bkhd->bqhgd', p[..., :kw], vb)
             + jnp.einsum('bhgqc,bchd->bqhgd', p[..., kw:kw + C], vc))
        return o.reshape(B, BLOCK, Q_DIM)

    o = lax.map(one_block, jnp.arange(S // BLOCK))
    y_lat = jnp.moveaxis(o, 0, 1).reshape(B, S, Q_DIM) @ w_o

    y_ctx = None
    if with_ctx_out:
        s = jnp.einsum('bqhgd,bkhd->bhgqk', qc * scale, kc).astype(jnp.float32)
        sinks = jnp.broadcast_to(sink_col, s.shape[:-1] + (1,))
        p = jax.nn.softmax(jnp.concatenate([s, sinks], axis=-1), axis=-1).astype(vc.dtype)
        oc = jnp.einsum('bhgqk,bkhd->bqhgd', p[..., :C], vc)
        y_ctx = oc.reshape(B, C, Q_DIM) @ w_o
    return y_lat, y_ctx


def setup_inputs(seed: int = 0) -> dict:
    key = jax.random.key(seed)
    ks = jax.random.split(key, 32)
    f32 = jnp.float32

    def nrm(k, shape, s):
        return jax.random.normal(k, shape, f32) * s

    D, F = D_MODEL, D_FF
    NC, NA = N_CONV_LAYERS, N_ATTN_LAYERS
    return {
        "x": nrm(ks[0], (BATCH, SEQ, D), 1.0),
        "c": nrm(ks[1], (BATCH, D), 1.0),
        "ctx": nrm(ks[2], (BATCH, CTX_LEN, D), 1.0),
        "c_ctx": nrm(ks[3], (D,), 1.0),
        "norm_g": 1.0 + nrm(ks[4], (DEPTH, 3, D), 0.05),
        "ada_w": nrm(ks[5], (DEPTH, D, N_MOD * D), D ** -0.5),
        "ada_b": nrm(ks[6], (DEPTH, N_MOD * D), 0.02),
        "ffn1_wi": nrm(ks[7], (DEPTH, D, 2 * F), D ** -0.5),
        "ffn1_wo": nrm(ks[8], (DEPTH, F, D), F ** -0.5),
        "ffn2_wi": nrm(ks[9], (DEPTH, D, 2 * F), D ** -0.5),
        "ffn2_wo": nrm(ks[10], (DEPTH, F, D), F ** -0.5),
        "conv_pw1_w": nrm(ks[11], (NC, D, 2 * D), D ** -0.5),
        "conv_pw1_b": nrm(ks[12], (NC, 2 * D), 0.02),
        "conv_dw_w": nrm(ks[13], (NC, CONV_WIDTH, D), CONV_WIDTH ** -0.5),
        "conv_dw_b": nrm(ks[14], (NC, D), 0.02),
        "conv_ln_g": 1.0 + nrm(ks[15], (NC, D), 0.05),
        "conv_ln_b": nrm(ks[16], (NC, D), 0.02),
        "conv_pw2_w": nrm(ks[17], (NC, D, D), D ** -0.5),
        "conv_pw2_b": nrm(ks[18], (NC, D), 0.02),
        "attn_w_qkv": nrm(ks[19], (NA, D, Q_DIM + 2 * KV_DIM), D ** -0.5),
        "attn_w_o": nrm(ks[20], (NA, Q_DIM, D), Q_DIM ** -0.5),
        "attn_sink": nrm(ks[21], (NA, N_Q_HEADS), 1.0),
        "final_g": 1.0 + nrm(ks[22], (D,), 0.05),
    }


def reference(x, c, ctx, c_ctx, norm_g, ada_w, ada_b, ffn1_wi, ffn1_wo, ffn2_wi, ffn2_wo,
              conv_pw1_w, conv_pw1_b, conv_dw_w, conv_dw_b, conv_ln_g, conv_ln_b,
              conv_pw2_w, conv_pw2_b, attn_w_qkv, attn_w_o, attn_sink, final_g):
    for i in range(DEPTH):
        last = i == DEPTH - 1
        m_lat = (jax.nn.silu(c) @ ada_w[i] + ada_b[i])[:, None, :]
        m_ctx = (jax.nn.silu(c_ctx) @ ada_w[i] + ada_b[i])[None, None, :]
        sh1, sc1, gt1, sh2, sc2, gt2, sh3, sc3, gt3 = jnp.split(m_lat, N_MOD, axis=-1)
        ch1, cs1, cg1, ch2, cs2, cg2, ch3, cs3, cg3 = jnp.split(m_ctx, N_MOD, axis=-1)
        g1, g2, g3 = norm_g[i, 0], norm_g[i, 1], norm_g[i, 2]

        x = x + 0.5 * gt1 * swiglu(modulate(x, g1, sh1, sc1), ffn1_wi[i], ffn1_wo[i])
        ctx = ctx + 0.5 * cg1 * swiglu(modulate(ctx, g1, ch1, cs1), ffn1_wi[i], ffn1_wo[i])

        hx = modulate(x, g2, sh2, sc2)
        hc = modulate(ctx, g2, ch2, cs2)
        j = i // 2
        if i % 2 == 0:
            conv_args = (conv_pw1_w[j], conv_pw1_b[j], conv_dw_w[j], conv_dw_b[j],
                         conv_ln_g[j], conv_ln_b[j], conv_pw2_w[j], conv_pw2_b[j])
            y_lat = conformer_conv(hx, *conv_args)
            y_ctx = None if last else conformer_conv(hc, *conv_args)
        else:
            y_lat, y_ctx = windowed_gqa_sink(hx, hc, attn_w_qkv[j], attn_w_o[j],
                                             attn_sink[j], not last)
        x = x + gt2 * y_lat

        x = x + 0.5 * gt3 * swiglu(modulate(x, g3, sh3, sc3), ffn2_wi[i], ffn2_wo[i])
        if not last:
            ctx = ctx + cg2 * y_ctx
            ctx = ctx + 0.5 * cg3 * swiglu(modulate(ctx, g3, ch3, cs3), ffn2_wi[i], ffn2_wo[i])
    return rmsnorm(x, final_g)
```

```python
import numpy as np
import concourse.bass as bass
import concourse.mybir as mybir
from concourse.bass_utils import run_bass_kernel_spmd

F32 = mybir.dt.float32
BF16 = mybir.dt.bfloat16
AF = mybir.ActivationFunctionType
ALU = mybir.AluOpType

D = 1024
NCH = 8
SEQ = 4096
OWN = 2048
T = 2200
TK = 2176
CTX = 256
TT = T + CTX
DFF = 2816
NFC = 22
CW = 31
EPS = 1e-6
SLOT = 6144
NSLOT = 4

TILES_A = [(i * 440, 440, 0) for i in range(5)] + [(T, 256, 1)]
TILES_OWN = [(i * 512, 512, 0) for i in range(4)]

PRM = {}
_o = 0
for _n, _w in [("cc", 16), ("ada_b", 144), ("norm_g", 48), ("pw1_b", 16), ("dw_w", 248), ("dw_b", 8),
               ("ln_g", 8), ("ln_b", 8), ("pw2_b", 8), ("sink", 8), ("final_g", 8)]:
    PRM[_n] = (_o, _w)
    _o += _w
NPRM = _o
NCST = 2 * 256 + 192

_XB = sorted(set([i * 440 for i in range(6)] + [i * 512 for i in range(5)] + [T]))


def _segs(a, b):
    return [i for i in range(len(_XB) - 1) if _XB[i] < b and _XB[i + 1] > a]


class Sched:
    def __init__(self, nc, sems):
        self.nc = nc
        self.sems = sems
        self.cnt = {k: 0 for k in sems}
        self.prog = {e: [] for e in ("pe", "act", "dve", "pool", "sp")}
        self.last_w = {}
        self.readers = {}
        self.waited = {e: {} for e in self.prog}
        self.pending = {e: False for e in self.prog}

    def _collect(self, eng, reads, writes):
        waits = {}

        def need(tok, raw):
            if tok is None:
                return
            k, v = tok
            if k == eng and (eng in ("pe", "sp") or not raw):
                return
            if waits.get(k, 0) < v:
                waits[k] = v
        for b in reads:
            need(self.last_w.get(b), True)
        for b in writes:
            need(self.last_w.get(b), False)
            for t in self.readers.get(b, ()):
                need(t, False)
        out = []
        wd = self.waited[eng]
        for k, v in waits.items():
            if wd.get(k, 0) < v:
                wd[k] = v
                out.append((k, v))
        return out

    def _record(self, tok, reads, writes):
        for b in reads:
            self.readers.setdefault(b, []).append(tok)
        for b in writes:
            self.last_w[b] = tok
            self.readers[b] = []

    def op(self, eng, fn, reads=(), writes=(), sig=True):
        waits = self._collect(eng, reads, writes)
        if sig:
            self.cnt[eng] += 1
            tok = (eng, self.cnt[eng])
            self.pending[eng] = False
        else:
            tok = (eng, self.cnt[eng] + 1)
            self.pending[eng] = True
        self._record(tok, reads, writes)
        self.prog[eng].append((waits, [fn], [(eng, 1)] if sig else []))

    def dma(self, eng, fns, chan, reads=(), writes=()):
        waits = self._collect(eng, reads, writes)
        self.cnt[chan] += 16 * len(fns)
        tok = (chan, self.cnt[chan])
        self._record(tok, reads, writes)
        self.prog[eng].append((waits, list(fns), [(chan, 16)] * len(fns)))
        return tok

    def wait_tokens(self, eng, toks):
        wl = []
        for k, v in toks:
            if self.waited[eng].get(k, 0) < v:
                self.waited[eng][k] = v
                wl.append((k, v))
        self.prog[eng].append((wl, [], []))

    def barrier(self):
        toks = [(e, self.cnt[e]) for e in ("pe", "act", "dve", "pool") if self.cnt[e] > 0]
        for e in ("pe", "act", "dve", "pool", "sp"):
            self.wait_tokens(e, [t for t in toks if t[0] != e])

    def emit(self, eng, e):
        assert not self.pending[eng], eng
        for waits, fns, incs in self.prog[eng]:
            for k, v in waits:
                e.wait_ge(self.sems[k], v)
            for i, fn in enumerate(fns):
                ins = fn(e)
                if i < len(incs):
                    ins.then_inc(self.sems[incs[i][0]], incs[i][1])


class Arena:
    def __init__(self, ap_f32, nbytes):
        self.ap = ap_f32
        self.n = nbytes
        self.off = 0

    def mark(self):
        return self.off

    def reset(self, m):
        self.off = m

    def alloc(self, shape, dt):
        es = 2 if dt == BF16 else 4
        free = 1
        for s in shape[1:]:
            free *= s
        nb = (free * es + 31) // 32 * 32
        assert self.off + nb <= self.n, ("arena overflow", self.off, nb, self.n)
        w0 = self.off // 4
        v = self.ap[:, w0:w0 + nb // 4]
        self.off += nb
        if dt == BF16:
            v = v.bitcast(BF16)
        v = v[:, 0:free]
        if len(shape) == 3:
            v = v.rearrange("p (a b) -> p a b", a=shape[1])
        elif len(shape) == 4:
            v = v.rearrange("p (a b c) -> p a b c", a=shape[1], b=shape[2])
        return v


def build_program(stop_after=None):
    nc = bass.Bass("TRN2", target_bir_lowering=False)
    dr = {}

    def din(name, shape):
        dr[name] = nc.dram_tensor(name, shape, F32, kind="ExternalInput").ap()
        return dr[name]

    xT = din("xT", [D, T])
    cT = din("cT", [D, CTX])
    prm_d = din("prm", [128, NPRM])
    cst_d = din("cst", [128, NCST])
    rope_d = din("rope", [2, 128, T])
    ada_w = din("ada_w", [2, D, 9 * D])
    f_wi = [din("ffn1_wi", [2, D, 2 * DFF]), din("ffn2_wi", [2, D, 2 * DFF])]
    f_wo = [din("ffn1_wo", [2, DFF, D]), din("ffn2_wo", [2, DFF, D])]
    pw1_w = din("pw1_w", [D, 2 * D])
    pw2_w = din("pw2_w", [D, D])
    watt = din("w_att", [D, 4 * 768])
    wv_d = din("w_v", [D, 256])
    wo_d = din("w_o", [D, D])
    outT = nc.dram_tensor("outT", [D, OWN], F32, kind="ExternalOutput").ap()

    semkeys = ["pe", "act", "dve", "pool", "sp", "ldp", "ldc", "ldxc", "st0", "st1", "rp0", "rp1"] + \
              [f"ldx{i}" for i in range(5)] + [f"w{i}" for i in range(NSLOT)]

    import contextlib
    es = contextlib.ExitStack()
    with es:
        sems = {k: es.enter_context(nc.semaphore(k)) for k in semkeys}
        X = es.enter_context(nc.sbuf_tensor("X", [128, NCH, T], F32))
        XC = es.enter_context(nc.sbuf_tensor("XC", [128, NCH, CTX], F32))
        Wr = es.enter_context(nc.sbuf_tensor("Wr", [128, NSLOT, SLOT], BF16))
        prm = es.enter_context(nc.sbuf_tensor("prm_sb", [128, NPRM], F32))
        cst = es.enter_context(nc.sbuf_tensor("cst_sb", [128, NCST], BF16))
        mod = es.enter_context(nc.sbuf_tensor("mod", [128, 2, 72, 2], F32))
        tabA = es.enter_context(nc.sbuf_tensor("tabA", [128, 3, NCH, 2], F32))
        tabG = es.enter_context(nc.sbuf_tensor("tabG", [128, 3, NCH, 2], F32))
        tabGb = es.enter_context(nc.sbuf_tensor("tabGb", [128, NCH, 2], F32))
        scT = es.enter_context(nc.sbuf_tensor("scT", [128, NCH, 2], BF16))
        onesf = es.enter_context(nc.sbuf_tensor("onesf", [128, 128], F32))
        esink = es.enter_context(nc.sbuf_tensor("esink", [128, NCH], F32))
        epsb = es.enter_context(nc.sbuf_tensor("epsb", [128, 1], F32))
        ARENA_BYTES = 212863 - (NCH * T * 4 + NCH * CTX * 4 + NSLOT * SLOT * 2 + NPRM * 4 + NCST * 2
                                + 2 * 72 * 2 * 4 + 2 * 3 * NCH * 2 * 4 + NCH * 2 * 4 + NCH * 2 * 2
                                + 128 * 4 + NCH * 4) - 1024
        ARENA_BYTES = ARENA_BYTES // 64 * 64
        ar_t = es.enter_context(nc.sbuf_tensor("arena", [128, ARENA_BYTES // 4], F32))
        PSALL = es.enter_context(nc.psum_tensor("psall", [128, 4096], F32))
        PS = [PSALL[:, i * 512:(i + 1) * 512] for i in range(8)]
        S = Sched(nc, sems)
        A = Arena(ar_t, ARENA_BYTES)

        def P(name, a=0, b=None):
            o, w = PRM[name]
            b = w if b is None else b
            return prm[:, o + a:o + b]

        def xbuf(which):
            return X if which == 0 else XC

        def xk(which, c, a, n):
            if which == 1:
                return [("XC", c)]
            return [("X", c, s) for s in _segs(a, a + n)]

        def hk(c, a, n):
            if a >= T:
                return [("H", c, "c")]
            return [("H", c, s) for s in _segs(a, a + n)]

        S.dma("sp", [lambda e: e.dma_start(out=prm[:], in_=prm_d[:, :])], "ldp", writes=["prm"])
        S.dma("pool", [lambda e: e.dma_start(out=cst[:], in_=cst_d[:, :])], "ldc", writes=["cst"])
        xT_v = xT.rearrange("(c p) t -> p c t", p=128)
        cT_v = cT.rearrange("(c p) t -> p c t", p=128)
        for i in range(5):
            S.dma("sp", [lambda e, i=i: e.dma_start(out=X[:, :, i * 440:(i + 1) * 440],
                                                   in_=xT_v[:, :, i * 440:(i + 1) * 440])],
                  f"ldx{i}", writes=[k for c in range(NCH) for k in xk(0, c, i * 440, 440)])
        S.dma("sp", [lambda e: e.dma_start(out=XC[:], in_=cT_v)], "ldxc",
              writes=[("XC", c) for c in range(NCH)])
        S.op("dve", lambda e: e.memset(onesf[:], 1.0 / D), writes=["onesf"])
        S.op("dve", lambda e: e.memset(epsb[:], EPS), writes=["epsb"])
        S.op("act", lambda e: e.activation(out=scT[:], in_=P("cc").rearrange("p (k w) -> p k w", w=2),
                                           func=AF.Silu), reads=["prm"], writes=["scT"])
        S.op("act", lambda e: e.activation(out=esink[:], in_=P("sink"), func=AF.Exp),
             reads=["prm"], writes=["esink"])

        ring = {"next": 0}

        def slot_alloc():
            s = ring["next"]
            ring["next"] = (s + 1) % NSLOT
            return s

        def load_slot(s, fns):
            S.dma("pool", fns, f"w{s}", writes=[("W", s)])

        def ada_layer(l):
            aw = ada_w[l].rearrange("(k p) n -> p k n", p=128)
            for u in range(12):
                s = slot_alloc()
                wv = Wr[:, s, :].rearrange("p (k n) -> p k n", k=NCH)
                load_slot(s, [lambda e, u=u, wv=wv: e.dma_start(out=wv, in_=aw[:, :, u * 768:(u + 1) * 768])])
                ps = PS[u % 2][:, 0:12].rearrange("p (a b) -> p a b", b=2)
                for oc in range(6):
                    for k in range(NCH):
                        S.op("pe", lambda e, oc=oc, k=k, wv=wv, ps=ps: e.matmul(
                            ps[:, oc, :], lhsT=wv[:, k, oc * 128:(oc + 1) * 128], rhs=scT[:, k, :],
                            start=(k == 0), stop=(k == NCH - 1)),
                            reads=[("W", s), "scT"], writes=[("ps", u % 2)], sig=(k == NCH - 1))
                ab = P("ada_b", l * 72 + u * 6, l * 72 + u * 6 + 6)
                S.op("dve", lambda e, ps=ps, ab=ab, u=u: e.tensor_tensor(
                    out=mod[:, l, u * 6:(u + 1) * 6, :], in0=ps,
                    in1=ab.unsqueeze(2).to_broadcast([128, 6, 2]), op=ALU.add),
                    reads=[("ps", u % 2), "prm"], writes=[("mod", l)])
            for j in range(3):
                g = P("norm_g", (l * 3 + j) * 8, (l * 3 + j) * 8 + 8)
                S.op("dve", lambda e, j=j, g=g: e.scalar_tensor_tensor(
                    out=tabA[:, j], in0=mod[:, l, (3 * j + 1) * 8:(3 * j + 2) * 8, :], scalar=1.0,
                    in1=g.unsqueeze(2).to_broadcast([128, NCH, 2]), op0=ALU.add, op1=ALU.mult),
                    reads=[("mod", l), "prm"], writes=[("tabA", j)])
                S.op("dve", lambda e, j=j: e.tensor_scalar(
                    out=tabG[:, j], in0=mod[:, l, (3 * j + 2) * 8:(3 * j + 3) * 8, :],
                    scalar1=(1.0 if j == 1 else 0.5), scalar2=None, op0=ALU.mult),
                    reads=[("mod", l)], writes=[("tabG", j)])
            if l == 0:
                S.op("dve", lambda e: e.tensor_tensor(
                    out=tabGb[:], in0=tabG[:, 1], in1=P("pw2_b").unsqueeze(2).to_broadcast([128, NCH, 2]),
                    op=ALU.mult), reads=[("tabG", 1), "prm"], writes=["tabGb"])

        def tabB(l, j):
            return mod[:, l, (3 * j) * 8:(3 * j + 1) * 8, :]

        def prepass(l, j, tiles, H):
            m = A.mark()
            sq = [A.alloc([128, 512], F32) for _ in range(3)]
            rstd = [A.alloc([128, 512], F32) for _ in range(2)]
            rsq = [A.alloc([128, 512], F32) for _ in range(2)]
            tmp = [A.alloc([128, 512], F32) for _ in range(3)]
            n_sq = 0
            n_tmp = 0
            for ti, (a, n, w) in enumerate(tiles):
                xb = xbuf(w)
                xa = a - T if w == 1 else a
                msb = 6 + (ti % 2)
                ms = PS[msb][:, 0:n]
                for c in range(NCH):
                    q = n_sq % 3
                    n_sq += 1
                    S.op("act", lambda e, q=q, c=c, xb=xb, xa=xa, n=n: e.activation(
                        out=sq[q][:, 0:n], in_=xb[:, c, xa:xa + n], func=AF.Square),
                        reads=xk(w, c, xa, n), writes=[("sq", q)])
                    S.op("pe", lambda e, q=q, c=c, ms=ms, n=n: e.matmul(
                        ms, lhsT=onesf[:], rhs=sq[q][:, 0:n], start=(c == 0), stop=(c == NCH - 1)),
                        reads=[("sq", q), "onesf"], writes=[("ps", msb)], sig=True)
                r = ti % 2
                S.op("act", lambda e, r=r, ms=ms, n=n: e.activation(
                    out=rsq[r][:, 0:n], in_=ms, func=AF.Sqrt, bias=epsb[:, 0:1], scale=1.0),
                    reads=[("ps", msb), "epsb"], writes=[("rsq", r)])
                S.op("dve", lambda e, r=r, n=n: e.reciprocal(out=rstd[r][:, 0:n], in_=rsq[r][:, 0:n]),
                     reads=[("rsq", r)], writes=[("rstd", r)])
                for c in range(NCH):
                    q = n_tmp % 3
                    n_tmp += 1
                    S.op("dve", lambda e, q=q, c=c, xb=xb, xa=xa, n=n, r=r, w=w: e.scalar_tensor_tensor(
                        out=tmp[q][:, 0:n], in0=xb[:, c, xa:xa + n], scalar=tabA[:, j, c, w:w + 1],
                        in1=rstd[r][:, 0:n], op0=ALU.mult, op1=ALU.mult),
                        reads=xk(w, c, xa, n) + [("rstd", r), ("tabA", j)], writes=[("ptmp", q)])
                    S.op("act", lambda e, q=q, c=c, a=a, n=n, w=w: e.activation(
                        out=H[:, c, a:a + n], in_=tmp[q][:, 0:n], func=AF.Identity,
                        bias=tabB(l, j)[:, c, w:w + 1], scale=1.0),
                        reads=[("ptmp", q), ("mod", l)], writes=hk(c, a, n))
            S.barrier()
            A.reset(m)

        def ffn(l, f, tiles, H):
            j = 0 if f == 0 else 2
            prepass(l, j, tiles, H)
            m = A.mark()
            act = [A.alloc([128, 4, 512], BF16) for _ in range(2)]
            sg = [A.alloc([128, 512], F32) for _ in range(2)]
            wi = f_wi[f][l].rearrange("(k p) (two n) -> p k two n", p=128, two=2)
            wo = f_wo[f][l].rearrange("(j p) n -> p j n", p=128)
            sweeps = [[0, 1], [2, 3], [4, 5], [6, 7], [8, 9], [10]]
            unit_slot = {}

            def load_unit(u):
                s = slot_alloc()
                unit_slot[u] = s
                wiv = Wr[:, s, 0:4096].rearrange("p (k two n) -> p k two n", k=NCH, two=2)
                wov = Wr[:, s, 4096:6144].rearrange("p (j n) -> p j n", j=2)
                load_slot(s, [lambda e, wiv=wiv, u=u: e.dma_start(out=wiv[:, :, 0, :], in_=wi[:, :, 0, u * 256:(u + 1) * 256]),
                              lambda e, wiv=wiv, u=u: e.dma_start(out=wiv[:, :, 1, :], in_=wi[:, :, 1, u * 256:(u + 1) * 256]),
                              lambda e, wov=wov, u=u: e.dma_start(out=wov, in_=wo[:, 2 * u:2 * u + 2, :])])

            tasks = []
            for si, sw in enumerate(sweeps):
                for ti in range(len(tiles)):
                    tasks.append((si, ti))
            st = {"sg": 0, "gu": 0, "y": 0}

            def stage_a(k):
                si, ti = tasks[k]
                a, n, w = tiles[ti]
                ab = k % 2
                chunks = [(u, jj) for u in sweeps[si] for jj in range(2)]
                for ci, (u, jj) in enumerate(chunks):
                    s = unit_slot[u]
                    wiv = Wr[:, s, 0:4096].rearrange("p (k two n) -> p k two n", k=NCH, two=2)
                    gb = st["gu"] % 2
                    st["gu"] += 1
                    gps = PS[gb * 2][:, 0:n]
                    ups = PS[gb * 2 + 1][:, 0:n]
                    for two, pp, pb in ((0, gps, gb * 2), (1, ups, gb * 2 + 1)):
                        for kk in range(NCH):
                            S.op("pe", lambda e, pp=pp, wiv=wiv, kk=kk, two=two, jj=jj, a=a, n=n: e.matmul(
                                pp, lhsT=wiv[:, kk, two, jj * 128:(jj + 1) * 128], rhs=H[:, kk, a:a + n],
                                start=(kk == 0), stop=(kk == NCH - 1)),
                                reads=[("W", s)] + hk(kk, a, n), writes=[("ps", pb)], sig=(kk == NCH - 1))
                    q = st["sg"] % 2
                    st["sg"] += 1
                    S.op("act", lambda e, q=q, gps=gps, n=n: e.activation(out=sg[q][:, 0:n], in_=gps, func=AF.Silu),
                         reads=[("ps", gb * 2)], writes=[("sg", q)])
                    S.op("dve", lambda e, q=q, ups=ups, n=n, ab=ab, ci=ci: e.tensor_tensor(
                        out=act[ab][:, ci, 0:n], in0=sg[q][:, 0:n], in1=ups, op=ALU.mult),
                        reads=[("sg", q), ("ps", gb * 2 + 1)], writes=[("act", ab, ci)])

            def stage_b(k):
                si, ti = tasks[k]
                a, n, w = tiles[ti]
                xb = xbuf(w)
                xa = a - T if w == 1 else a
                ab = k % 2
                chunks = [(u, jj) for u in sweeps[si] for jj in range(2)]
                for d in range(NCH):
                    yb = 4 + st["y"] % 2
                    st["y"] += 1
                    yps = PS[yb][:, 0:n]
                    for ci, (u, jj) in enumerate(chunks):
                        s = unit_slot[u]
                        wov = Wr[:, s, 4096:6144].rearrange("p (j n) -> p j n", j=2)
                        S.op("pe", lambda e, yps=yps, wov=wov, jj=jj, d=d, ab=ab, ci=ci, n=n: e.matmul(
                            yps, lhsT=wov[:, jj, d * 128:(d + 1) * 128], rhs=act[ab][:, ci, 0:n],
                            start=(ci == 0), stop=(ci == len(chunks) - 1)),
                            reads=[("W", s), ("act", ab, ci)], writes=[("ps", yb)], sig=(ci == len(chunks) - 1))
                    S.op("dve", lambda e, yps=yps, d=d, xb=xb, xa=xa, n=n, w=w: e.scalar_tensor_tensor(
                        out=xb[:, d, xa:xa + n], in0=yps, scalar=tabG[:, j, d, w:w + 1], in1=xb[:, d, xa:xa + n],
                        op0=ALU.mult, op1=ALU.add),
                        reads=[("ps", yb), ("tabG", j)] + xk(w, d, xa, n), writes=xk(w, d, xa, n))

            loaded = 0

            def ensure_loaded(si):
                nonlocal loaded
                while loaded <= min(si, len(sweeps) - 1):
                    for u in sweeps[loaded]:
                        load_unit(u)
                    loaded += 1
            ensure_loaded(1)
            nt = len(tiles)
            for k in range(len(tasks) + 1):
                if k < len(tasks):
                    si, ti = tasks[k]
                    if ti == 1:
                        ensure_loaded(si + 1)
                    stage_a(k)
                if k >= 1:
                    stage_b(k - 1)
            S.barrier()
            A.reset(m)

        def conv_phase(l, H):
            prepass(l, 1, TILES_A, H)
            m = A.mark()
            p1 = pw1_w.rearrange("(k p) n -> p k n", p=128)
            p2 = pw2_w.rearrange("(k p) n -> p k n", p=128)
            slots = []
            for i in range(4):
                s = slot_alloc()
                slots.append(s)
                v1 = Wr[:, s, 0:4096].rearrange("p (k two n) -> p k two n", k=NCH, two=2)
                v2 = Wr[:, s, 4096:6144].rearrange("p (k n) -> p k n", k=NCH)
                load_slot(s, [lambda e, v1=v1, i=i: e.dma_start(out=v1[:, :, 0, :], in_=p1[:, :, i * 256:(i + 1) * 256]),
                              lambda e, v1=v1, i=i: e.dma_start(out=v1[:, :, 1, :], in_=p1[:, :, D + i * 256:D + (i + 1) * 256]),
                              lambda e, v2=v2, i=i: e.dma_start(out=v2, in_=p2[:, :, i * 256:(i + 1) * 256])])
            UW = 472
            U = [A.alloc([128, UW], F32) for _ in range(2)]
            prod = [A.alloc([128, 440], F32) for _ in range(3)]
            npd = [0]
            V = A.alloc([128, NCH, 440], F32)
            accB = [A.alloc([128, 440], F32) for _ in range(2)]
            sqv = [A.alloc([128, 440], F32) for _ in range(1)]
            mean_sb = A.alloc([128, 440], F32)
            varb = A.alloc([128, 440], F32)
            HN = A.alloc([128, NCH, 440], BF16)
            dww = P("dw_w").rearrange("p (c k) -> p c k", k=CW)
            K1 = 13
            for ti, (a, n, w) in enumerate(TILES_A):
                xb = xbuf(w)
                xa = a - T if w == 1 else a
                s0, s1 = (0, T) if w == 0 else (T, T + CTX)
                lo, hi = max(a - 15, s0), min(a + n + 15, s1)
                ulo, uhi = lo - (a - 15), hi - (a - 15)
                nu = hi - lo
                for jp in range(4):
                    for jj in range(2):
                        j = 2 * jp + jj
                        s = slots[jp]
                        v1 = Wr[:, s, 0:4096].rearrange("p (k two n) -> p k two n", k=NCH, two=2)
                        for two in range(2):
                            pb = 2 * jj + two
                            pp = PS[pb][:, 0:nu]
                            for kk in range(NCH):
                                S.op("pe", lambda e, pp=pp, v1=v1, kk=kk, two=two, jj=jj, lo=lo, hi=hi: e.matmul(
                                    pp, lhsT=v1[:, kk, two, jj * 128:(jj + 1) * 128], rhs=H[:, kk, lo:hi],
                                    start=(kk == 0), stop=(kk == NCH - 1)),
                                    reads=[("W", s)] + hk(kk, lo, nu), writes=[("ps", pb)], sig=(kk == NCH - 1))
                        if ulo > 0:
                            S.op("pool", lambda e, jj=jj, ulo=ulo: e.memset(U[jj][:, 0:ulo], 0.0), writes=[("U", jj)])
                        if uhi < n + 30:
                            S.op("pool", lambda e, jj=jj, uhi=uhi, n=n: e.memset(U[jj][:, uhi:n + 30], 0.0), writes=[("U", jj)])
                        S.op("act", lambda e, jj=jj, j=j, nu=nu, ulo=ulo, uhi=uhi: e.activation(
                            out=U[jj][:, ulo:uhi], in_=PS[2 * jj + 1][:, 0:nu], func=AF.Sigmoid,
                            bias=P("pw1_b", 8 + j, 9 + j), scale=1.0),
                            reads=[("ps", 2 * jj + 1), "prm"], writes=[("U", jj)])
                        S.op("dve", lambda e, jj=jj, j=j, nu=nu, ulo=ulo, uhi=uhi: e.scalar_tensor_tensor(
                            out=U[jj][:, ulo:uhi], in0=PS[2 * jj][:, 0:nu], scalar=P("pw1_b", j, j + 1),
                            in1=U[jj][:, ulo:uhi], op0=ALU.add, op1=ALU.mult),
                            reads=[("ps", 2 * jj), "prm", ("U", jj)], writes=[("U", jj)])
                    for k in range(K1):
                        for jj in range(2):
                            j = 2 * jp + jj
                            if k == 0:
                                S.op("dve", lambda e, jj=jj, j=j, n=n: e.tensor_scalar(
                                    out=V[:, j, 0:n], in0=U[jj][:, 0:n], scalar1=dww[:, j, 0:1],
                                    scalar2=P("dw_b", j, j + 1), op0=ALU.mult, op1=ALU.add),
                                    reads=[("U", jj), "prm"], writes=[("V", j)])
                            else:
                                S.op("dve", lambda e, jj=jj, j=j, n=n, k=k: e.scalar_tensor_tensor(
                                    out=V[:, j, 0:n], in0=U[jj][:, k:k + n], scalar=dww[:, j, k:k + 1],
                                    in1=V[:, j, 0:n], op0=ALU.mult, op1=ALU.add),
                                    reads=[("U", jj), ("V", j), "prm"], writes=[("V", j)])
                    for k in range(K1, CW):
                        for jj in range(2):
                            j = 2 * jp + jj
                            if k == K1:
                                S.op("act", lambda e, jj=jj, j=j, n=n, k=k: e.activation(
                                    out=accB[jj][:, 0:n], in_=U[jj][:, k:k + n], func=AF.Identity,
                                    scale=dww[:, j, k:k + 1]),
                                    reads=[("U", jj), "prm"], writes=[("accB", jj)])
                            else:
                                q = npd[0] % 3
                                npd[0] += 1
                                S.op("act", lambda e, jj=jj, j=j, n=n, k=k, q=q: e.activation(
                                    out=prod[q][:, 0:n], in_=U[jj][:, k:k + n], func=AF.Identity,
                                    scale=dww[:, j, k:k + 1]),
                                    reads=[("U", jj), "prm"], writes=[("prod", q)])
                                S.op("pool", lambda e, jj=jj, n=n, q=q: e.tensor_tensor(
                                    out=accB[jj][:, 0:n], in0=accB[jj][:, 0:n], in1=prod[q][:, 0:n], op=ALU.add),
                                    reads=[("prod", q), ("accB", jj)], writes=[("accB", jj)])
                    for jj in range(2):
                        j = 2 * jp + jj
                        S.op("dve", lambda e, jj=jj, j=j, n=n: e.tensor_tensor(
                            out=V[:, j, 0:n], in0=V[:, j, 0:n], in1=accB[jj][:, 0:n], op=ALU.add),
                            reads=[("V", j), ("accB", jj)], writes=[("V", j)])
                for j in range(NCH):
                    S.op("pe", lambda e, j=j, n=n: e.matmul(PS[6][:, 0:n], lhsT=onesf[:], rhs=V[:, j, 0:n],
                                                           start=(j == 0), stop=(j == NCH - 1)),
                         reads=[("V", j), "onesf"], writes=[("ps", 6)], sig=True)
                    S.op("act", lambda e, j=j, n=n: e.activation(out=sqv[0][:, 0:n], in_=V[:, j, 0:n], func=AF.Square),
                         reads=[("V", j)], writes=[("sqv", 0)])
                    S.op("pe", lambda e, j=j, n=n: e.matmul(PS[7][:, 0:n], lhsT=onesf[:], rhs=sqv[0][:, 0:n],
                                                           start=(j == 0), stop=(j == NCH - 1)),
                         reads=[("sqv", 0), "onesf"], writes=[("ps", 7)], sig=True)
                S.op("act", lambda e, n=n: e.activation(out=mean_sb[:, 0:n], in_=PS[6][:, 0:n], func=AF.Identity),
                     reads=[("ps", 6)], writes=["mean_sb"])
                S.op("dve", lambda e, n=n: e.tensor_tensor(out=varb[:, 0:n], in0=mean_sb[:, 0:n], in1=mean_sb[:, 0:n],
                                                           op=ALU.mult), reads=["mean_sb"], writes=["varb"])
                S.op("dve", lambda e, n=n: e.tensor_tensor(out=varb[:, 0:n], in0=PS[7][:, 0:n], in1=varb[:, 0:n],
                                                           op=ALU.subtract), reads=[("ps", 7), "varb"], writes=["varb"])
                S.op("act", lambda e, n=n: e.activation(out=varb[:, 0:n], in_=varb[:, 0:n], func=AF.Sqrt,
                                                        bias=epsb[:, 0:1], scale=1.0),
                     reads=["varb", "epsb"], writes=["varb"])
                S.op("dve", lambda e, n=n: e.reciprocal(out=varb[:, 0:n], in_=varb[:, 0:n]),
                     reads=["varb"], writes=["varb"])
                for j in range(NCH):
                    S.op("dve", lambda e, j=j, n=n: e.tensor_tensor(
                        out=V[:, j, 0:n], in0=V[:, j, 0:n], in1=mean_sb[:, 0:n], op=ALU.subtract),
                        reads=[("V", j), "mean_sb"], writes=[("V", j)])
                    S.op("pool", lambda e, j=j, n=n: e.tensor_tensor(
                        out=V[:, j, 0:n], in0=V[:, j, 0:n], in1=varb[:, 0:n], op=ALU.mult),
                        reads=[("V", j), "varb"], writes=[("V", j)])
                    S.op("act", lambda e, j=j, n=n: e.activation(
                        out=HN[:, j, 0:n], in_=V[:, j, 0:n], func=AF.Silu,
                        bias=P("ln_b", j, j + 1), scale=P("ln_g", j, j + 1)),
                        reads=[("V", j), "prm"], writes=[("HN", j)])
                for d in range(NCH):
                    s = slots[d // 2]
                    v2 = Wr[:, s, 4096:6144].rearrange("p (k n) -> p k n", k=NCH)
                    yb = 4 + d % 2
                    S.op("pool", lambda e, d=d, xb=xb, xa=xa, n=n, w=w: e.tensor_scalar(
                        out=xb[:, d, xa:xa + n], in0=xb[:, d, xa:xa + n], scalar1=tabGb[:, d, w:w + 1],
                        scalar2=None, op0=ALU.add),
                        reads=xk(w, d, xa, n) + ["tabGb"], writes=xk(w, d, xa, n))
                    for kk in range(NCH):
                        S.op("pe", lambda e, v2=v2, kk=kk, d=d, yb=yb, n=n: e.matmul(
                            PS[yb][:, 0:n], lhsT=v2[:, kk, (d % 2) * 128:(d % 2 + 1) * 128], rhs=HN[:, kk, 0:n],
                            start=(kk == 0), stop=(kk == NCH - 1)),
                            reads=[("W", s), ("HN", kk)], writes=[("ps", yb)], sig=(kk == NCH - 1))
                    S.op("dve", lambda e, d=d, xb=xb, xa=xa, n=n, w=w, yb=yb: e.scalar_tensor_tensor(
                        out=xb[:, d, xa:xa + n], in0=PS[yb][:, 0:n], scalar=tabG[:, 1, d, w:w + 1],
                        in1=xb[:, d, xa:xa + n], op0=ALU.mult, op1=ALU.add),
                        reads=[("ps", yb), ("tabG", 1)] + xk(w, d, xa, n), writes=xk(w, d, xa, n))
            S.barrier()
            A.reset(m)

        def attn_phase(l, H):
            prepass(l, 1, TILES_A, H)
            m = A.mark()
            Qg = A.alloc([128, 2, OWN], BF16)
            Kg = A.alloc([128, TK + CTX], BF16)
            Vpad = A.alloc([128, 19, 192], BF16)
            Og = [A.alloc([128, 2, 512], BF16) for _ in range(2)]
            PT = [A.alloc([128, 2, 5, 128], BF16) for _ in range(2)]
            cosb = A.alloc([128, 440], F32)
            sinb = A.alloc([128, 440], F32)
            t1 = A.alloc([128, 440], F32)
            t2 = A.alloc([128, 440], F32)
            rden = [A.alloc([128, 128], F32) for _ in range(2)]
            mask_lo = cst[:, 0:256].rearrange("p (h q) -> p h q", h=2)
            mask_hi = cst[:, 256:512].rearrange("p (h q) -> p h q", h=2)
            onespad = cst[:, 512:704]
            wa = watt.rearrange("(k p) n -> p k n", p=128)
            wvv = wv_d.rearrange("(k p) n -> p k n", p=128)
            wov = wo_d.rearrange("(j p) n -> p j n", p=128)
            S.op("pool", lambda e: e.memset(Vpad[:], 0.0), writes=["Vpad"])
            cnt = {"st": 0, "od": 0, "og": 0}
            for g in range(4):
                sA = slot_alloc()
                vA = Wr[:, sA, :].rearrange("p (k n) -> p k n", k=NCH)
                load_slot(sA, [lambda e, vA=vA, g=g: e.dma_start(out=vA, in_=wa[:, :, g * 768:(g + 1) * 768])])
                sB = slot_alloc()
                vBo = Wr[:, sB, 0:2048].rearrange("p (j n) -> p j n", j=2)
                vBv = Wr[:, sB, 2048:2560].rearrange("p (k n) -> p k n", k=NCH)
                load_slot(sB, [lambda e, vBo=vBo, g=g: e.dma_start(out=vBo, in_=wov[:, 2 * g:2 * g + 2, :]),
                               lambda e, vBv=vBv, g=g: e.dma_start(out=vBv, in_=wvv[:, :, 64 * g:64 * g + 64])])
                for ti in range(5):
                    a, n = 440 * ti, 440
                    S.dma("sp", [lambda e, a=a, n=n: e.dma_start(out=cosb[:, 0:n], in_=rope_d[0][:, a:a + n]),
                                 lambda e, a=a, n=n: e.dma_start(out=sinb[:, 0:n], in_=rope_d[1][:, a:a + n])],
                          "rp0", writes=["rope"])
                    items = [("q", 0), ("q", 1), ("k", 0)]
                    for ii, (kind, cc) in enumerate(items):
                        if kind == "q":
                            nn = max(0, min(a + n, OWN) - a)
                            base, bsw = cc * 128, 256 + cc * 128
                            dst = Qg[:, cc, a:a + nn] if nn > 0 else None
                            dkey = ("Qg", cc)
                        else:
                            nn = max(0, min(a + n, TK) - a)
                            base, bsw = 512, 640
                            dst = Kg[:, a:a + nn]
                            dkey = ("Kg",)
                        if nn == 0:
                            continue
                        pa, pb = 2 * (ii % 2), 2 * (ii % 2) + 1
                        for (pp, bb) in ((pa, base), (pb, bsw)):
                            for kk in range(NCH):
                                S.op("pe", lambda e, pp=pp, bb=bb, kk=kk, a=a, nn=nn, vA=vA: e.matmul(
                                    PS[pp][:, 0:nn], lhsT=vA[:, kk, bb:bb + 128], rhs=H[:, kk, a:a + nn],
                                    start=(kk == 0), stop=(kk == NCH - 1)),
                                    reads=[("W", sA)] + hk(kk, a, nn), writes=[("ps", pp)], sig=(kk == NCH - 1))
                        S.op("dve", lambda e, pa=pa, nn=nn: e.tensor_tensor(
                            out=t1[:, 0:nn], in0=PS[pa][:, 0:nn], in1=cosb[:, 0:nn], op=ALU.mult),
                            reads=[("ps", pa), "rope"], writes=["t1"])
                        S.op("dve", lambda e, pb=pb, nn=nn: e.tensor_tensor(
                            out=t2[:, 0:nn], in0=PS[pb][:, 0:nn], in1=sinb[:, 0:nn], op=ALU.mult),
                            reads=[("ps", pb), "rope"], writes=["t2"])
                        S.op("pool", lambda e, dst=dst, nn=nn: e.tensor_tensor(
                            out=dst, in0=t1[:, 0:nn], in1=t2[:, 0:nn], op=ALU.add),
                            reads=["t1", "t2"], writes=[dkey])
                for kk in range(NCH):
                    S.op("pe", lambda e, kk=kk, vA=vA: e.matmul(
                        PS[0][:, 0:CTX], lhsT=vA[:, kk, 512:640], rhs=H[:, kk, T:T + CTX],
                        start=(kk == 0), stop=(kk == NCH - 1)),
                        reads=[("W", sA)] + hk(kk, T, CTX), writes=[("ps", 0)], sig=(kk == NCH - 1))
                S.op("act", lambda e: e.activation(out=Kg[:, TK:TK + CTX], in_=PS[0][:, 0:CTX], func=AF.Identity),
                     reads=[("ps", 0)], writes=[("Kg",)])
                for b0 in range(0, 19, 8):
                    nb = min(8, 19 - b0)
                    vb = 6 + (b0 // 8) % 2
                    psv = PS[vb].rearrange("p (b d) -> p b d", d=64)
                    for bi in range(nb):
                        blk = b0 + bi
                        c0 = 128 * blk if blk < 17 else T + 128 * (blk - 17)
                        for kk in range(NCH):
                            S.op("pe", lambda e, psv=psv, bi=bi, kk=kk, c0=c0, vBv=vBv: e.matmul(
                                psv[:, bi, :], lhsT=H[:, kk, c0:c0 + 128], rhs=vBv[:, kk, :],
                                start=(kk == 0), stop=(kk == NCH - 1)),
                                reads=[("W", sB)] + hk(kk, c0, 128), writes=[("ps", vb)],
                                sig=(kk == NCH - 1 and bi == nb - 1))
                    S.op("act", lambda e, psv=psv, b0=b0, nb=nb: e.activation(
                        out=Vpad[:, b0:b0 + nb, 64:128], in_=psv[:, 0:nb, :], func=AF.Identity),
                        reads=[("ps", vb)], writes=["Vpad"])
                for qt in range(4):
                    ob = cnt["og"] % 2
                    cnt["og"] += 1
                    for qb in range(4):
                        i = 4 * qt + qb
                        for cc in range(2):
                            sb = cnt["st"] % 2
                            cnt["st"] += 1
                            stv = PSALL[:, sb * 1536:sb * 1536 + 1280].rearrange("p (h k q) -> p h k q", h=2, k=5)
                            stkeys = [("ps", sb * 3 + z) for z in range(3)]
                            k0 = 1 if i == 0 else 0
                            srcs = []
                            for kbi in range(k0, 5):
                                if kbi < 3:
                                    blk = i - 1 + kbi
                                    srcs.append((kbi, 128 * blk, blk))
                                else:
                                    srcs.append((kbi, TK + 128 * (kbi - 3), 17 + kbi - 3))
                            nmm = 2 * len(srcs)
                            z = 0
                            for h in range(2):
                                for (kbi, kc, vbk) in srcs:
                                    z += 1
                                    S.op("pe", lambda e, stv=stv, h=h, kbi=kbi, kc=kc, cc=cc, i=i: e.matmul(
                                        stv[:, h, kbi, :], lhsT=Kg[h * 64:(h + 1) * 64, kc:kc + 128],
                                        rhs=Qg[h * 64:(h + 1) * 64, cc, 128 * i:128 * i + 128], start=True, stop=True),
                                        reads=[("Kg",), ("Qg", cc)], writes=stkeys, sig=(z == nmm))
                            pb = sb
                            S.op("act", lambda e, stv=stv, pb=pb, k0=k0: e.activation(
                                out=PT[pb][:, :, k0:5, :], in_=stv[:, :, k0:5, :], func=AF.Exp, scale=0.125),
                                reads=stkeys, writes=[("PT", pb)])
                            if i > 0:
                                S.op("pool", lambda e, pb=pb: e.tensor_tensor(
                                    out=PT[pb][:, :, 0, :], in0=PT[pb][:, :, 0, :], in1=mask_lo, op=ALU.mult),
                                    reads=[("PT", pb), "cst"], writes=[("PT", pb)])
                            S.op("pool", lambda e, pb=pb: e.tensor_tensor(
                                out=PT[pb][:, :, 2, :], in0=PT[pb][:, :, 2, :], in1=mask_hi, op=ALU.mult),
                                reads=[("PT", pb), "cst"], writes=[("PT", pb)])
                            odb = cnt["od"] % 2
                            cnt["od"] += 1
                            o_ps = PS[6][:, (2 * odb) * 128:(2 * odb + 1) * 128]
                            d_ps = PS[6][:, (2 * odb + 1) * 128:(2 * odb + 2) * 128]
                            for (tgt, is_den) in ((o_ps, False), (d_ps, True)):
                                z = 0
                                for h in range(2):
                                    c_lo = 64 if h == 0 else 0
                                    for (kbi, kc, vbk) in srcs:
                                        z += 1
                                        if is_den:
                                            lh = onespad[:, c_lo:c_lo + 128]
                                        else:
                                            lh = Vpad[:, vbk, c_lo:c_lo + 128]
                                        S.op("pe", lambda e, tgt=tgt, lh=lh, pb=pb, h=h, kbi=kbi, z=z, nmm=nmm: e.matmul(
                                            tgt, lhsT=lh, rhs=PT[pb][:, h, kbi, :], start=(z == 1), stop=(z == nmm)),
                                            reads=[("PT", pb), "Vpad", "cst"], writes=[("ps6", odb, is_den)],
                                            sig=(z == nmm))
                            S.op("dve", lambda e, d_ps=d_ps, odb=odb, g=g, cc=cc: e.tensor_scalar(
                                out=rden[odb][:], in0=d_ps, scalar1=esink[:, 2 * g + cc:2 * g + cc + 1], scalar2=None,
                                op0=ALU.add), reads=[("ps6", odb, True), "esink"], writes=[("rden", odb)])
                            S.op("dve", lambda e, odb=odb: e.reciprocal(out=rden[odb][:], in_=rden[odb][:]),
                                 reads=[("rden", odb)], writes=[("rden", odb)])
                            S.op("dve", lambda e, o_ps=o_ps, odb=odb, ob=ob, cc=cc, qb=qb: e.tensor_tensor(
                                out=Og[ob][:, cc, qb * 128:(qb + 1) * 128], in0=o_ps, in1=rden[odb][:], op=ALU.mult),
                                reads=[("ps6", odb, False), ("rden", odb)], writes=[("Og", ob, cc)])
                    for d in range(NCH):
                        for cc in range(2):
                            S.op("pe", lambda e, d=d, cc=cc, ob=ob, vBo=vBo: e.matmul(
                                PS[7][:, 0:512], lhsT=vBo[:, cc, d * 128:(d + 1) * 128], rhs=Og[ob][:, cc, :],
                                start=(cc == 0), stop=(cc == 1)),
                                reads=[("W", sB), ("Og", ob, cc)], writes=[("ps", 7)], sig=(cc == 1))
                        S.op("dve", lambda e, d=d, qt=qt: e.scalar_tensor_tensor(
                            out=X[:, d, qt * 512:(qt + 1) * 512], in0=PS[7][:, 0:512], scalar=tabG[:, 1, d, 0:1],
                            in1=X[:, d, qt * 512:(qt + 1) * 512], op0=ALU.mult, op1=ALU.add),
                            reads=[("ps", 7), ("tabG", 1)] + xk(0, d, qt * 512, 512), writes=xk(0, d, qt * 512, 512))
            S.barrier()
            A.reset(m)

        H = A.alloc([128, NCH, TT], BF16)
        ada_layer(0)
        ffn(0, 0, TILES_A, H)
        if stop_after not in ("x_ffn1_0",):
            conv_phase(0, H)
        if stop_after not in ("x_ffn1_0", "x_mix_0"):
            ffn(0, 1, TILES_A, H)
        if stop_after not in ("x_ffn1_0", "x_mix_0", "x_out_0"):
            ada_layer(1)
            ffn(1, 0, TILES_A, H)
        if stop_after not in ("x_ffn1_0", "x_mix_0", "x_out_0", "x_ffn1_1"):
            attn_phase(1, H)
        if stop_after not in ("x_ffn1_0", "x_mix_0", "x_out_0", "x_ffn1_1", "x_mix_1"):
            ffn(1, 1, TILES_OWN, H)

        def final_out(do_norm):
            S.barrier()
            m = A.mark()
            A.reset(0)
            stg = [A.alloc([128, NCH, 512], F32) for _ in range(2)]
            sq = [A.alloc([128, 512], F32) for _ in range(3)]
            rstd = [A.alloc([128, 512], F32) for _ in range(2)]
            rsq = [A.alloc([128, 512], F32) for _ in range(2)]
            oT_v = outT.rearrange("(c p) t -> p c t", p=128)
            toks = []
            n_sq = 0
            for ti, (a, n, w) in enumerate(TILES_OWN):
                b = ti % 2
                if do_norm:
                    msb = 6 + b
                    ms = PS[msb][:, 0:n]
                    for c in range(NCH):
                        q = n_sq % 3
                        n_sq += 1
                        S.op("act", lambda e, q=q, c=c, a=a, n=n: e.activation(
                            out=sq[q][:, 0:n], in_=X[:, c, a:a + n], func=AF.Square),
                            reads=xk(0, c, a, n), writes=[("sq", q)])
                        S.op("pe", lambda e, q=q, c=c, ms=ms, n=n: e.matmul(
                            ms, lhsT=onesf[:], rhs=sq[q][:, 0:n], start=(c == 0), stop=(c == NCH - 1)),
                            reads=[("sq", q), "onesf"], writes=[("ps", msb)], sig=True)
                    S.op("act", lambda e, b=b, ms=ms, n=n: e.activation(
                        out=rsq[b][:, 0:n], in_=ms, func=AF.Sqrt, bias=epsb[:, 0:1], scale=1.0),
                        reads=[("ps", msb), "epsb"], writes=[("rsq", b)])
                    S.op("dve", lambda e, b=b, n=n: e.reciprocal(out=rstd[b][:, 0:n], in_=rsq[b][:, 0:n]),
                         reads=[("rsq", b)], writes=[("rstd", b)])
                    for c in range(NCH):
                        S.op("dve", lambda e, c=c, a=a, n=n, b=b: e.scalar_tensor_tensor(
                            out=stg[b][:, c, 0:n], in0=X[:, c, a:a + n], scalar=P("final_g", c, c + 1),
                            in1=rstd[b][:, 0:n], op0=ALU.mult, op1=ALU.mult),
                            reads=xk(0, c, a, n) + [("rstd", b), "prm"], writes=[("stg", b)],
                            sig=True)
                else:
                    for c in range(NCH):
                        S.op("act", lambda e, c=c, a=a, n=n, b=b: e.activation(
                            out=stg[b][:, c, 0:n], in_=X[:, c, a:a + n], func=AF.Identity),
                            reads=xk(0, c, a, n), writes=[("stg", b)])
                toks.append(S.dma("sp", [lambda e, b=b, a=a, n=n: e.dma_start(
                    out=oT_v[:, :, a:a + n], in_=stg[b][:, :, 0:n])], f"st{b}", reads=[("stg", b)]))
            S.wait_tokens("sp", toks)
            A.reset(m)

        final_out(stop_after is None)

        with nc.Block() as block:
            @block.tensor
            def _(e):
                S.emit("pe", e)

            @block.scalar
            def _(e):
                S.emit("act", e)

            @block.vector
            def _(e):
                S.emit("dve", e)

            @block.gpsimd
            def _(e):
                S.emit("pool", e)

            @block.sync
            def _(e):
                S.emit("sp", e)
    return nc


def _partner_cols(n):
    j = np.arange(n)
    d = j % 64
    pd = np.where((d % 32) < 16, d + 16, d - 16)
    return j - d + pd


def prepare_inputs(inp):
    f = lambda a: np.ascontiguousarray(np.asarray(a, dtype=np.float32))
    x, c, ctx, c_ctx = f(inp["x"]), f(inp["c"]), f(inp["ctx"]), f(inp["c_ctx"])
    w_qkv = f(inp["attn_w_qkv"])[0]
    chunk = lambda v: np.ascontiguousarray(v.reshape(-1, 128).T)
    shared = {
        "ada_w": f(inp["ada_w"]), "ffn1_wi": f(inp["ffn1_wi"]), "ffn1_wo": f(inp["ffn1_wo"]),
        "ffn2_wi": f(inp["ffn2_wi"]), "ffn2_wo": f(inp["ffn2_wo"]),
        "pw1_w": f(inp["conv_pw1_w"])[0], "pw2_w": f(inp["conv_pw2_w"])[0],
        "w_v": np.ascontiguousarray(w_qkv[:, 1280:1536]), "w_o": f(inp["attn_w_o"])[0],
    }
    watt = np.zeros((D, 4 * 768), np.float32)
    for g in range(4):
        q = w_qkv[:, 256 * g:256 * g + 256]
        k = w_qkv[:, 1024 + 64 * g:1024 + 64 * g + 64]
        kd = np.concatenate([k, k], axis=1)
        o = 768 * g
        watt[:, o:o + 256] = q
        watt[:, o + 256:o + 512] = q[:, _partner_cols(256)]
        watt[:, o + 512:o + 640] = kd
        watt[:, o + 640:o + 768] = kd[:, _partner_cols(128)]
    shared["w_att"] = watt
    cst = np.zeros((128, NCST), np.float32)
    kk = np.arange(128)[:, None]
    qq = np.arange(128)[None, :]
    lo = (kk >= qq).astype(np.float32)
    hi = (kk <= qq).astype(np.float32)
    cst[:, 0:128] = lo
    cst[:, 128:256] = lo
    cst[:, 256:384] = hi
    cst[:, 384:512] = hi
    cst[:, 512 + 64:512 + 128] = 1.0
    shared["cst"] = cst
    p = np.arange(128)
    d = p % 64
    ax = d // 32
    hh = (d % 32) // 16
    fr = d % 16
    inv = (np.float32(10000.0) ** (-np.arange(16, dtype=np.float32) / np.float32(16))).astype(np.float32)
    in_maps = []
    for core in range(8):
        b, hf = core // 2, core % 2
        idx = np.arange(T) if hf == 0 else (SEQ - 1 - np.arange(T))
        m = dict(shared)
        m["xT"] = np.ascontiguousarray(x[b, idx, :].T)
        cb = ctx[b] if hf == 0 else ctx[b, ::-1]
        m["cT"] = np.ascontiguousarray(cb.T)
        prm = np.zeros((128, NPRM), np.float32)

        def put(name, arr):
            o, w = PRM[name]
            assert arr.shape == (128, w), (name, arr.shape)
            prm[:, o:o + w] = arr
        cc = np.stack([chunk(c[b]), chunk(c_ctx)], axis=2).reshape(128, 16)
        put("cc", cc)
        put("ada_b", np.concatenate([chunk(f(inp["ada_b"])[l]) for l in range(2)], axis=1))
        put("norm_g", np.concatenate([chunk(f(inp["norm_g"])[l, j]) for l in range(2) for j in range(3)], axis=1))
        put("pw1_b", chunk(f(inp["conv_pw1_b"])[0]))
        dw = f(inp["conv_dw_w"])[0]
        if hf == 1:
            dw = dw[::-1]
        put("dw_w", np.ascontiguousarray(dw.T.reshape(NCH, 128, CW).transpose(1, 0, 2)).reshape(128, NCH * CW))
        put("dw_b", chunk(f(inp["conv_dw_b"])[0]))
        put("ln_g", chunk(f(inp["conv_ln_g"])[0]))
        put("ln_b", chunk(f(inp["conv_ln_b"])[0]))
        put("pw2_b", chunk(f(inp["conv_pw2_b"])[0]))
        sk = f(inp["attn_sink"])[0]
        put("sink", np.stack([np.where(p >= 64, sk[2 * cch + 1], sk[2 * cch]) for cch in range(NCH)], axis=1).astype(np.float32))
        put("final_g", chunk(f(inp["final_g"])))
        m["prm"] = prm
        row = (idx // 64).astype(np.float32)
        col = (idx % 64).astype(np.float32)
        pos = np.where(ax[:, None] == 0, row[None, :], col[None, :]).astype(np.float32)
        ang = (pos * inv[fr][:, None]).astype(np.float32)
        cs = np.cos(ang).astype(np.float32)
        sn = np.sin(ang).astype(np.float32)
        sn = np.where(hh[:, None] == 0, -sn, sn).astype(np.float32)
        m["rope"] = np.ascontiguousarray(np.stack([cs, sn], axis=0))
        in_maps.append(m)
    return in_maps


def assemble(results):
    out = np.zeros((4, SEQ, D), np.float32)
    for core in range(8):
        b, hf = core // 2, core % 2
        idx = np.arange(OWN) if hf == 0 else (SEQ - 1 - np.arange(OWN))
        out[b, idx, :] = np.asarray(results[core]["outT"]).T
    return out


def kernel(**inputs):
    nc = build_program()
    in_maps = prepare_inputs(inputs)
    res = run_bass_kernel_spmd(nc, in_maps, core_ids=list(range(8)))
    return assemble(res.results)
```

```python
import numpy as np
import concourse.bass as bass
import concourse.mybir as mybir
from concourse.bass_utils import run_bass_kernel_spmd

F32 = mybir.dt.float32
BF16 = mybir.dt.bfloat16
AF = mybir.ActivationFunctionType
ALU = mybir.AluOpType

D = 1024
NCH = 8
SEQ = 4096
OWN = 2048
T = 2200
TK = 2176
CTX = 256
TT = T + CTX
DFF = 2816
NFC = 22
CW = 31
EPS = 1e-6
SLOT = 6144
import os
ATT_PIPE = int(os.environ.get('ATT_PIPE', '0'))
ATT_MASK = os.environ.get('ATT_MASK', 'pool')
SAME_ENG = int(os.environ.get('SAME_ENG', '1'))
ONLY_ATTN = int(os.environ.get('ONLY_ATTN', '0'))
NSLOT = 4

TILES_A = [(i * 440, 440, 0) for i in range(5)] + [(T, 256, 1)]
TILES_OWN = [(i * 512, 512, 0) for i in range(4)]

PRM = {}
_o = 0
for _n, _w in [("cc", 16), ("ada_b", 144), ("norm_g", 48), ("pw1_b", 16), ("dw_w", 248), ("dw_b", 8),
               ("ln_g", 8), ("ln_b", 8), ("pw2_b", 8), ("sink", 8), ("final_g", 8)]:
    PRM[_n] = (_o, _w)
    _o += _w
NPRM = _o
NCST = 2 * 256 + 192 + 128

_XB = sorted(set([i * 440 for i in range(6)] + [i * 512 for i in range(5)] + [T]))


def _segs(a, b):
    return [i for i in range(len(_XB) - 1) if _XB[i] < b and _XB[i + 1] > a]


class Sched:
    CE = ("pe", "act", "dve", "pool")
    NEPOCH = 16

    def __init__(self, nc, sems):
        self.nc = nc
        self.sems = sems
        self.cnt = {k: 0 for k in sems}
        self.prog = {e: [] for e in ("pe", "act", "dve", "pool", "sp")}
        self.last_w = {}
        self.readers = {}
        self.waited = {e: {} for e in self.prog}
        self.pending = {e: False for e in self.prog}
        self.epoch = 0

    def ek(self, eng):
        return f"{eng}@{self.epoch}"

    def _live(self, k):
        if "@" in k:
            return int(k.split("@")[1]) == self.epoch
        return True

    def _collect(self, eng, reads, writes):
        waits = {}
        own = self.ek(eng) if eng in self.CE else None

        def need(tok, raw=True):
            if tok is None:
                return
            k, v = tok
            if not self._live(k):
                return
            if k == own and (eng == "pe" or (not SAME_ENG and not raw)):
                return
            if waits.get(k, 0) < v:
                waits[k] = v
        for b in reads:
            need(self.last_w.get(b))
        for b in writes:
            need(self.last_w.get(b), False)
            for t in self.readers.get(b, ()):
                need(t, False)
        out = []
        wd = self.waited[eng]
        for k, v in waits.items():
            if wd.get(k, 0) < v:
                wd[k] = v
                out.append((k, v))
        return out

    def _record(self, tok, reads, writes):
        for b in reads:
            self.readers.setdefault(b, []).append(tok)
        for b in writes:
            self.last_w[b] = tok
            self.readers[b] = []

    def op(self, eng, fn, reads=(), writes=(), sig=True):
        waits = self._collect(eng, reads, writes)
        k = self.ek(eng)
        if sig:
            self.cnt[k] += 1
            tok = (k, self.cnt[k])
            self.pending[eng] = False
        else:
            tok = (k, self.cnt[k] + 1)
            self.pending[eng] = True
        self._record(tok, reads, writes)
        self.prog[eng].append((waits, [fn], [(k, 1)] if sig else []))

    def dma(self, eng, fns, chan, reads=(), writes=()):
        waits = self._collect(eng, reads, writes)
        self.cnt[chan] += 16 * len(fns)
        tok = (chan, self.cnt[chan])
        self._record(tok, reads, writes)
        self.prog[eng].append((waits, list(fns), [(chan, 16)] * len(fns)))
        return tok

    def wait_tokens(self, eng, toks):
        wl = []
        for k, v in toks:
            if self.waited[eng].get(k, 0) < v:
                self.waited[eng][k] = v
                wl.append((k, v))
        self.prog[eng].append((wl, [], []))

    def barrier(self):
        for e in self.CE:
            assert not self.pending[e], e
        toks = [(self.ek(e), self.cnt[self.ek(e)]) for e in self.CE if self.cnt[self.ek(e)] > 0]
        for e in ("pe", "act", "dve", "pool", "sp"):
            self.wait_tokens(e, [t for t in toks if t[0] != self.ek(e)])
        self.epoch += 1
        assert self.epoch < self.NEPOCH

    def emit(self, eng, e):
        assert not self.pending[eng], eng
        for waits, fns, incs in self.prog[eng]:
            for k, v in waits:
                e.wait_ge(self.sems[k], v)
            for i, fn in enumerate(fns):
                ins = fn(e)
                if i < len(incs):
                    ins.then_inc(self.sems[incs[i][0]], incs[i][1])


class Arena:
    def __init__(self, ap_f32, nbytes):
        self.ap = ap_f32
        self.n = nbytes
        self.off = 0

    def mark(self):
        return self.off

    def reset(self, m):
        self.off = m

    def alloc(self, shape, dt):
        es = 2 if dt == BF16 else 4
        free = 1
        for s in shape[1:]:
            free *= s
        nb = (free * es + 31) // 32 * 32
        assert self.off + nb <= self.n, ("arena overflow", self.off, nb, self.n)
        w0 = self.off // 4
        v = self.ap[:, w0:w0 + nb // 4]
        self.off += nb
        if dt == BF16:
            v = v.bitcast(BF16)
        v = v[:, 0:free]
        if len(shape) == 3:
            v = v.rearrange("p (a b) -> p a b", a=shape[1])
        elif len(shape) == 4:
            v = v.rearrange("p (a b c) -> p a b c", a=shape[1], b=shape[2])
        return v


def build_program(stop_after=None):
    nc = bass.Bass("TRN2", target_bir_lowering=False)
    dr = {}

    def din(name, shape):
        dr[name] = nc.dram_tensor(name, shape, F32, kind="ExternalInput").ap()
        return dr[name]

    xT = din("xT", [D, T])
    cT = din("cT", [D, CTX])
    prm_d = din("prm", [128, NPRM])
    cst_d = din("cst", [128, NCST])
    rope_d = din("rope", [2, 128, T])
    ada_w = din("ada_w", [2, D, 9 * D])
    f_wi = [din("ffn1_wi", [2, D, 2 * DFF]), din("ffn2_wi", [2, D, 2 * DFF])]
    f_wo = [din("ffn1_wo", [2, DFF, D]), din("ffn2_wo", [2, DFF, D])]
    pw1_w = din("pw1_w", [D, 2 * D])
    pw2_w = din("pw2_w", [D, D])
    watt = din("w_att", [D, 4 * 768])
    wv_d = din("w_v", [D, 256])
    wo_d = din("w_o", [D, D])
    outT = nc.dram_tensor("outT", [D, OWN], F32, kind="ExternalOutput").ap()

    semkeys = [f"{e}@{i}" for e in Sched.CE for i in range(Sched.NEPOCH)] + ["ldp", "ldc", "ldxc", "st0", "st1", "rp0", "rp1"] + \
              [f"ldx{i}" for i in range(5)] + [f"w{i}" for i in range(NSLOT)]

    import contextlib
    es = contextlib.ExitStack()
    with es:
        sems = {k: es.enter_context(nc.semaphore(k)) for k in semkeys}
        X = es.enter_context(nc.sbuf_tensor("X", [128, NCH, T], F32))
        XC = es.enter_context(nc.sbuf_tensor("XC", [128, NCH, CTX], F32))
        Wr = es.enter_context(nc.sbuf_tensor("Wr", [128, NSLOT, SLOT], BF16))
        prm = es.enter_context(nc.sbuf_tensor("prm_sb", [128, NPRM], F32))
        cst = es.enter_context(nc.sbuf_tensor("cst_sb", [128, NCST], BF16))
        mod = es.enter_context(nc.sbuf_tensor("mod", [128, 2, 72, 2], F32))
        tabA = es.enter_context(nc.sbuf_tensor("tabA", [128, 3, NCH, 2], F32))
        tabG = es.enter_context(nc.sbuf_tensor("tabG", [128, 3, NCH, 2], F32))
        tabGb = es.enter_context(nc.sbuf_tensor("tabGb", [128, NCH, 2], F32))
        scT = es.enter_context(nc.sbuf_tensor("scT", [128, NCH, 2], BF16))
        onesf = es.enter_context(nc.sbuf_tensor("onesf", [128, 128], F32))
        esink = es.enter_context(nc.sbuf_tensor("esink", [128, NCH], F32))
        epsb = es.enter_context(nc.sbuf_tensor("epsb", [128, 1], F32))
        ARENA_BYTES = 212863 - (NCH * T * 4 + NCH * CTX * 4 + NSLOT * SLOT * 2 + NPRM * 4 + NCST * 2
                                + 2 * 72 * 2 * 4 + 2 * 3 * NCH * 2 * 4 + NCH * 2 * 4 + NCH * 2 * 2
                                + 128 * 4 + NCH * 4) - 1024
        ARENA_BYTES = ARENA_BYTES // 64 * 64
        ar_t = es.enter_context(nc.sbuf_tensor("arena", [128, ARENA_BYTES // 4], F32))
        PSALL = es.enter_context(nc.psum_tensor("psall", [128, 4096], F32))
        PS = [PSALL[:, i * 512:(i + 1) * 512] for i in range(8)]
        S = Sched(nc, sems)
        A = Arena(ar_t, ARENA_BYTES)

        def P(name, a=0, b=None):
            o, w = PRM[name]
            b = w if b is None else b
            return prm[:, o + a:o + b]

        def xbuf(which):
            return X if which == 0 else XC

        def xk(which, c, a, n):
            if which == 1:
                return [("XC", c)]
            return [("X", c, s) for s in _segs(a, a + n)]

        def hk(c, a, n):
            if a >= T:
                return [("H", c, "c")]
            return [("H", c, s) for s in _segs(a, a + n)]

        S.dma("sp", [lambda e: e.dma_start(out=prm[:], in_=prm_d[:, :])], "ldp", writes=["prm"])
        S.dma("pool", [lambda e: e.dma_start(out=cst[:], in_=cst_d[:, :])], "ldc", writes=["cst"])
        xT_v = xT.rearrange("(c p) t -> p c t", p=128)
        cT_v = cT.rearrange("(c p) t -> p c t", p=128)
        for i in range(5):
            S.dma("sp", [lambda e, i=i: e.dma_start(out=X[:, :, i * 440:(i + 1) * 440],
                                                   in_=xT_v[:, :, i * 440:(i + 1) * 440])],
                  f"ldx{i}", writes=[k for c in range(NCH) for k in xk(0, c, i * 440, 440)])
        S.dma("sp", [lambda e: e.dma_start(out=XC[:], in_=cT_v)], "ldxc",
              writes=[("XC", c) for c in range(NCH)])
        S.op("dve", lambda e: e.memset(onesf[:], 1.0 / D), writes=["onesf"])
        S.op("dve", lambda e: e.memset(epsb[:], EPS), writes=["epsb"])
        S.op("act", lambda e: e.activation(out=scT[:], in_=P("cc").rearrange("p (k w) -> p k w", w=2),
                                           func=AF.Silu), reads=["prm"], writes=["scT"])
        S.op("act", lambda e: e.activation(out=esink[:], in_=P("sink"), func=AF.Exp),
             reads=["prm"], writes=["esink"])

        ring = {"next": 0}

        def slot_alloc():
            s = ring["next"]
            ring["next"] = (s + 1) % NSLOT
            return s

        def load_slot(s, fns):
            S.dma("pool", fns, f"w{s}", writes=[("W", s)])

        def ada_layer(l):
            aw = ada_w[l].rearrange("(k p) n -> p k n", p=128)
            for u in range(12):
                s = slot_alloc()
                wv = Wr[:, s, :].rearrange("p (k n) -> p k n", k=NCH)
                load_slot(s, [lambda e, u=u, wv=wv: e.dma_start(out=wv, in_=aw[:, :, u * 768:(u + 1) * 768])])
                ps = PS[u % 2][:, 0:12].rearrange("p (a b) -> p a b", b=2)
                for oc in range(6):
                    for k in range(NCH):
                        S.op("pe", lambda e, oc=oc, k=k, wv=wv, ps=ps: e.matmul(
                            ps[:, oc, :], lhsT=wv[:, k, oc * 128:(oc + 1) * 128], rhs=scT[:, k, :],
                            start=(k == 0), stop=(k == NCH - 1)),
                            reads=[("W", s), "scT"], writes=[("ps", u % 2)], sig=(k == NCH - 1))
                ab = P("ada_b", l * 72 + u * 6, l * 72 + u * 6 + 6)
                S.op("dve", lambda e, ps=ps, ab=ab, u=u: e.tensor_tensor(
                    out=mod[:, l, u * 6:(u + 1) * 6, :], in0=ps,
                    in1=ab.unsqueeze(2).to_broadcast([128, 6, 2]), op=ALU.add),
                    reads=[("ps", u % 2), "prm"], writes=[("mod", l)])
            for j in range(3):
                g = P("norm_g", (l * 3 + j) * 8, (l * 3 + j) * 8 + 8)
                S.op("dve", lambda e, j=j, g=g: e.scalar_tensor_tensor(
                    out=tabA[:, j], in0=mod[:, l, (3 * j + 1) * 8:(3 * j + 2) * 8, :], scalar=1.0,
                    in1=g.unsqueeze(2).to_broadcast([128, NCH, 2]), op0=ALU.add, op1=ALU.mult),
                    reads=[("mod", l), "prm"], writes=[("tabA", j)])
                S.op("dve", lambda e, j=j: e.tensor_scalar(
                    out=tabG[:, j], in0=mod[:, l, (3 * j + 2) * 8:(3 * j + 3) * 8, :],
                    scalar1=(1.0 if j == 1 else 0.5), scalar2=None, op0=ALU.mult),
                    reads=[("mod", l)], writes=[("tabG", j)])
            if l == 0:
                S.op("dve", lambda e: e.tensor_tensor(
                    out=tabGb[:], in0=tabG[:, 1], in1=P("pw2_b").unsqueeze(2).to_broadcast([128, NCH, 2]),
                    op=ALU.mult), reads=[("tabG", 1), "prm"], writes=["tabGb"])

        def tabB(l, j):
            return mod[:, l, (3 * j) * 8:(3 * j + 1) * 8, :]

        def prepass(l, j, tiles, H):
            m = A.mark()
            sq = [A.alloc([128, 512], F32) for _ in range(3)]
            rstd = [A.alloc([128, 512], F32) for _ in range(2)]
            rsq = [A.alloc([128, 512], F32) for _ in range(2)]
            tmp = [A.alloc([128, 512], F32) for _ in range(3)]
            n_sq = 0
            n_tmp = 0
            for ti, (a, n, w) in enumerate(tiles):
                xb = xbuf(w)
                xa = a - T if w == 1 else a
                msb = 6 + (ti % 2)
                ms = PS[msb][:, 0:n]
                for c in range(NCH):
                    q = n_sq % 3
                    n_sq += 1
                    S.op("act", lambda e, q=q, c=c, xb=xb, xa=xa, n=n: e.activation(
                        out=sq[q][:, 0:n], in_=xb[:, c, xa:xa + n], func=AF.Square),
                        reads=xk(w, c, xa, n), writes=[("sq", q)])
                    S.op("pe", lambda e, q=q, c=c, ms=ms, n=n: e.matmul(
                        ms, lhsT=onesf[:], rhs=sq[q][:, 0:n], start=(c == 0), stop=(c == NCH - 1)),
                        reads=[("sq", q), "onesf"], writes=[("ps", msb)], sig=True)
                r = ti % 2
                S.op("act", lambda e, r=r, ms=ms, n=n: e.activation(
                    out=rsq[r][:, 0:n], in_=ms, func=AF.Sqrt, bias=epsb[:, 0:1], scale=1.0),
                    reads=[("ps", msb), "epsb"], writes=[("rsq", r)])
                S.op("dve", lambda e, r=r, n=n: e.reciprocal(out=rstd[r][:, 0:n], in_=rsq[r][:, 0:n]),
                     reads=[("rsq", r)], writes=[("rstd", r)])
                for c in range(NCH):
                    q = n_tmp % 3
                    n_tmp += 1
                    S.op("dve", lambda e, q=q, c=c, xb=xb, xa=xa, n=n, r=r, w=w: e.scalar_tensor_tensor(
                        out=tmp[q][:, 0:n], in0=xb[:, c, xa:xa + n], scalar=tabA[:, j, c, w:w + 1],
                        in1=rstd[r][:, 0:n], op0=ALU.mult, op1=ALU.mult),
                        reads=xk(w, c, xa, n) + [("rstd", r), ("tabA", j)], writes=[("ptmp", q)])
                    S.op("act", lambda e, q=q, c=c, a=a, n=n, w=w: e.activation(
                        out=H[:, c, a:a + n], in_=tmp[q][:, 0:n], func=AF.Identity,
                        bias=tabB(l, j)[:, c, w:w + 1], scale=1.0),
                        reads=[("ptmp", q), ("mod", l)], writes=hk(c, a, n))
            S.barrier()
            A.reset(m)

        def ffn(l, f, tiles, H):
            j = 0 if f == 0 else 2
            prepass(l, j, tiles, H)
            m = A.mark()
            act = [A.alloc([128, 4, 512], BF16) for _ in range(2)]
            sg = [A.alloc([128, 512], F32) for _ in range(2)]
            wi = f_wi[f][l].rearrange("(k p) (two n) -> p k two n", p=128, two=2)
            wo = f_wo[f][l].rearrange("(j p) n -> p j n", p=128)
            sweeps = [[0, 1], [2, 3], [4, 5], [6, 7], [8, 9], [10]]
            unit_slot = {}

            def load_unit(u):
                s = slot_alloc()
                unit_slot[u] = s
                wiv = Wr[:, s, 0:4096].rearrange("p (k two n) -> p k two n", k=NCH, two=2)
                wov = Wr[:, s, 4096:6144].rearrange("p (j n) -> p j n", j=2)
                load_slot(s, [lambda e, wiv=wiv, u=u: e.dma_start(out=wiv[:, :, 0, :], in_=wi[:, :, 0, u * 256:(u + 1) * 256]),
                              lambda e, wiv=wiv, u=u: e.dma_start(out=wiv[:, :, 1, :], in_=wi[:, :, 1, u * 256:(u + 1) * 256]),
                              lambda e, wov=wov, u=u: e.dma_start(out=wov, in_=wo[:, 2 * u:2 * u + 2, :])])

            tasks = []
            for si, sw in enumerate(sweeps):
                for ti in range(len(tiles)):
                    tasks.append((si, ti))
            st = {"sg": 0, "gu": 0, "y": 0}

            def stage_a(k):
                si, ti = tasks[k]
                a, n, w = tiles[ti]
                ab = k % 2
                chunks = [(u, jj) for u in sweeps[si] for jj in range(2)]
                for ci, (u, jj) in enumerate(chunks):
                    s = unit_slot[u]
                    wiv = Wr[:, s, 0:4096].rearrange("p (k two n) -> p k two n", k=NCH, two=2)
                    gb = st["gu"] % 2
                    st["gu"] += 1
                    gps = PS[gb * 2][:, 0:n]
                    ups = PS[gb * 2 + 1][:, 0:n]
                    for two, pp, pb in ((0, gps, gb * 2), (1, ups, gb * 2 + 1)):
                        for kk in range(NCH):
                            S.op("pe", lambda e, pp=pp, wiv=wiv, kk=kk, two=two, jj=jj, a=a, n=n: e.matmul(
                                pp, lhsT=wiv[:, kk, two, jj * 128:(jj + 1) * 128], rhs=H[:, kk, a:a + n],
                                start=(kk == 0), stop=(kk == NCH - 1)),
                                reads=[("W", s)] + hk(kk, a, n), writes=[("ps", pb)], sig=(kk == NCH - 1))
                    q = st["sg"] % 2
                    st["sg"] += 1
                    S.op("act", lambda e, q=q, gps=gps, n=n: e.activation(out=sg[q][:, 0:n], in_=gps, func=AF.Silu),
                         reads=[("ps", gb * 2)], writes=[("sg", q)])
                    S.op("dve", lambda e, q=q, ups=ups, n=n, ab=ab, ci=ci: e.tensor_tensor(
                        out=act[ab][:, ci, 0:n], in0=sg[q][:, 0:n], in1=ups, op=ALU.mult),
                        reads=[("sg", q), ("ps", gb * 2 + 1)], writes=[("act", ab, ci)])

            def stage_b(k):
                si, ti = tasks[k]
                a, n, w = tiles[ti]
                xb = xbuf(w)
                xa = a - T if w == 1 else a
                ab = k % 2
                chunks = [(u, jj) for u in sweeps[si] for jj in range(2)]
                for d in range(NCH):
                    yb = 4 + st["y"] % 2
                    st["y"] += 1
                    yps = PS[yb][:, 0:n]
                    for ci, (u, jj) in enumerate(chunks):
                        s = unit_slot[u]
                        wov = Wr[:, s, 4096:6144].rearrange("p (j n) -> p j n", j=2)
                        S.op("pe", lambda e, yps=yps, wov=wov, jj=jj, d=d, ab=ab, ci=ci, n=n: e.matmul(
                            yps, lhsT=wov[:, jj, d * 128:(d + 1) * 128], rhs=act[ab][:, ci, 0:n],
                            start=(ci == 0), stop=(ci == len(chunks) - 1)),
                            reads=[("W", s), ("act", ab, ci)], writes=[("ps", yb)], sig=(ci == len(chunks) - 1))
                    S.op("dve", lambda e, yps=yps, d=d, xb=xb, xa=xa, n=n, w=w: e.scalar_tensor_tensor(
                        out=xb[:, d, xa:xa + n], in0=yps, scalar=tabG[:, j, d, w:w + 1], in1=xb[:, d, xa:xa + n],
                        op0=ALU.mult, op1=ALU.add),
                        reads=[("ps", yb), ("tabG", j)] + xk(w, d, xa, n), writes=xk(w, d, xa, n))

            loaded = 0

            def ensure_loaded(si):
                nonlocal loaded
                while loaded <= min(si, len(sweeps) - 1):
                    for u in sweeps[loaded]:
                        load_unit(u)
                    loaded += 1
            ensure_loaded(1)
            nt = len(tiles)
            for k in range(len(tasks) + 1):
                if k < len(tasks):
                    si, ti = tasks[k]
                    if ti == 1:
                        ensure_loaded(si + 1)
                    stage_a(k)
                if k >= 1:
                    stage_b(k - 1)
            S.barrier()
            A.reset(m)

        def conv_phase(l, H):
            prepass(l, 1, TILES_A, H)
            m = A.mark()
            p1 = pw1_w.rearrange("(k p) n -> p k n", p=128)
            p2 = pw2_w.rearrange("(k p) n -> p k n", p=128)
            slots = []
            for i in range(4):
                s = slot_alloc()
                slots.append(s)
                v1 = Wr[:, s, 0:4096].rearrange("p (k two n) -> p k two n", k=NCH, two=2)
                v2 = Wr[:, s, 4096:6144].rearrange("p (k n) -> p k n", k=NCH)
                load_slot(s, [lambda e, v1=v1, i=i: e.dma_start(out=v1[:, :, 0, :], in_=p1[:, :, i * 256:(i + 1) * 256]),
                              lambda e, v1=v1, i=i: e.dma_start(out=v1[:, :, 1, :], in_=p1[:, :, D + i * 256:D + (i + 1) * 256]),
                              lambda e, v2=v2, i=i: e.dma_start(out=v2, in_=p2[:, :, i * 256:(i + 1) * 256])])
            UW = 472
            NPE = 16
            U = [A.alloc([128, UW], F32) for _ in range(2)]
            Ub = [A.alloc([128, UW], BF16) for _ in range(2)]
            Dg = A.alloc([128, 2, NPE, 128], BF16)
            V = A.alloc([128, NCH, 440], F32)
            sqv = [A.alloc([128, 440], F32) for _ in range(1)]
            varb = A.alloc([128, 440], F32)
            HN = A.alloc([128, NCH, 440], BF16)
            ident = cst[:, 704:832]
            dww = P("dw_w").rearrange("p (c k) -> p c k", k=CW)
            K1 = CW - NPE
            for ti, (a, n, w) in enumerate(TILES_A):
                xb = xbuf(w)
                xa = a - T if w == 1 else a
                s0, s1 = (0, T) if w == 0 else (T, T + CTX)
                lo, hi = max(a - 15, s0), min(a + n + 15, s1)
                ulo, uhi = lo - (a - 15), hi - (a - 15)
                nu = hi - lo
                for jp in range(4):
                    for jj in range(2):
                        j = 2 * jp + jj
                        s = slots[jp]
                        v1 = Wr[:, s, 0:4096].rearrange("p (k two n) -> p k two n", k=NCH, two=2)
                        for two in range(2):
                            pb = 2 * jj + two
                            pp = PS[pb][:, 0:nu]
                            for kk in range(NCH):
                                S.op("pe", lambda e, pp=pp, v1=v1, kk=kk, two=two, jj=jj, lo=lo, hi=hi: e.matmul(
                                    pp, lhsT=v1[:, kk, two, jj * 128:(jj + 1) * 128], rhs=H[:, kk, lo:hi],
                                    start=(kk == 0), stop=(kk == NCH - 1)),
                                    reads=[("W", s)] + hk(kk, lo, nu), writes=[("ps", pb)], sig=(kk == NCH - 1))
                        if ulo > 0:
                            S.op("pool", lambda e, jj=jj, ulo=ulo: e.memset(U[jj][:, 0:ulo], 0.0), writes=[("U", jj)])
                        if uhi < n + 30:
                            S.op("pool", lambda e, jj=jj, uhi=uhi, n=n: e.memset(U[jj][:, uhi:n + 30], 0.0), writes=[("U", jj)])
                        S.op("act", lambda e, jj=jj, j=j, nu=nu, ulo=ulo, uhi=uhi: e.activation(
                            out=U[jj][:, ulo:uhi], in_=PS[2 * jj + 1][:, 0:nu], func=AF.Sigmoid,
                            bias=P("pw1_b", 8 + j, 9 + j), scale=1.0),
                            reads=[("ps", 2 * jj + 1), "prm"], writes=[("U", jj)])
                        S.op("dve", lambda e, jj=jj, j=j, nu=nu, ulo=ulo, uhi=uhi: e.scalar_tensor_tensor(
                            out=U[jj][:, ulo:uhi], in0=PS[2 * jj][:, 0:nu], scalar=P("pw1_b", j, j + 1),
                            in1=U[jj][:, ulo:uhi], op0=ALU.add, op1=ALU.mult),
                            reads=[("ps", 2 * jj), "prm", ("U", jj)], writes=[("U", jj)])
                    for k in range(K1):
                        for jj in range(2):
                            j = 2 * jp + jj
                            if k == 0:
                                S.op("dve", lambda e, jj=jj, j=j, n=n: e.tensor_scalar(
                                    out=V[:, j, 0:n], in0=U[jj][:, 0:n], scalar1=dww[:, j, 0:1],
                                    scalar2=P("dw_b", j, j + 1), op0=ALU.mult, op1=ALU.add),
                                    reads=[("U", jj), "prm"], writes=[("V", j)])
                            else:
                                S.op("dve", lambda e, jj=jj, j=j, n=n, k=k: e.scalar_tensor_tensor(
                                    out=V[:, j, 0:n], in0=U[jj][:, k:k + n], scalar=dww[:, j, k:k + 1],
                                    in1=V[:, j, 0:n], op0=ALU.mult, op1=ALU.add),
                                    reads=[("U", jj), ("V", j), "prm"], writes=[("V", j)])
                    for jj in range(2):
                        j = 2 * jp + jj
                        S.op("act", lambda e, jj=jj, n=n: e.activation(
                            out=Ub[jj][:, 0:n + 30], in_=U[jj][:, 0:n + 30], func=AF.Identity),
                            reads=[("U", jj)], writes=[("Ub", jj)])
                        for kidx in range(NPE):
                            k = K1 + kidx
                            if kidx % 2 == 0:
                                S.op("pool", lambda e, jj=jj, j=j, k=k, kidx=kidx: e.tensor_scalar(
                                    out=Dg[:, jj, kidx, :], in0=ident, scalar1=dww[:, j, k:k + 1], scalar2=None,
                                    op0=ALU.mult), reads=["cst", "prm"], writes=[("Dg", jj, kidx)])
                            else:
                                S.op("act", lambda e, jj=jj, j=j, k=k, kidx=kidx: e.activation(
                                    out=Dg[:, jj, kidx, :], in_=ident, func=AF.Identity, scale=dww[:, j, k:k + 1]),
                                    reads=["cst", "prm"], writes=[("Dg", jj, kidx)])
                    for jj in range(2):
                        j = 2 * jp + jj
                        for kidx in range(NPE):
                            k = K1 + kidx
                            S.op("pe", lambda e, jj=jj, k=k, kidx=kidx, n=n: e.matmul(
                                PS[4 + jj][:, 0:n], lhsT=Dg[:, jj, kidx, :], rhs=Ub[jj][:, k:k + n],
                                start=(kidx == 0), stop=(kidx == NPE - 1)),
                                reads=[("Dg", jj, kidx), ("Ub", jj)], writes=[("ps", 4 + jj)], sig=(kidx == NPE - 1))
                    for jj in range(2):
                        j = 2 * jp + jj
                        S.op("dve", lambda e, jj=jj, j=j, n=n: e.tensor_tensor(
                            out=V[:, j, 0:n], in0=V[:, j, 0:n], in1=PS[4 + jj][:, 0:n], op=ALU.add),
                            reads=[("V", j), ("ps", 4 + jj)], writes=[("V", j)])
                for j in range(NCH):
                    S.op("pe", lambda e, j=j, n=n: e.matmul(PS[4][:, 0:n], lhsT=onesf[:], rhs=V[:, j, 0:n],
                                                           start=(j == 0), stop=(j == NCH - 1)),
                         reads=[("V", j), "onesf"], writes=[("ps", 4)], sig=True)
                    S.op("act", lambda e, j=j, n=n: e.activation(out=sqv[0][:, 0:n], in_=V[:, j, 0:n], func=AF.Square),
                         reads=[("V", j)], writes=[("sqv", 0)])
                    S.op("pe", lambda e, j=j, n=n: e.matmul(PS[5][:, 0:n], lhsT=onesf[:], rhs=sqv[0][:, 0:n],
                                                           start=(j == 0), stop=(j == NCH - 1)),
                         reads=[("sqv", 0), "onesf"], writes=[("ps", 5)], sig=True)
                S.op("act", lambda e, n=n: e.activation(out=varb[:, 0:n], in_=PS[4][:, 0:n], func=AF.Square),
                     reads=[("ps", 4)], writes=["varb"])
                S.op("dve", lambda e, n=n: e.tensor_tensor(out=varb[:, 0:n], in0=PS[5][:, 0:n], in1=varb[:, 0:n],
                                                           op=ALU.subtract), reads=[("ps", 5), "varb"], writes=["varb"])
                S.op("act", lambda e, n=n: e.activation(out=varb[:, 0:n], in_=varb[:, 0:n], func=AF.Sqrt,
                                                        bias=epsb[:, 0:1], scale=1.0),
                     reads=["varb", "epsb"], writes=["varb"])
                S.op("dve", lambda e, n=n: e.reciprocal(out=varb[:, 0:n], in_=varb[:, 0:n]),
                     reads=["varb"], writes=["varb"])
                for j in range(NCH):
                    S.op("dve", lambda e, j=j, n=n: e.tensor_tensor(
                        out=V[:, j, 0:n], in0=V[:, j, 0:n], in1=PS[4][:, 0:n], op=ALU.subtract),
                        reads=[("V", j), ("ps", 4)], writes=[("V", j)])
                    S.op("pool", lambda e, j=j, n=n: e.tensor_tensor(
                        out=V[:, j, 0:n], in0=V[:, j, 0:n], in1=varb[:, 0:n], op=ALU.mult),
                        reads=[("V", j), "varb"], writes=[("V", j)])
                    S.op("act", lambda e, j=j, n=n: e.activation(
                        out=HN[:, j, 0:n], in_=V[:, j, 0:n], func=AF.Silu,
                        bias=P("ln_b", j, j + 1), scale=P("ln_g", j, j + 1)),
                        reads=[("V", j), "prm"], writes=[("HN", j)])
                for d in range(NCH):
                    s = slots[d // 2]
                    v2 = Wr[:, s, 4096:6144].rearrange("p (k n) -> p k n", k=NCH)
                    yb = 6 + d % 2
                    S.op("pool", lambda e, d=d, xb=xb, xa=xa, n=n, w=w: e.tensor_scalar(
                        out=xb[:, d, xa:xa + n], in0=xb[:, d, xa:xa + n], scalar1=tabGb[:, d, w:w + 1],
                        scalar2=None, op0=ALU.add),
                        reads=xk(w, d, xa, n) + ["tabGb"], writes=xk(w, d, xa, n))
                    for kk in range(NCH):
                        S.op("pe", lambda e, v2=v2, kk=kk, d=d, yb=yb, n=n: e.matmul(
                            PS[yb][:, 0:n], lhsT=v2[:, kk, (d % 2) * 128:(d % 2 + 1) * 128], rhs=HN[:, kk, 0:n],
                            start=(kk == 0), stop=(kk == NCH - 1)),
                            reads=[("W", s), ("HN", kk)], writes=[("ps", yb)], sig=(kk == NCH - 1))
                    S.op("dve", lambda e, d=d, xb=xb, xa=xa, n=n, w=w, yb=yb: e.scalar_tensor_tensor(
                        out=xb[:, d, xa:xa + n], in0=PS[yb][:, 0:n], scalar=tabG[:, 1, d, w:w + 1],
                        in1=xb[:, d, xa:xa + n], op0=ALU.mult, op1=ALU.add),
                        reads=[("ps", yb), ("tabG", 1)] + xk(w, d, xa, n), writes=xk(w, d, xa, n))
            S.barrier()
            A.reset(m)

        def attn_phase(l, H):
            prepass(l, 1, TILES_A, H)
            m = A.mark()
            Qg = A.alloc([128, 2, OWN], BF16)
            Kg = A.alloc([128, TK + CTX], BF16)
            Vpad = A.alloc([128, 19, 192], BF16)
            Og = [A.alloc([128, 2, 512], BF16) for _ in range(2)]
            PT = [A.alloc([128, 2, 5, 128], BF16) for _ in range(2)]
            cosb = A.alloc([128, 440], F32)
            sinb = A.alloc([128, 440], F32)
            t1 = A.alloc([128, 440], F32)
            t2 = A.alloc([128, 440], F32)
            rden = [A.alloc([128, 128], F32) for _ in range(2)]
            mask_lo = cst[:, 0:256].rearrange("p (h q) -> p h q", h=2)
            mask_hi = cst[:, 256:512].rearrange("p (h q) -> p h q", h=2)
            onespad = cst[:, 512:704]
            wa = watt.rearrange("(k p) n -> p k n", p=128)
            wvv = wv_d.rearrange("(k p) n -> p k n", p=128)
            wov = wo_d.rearrange("(j p) n -> p j n", p=128)
            S.op("pool", lambda e: e.memset(Vpad[:], 0.0), writes=["Vpad"])
            cnt = {"st": 0, "od": 0, "og": 0}
            for g in range(4):
                sA = slot_alloc()
                vA = Wr[:, sA, :].rearrange("p (k n) -> p k n", k=NCH)
                load_slot(sA, [lambda e, vA=vA, g=g: e.dma_start(out=vA, in_=wa[:, :, g * 768:(g + 1) * 768])])
                sB = slot_alloc()
                vBo = Wr[:, sB, 0:2048].rearrange("p (j n) -> p j n", j=2)
                vBv = Wr[:, sB, 2048:2560].rearrange("p (k n) -> p k n", k=NCH)
                load_slot(sB, [lambda e, vBo=vBo, g=g: e.dma_start(out=vBo, in_=wov[:, 2 * g:2 * g + 2, :]),
                               lambda e, vBv=vBv, g=g: e.dma_start(out=vBv, in_=wvv[:, :, 64 * g:64 * g + 64])])
                for ti in range(5):
                    a, n = 440 * ti, 440
                    S.dma("sp", [lambda e, a=a, n=n: e.dma_start(out=cosb[:, 0:n], in_=rope_d[0][:, a:a + n]),
                                 lambda e, a=a, n=n: e.dma_start(out=sinb[:, 0:n], in_=rope_d[1][:, a:a + n])],
                          "rp0", writes=["rope"])
                    items = [("q", 0), ("q", 1), ("k", 0)]
                    for ii, (kind, cc) in enumerate(items):
                        if kind == "q":
                            nn = max(0, min(a + n, OWN) - a)
                            base, bsw = cc * 128, 256 + cc * 128
                            dst = Qg[:, cc, a:a + nn] if nn > 0 else None
                            dkey = ("Qg", cc)
                        else:
                            nn = max(0, min(a + n, TK) - a)
                            base, bsw = 512, 640
                            dst = Kg[:, a:a + nn]
                            dkey = ("Kg",)
                        if nn == 0:
                            continue
                        pa, pb = 2 * (ii % 2), 2 * (ii % 2) + 1
                        for (pp, bb) in ((pa, base), (pb, bsw)):
                            for kk in range(NCH):
                                S.op("pe", lambda e, pp=pp, bb=bb, kk=kk, a=a, nn=nn, vA=vA: e.matmul(
                                    PS[pp][:, 0:nn], lhsT=vA[:, kk, bb:bb + 128], rhs=H[:, kk, a:a + nn],
                                    start=(kk == 0), stop=(kk == NCH - 1)),
                                    reads=[("W", sA)] + hk(kk, a, nn), writes=[("ps", pp)], sig=(kk == NCH - 1))
                        S.op("dve", lambda e, pa=pa, nn=nn: e.tensor_tensor(
                            out=t1[:, 0:nn], in0=PS[pa][:, 0:nn], in1=cosb[:, 0:nn], op=ALU.mult),
                            reads=[("ps", pa), "rope"], writes=["t1"])
                        S.op("dve", lambda e, pb=pb, nn=nn: e.tensor_tensor(
                            out=t2[:, 0:nn], in0=PS[pb][:, 0:nn], in1=sinb[:, 0:nn], op=ALU.mult),
                            reads=[("ps", pb), "rope"], writes=["t2"])
                        S.op("pool", lambda e, dst=dst, nn=nn: e.tensor_tensor(
                            out=dst, in0=t1[:, 0:nn], in1=t2[:, 0:nn], op=ALU.add),
                            reads=["t1", "t2"], writes=[dkey])
                for kk in range(NCH):
                    S.op("pe", lambda e, kk=kk, vA=vA: e.matmul(
                        PS[0][:, 0:CTX], lhsT=vA[:, kk, 512:640], rhs=H[:, kk, T:T + CTX],
                        start=(kk == 0), stop=(kk == NCH - 1)),
                        reads=[("W", sA)] + hk(kk, T, CTX), writes=[("ps", 0)], sig=(kk == NCH - 1))
                S.op("act", lambda e: e.activation(out=Kg[:, TK:TK + CTX], in_=PS[0][:, 0:CTX], func=AF.Identity),
                     reads=[("ps", 0)], writes=[("Kg",)])
                for b0 in range(0, 19, 8):
                    nb = min(8, 19 - b0)
                    vb = 6 + (b0 // 8) % 2
                    psv = PS[vb].rearrange("p (b d) -> p b d", d=64)
                    for bi in range(nb):
                        blk = b0 + bi
                        c0 = 128 * blk if blk < 17 else T + 128 * (blk - 17)
                        for kk in range(NCH):
                            S.op("pe", lambda e, psv=psv, bi=bi, kk=kk, c0=c0, vBv=vBv: e.matmul(
                                psv[:, bi, :], lhsT=H[:, kk, c0:c0 + 128], rhs=vBv[:, kk, :],
                                start=(kk == 0), stop=(kk == NCH - 1)),
                                reads=[("W", sB)] + hk(kk, c0, 128), writes=[("ps", vb)],
                                sig=(kk == NCH - 1 and bi == nb - 1))
                    S.op("act", lambda e, psv=psv, b0=b0, nb=nb: e.activation(
                        out=Vpad[:, b0:b0 + nb, 64:128], in_=psv[:, 0:nb, :], func=AF.Identity),
                        reads=[("ps", vb)], writes=["Vpad"])
                its = [(qt, qb, cc) for qt in range(4) for qb in range(4) for cc in range(2)]
                info = {}

                def stage_s(it):
                    qt, qb, cc = its[it]
                    i = 4 * qt + qb
                    sb = cnt["st"] % 2
                    cnt["st"] += 1
                    stv = PSALL[:, sb * 1536:sb * 1536 + 1280].rearrange("p (h k q) -> p h k q", h=2, k=5)
                    stkeys = [("ps", sb * 3 + z) for z in range(3)]
                    k0 = 1 if i == 0 else 0
                    srcs = []
                    for kbi in range(k0, 5):
                        if kbi < 3:
                            blk = i - 1 + kbi
                            srcs.append((kbi, 128 * blk, blk))
                        else:
                            srcs.append((kbi, TK + 128 * (kbi - 3), 17 + kbi - 3))
                    nmm = 2 * len(srcs)
                    z = 0
                    for h in range(2):
                        for (kbi, kc, vbk) in srcs:
                            z += 1
                            S.op("pe", lambda e, stv=stv, h=h, kbi=kbi, kc=kc, cc=cc, i=i: e.matmul(
                                stv[:, h, kbi, :], lhsT=Kg[h * 64:(h + 1) * 64, kc:kc + 128],
                                rhs=Qg[h * 64:(h + 1) * 64, cc, 128 * i:128 * i + 128], start=True, stop=True),
                                reads=[("Kg",), ("Qg", cc)], writes=stkeys, sig=(z == nmm))
                    pb = sb
                    S.op("act", lambda e, stv=stv, pb=pb, k0=k0: e.activation(
                        out=PT[pb][:, :, k0:5, :], in_=stv[:, :, k0:5, :], func=AF.Exp, scale=0.125),
                        reads=stkeys, writes=[("PT", pb)])
                    if i > 0:
                        S.op(ATT_MASK, lambda e, pb=pb: e.tensor_tensor(
                            out=PT[pb][:, :, 0, :], in0=PT[pb][:, :, 0, :], in1=mask_lo, op=ALU.mult),
                            reads=[("PT", pb), "cst"], writes=[("PT", pb)])
                    S.op(ATT_MASK, lambda e, pb=pb: e.tensor_tensor(
                        out=PT[pb][:, :, 2, :], in0=PT[pb][:, :, 2, :], in1=mask_hi, op=ALU.mult),
                        reads=[("PT", pb), "cst"], writes=[("PT", pb)])
                    info[it] = (pb, srcs, nmm)

                def stage_p(it):
                    qt, qb, cc = its[it]
                    pb, srcs, nmm = info.pop(it)
                    ob = (4 * g + qt) % 2
                    odb = cnt["od"] % 2
                    cnt["od"] += 1
                    o_ps = PS[6][:, (2 * odb) * 128:(2 * odb + 1) * 128]
                    d_ps = PS[6][:, (2 * odb + 1) * 128:(2 * odb + 2) * 128]
                    for (tgt, is_den) in ((o_ps, False), (d_ps, True)):
                        z = 0
                        for h in range(2):
                            c_lo = 64 if h == 0 else 0
                            for (kbi, kc, vbk) in srcs:
                                z += 1
                                if is_den:
                                    lh = onespad[:, c_lo:c_lo + 128]
                                else:
                                    lh = Vpad[:, vbk, c_lo:c_lo + 128]
                                S.op("pe", lambda e, tgt=tgt, lh=lh, pb=pb, h=h, kbi=kbi, z=z, nmm=nmm: e.matmul(
                                    tgt, lhsT=lh, rhs=PT[pb][:, h, kbi, :], start=(z == 1), stop=(z == nmm)),
                                    reads=[("PT", pb), "Vpad", "cst"], writes=[("ps6", odb, is_den)],
                                    sig=(z == nmm))
                    S.op("dve", lambda e, d_ps=d_ps, odb=odb, cc=cc, g=g: e.tensor_scalar(
                        out=rden[odb][:], in0=d_ps, scalar1=esink[:, 2 * g + cc:2 * g + cc + 1], scalar2=None,
                        op0=ALU.add), reads=[("ps6", odb, True), "esink"], writes=[("rden", odb)])
                    S.op("dve", lambda e, odb=odb: e.reciprocal(out=rden[odb][:], in_=rden[odb][:]),
                         reads=[("rden", odb)], writes=[("rden", odb)])
                    S.op("dve", lambda e, o_ps=o_ps, odb=odb, ob=ob, cc=cc, qb=qb: e.tensor_tensor(
                        out=Og[ob][:, cc, qb * 128:(qb + 1) * 128], in0=o_ps, in1=rden[odb][:], op=ALU.mult),
                        reads=[("ps6", odb, False), ("rden", odb)], writes=[("Og", ob, cc)])
                    if qb == 3 and cc == 1:
                        for d in range(NCH):
                            for c2 in range(2):
                                S.op("pe", lambda e, d=d, c2=c2, ob=ob, vBo=vBo: e.matmul(
                                    PS[7][:, 0:512], lhsT=vBo[:, c2, d * 128:(d + 1) * 128], rhs=Og[ob][:, c2, :],
                                    start=(c2 == 0), stop=(c2 == 1)),
                                    reads=[("W", sB), ("Og", ob, c2)], writes=[("ps", 7)], sig=(c2 == 1))
                            S.op("dve", lambda e, d=d, qt=qt: e.scalar_tensor_tensor(
                                out=X[:, d, qt * 512:(qt + 1) * 512], in0=PS[7][:, 0:512], scalar=tabG[:, 1, d, 0:1],
                                in1=X[:, d, qt * 512:(qt + 1) * 512], op0=ALU.mult, op1=ALU.add),
                                reads=[("ps", 7), ("tabG", 1)] + xk(0, d, qt * 512, 512),
                                writes=xk(0, d, qt * 512, 512))

                if ATT_PIPE:
                    stage_s(0)
                    for it in range(len(its)):
                        if it + 1 < len(its):
                            stage_s(it + 1)
                        stage_p(it)
                else:
                    for it in range(len(its)):
                        stage_s(it)
                        stage_p(it)
            S.barrier()
            A.reset(m)

        H = A.alloc([128, NCH, TT], BF16)
        if ONLY_ATTN:
            ada_layer(1)
            attn_phase(1, H)
            stop_after = "x_ffn1_0"
        else:
            ada_layer(0)
            ffn(0, 0, TILES_A, H)
        if stop_after not in ("x_ffn1_0",):
            conv_phase(0, H)
        if stop_after not in ("x_ffn1_0", "x_mix_0"):
            ffn(0, 1, TILES_A, H)
        if stop_after not in ("x_ffn1_0", "x_mix_0", "x_out_0"):
            ada_layer(1)
            ffn(1, 0, TILES_A, H)
        if stop_after not in ("x_ffn1_0", "x_mix_0", "x_out_0", "x_ffn1_1"):
            attn_phase(1, H)
        if stop_after not in ("x_ffn1_0", "x_mix_0", "x_out_0", "x_ffn1_1", "x_mix_1"):
            ffn(1, 1, TILES_OWN, H)

        def final_out(do_norm):
            S.barrier()
            m = A.mark()
            A.reset(0)
            stg = [A.alloc([128, NCH, 512], F32) for _ in range(2)]
            sq = [A.alloc([128, 512], F32) for _ in range(3)]
            rstd = [A.alloc([128, 512], F32) for _ in range(2)]
            rsq = [A.alloc([128, 512], F32) for _ in range(2)]
            oT_v = outT.rearrange("(c p) t -> p c t", p=128)
            toks = []
            n_sq = 0
            for ti, (a, n, w) in enumerate(TILES_OWN):
                b = ti % 2
                if do_norm:
                    msb = 6 + b
                    ms = PS[msb][:, 0:n]
                    for c in range(NCH):
                        q = n_sq % 3
                        n_sq += 1
                        S.op("act", lambda e, q=q, c=c, a=a, n=n: e.activation(
                            out=sq[q][:, 0:n], in_=X[:, c, a:a + n], func=AF.Square),
                            reads=xk(0, c, a, n), writes=[("sq", q)])
                        S.op("pe", lambda e, q=q, c=c, ms=ms, n=n: e.matmul(
                            ms, lhsT=onesf[:], rhs=sq[q][:, 0:n], start=(c == 0), stop=(c == NCH - 1)),
                            reads=[("sq", q), "onesf"], writes=[("ps", msb)], sig=True)
                    S.op("act", lambda e, b=b, ms=ms, n=n: e.activation(
                        out=rsq[b][:, 0:n], in_=ms, func=AF.Sqrt, bias=epsb[:, 0:1], scale=1.0),
                        reads=[("ps", msb), "epsb"], writes=[("rsq", b)])
                    S.op("dve", lambda e, b=b, n=n: e.reciprocal(out=rstd[b][:, 0:n], in_=rsq[b][:, 0:n]),
                         reads=[("rsq", b)], writes=[("rstd", b)])
                    for c in range(NCH):
                        S.op("dve", lambda e, c=c, a=a, n=n, b=b: e.scalar_tensor_tensor(
                            out=stg[b][:, c, 0:n], in0=X[:, c, a:a + n], scalar=P("final_g", c, c + 1),
                            in1=rstd[b][:, 0:n], op0=ALU.mult, op1=ALU.mult),
                            reads=xk(0, c, a, n) + [("rstd", b), "prm"], writes=[("stg", b)],
                            sig=True)
                else:
                    for c in range(NCH):
                        S.op("act", lambda e, c=c, a=a, n=n, b=b: e.activation(
                            out=stg[b][:, c, 0:n], in_=X[:, c, a:a + n], func=AF.Identity),
                            reads=xk(0, c, a, n), writes=[("stg", b)])
                toks.append(S.dma("sp", [lambda e, b=b, a=a, n=n: e.dma_start(
                    out=oT_v[:, :, a:a + n], in_=stg[b][:, :, 0:n])], f"st{b}", reads=[("stg", b)]))
            S.wait_tokens("sp", toks)
            A.reset(m)

        final_out(stop_after is None)

        with nc.Block() as block:
            @block.tensor
            def _(e):
                S.emit("pe", e)

            @block.scalar
            def _(e):
                S.emit("act", e)

            @block.vector
            def _(e):
                S.emit("dve", e)

            @block.gpsimd
            def _(e):
                S.emit("pool", e)

            @block.sync
            def _(e):
                S.emit("sp", e)
    return nc


def _partner_cols(n):
    j = np.arange(n)
    d = j % 64
    pd = np.where((d % 32) < 16, d + 16, d - 16)
    return j - d + pd


def prepare_inputs(inp):
    f = lambda a: np.ascontiguousarray(np.asarray(a, dtype=np.float32))
    x, c, ctx, c_ctx = f(inp["x"]), f(inp["c"]), f(inp["ctx"]), f(inp["c_ctx"])
    w_qkv = f(inp["attn_w_qkv"])[0]
    chunk = lambda v: np.ascontiguousarray(v.reshape(-1, 128).T)
    shared = {
        "ada_w": f(inp["ada_w"]), "ffn1_wi": f(inp["ffn1_wi"]), "ffn1_wo": f(inp["ffn1_wo"]),
        "ffn2_wi": f(inp["ffn2_wi"]), "ffn2_wo": f(inp["ffn2_wo"]),
        "pw1_w": f(inp["conv_pw1_w"])[0], "pw2_w": f(inp["conv_pw2_w"])[0],
        "w_v": np.ascontiguousarray(w_qkv[:, 1280:1536]), "w_o": f(inp["attn_w_o"])[0],
    }
    watt = np.zeros((D, 4 * 768), np.float32)
    for g in range(4):
        q = w_qkv[:, 256 * g:256 * g + 256]
        k = w_qkv[:, 1024 + 64 * g:1024 + 64 * g + 64]
        kd = np.concatenate([k, k], axis=1)
        o = 768 * g
        watt[:, o:o + 256] = q
        watt[:, o + 256:o + 512] = q[:, _partner_cols(256)]
        watt[:, o + 512:o + 640] = kd
        watt[:, o + 640:o + 768] = kd[:, _partner_cols(128)]
    shared["w_att"] = watt
    cst = np.zeros((128, NCST), np.float32)
    kk = np.arange(128)[:, None]
    qq = np.arange(128)[None, :]
    lo = (kk >= qq).astype(np.float32)
    hi = (kk <= qq).astype(np.float32)
    cst[:, 0:128] = lo
    cst[:, 128:256] = lo
    cst[:, 256:384] = hi
    cst[:, 384:512] = hi
    cst[:, 512 + 64:512 + 128] = 1.0
    cst[:, 704:832] = np.eye(128, dtype=np.float32)
    shared["cst"] = cst
    p = np.arange(128)
    d = p % 64
    ax = d // 32
    hh = (d % 32) // 16
    fr = d % 16
    inv = (np.float32(10000.0) ** (-np.arange(16, dtype=np.float32) / np.float32(16))).astype(np.float32)
    in_maps = []
    for core in range(8):
        b, hf = core // 2, core % 2
        idx = np.arange(T) if hf == 0 else (SEQ - 1 - np.arange(T))
        m = dict(shared)
        m["xT"] = np.ascontiguousarray(x[b, idx, :].T)
        cb = ctx[b] if hf == 0 else ctx[b, ::-1]
        m["cT"] = np.ascontiguousarray(cb.T)
        prm = np.zeros((128, NPRM), np.float32)

        def put(name, arr):
            o, w = PRM[name]
            assert arr.shape == (128, w), (name, arr.shape)
            prm[:, o:o + w] = arr
        cc = np.stack([chunk(c[b]), chunk(c_ctx)], axis=2).reshape(128, 16)
        put("cc", cc)
        put("ada_b", np.concatenate([chunk(f(inp["ada_b"])[l]) for l in range(2)], axis=1))
        put("norm_g", np.concatenate([chunk(f(inp["norm_g"])[l, j]) for l in range(2) for j in range(3)], axis=1))
        put("pw1_b", chunk(f(inp["conv_pw1_b"])[0]))
        dw = f(inp["conv_dw_w"])[0]
        if hf == 1:
            dw = dw[::-1]
        put("dw_w", np.ascontiguousarray(dw.T.reshape(NCH, 128, CW).transpose(1, 0, 2)).reshape(128, NCH * CW))
        put("dw_b", chunk(f(inp["conv_dw_b"])[0]))
        put("ln_g", chunk(f(inp["conv_ln_g"])[0]))
        put("ln_b", chunk(f(inp["conv_ln_b"])[0]))
        put("pw2_b", chunk(f(inp["conv_pw2_b"])[0]))
        sk = f(inp["attn_sink"])[0]
        put("sink", np.stack([np.where(p >= 64, sk[2 * cch + 1], sk[2 * cch]) for cch in range(NCH)], axis=1).astype(np.float32))
        put("final_g", chunk(f(inp["final_g"])))
        m["prm"] = prm
        row = (idx // 64).astype(np.float32)
        col = (idx % 64).astype(np.float32)
        pos = np.where(ax[:, None] == 0, row[None, :], col[None, :]).astype(np.float32)
        ang = (pos * inv[fr][:, None]).astype(np.float32)
        cs = np.cos(ang).astype(np.float32)
        sn = np.sin(ang).astype(np.float32)
        sn = np.where(hh[:, None] == 0, -sn, sn).astype(np.float32)
        m["rope"] = np.ascontiguousarray(np.stack([cs, sn], axis=0))
        in_maps.append(m)
    return in_maps


def assemble(results):
    out = np.zeros((4, SEQ, D), np.float32)
    for core in range(8):
        b, hf = core // 2, core % 2
        idx = np.arange(OWN) if hf == 0 else (SEQ - 1 - np.arange(OWN))
        out[b, idx, :] = np.asarray(results[core]["outT"]).T
    return out


def kernel(**inputs):
    nc = build_program()
    in_maps = prepare_inputs(inputs)
    res = run_bass_kernel_spmd(nc, in_maps, core_ids=list(range(8)))
    return assemble(res.results)
```

```python
import numpy as np
import concourse.bass as bass
import concourse.mybir as mybir
from concourse.bass_utils import run_bass_kernel_spmd

F32 = mybir.dt.float32
BF16 = mybir.dt.bfloat16
AF = mybir.ActivationFunctionType
ALU = mybir.AluOpType

D = 1024
NCH = 8
SEQ = 4096
OWN = 2048
T = 2200
TK = 2176
CTX = 256
TT = T + CTX
DFF = 2816
NFC = 22
CW = 31
EPS = 1e-6
SLOT = 6144
import os
ATT_PIPE = int(os.environ.get('ATT_PIPE', '0'))
ATT_MASK = os.environ.get('ATT_MASK', 'pool')
SAME_ENG = int(os.environ.get('SAME_ENG', '1'))
ONLY_ATTN = int(os.environ.get('ONLY_ATTN', '0'))
NSLOT = 4

TILES_A = [(i * 440, 440, 0) for i in range(5)] + [(T, 256, 1)]
TILES_OWN = [(i * 512, 512, 0) for i in range(4)]

PRM = {}
_o = 0
for _n, _w in [("cc", 16), ("ada_b", 144), ("norm_g", 48), ("pw1_b", 16), ("dw_w", 248), ("dw_b", 8),
               ("ln_g", 8), ("ln_b", 8), ("pw2_b", 8), ("sink", 8), ("final_g", 8)]:
    PRM[_n] = (_o, _w)
    _o += _w
NPRM = _o
NCST = 2 * 256 + 192 + 128

_XB = sorted(set([i * 440 for i in range(6)] + [i * 512 for i in range(5)] + [T]))


def _segs(a, b):
    return [i for i in range(len(_XB) - 1) if _XB[i] < b and _XB[i + 1] > a]


class Sched:
    CE = ("pe", "act", "dve", "pool")
    NEPOCH = 16

    def __init__(self, nc, sems):
        self.nc = nc
        self.sems = sems
        self.cnt = {k: 0 for k in sems}
        self.ops = []
        self.byeng = {e: [] for e in ("pe", "act", "dve", "pool", "sp")}
        self.last_w = {}
        self.readers = {}
        self.unsig = {e: [] for e in self.byeng}
        self.redirect = {}
        self.epoch = 0
        self.last_sig = {}

    def _deps(self, reads, writes):
        deps = []
        for b in reads:
            t = self.last_w.get(b)
            if t is not None:
                deps.append((t, True))
        for b in writes:
            t = self.last_w.get(b)
            if t is not None:
                deps.append((t, False))
            for t in self.readers.get(b, ()):
                deps.append((t, False))
        return deps

    def _record(self, tok, reads, writes):
        for b in reads:
            self.readers.setdefault(b, []).append(tok)
        for b in writes:
            self.last_w[b] = tok
            self.readers[b] = []

    def _new(self, eng, fns, deps, kind, sig=True, chan=None):
        rec = dict(id=len(self.ops), eng=eng, epoch=self.epoch, fns=fns, deps=deps, kind=kind, sig=sig, chan=chan)
        self.ops.append(rec)
        self.byeng[eng].append(rec)
        return rec

    def op(self, eng, fn, reads=(), writes=(), sig=True):
        rec = self._new(eng, [fn], self._deps(reads, writes), "op", sig=sig)
        if sig:
            for i in self.unsig[eng]:
                self.redirect[i] = rec["id"]
            self.unsig[eng] = []
            self.last_sig[eng] = rec["id"]
        else:
            self.unsig[eng].append(rec["id"])
        self._record(("op", rec["id"]), reads, writes)

    def dma(self, eng, fns, chan, reads=(), writes=()):
        rec = self._new(eng, list(fns), self._deps(reads, writes), "dma", chan=chan)
        self.cnt[chan] += 16 * len(fns)
        tok = ("dma", chan, self.cnt[chan])
        self._record(tok, reads, writes)
        return tok

    def wait_tokens(self, eng, toks):
        self._new(eng, [], [(t, True) for t in toks], "wait")

    def barrier(self):
        for e in self.CE:
            assert not self.unsig[e], e
        toks = {e: ("op", self.last_sig[e]) for e in self.CE
                if e in self.last_sig and self.ops[self.last_sig[e]]["epoch"] == self.epoch}
        for e in ("pe", "act", "dve", "pool", "sp"):
            self.wait_tokens(e, [t for k, t in toks.items() if k != e])
        self.epoch += 1
        assert self.epoch < self.NEPOCH

    def finalize(self):
        for e in self.CE:
            assert not self.unsig[e], e

        def resolve(tok, rec, raw):
            if tok[0] == "dma":
                return tok
            i = self.redirect.get(tok[1], tok[1])
            d = self.ops[i]
            if d["epoch"] != rec["epoch"]:
                return None
            if d["eng"] == rec["eng"]:
                if rec["eng"] in ("pe", "sp"):
                    return None
                if not SAME_ENG and not raw:
                    return None
            return ("op", i)
        awaited = set()
        for rec in self.ops:
            r = []
            for tok, raw in rec["deps"]:
                t = resolve(tok, rec, raw)
                if t is not None:
                    r.append(t)
                    if t[0] == "op":
                        awaited.add(t[1])
            rec["rdeps"] = r
        val = {}
        cnt = {}
        for e in self.CE:
            for rec in self.byeng[e]:
                if rec["kind"] == "op" and rec["id"] in awaited:
                    k = f"{e}@{rec['epoch']}"
                    cnt[k] = cnt.get(k, 0) + 1
                    val[rec["id"]] = (k, cnt[k])
        self.prog = {}
        self.nsig = len(val)
        for e, recs in self.byeng.items():
            waited = {}
            out = []
            for rec in recs:
                w = {}
                for t in rec["rdeps"]:
                    k, v = val[t[1]] if t[0] == "op" else (t[1], t[2])
                    if w.get(k, 0) < v:
                        w[k] = v
                wl = []
                for k, v in w.items():
                    if waited.get(k, 0) < v:
                        waited[k] = v
                        wl.append((k, v))
                if rec["kind"] == "dma":
                    incs = [(rec["chan"], 16)] * len(rec["fns"])
                elif rec["id"] in val:
                    incs = [(val[rec["id"]][0], 1)]
                else:
                    incs = []
                out.append((wl, rec["fns"], incs))
            self.prog[e] = out

    def emit(self, eng, e):
        for waits, fns, incs in self.prog[eng]:
            for k, v in waits:
                e.wait_ge(self.sems[k], v)
            for i, fn in enumerate(fns):
                ins = fn(e)
                if i < len(incs):
                    ins.then_inc(self.sems[incs[i][0]], incs[i][1])


class Arena:
    def __init__(self, ap_f32, nbytes):
        self.ap = ap_f32
        self.n = nbytes
        self.off = 0

    def mark(self):
        return self.off

    def reset(self, m):
        self.off = m

    def alloc(self, shape, dt):
        es = 2 if dt == BF16 else 4
        free = 1
        for s in shape[1:]:
            free *= s
        nb = (free * es + 31) // 32 * 32
        assert self.off + nb <= self.n, ("arena overflow", self.off, nb, self.n)
        w0 = self.off // 4
        v = self.ap[:, w0:w0 + nb // 4]
        self.off += nb
        if dt == BF16:
            v = v.bitcast(BF16)
        v = v[:, 0:free]
        if len(shape) == 3:
            v = v.rearrange("p (a b) -> p a b", a=shape[1])
        elif len(shape) == 4:
            v = v.rearrange("p (a b c) -> p a b c", a=shape[1], b=shape[2])
        return v


def build_program(stop_after=None):
    nc = bass.Bass("TRN2", target_bir_lowering=False)
    dr = {}

    def din(name, shape):
        dr[name] = nc.dram_tensor(name, shape, F32, kind="ExternalInput").ap()
        return dr[name]

    xT = din("xT", [D, T])
    cT = din("cT", [D, CTX])
    prm_d = din("prm", [128, NPRM])
    cst_d = din("cst", [128, NCST])
    rope_d = din("rope", [2, 128, T])
    ada_w = din("ada_w", [2, D, 9 * D])
    f_wi = [din("ffn1_wi", [2, D, 2 * DFF]), din("ffn2_wi", [2, D, 2 * DFF])]
    f_wo = [din("ffn1_wo", [2, DFF, D]), din("ffn2_wo", [2, DFF, D])]
    pw1_w = din("pw1_w", [D, 2 * D])
    pw2_w = din("pw2_w", [D, D])
    watt = din("w_att", [D, 4 * 768])
    wv_d = din("w_v", [D, 256])
    wo_d = din("w_o", [D, D])
    outT = nc.dram_tensor("outT", [D, OWN], F32, kind="ExternalOutput").ap()

    semkeys = [f"{e}@{i}" for e in Sched.CE for i in range(Sched.NEPOCH)] + ["ldp", "ldc", "ldxc", "st0", "st1", "rp0", "rp1"] + \
              [f"ldx{i}" for i in range(5)] + [f"w{i}" for i in range(NSLOT)]

    import contextlib
    es = contextlib.ExitStack()
    with es:
        sems = {k: es.enter_context(nc.semaphore(k)) for k in semkeys}
        X = es.enter_context(nc.sbuf_tensor("X", [128, NCH, T], F32))
        XC = es.enter_context(nc.sbuf_tensor("XC", [128, NCH, CTX], F32))
        Wr = es.enter_context(nc.sbuf_tensor("Wr", [128, NSLOT, SLOT], BF16))
        prm = es.enter_context(nc.sbuf_tensor("prm_sb", [128, NPRM], F32))
        cst = es.enter_context(nc.sbuf_tensor("cst_sb", [128, NCST], BF16))
        mod = es.enter_context(nc.sbuf_tensor("mod", [128, 2, 72, 2], F32))
        tabA = es.enter_context(nc.sbuf_tensor("tabA", [128, 3, NCH, 2], F32))
        tabG = es.enter_context(nc.sbuf_tensor("tabG", [128, 3, NCH, 2], F32))
        tabGb = es.enter_context(nc.sbuf_tensor("tabGb", [128, NCH, 2], F32))
        scT = es.enter_context(nc.sbuf_tensor("scT", [128, NCH, 2], BF16))
        onesf = es.enter_context(nc.sbuf_tensor("onesf", [128, 128], F32))
        esink = es.enter_context(nc.sbuf_tensor("esink", [128, NCH], F32))
        epsb = es.enter_context(nc.sbuf_tensor("epsb", [128, 1], F32))
        ARENA_BYTES = 212863 - (NCH * T * 4 + NCH * CTX * 4 + NSLOT * SLOT * 2 + NPRM * 4 + NCST * 2
                                + 2 * 72 * 2 * 4 + 2 * 3 * NCH * 2 * 4 + NCH * 2 * 4 + NCH * 2 * 2
                                + 128 * 4 + NCH * 4) - 1024
        ARENA_BYTES = ARENA_BYTES // 64 * 64
        ar_t = es.enter_context(nc.sbuf_tensor("arena", [128, ARENA_BYTES // 4], F32))
        PSALL = es.enter_context(nc.psum_tensor("psall", [128, 4096], F32))
        PS = [PSALL[:, i * 512:(i + 1) * 512] for i in range(8)]
        S = Sched(nc, sems)
        A = Arena(ar_t, ARENA_BYTES)

        def P(name, a=0, b=None):
            o, w = PRM[name]
            b = w if b is None else b
            return prm[:, o + a:o + b]

        def xbuf(which):
            return X if which == 0 else XC

        def xk(which, c, a, n):
            if which == 1:
                return [("XC", c)]
            return [("X", c, s) for s in _segs(a, a + n)]

        def hk(c, a, n):
            if a >= T:
                return [("H", c, "c")]
            return [("H", c, s) for s in _segs(a, a + n)]

        S.dma("sp", [lambda e: e.dma_start(out=prm[:], in_=prm_d[:, :])], "ldp", writes=["prm"])
        S.dma("pool", [lambda e: e.dma_start(out=cst[:], in_=cst_d[:, :])], "ldc", writes=["cst"])
        xT_v = xT.rearrange("(c p) t -> p c t", p=128)
        cT_v = cT.rearrange("(c p) t -> p c t", p=128)
        for i in range(5):
            S.dma("sp", [lambda e, i=i: e.dma_start(out=X[:, :, i * 440:(i + 1) * 440],
                                                   in_=xT_v[:, :, i * 440:(i + 1) * 440])],
                  f"ldx{i}", writes=[k for c in range(NCH) for k in xk(0, c, i * 440, 440)])
        S.dma("sp", [lambda e: e.dma_start(out=XC[:], in_=cT_v)], "ldxc",
              writes=[("XC", c) for c in range(NCH)])
        S.op("dve", lambda e: e.memset(onesf[:], 1.0 / D), writes=["onesf"])
        S.op("dve", lambda e: e.memset(epsb[:], EPS), writes=["epsb"])
        S.op("act", lambda e: e.activation(out=scT[:], in_=P("cc").rearrange("p (k w) -> p k w", w=2),
                                           func=AF.Silu), reads=["prm"], writes=["scT"])
        S.op("act", lambda e: e.activation(out=esink[:], in_=P("sink"), func=AF.Exp),
             reads=["prm"], writes=["esink"])

        ring = {"next": 0}

        def slot_alloc():
            s = ring["next"]
            ring["next"] = (s + 1) % NSLOT
            return s

        def load_slot(s, fns):
            S.dma("pool", fns, f"w{s}", writes=[("W", s)])

        def ada_layer(l):
            aw = ada_w[l].rearrange("(k p) n -> p k n", p=128)
            for u in range(12):
                s = slot_alloc()
                wv = Wr[:, s, :].rearrange("p (k n) -> p k n", k=NCH)
                load_slot(s, [lambda e, u=u, wv=wv: e.dma_start(out=wv, in_=aw[:, :, u * 768:(u + 1) * 768])])
                ps = PS[u % 2][:, 0:12].rearrange("p (a b) -> p a b", b=2)
                for oc in range(6):
                    for k in range(NCH):
                        S.op("pe", lambda e, oc=oc, k=k, wv=wv, ps=ps: e.matmul(
                            ps[:, oc, :], lhsT=wv[:, k, oc * 128:(oc + 1) * 128], rhs=scT[:, k, :],
                            start=(k == 0), stop=(k == NCH - 1)),
                            reads=[("W", s), "scT"], writes=[("ps", u % 2)], sig=(k == NCH - 1))
                ab = P("ada_b", l * 72 + u * 6, l * 72 + u * 6 + 6)
                S.op("dve", lambda e, ps=ps, ab=ab, u=u: e.tensor_tensor(
                    out=mod[:, l, u * 6:(u + 1) * 6, :], in0=ps,
                    in1=ab.unsqueeze(2).to_broadcast([128, 6, 2]), op=ALU.add),
                    reads=[("ps", u % 2), "prm"], writes=[("mod", l)])
            for j in range(3):
                g = P("norm_g", (l * 3 + j) * 8, (l * 3 + j) * 8 + 8)
                S.op("dve", lambda e, j=j, g=g: e.scalar_tensor_tensor(
                    out=tabA[:, j], in0=mod[:, l, (3 * j + 1) * 8:(3 * j + 2) * 8, :], scalar=1.0,
                    in1=g.unsqueeze(2).to_broadcast([128, NCH, 2]), op0=ALU.add, op1=ALU.mult),
                    reads=[("mod", l), "prm"], writes=[("tabA", j)])
                S.op("dve", lambda e, j=j: e.tensor_scalar(
                    out=tabG[:, j], in0=mod[:, l, (3 * j + 2) * 8:(3 * j + 3) * 8, :],
                    scalar1=(1.0 if j == 1 else 0.5), scalar2=None, op0=ALU.mult),
                    reads=[("mod", l)], writes=[("tabG", j)])
            if l == 0:
                S.op("dve", lambda e: e.tensor_tensor(
                    out=tabGb[:], in0=tabG[:, 1], in1=P("pw2_b").unsqueeze(2).to_broadcast([128, NCH, 2]),
                    op=ALU.mult), reads=[("tabG", 1), "prm"], writes=["tabGb"])

        def tabB(l, j):
            return mod[:, l, (3 * j) * 8:(3 * j + 1) * 8, :]

        def prepass(l, j, tiles, H):
            m = A.mark()
            sq = [A.alloc([128, 512], F32) for _ in range(3)]
            rstd = [A.alloc([128, 512], F32) for _ in range(2)]
            rsq = [A.alloc([128, 512], F32) for _ in range(2)]
            tmp = [A.alloc([128, 512], F32) for _ in range(3)]
            n_sq = 0
            n_tmp = 0
            for ti, (a, n, w) in enumerate(tiles):
                xb = xbuf(w)
                xa = a - T if w == 1 else a
                msb = 6 + (ti % 2)
                ms = PS[msb][:, 0:n]
                for c in range(NCH):
                    q = n_sq % 3
                    n_sq += 1
                    S.op("act", lambda e, q=q, c=c, xb=xb, xa=xa, n=n: e.activation(
                        out=sq[q][:, 0:n], in_=xb[:, c, xa:xa + n], func=AF.Square),
                        reads=xk(w, c, xa, n), writes=[("sq", q)])
                    S.op("pe", lambda e, q=q, c=c, ms=ms, n=n: e.matmul(
                        ms, lhsT=onesf[:], rhs=sq[q][:, 0:n], start=(c == 0), stop=(c == NCH - 1)),
                        reads=[("sq", q), "onesf"], writes=[("ps", msb)], sig=True)
                r = ti % 2
                S.op("act", lambda e, r=r, ms=ms, n=n: e.activation(
                    out=rsq[r][:, 0:n], in_=ms, func=AF.Sqrt, bias=epsb[:, 0:1], scale=1.0),
                    reads=[("ps", msb), "epsb"], writes=[("rsq", r)])
                S.op("dve", lambda e, r=r, n=n: e.reciprocal(out=rstd[r][:, 0:n], in_=rsq[r][:, 0:n]),
                     reads=[("rsq", r)], writes=[("rstd", r)])
                for c in range(NCH):
                    q = n_tmp % 3
                    n_tmp += 1
                    S.op("dve", lambda e, q=q, c=c, xb=xb, xa=xa, n=n, r=r, w=w: e.scalar_tensor_tensor(
                        out=tmp[q][:, 0:n], in0=xb[:, c, xa:xa + n], scalar=tabA[:, j, c, w:w + 1],
                        in1=rstd[r][:, 0:n], op0=ALU.mult, op1=ALU.mult),
                        reads=xk(w, c, xa, n) + [("rstd", r), ("tabA", j)], writes=[("ptmp", q)])
                    S.op("act", lambda e, q=q, c=c, a=a, n=n, w=w: e.activation(
                        out=H[:, c, a:a + n], in_=tmp[q][:, 0:n], func=AF.Identity,
                        bias=tabB(l, j)[:, c, w:w + 1], scale=1.0),
                        reads=[("ptmp", q), ("mod", l)], writes=hk(c, a, n))
            S.barrier()
            A.reset(m)

        def ffn(l, f, tiles, H):
            j = 0 if f == 0 else 2
            prepass(l, j, tiles, H)
            m = A.mark()
            act = [A.alloc([128, 4, 512], BF16) for _ in range(2)]
            sg = [A.alloc([128, 512], F32) for _ in range(2)]
            wi = f_wi[f][l].rearrange("(k p) (two n) -> p k two n", p=128, two=2)
            wo = f_wo[f][l].rearrange("(j p) n -> p j n", p=128)
            sweeps = [[0, 1], [2, 3], [4, 5], [6, 7], [8, 9], [10]]
            unit_slot = {}

            def load_unit(u):
                s = slot_alloc()
                unit_slot[u] = s
                wiv = Wr[:, s, 0:4096].rearrange("p (k two n) -> p k two n", k=NCH, two=2)
                wov = Wr[:, s, 4096:6144].rearrange("p (j n) -> p j n", j=2)
                load_slot(s, [lambda e, wiv=wiv, u=u: e.dma_start(out=wiv[:, :, 0, :], in_=wi[:, :, 0, u * 256:(u + 1) * 256]),
                              lambda e, wiv=wiv, u=u: e.dma_start(out=wiv[:, :, 1, :], in_=wi[:, :, 1, u * 256:(u + 1) * 256]),
                              lambda e, wov=wov, u=u: e.dma_start(out=wov, in_=wo[:, 2 * u:2 * u + 2, :])])

            tasks = []
            for si, sw in enumerate(sweeps):
                for ti in range(len(tiles)):
                    tasks.append((si, ti))
            st = {"sg": 0, "gu": 0, "y": 0}

            def stage_a(k):
                si, ti = tasks[k]
                a, n, w = tiles[ti]
                ab = k % 2
                chunks = [(u, jj) for u in sweeps[si] for jj in range(2)]
                for ci, (u, jj) in enumerate(chunks):
                    s = unit_slot[u]
                    wiv = Wr[:, s, 0:4096].rearrange("p (k two n) -> p k two n", k=NCH, two=2)
                    gb = st["gu"] % 2
                    st["gu"] += 1
                    gps = PS[gb * 2][:, 0:n]
                    ups = PS[gb * 2 + 1][:, 0:n]
                    for two, pp, pb in ((0, gps, gb * 2), (1, ups, gb * 2 + 1)):
                        for kk in range(NCH):
                            S.op("pe", lambda e, pp=pp, wiv=wiv, kk=kk, two=two, jj=jj, a=a, n=n: e.matmul(
                                pp, lhsT=wiv[:, kk, two, jj * 128:(jj + 1) * 128], rhs=H[:, kk, a:a + n],
                                start=(kk == 0), stop=(kk == NCH - 1)),
                                reads=[("W", s)] + hk(kk, a, n), writes=[("ps", pb)], sig=(kk == NCH - 1))
                    q = st["sg"] % 2
                    st["sg"] += 1
                    S.op("act", lambda e, q=q, gps=gps, n=n: e.activation(out=sg[q][:, 0:n], in_=gps, func=AF.Silu),
                         reads=[("ps", gb * 2)], writes=[("sg", q)])
                    S.op("dve", lambda e, q=q, ups=ups, n=n, ab=ab, ci=ci: e.tensor_tensor(
                        out=act[ab][:, ci, 0:n], in0=sg[q][:, 0:n], in1=ups, op=ALU.mult),
                        reads=[("sg", q), ("ps", gb * 2 + 1)], writes=[("act", ab, ci)])

            def stage_b(k):
                si, ti = tasks[k]
                a, n, w = tiles[ti]
                xb = xbuf(w)
                xa = a - T if w == 1 else a
                ab = k % 2
                chunks = [(u, jj) for u in sweeps[si] for jj in range(2)]
                for d in range(NCH):
                    yb = 4 + st["y"] % 2
                    st["y"] += 1
                    yps = PS[yb][:, 0:n]
                    for ci, (u, jj) in enumerate(chunks):
                        s = unit_slot[u]
                        wov = Wr[:, s, 4096:6144].rearrange("p (j n) -> p j n", j=2)
                        S.op("pe", lambda e, yps=yps, wov=wov, jj=jj, d=d, ab=ab, ci=ci, n=n: e.matmul(
                            yps, lhsT=wov[:, jj, d * 128:(d + 1) * 128], rhs=act[ab][:, ci, 0:n],
                            start=(ci == 0), stop=(ci == len(chunks) - 1)),
                            reads=[("W", s), ("act", ab, ci)], writes=[("ps", yb)], sig=(ci == len(chunks) - 1))
                    S.op("dve", lambda e, yps=yps, d=d, xb=xb, xa=xa, n=n, w=w: e.scalar_tensor_tensor(
                        out=xb[:, d, xa:xa + n], in0=yps, scalar=tabG[:, j, d, w:w + 1], in1=xb[:, d, xa:xa + n],
                        op0=ALU.mult, op1=ALU.add),
                        reads=[("ps", yb), ("tabG", j)] + xk(w, d, xa, n), writes=xk(w, d, xa, n))

            loaded = 0

            def ensure_loaded(si):
                nonlocal loaded
                while loaded <= min(si, len(sweeps) - 1):
                    for u in sweeps[loaded]:
                        load_unit(u)
                    loaded += 1
            ensure_loaded(1)
            nt = len(tiles)
            for k in range(len(tasks) + 1):
                if k < len(tasks):
                    si, ti = tasks[k]
                    if ti == 1:
                        ensure_loaded(si + 1)
                    stage_a(k)
                if k >= 1:
                    stage_b(k - 1)
            S.barrier()
            A.reset(m)

        def conv_phase(l, H):
            prepass(l, 1, TILES_A, H)
            m = A.mark()
            p1 = pw1_w.rearrange("(k p) n -> p k n", p=128)
            p2 = pw2_w.rearrange("(k p) n -> p k n", p=128)
            slots = []
            for i in range(4):
                s = slot_alloc()
                slots.append(s)
                v1 = Wr[:, s, 0:4096].rearrange("p (k two n) -> p k two n", k=NCH, two=2)
                v2 = Wr[:, s, 4096:6144].rearrange("p (k n) -> p k n", k=NCH)
                load_slot(s, [lambda e, v1=v1, i=i: e.dma_start(out=v1[:, :, 0, :], in_=p1[:, :, i * 256:(i + 1) * 256]),
                              lambda e, v1=v1, i=i: e.dma_start(out=v1[:, :, 1, :], in_=p1[:, :, D + i * 256:D + (i + 1) * 256]),
                              lambda e, v2=v2, i=i: e.dma_start(out=v2, in_=p2[:, :, i * 256:(i + 1) * 256])])
            UW = 472
            NPE = 16
            U = [A.alloc([128, UW], F32) for _ in range(2)]
            Ub = [A.alloc([128, UW], BF16) for _ in range(2)]
            Dg = A.alloc([128, 2, NPE, 128], BF16)
            V = A.alloc([128, NCH, 440], F32)
            sqv = [A.alloc([128, 440], F32) for _ in range(1)]
            varb = A.alloc([128, 440], F32)
            HN = A.alloc([128, NCH, 440], BF16)
            ident = cst[:, 704:832]
            dww = P("dw_w").rearrange("p (c k) -> p c k", k=CW)
            K1 = CW - NPE
            for ti, (a, n, w) in enumerate(TILES_A):
                xb = xbuf(w)
                xa = a - T if w == 1 else a
                s0, s1 = (0, T) if w == 0 else (T, T + CTX)
                lo, hi = max(a - 15, s0), min(a + n + 15, s1)
                ulo, uhi = lo - (a - 15), hi - (a - 15)
                nu = hi - lo
                for jp in range(4):
                    for jj in range(2):
                        j = 2 * jp + jj
                        s = slots[jp]
                        v1 = Wr[:, s, 0:4096].rearrange("p (k two n) -> p k two n", k=NCH, two=2)
                        for two in range(2):
                            pb = 2 * jj + two
                            pp = PS[pb][:, 0:nu]
                            for kk in range(NCH):
                                S.op("pe", lambda e, pp=pp, v1=v1, kk=kk, two=two, jj=jj, lo=lo, hi=hi: e.matmul(
                                    pp, lhsT=v1[:, kk, two, jj * 128:(jj + 1) * 128], rhs=H[:, kk, lo:hi],
                                    start=(kk == 0), stop=(kk == NCH - 1)),
                                    reads=[("W", s)] + hk(kk, lo, nu), writes=[("ps", pb)], sig=(kk == NCH - 1))
                        if ulo > 0:
                            S.op("dve", lambda e, jj=jj, ulo=ulo: e.memset(U[jj][:, 0:ulo], 0.0), writes=[("U", jj)])
                        if uhi < n + 30:
                            S.op("dve", lambda e, jj=jj, uhi=uhi, n=n: e.memset(U[jj][:, uhi:n + 30], 0.0), writes=[("U", jj)])
                        S.op("act", lambda e, jj=jj, j=j, nu=nu, ulo=ulo, uhi=uhi: e.activation(
                            out=U[jj][:, ulo:uhi], in_=PS[2 * jj + 1][:, 0:nu], func=AF.Sigmoid,
                            bias=P("pw1_b", 8 + j, 9 + j), scale=1.0),
                            reads=[("ps", 2 * jj + 1), "prm"], writes=[("U", jj)])
                        S.op("dve", lambda e, jj=jj, j=j, nu=nu, ulo=ulo, uhi=uhi: e.scalar_tensor_tensor(
                            out=U[jj][:, ulo:uhi], in0=PS[2 * jj][:, 0:nu], scalar=P("pw1_b", j, j + 1),
                            in1=U[jj][:, ulo:uhi], op0=ALU.add, op1=ALU.mult),
                            reads=[("ps", 2 * jj), "prm", ("U", jj)], writes=[("U", jj)])
                    for k in range(K1):
                        for jj in range(2):
                            j = 2 * jp + jj
                            if k == 0:
                                S.op("dve", lambda e, jj=jj, j=j, n=n: e.tensor_scalar(
                                    out=V[:, j, 0:n], in0=U[jj][:, 0:n], scalar1=dww[:, j, 0:1],
                                    scalar2=P("dw_b", j, j + 1), op0=ALU.mult, op1=ALU.add),
                                    reads=[("U", jj), "prm"], writes=[("V", j)])
                            else:
                                S.op("dve", lambda e, jj=jj, j=j, n=n, k=k: e.scalar_tensor_tensor(
                                    out=V[:, j, 0:n], in0=U[jj][:, k:k + n], scalar=dww[:, j, k:k + 1],
                                    in1=V[:, j, 0:n], op0=ALU.mult, op1=ALU.add),
                                    reads=[("U", jj), ("V", j), "prm"], writes=[("V", j)])
                    for jj in range(2):
                        j = 2 * jp + jj
                        S.op("act", lambda e, jj=jj, n=n: e.activation(
                            out=Ub[jj][:, 0:n + 30], in_=U[jj][:, 0:n + 30], func=AF.Identity),
                            reads=[("U", jj)], writes=[("Ub", jj)])
                        for kidx in range(NPE):
                            k = K1 + kidx
                            S.op("act", lambda e, jj=jj, j=j, k=k, kidx=kidx: e.activation(
                                out=Dg[:, jj, kidx, :], in_=ident, func=AF.Identity, scale=dww[:, j, k:k + 1]),
                                reads=["cst", "prm"], writes=[("Dg", jj, kidx)])
                    for jj in range(2):
                        j = 2 * jp + jj
                        for kidx in range(NPE):
                            k = K1 + kidx
                            S.op("pe", lambda e, jj=jj, k=k, kidx=kidx, n=n: e.matmul(
                                PS[4 + jj][:, 0:n], lhsT=Dg[:, jj, kidx, :], rhs=Ub[jj][:, k:k + n],
                                start=(kidx == 0), stop=(kidx == NPE - 1)),
                                reads=[("Dg", jj, kidx), ("Ub", jj)], writes=[("ps", 4 + jj)], sig=(kidx == NPE - 1))
                    for jj in range(2):
                        j = 2 * jp + jj
                        S.op("dve", lambda e, jj=jj, j=j, n=n: e.tensor_tensor(
                            out=V[:, j, 0:n], in0=V[:, j, 0:n], in1=PS[4 + jj][:, 0:n], op=ALU.add),
                            reads=[("V", j), ("ps", 4 + jj)], writes=[("V", j)])
                for j in range(NCH):
                    S.op("pe", lambda e, j=j, n=n: e.matmul(PS[4][:, 0:n], lhsT=onesf[:], rhs=V[:, j, 0:n],
                                                           start=(j == 0), stop=(j == NCH - 1)),
                         reads=[("V", j), "onesf"], writes=[("ps", 4)], sig=True)
                    S.op("act", lambda e, j=j, n=n: e.activation(out=sqv[0][:, 0:n], in_=V[:, j, 0:n], func=AF.Square),
                         reads=[("V", j)], writes=[("sqv", 0)])
                    S.op("pe", lambda e, j=j, n=n: e.matmul(PS[5][:, 0:n], lhsT=onesf[:], rhs=sqv[0][:, 0:n],
                                                           start=(j == 0), stop=(j == NCH - 1)),
                         reads=[("sqv", 0), "onesf"], writes=[("ps", 5)], sig=True)
                S.op("act", lambda e, n=n: e.activation(out=varb[:, 0:n], in_=PS[4][:, 0:n], func=AF.Square),
                     reads=[("ps", 4)], writes=["varb"])
                S.op("dve", lambda e, n=n: e.tensor_tensor(out=varb[:, 0:n], in0=PS[5][:, 0:n], in1=varb[:, 0:n],
                                                           op=ALU.subtract), reads=[("ps", 5), "varb"], writes=["varb"])
                S.op("act", lambda e, n=n: e.activation(out=varb[:, 0:n], in_=varb[:, 0:n], func=AF.Sqrt,
                                                        bias=epsb[:, 0:1], scale=1.0),
                     reads=["varb", "epsb"], writes=["varb"])
                S.op("dve", lambda e, n=n: e.reciprocal(out=varb[:, 0:n], in_=varb[:, 0:n]),
                     reads=["varb"], writes=["varb"])
                for j in range(NCH):
                    S.op("dve", lambda e, j=j, n=n: e.tensor_tensor(
                        out=V[:, j, 0:n], in0=V[:, j, 0:n], in1=PS[4][:, 0:n], op=ALU.subtract),
                        reads=[("V", j), ("ps", 4)], writes=[("V", j)])
                    S.op("dve", lambda e, j=j, n=n: e.tensor_tensor(
                        out=V[:, j, 0:n], in0=V[:, j, 0:n], in1=varb[:, 0:n], op=ALU.mult),
                        reads=[("V", j), "varb"], writes=[("V", j)])
                    S.op("act", lambda e, j=j, n=n: e.activation(
                        out=HN[:, j, 0:n], in_=V[:, j, 0:n], func=AF.Silu,
                        bias=P("ln_b", j, j + 1), scale=P("ln_g", j, j + 1)),
                        reads=[("V", j), "prm"], writes=[("HN", j)])
                for d in range(NCH):
                    s = slots[d // 2]
                    v2 = Wr[:, s, 4096:6144].rearrange("p (k n) -> p k n", k=NCH)
                    yb = 6 + d % 2
                    for kk in range(NCH):
                        S.op("pe", lambda e, v2=v2, kk=kk, d=d, yb=yb, n=n: e.matmul(
                            PS[yb][:, 0:n], lhsT=v2[:, kk, (d % 2) * 128:(d % 2 + 1) * 128], rhs=HN[:, kk, 0:n],
                            start=(kk == 0), stop=(kk == NCH - 1)),
                            reads=[("W", s), ("HN", kk)], writes=[("ps", yb)], sig=(kk == NCH - 1))
                    S.op("act", lambda e, d=d, n=n, w=w, yb=yb: e.activation(
                        out=sqv[0][:, 0:n], in_=PS[yb][:, 0:n], func=AF.Identity,
                        bias=tabGb[:, d, w:w + 1], scale=tabG[:, 1, d, w:w + 1]),
                        reads=[("ps", yb), ("tabG", 1), "tabGb"], writes=[("sqv", 0)])
                    S.op("dve", lambda e, d=d, xb=xb, xa=xa, n=n: e.tensor_tensor(
                        out=xb[:, d, xa:xa + n], in0=xb[:, d, xa:xa + n], in1=sqv[0][:, 0:n], op=ALU.add),
                        reads=[("sqv", 0)] + xk(w, d, xa, n), writes=xk(w, d, xa, n))
            S.barrier()
            A.reset(m)

        def attn_phase(l, H):
            prepass(l, 1, TILES_A, H)
            m = A.mark()
            Qg = A.alloc([128, 2, OWN], BF16)
            Kg = A.alloc([128, TK + CTX], BF16)
            Vpad = A.alloc([128, 19, 192], BF16)
            Og = [A.alloc([128, 2, 512], BF16) for _ in range(2)]
            PT = [A.alloc([128, 2, 5, 128], BF16) for _ in range(2)]
            cosb = A.alloc([128, 440], F32)
            sinb = A.alloc([128, 440], F32)
            t1 = A.alloc([128, 440], F32)
            t2 = A.alloc([128, 440], F32)
            rden = [A.alloc([128, 128], F32) for _ in range(2)]
            mask_lo = cst[:, 0:256].rearrange("p (h q) -> p h q", h=2)
            mask_hi = cst[:, 256:512].rearrange("p (h q) -> p h q", h=2)
            onespad = cst[:, 512:704]
            wa = watt.rearrange("(k p) n -> p k n", p=128)
            wvv = wv_d.rearrange("(k p) n -> p k n", p=128)
            wov = wo_d.rearrange("(j p) n -> p j n", p=128)
            S.op("pool", lambda e: e.memset(Vpad[:], 0.0), writes=["Vpad"])
            cnt = {"st": 0, "od": 0, "og": 0}
            for g in range(4):
                sA = slot_alloc()
                vA = Wr[:, sA, :].rearrange("p (k n) -> p k n", k=NCH)
                load_slot(sA, [lambda e, vA=vA, g=g: e.dma_start(out=vA, in_=wa[:, :, g * 768:(g + 1) * 768])])
                sB = slot_alloc()
                vBo = Wr[:, sB, 0:2048].rearrange("p (j n) -> p j n", j=2)
                vBv = Wr[:, sB, 2048:2560].rearrange("p (k n) -> p k n", k=NCH)
                load_slot(sB, [lambda e, vBo=vBo, g=g: e.dma_start(out=vBo, in_=wov[:, 2 * g:2 * g + 2, :]),
                               lambda e, vBv=vBv, g=g: e.dma_start(out=vBv, in_=wvv[:, :, 64 * g:64 * g + 64])])
                for ti in range(5):
                    a, n = 440 * ti, 440
                    S.dma("sp", [lambda e, a=a, n=n: e.dma_start(out=cosb[:, 0:n], in_=rope_d[0][:, a:a + n]),
                                 lambda e, a=a, n=n: e.dma_start(out=sinb[:, 0:n], in_=rope_d[1][:, a:a + n])],
                          "rp0", writes=["rope"])
                    items = [("q", 0), ("q", 1), ("k", 0)]
                    for ii, (kind, cc) in enumerate(items):
                        if kind == "q":
                            nn = max(0, min(a + n, OWN) - a)
                            base, bsw = cc * 128, 256 + cc * 128
                            dst = Qg[:, cc, a:a + nn] if nn > 0 else None
                            dkey = ("Qg", cc)
                        else:
                            nn = max(0, min(a + n, TK) - a)
                            base, bsw = 512, 640
                            dst = Kg[:, a:a + nn]
                            dkey = ("Kg",)
                        if nn == 0:
                            continue
                        pa, pb = 2 * (ii % 2), 2 * (ii % 2) + 1
                        for (pp, bb) in ((pa, base), (pb, bsw)):
                            for kk in range(NCH):
                                S.op("pe", lambda e, pp=pp, bb=bb, kk=kk, a=a, nn=nn, vA=vA: e.matmul(
                                    PS[pp][:, 0:nn], lhsT=vA[:, kk, bb:bb + 128], rhs=H[:, kk, a:a + nn],
                                    start=(kk == 0), stop=(kk == NCH - 1)),
                                    reads=[("W", sA)] + hk(kk, a, nn), writes=[("ps", pp)], sig=(kk == NCH - 1))
                        S.op("dve", lambda e, pa=pa, nn=nn: e.tensor_tensor(
                            out=t1[:, 0:nn], in0=PS[pa][:, 0:nn], in1=cosb[:, 0:nn], op=ALU.mult),
                            reads=[("ps", pa), "rope"], writes=["t1"])
                        S.op("dve", lambda e, pb=pb, nn=nn: e.tensor_tensor(
                            out=t2[:, 0:nn], in0=PS[pb][:, 0:nn], in1=sinb[:, 0:nn], op=ALU.mult),
                            reads=[("ps", pb), "rope"], writes=["t2"])
                        S.op("pool", lambda e, dst=dst, nn=nn: e.tensor_tensor(
                            out=dst, in0=t1[:, 0:nn], in1=t2[:, 0:nn], op=ALU.add),
                            reads=["t1", "t2"], writes=[dkey])
                for kk in range(NCH):
                    S.op("pe", lambda e, kk=kk, vA=vA: e.matmul(
                        PS[0][:, 0:CTX], lhsT=vA[:, kk, 512:640], rhs=H[:, kk, T:T + CTX],
                        start=(kk == 0), stop=(kk == NCH - 1)),
                        reads=[("W", sA)] + hk(kk, T, CTX), writes=[("ps", 0)], sig=(kk == NCH - 1))
                S.op("act", lambda e: e.activation(out=Kg[:, TK:TK + CTX], in_=PS[0][:, 0:CTX], func=AF.Identity),
                     reads=[("ps", 0)], writes=[("Kg",)])
                for b0 in range(0, 19, 8):
                    nb = min(8, 19 - b0)
                    vb = 6 + (b0 // 8) % 2
                    psv = PS[vb].rearrange("p (b d) -> p b d", d=64)
                    for bi in range(nb):
                        blk = b0 + bi
                        c0 = 128 * blk if blk < 17 else T + 128 * (blk - 17)
                        for kk in range(NCH):
                            S.op("pe", lambda e, psv=psv, bi=bi, kk=kk, c0=c0, vBv=vBv: e.matmul(
                                psv[:, bi, :], lhsT=H[:, kk, c0:c0 + 128], rhs=vBv[:, kk, :],
                                start=(kk == 0), stop=(kk == NCH - 1)),
                                reads=[("W", sB)] + hk(kk, c0, 128), writes=[("ps", vb)],
                                sig=(kk == NCH - 1 and bi == nb - 1))
                    S.op("act", lambda e, psv=psv, b0=b0, nb=nb: e.activation(
                        out=Vpad[:, b0:b0 + nb, 64:128], in_=psv[:, 0:nb, :], func=AF.Identity),
                        reads=[("ps", vb)], writes=["Vpad"])
                its = [(qt, qb, cc) for qt in range(4) for qb in range(4) for cc in range(2)]
                info = {}

                def stage_s(it):
                    qt, qb, cc = its[it]
                    i = 4 * qt + qb
                    sb = cnt["st"] % 2
                    cnt["st"] += 1
                    stv = PSALL[:, sb * 1536:sb * 1536 + 1280].rearrange("p (h k q) -> p h k q", h=2, k=5)
                    stkeys = [("ps", sb * 3 + z) for z in range(3)]
                    k0 = 1 if i == 0 else 0
                    srcs = []
                    for kbi in range(k0, 5):
                        if kbi < 3:
                            blk = i - 1 + kbi
                            srcs.append((kbi, 128 * blk, blk))
                        else:
                            srcs.append((kbi, TK + 128 * (kbi - 3), 17 + kbi - 3))
                    nmm = 2 * len(srcs)
                    z = 0
                    for h in range(2):
                        for (kbi, kc, vbk) in srcs:
                            z += 1
                            S.op("pe", lambda e, stv=stv, h=h, kbi=kbi, kc=kc, cc=cc, i=i: e.matmul(
                                stv[:, h, kbi, :], lhsT=Kg[h * 64:(h + 1) * 64, kc:kc + 128],
                                rhs=Qg[h * 64:(h + 1) * 64, cc, 128 * i:128 * i + 128], start=True, stop=True),
                                reads=[("Kg",), ("Qg", cc)], writes=stkeys, sig=(z == nmm))
                    pb = sb
                    S.op("act", lambda e, stv=stv, pb=pb, k0=k0: e.activation(
                        out=PT[pb][:, :, k0:5, :], in_=stv[:, :, k0:5, :], func=AF.Exp, scale=0.125),
                        reads=stkeys, writes=[("PT", pb)])
                    if i > 0:
                        S.op(ATT_MASK, lambda e, pb=pb: e.tensor_tensor(
                            out=PT[pb][:, :, 0, :], in0=PT[pb][:, :, 0, :], in1=mask_lo, op=ALU.mult),
                            reads=[("PT", pb), "cst"], writes=[("PT", pb)])
                    S.op(ATT_MASK, lambda e, pb=pb: e.tensor_tensor(
                        out=PT[pb][:, :, 2, :], in0=PT[pb][:, :, 2, :], in1=mask_hi, op=ALU.mult),
                        reads=[("PT", pb), "cst"], writes=[("PT", pb)])
                    info[it] = (pb, srcs, nmm)

                def stage_p(it):
                    qt, qb, cc = its[it]
                    pb, srcs, nmm = info.pop(it)
                    ob = (4 * g + qt) % 2
                    odb = cnt["od"] % 2
                    cnt["od"] += 1
                    o_ps = PS[6][:, (2 * odb) * 128:(2 * odb + 1) * 128]
                    d_ps = PS[6][:, (2 * odb + 1) * 128:(2 * odb + 2) * 128]
                    for (tgt, is_den) in ((o_ps, False), (d_ps, True)):
                        z = 0
                        for h in range(2):
                            c_lo = 64 if h == 0 else 0
                            for (kbi, kc, vbk) in srcs:
                                z += 1
                                if is_den:
                                    lh = onespad[:, c_lo:c_lo + 128]
                                else:
                                    lh = Vpad[:, vbk, c_lo:c_lo + 128]
                                S.op("pe", lambda e, tgt=tgt, lh=lh, pb=pb, h=h, kbi=kbi, z=z, nmm=nmm: e.matmul(
                                    tgt, lhsT=lh, rhs=PT[pb][:, h, kbi, :], start=(z == 1), stop=(z == nmm)),
                                    reads=[("PT", pb), "Vpad", "cst"], writes=[("ps6", odb, is_den)],
                                    sig=(z == nmm))
                    S.op("dve", lambda e, d_ps=d_ps, odb=odb, cc=cc, g=g: e.tensor_scalar(
                        out=rden[odb][:], in0=d_ps, scalar1=esink[:, 2 * g + cc:2 * g + cc + 1], scalar2=None,
                        op0=ALU.add), reads=[("ps6", odb, True), "esink"], writes=[("rden", odb)])
                    S.op("dve", lambda e, odb=odb: e.reciprocal(out=rden[odb][:], in_=rden[odb][:]),
                         reads=[("rden", odb)], writes=[("rden", odb)])
                    S.op("dve", lambda e, o_ps=o_ps, odb=odb, ob=ob, cc=cc, qb=qb: e.tensor_tensor(
                        out=Og[ob][:, cc, qb * 128:(qb + 1) * 128], in0=o_ps, in1=rden[odb][:], op=ALU.mult),
                        reads=[("ps6", odb, False), ("rden", odb)], writes=[("Og", ob, cc)])
                    if qb == 3 and cc == 1:
                        for d in range(NCH):
                            for c2 in range(2):
                                S.op("pe", lambda e, d=d, c2=c2, ob=ob, vBo=vBo: e.matmul(
                                    PS[7][:, 0:512], lhsT=vBo[:, c2, d * 128:(d + 1) * 128], rhs=Og[ob][:, c2, :],
                                    start=(c2 == 0), stop=(c2 == 1)),
                                    reads=[("W", sB), ("Og", ob, c2)], writes=[("ps", 7)], sig=(c2 == 1))
                            S.op("dve", lambda e, d=d, qt=qt: e.scalar_tensor_tensor(
                                out=X[:, d, qt * 512:(qt + 1) * 512], in0=PS[7][:, 0:512], scalar=tabG[:, 1, d, 0:1],
                                in1=X[:, d, qt * 512:(qt + 1) * 512], op0=ALU.mult, op1=ALU.add),
                                reads=[("ps", 7), ("tabG", 1)] + xk(0, d, qt * 512, 512),
                                writes=xk(0, d, qt * 512, 512))

                if ATT_PIPE:
                    stage_s(0)
                    for it in range(len(its)):
                        if it + 1 < len(its):
                            stage_s(it + 1)
                        stage_p(it)
                else:
                    for it in range(len(its)):
                        stage_s(it)
                        stage_p(it)
            S.barrier()
            A.reset(m)

        H = A.alloc([128, NCH, TT], BF16)
        if ONLY_ATTN:
            ada_layer(1)
            attn_phase(1, H)
            stop_after = "x_ffn1_0"
        else:
            ada_layer(0)
            ffn(0, 0, TILES_A, H)
        if stop_after not in ("x_ffn1_0",):
            conv_phase(0, H)
        if stop_after not in ("x_ffn1_0", "x_mix_0"):
            ffn(0, 1, TILES_A, H)
        if stop_after not in ("x_ffn1_0", "x_mix_0", "x_out_0"):
            ada_layer(1)
            ffn(1, 0, TILES_A, H)
        if stop_after not in ("x_ffn1_0", "x_mix_0", "x_out_0", "x_ffn1_1"):
            attn_phase(1, H)
        if stop_after not in ("x_ffn1_0", "x_mix_0", "x_out_0", "x_ffn1_1", "x_mix_1"):
            ffn(1, 1, TILES_OWN, H)

        def final_out(do_norm):
            S.barrier()
            m = A.mark()
            A.reset(0)
            stg = [A.alloc([128, NCH, 512], F32) for _ in range(2)]
            sq = [A.alloc([128, 512], F32) for _ in range(3)]
            rstd = [A.alloc([128, 512], F32) for _ in range(2)]
            rsq = [A.alloc([128, 512], F32) for _ in range(2)]
            oT_v = outT.rearrange("(c p) t -> p c t", p=128)
            toks = []
            n_sq = 0
            for ti, (a, n, w) in enumerate(TILES_OWN):
                b = ti % 2
                if do_norm:
                    msb = 6 + b
                    ms = PS[msb][:, 0:n]
                    for c in range(NCH):
                        q = n_sq % 3
                        n_sq += 1
                        S.op("act", lambda e, q=q, c=c, a=a, n=n: e.activation(
                            out=sq[q][:, 0:n], in_=X[:, c, a:a + n], func=AF.Square),
                            reads=xk(0, c, a, n), writes=[("sq", q)])
                        S.op("pe", lambda e, q=q, c=c, ms=ms, n=n: e.matmul(
                            ms, lhsT=onesf[:], rhs=sq[q][:, 0:n], start=(c == 0), stop=(c == NCH - 1)),
                            reads=[("sq", q), "onesf"], writes=[("ps", msb)], sig=True)
                    S.op("act", lambda e, b=b, ms=ms, n=n: e.activation(
                        out=rsq[b][:, 0:n], in_=ms, func=AF.Sqrt, bias=epsb[:, 0:1], scale=1.0),
                        reads=[("ps", msb), "epsb"], writes=[("rsq", b)])
                    S.op("dve", lambda e, b=b, n=n: e.reciprocal(out=rstd[b][:, 0:n], in_=rsq[b][:, 0:n]),
                         reads=[("rsq", b)], writes=[("rstd", b)])
                    for c in range(NCH):
                        S.op("dve", lambda e, c=c, a=a, n=n, b=b: e.scalar_tensor_tensor(
                            out=stg[b][:, c, 0:n], in0=X[:, c, a:a + n], scalar=P("final_g", c, c + 1),
                            in1=rstd[b][:, 0:n], op0=ALU.mult, op1=ALU.mult),
                            reads=xk(0, c, a, n) + [("rstd", b), "prm"], writes=[("stg", b)],
                            sig=True)
                else:
                    for c in range(NCH):
                        S.op("act", lambda e, c=c, a=a, n=n, b=b: e.activation(
                            out=stg[b][:, c, 0:n], in_=X[:, c, a:a + n], func=AF.Identity),
                            reads=xk(0, c, a, n), writes=[("stg", b)])
                toks.append(S.dma("sp", [lambda e, b=b, a=a, n=n: e.dma_start(
                    out=oT_v[:, :, a:a + n], in_=stg[b][:, :, 0:n])], f"st{b}", reads=[("stg", b)]))
            S.wait_tokens("sp", toks)
            A.reset(m)

        final_out(stop_after is None)

        S.finalize()
        with nc.Block() as block:
            @block.tensor
            def _(e):
                S.emit("pe", e)

            @block.scalar
            def _(e):
                S.emit("act", e)

            @block.vector
            def _(e):
                S.emit("dve", e)

            @block.gpsimd
            def _(e):
                S.emit("pool", e)

            @block.sync
            def _(e):
                S.emit("sp", e)
    return nc


def _partner_cols(n):
    j = np.arange(n)
    d = j % 64
    pd = np.where((d % 32) < 16, d + 16, d - 16)
    return j - d + pd


def prepare_inputs(inp):
    f = lambda a: np.ascontiguousarray(np.asarray(a, dtype=np.float32))
    x, c, ctx, c_ctx = f(inp["x"]), f(inp["c"]), f(inp["ctx"]), f(inp["c_ctx"])
    w_qkv = f(inp["attn_w_qkv"])[0]
    chunk = lambda v: np.ascontiguousarray(v.reshape(-1, 128).T)
    shared = {
        "ada_w": f(inp["ada_w"]), "ffn1_wi": f(inp["ffn1_wi"]), "ffn1_wo": f(inp["ffn1_wo"]),
        "ffn2_wi": f(inp["ffn2_wi"]), "ffn2_wo": f(inp["ffn2_wo"]),
        "pw1_w": f(inp["conv_pw1_w"])[0], "pw2_w": f(inp["conv_pw2_w"])[0],
        "w_v": np.ascontiguousarray(w_qkv[:, 1280:1536]), "w_o": f(inp["attn_w_o"])[0],
    }
    watt = np.zeros((D, 4 * 768), np.float32)
    for g in range(4):
        q = w_qkv[:, 256 * g:256 * g + 256]
        k = w_qkv[:, 1024 + 64 * g:1024 + 64 * g + 64]
        kd = np.concatenate([k, k], axis=1)
        o = 768 * g
        watt[:, o:o + 256] = q
        watt[:, o + 256:o + 512] = q[:, _partner_cols(256)]
        watt[:, o + 512:o + 640] = kd
        watt[:, o + 640:o + 768] = kd[:, _partner_cols(128)]
    shared["w_att"] = watt
    cst = np.zeros((128, NCST), np.float32)
    kk = np.arange(128)[:, None]
    qq = np.arange(128)[None, :]
    lo = (kk >= qq).astype(np.float32)
    hi = (kk <= qq).astype(np.float32)
    cst[:, 0:128] = lo
    cst[:, 128:256] = lo
    cst[:, 256:384] = hi
    cst[:, 384:512] = hi
    cst[:, 512 + 64:512 + 128] = 1.0
    cst[:, 704:832] = np.eye(128, dtype=np.float32)
    shared["cst"] = cst
    p = np.arange(128)
    d = p % 64
    ax = d // 32
    hh = (d % 32) // 16
    fr = d % 16
    inv = (np.float32(10000.0) ** (-np.arange(16, dtype=np.float32) / np.float32(16))).astype(np.float32)
    in_maps = []
    for core in range(8):
        b, hf = core // 2, core % 2
        idx = np.arange(T) if hf == 0 else (SEQ - 1 - np.arange(T))
        m = dict(shared)
        m["xT"] = np.ascontiguousarray(x[b, idx, :].T)
        cb = ctx[b] if hf == 0 else ctx[b, ::-1]
        m["cT"] = np.ascontiguousarray(cb.T)
        prm = np.zeros((128, NPRM), np.float32)

        def put(name, arr):
            o, w = PRM[name]
            assert arr.shape == (128, w), (name, arr.shape)
            prm[:, o:o + w] = arr
        cc = np.stack([chunk(c[b]), chunk(c_ctx)], axis=2).reshape(128, 16)
        put("cc", cc)
        put("ada_b", np.concatenate([chunk(f(inp["ada_b"])[l]) for l in range(2)], axis=1))
        put("norm_g", np.concatenate([chunk(f(inp["norm_g"])[l, j]) for l in range(2) for j in range(3)], axis=1))
        put("pw1_b", chunk(f(inp["conv_pw1_b"])[0]))
        dw = f(inp["conv_dw_w"])[0]
        if hf == 1:
            dw = dw[::-1]
        put("dw_w", np.ascontiguousarray(dw.T.reshape(NCH, 128, CW).transpose(1, 0, 2)).reshape(128, NCH * CW))
        put("dw_b", chunk(f(inp["conv_dw_b"])[0]))
        put("ln_g", chunk(f(inp["conv_ln_g"])[0]))
        put("ln_b", chunk(f(inp["conv_ln_b"])[0]))
        put("pw2_b", chunk(f(inp["conv_pw2_b"])[0]))
        sk = f(inp["attn_sink"])[0]
        put("sink", np.stack([np.where(p >= 64, sk[2 * cch + 1], sk[2 * cch]) for cch in range(NCH)], axis=1).astype(np.float32))
        put("final_g", chunk(f(inp["final_g"])))
        m["prm"] = prm
        row = (idx // 64).astype(np.float32)
        col = (idx % 64).astype(np.float32)
        pos = np.where(ax[:, None] == 0, row[None, :], col[None, :]).astype(np.float32)
        ang = (pos * inv[fr][:, None]).astype(np.float32)
        cs = np.cos(ang).astype(np.float32)
        sn = np.sin(ang).astype(np.float32)
        sn = np.where(hh[:, None] == 0, -sn, sn).astype(np.float32)
        m["rope"] = np.ascontiguousarray(np.stack([cs, sn], axis=0))
        in_maps.append(m)
    return in_maps


def assemble(results):
    out = np.zeros((4, SEQ, D), np.float32)
    for core in range(8):
        b, hf = core // 2, core % 2
        idx = np.arange(OWN) if hf == 0 else (SEQ - 1 - np.arange(OWN))
        out[b, idx, :] = np.asarray(results[core]["outT"]).T
    return out


def kernel(**inputs):
    nc = build_program()
    in_maps = prepare_inputs(inputs)
    res = run_bass_kernel_spmd(nc, in_maps, core_ids=list(range(8)))
    return assemble(res.results)
```

```python
import numpy as np
import concourse.bass as bass
import concourse.mybir as mybir
from concourse.bass_utils import run_bass_kernel_spmd

F32 = mybir.dt.float32
BF16 = mybir.dt.bfloat16
AF = mybir.ActivationFunctionType
ALU = mybir.AluOpType

D = 1024
NCH = 8
SEQ = 4096
OWN = 2048
T = 2200
TK = 2176
CTX = 256
TT = T + CTX
DFF = 2816
NFC = 22
CW = 31
EPS = 1e-6
SLOT = 6144
import os
ATT_PIPE = int(os.environ.get('ATT_PIPE', '0'))
ATT_MASK = os.environ.get('ATT_MASK', 'pool')
SAME_ENG = int(os.environ.get('SAME_ENG', '1'))
ONLY_ATTN = int(os.environ.get('ONLY_ATTN', '0'))
ATT_EXP = int(os.environ.get('ATT_EXP', '0'))
NSLOT = 4

TILES_A = [(i * 440, 440, 0) for i in range(5)] + [(T, 256, 1)]
TILES_OWN = [(i * 512, 512, 0) for i in range(4)]

PRM = {}
_o = 0
for _n, _w in [("cc", 16), ("ada_b", 144), ("norm_g", 48), ("pw1_b", 16), ("dw_w", 248), ("dw_b", 8),
               ("ln_g", 8), ("ln_b", 8), ("pw2_b", 8), ("sink", 8), ("final_g", 8)]:
    PRM[_n] = (_o, _w)
    _o += _w
NPRM = _o
NCST = 2 * 256 + 192 + 128

_XB = sorted(set([i * 440 for i in range(6)] + [i * 512 for i in range(5)] + [T]))


def _segs(a, b):
    return [i for i in range(len(_XB) - 1) if _XB[i] < b and _XB[i + 1] > a]


class Sched:
    CE = ("pe", "act", "dve", "pool")
    NEPOCH = 16

    def __init__(self, nc, sems):
        self.nc = nc
        self.sems = sems
        self.cnt = {k: 0 for k in sems}
        self.ops = []
        self.byeng = {e: [] for e in ("pe", "act", "dve", "pool", "sp")}
        self.last_w = {}
        self.readers = {}
        self.unsig = {e: [] for e in self.byeng}
        self.redirect = {}
        self.epoch = 0
        self.last_sig = {}

    def _deps(self, reads, writes):
        deps = []
        for b in reads:
            t = self.last_w.get(b)
            if t is not None:
                deps.append((t, True))
        for b in writes:
            t = self.last_w.get(b)
            if t is not None:
                deps.append((t, False))
            for t in self.readers.get(b, ()):
                deps.append((t, False))
        return deps

    def _record(self, tok, reads, writes):
        for b in reads:
            self.readers.setdefault(b, []).append(tok)
        for b in writes:
            self.last_w[b] = tok
            self.readers[b] = []

    def _new(self, eng, fns, deps, kind, sig=True, chan=None):
        rec = dict(id=len(self.ops), eng=eng, epoch=self.epoch, fns=fns, deps=deps, kind=kind, sig=sig, chan=chan)
        self.ops.append(rec)
        self.byeng[eng].append(rec)
        return rec

    def op(self, eng, fn, reads=(), writes=(), sig=True):
        rec = self._new(eng, [fn], self._deps(reads, writes), "op", sig=sig)
        if sig:
            for i in self.unsig[eng]:
                self.redirect[i] = rec["id"]
            self.unsig[eng] = []
            self.last_sig[eng] = rec["id"]
        else:
            self.unsig[eng].append(rec["id"])
        self._record(("op", rec["id"]), reads, writes)

    def dma(self, eng, fns, chan, reads=(), writes=()):
        rec = self._new(eng, list(fns), self._deps(reads, writes), "dma", chan=chan)
        self.cnt[chan] += 16 * len(fns)
        tok = ("dma", chan, self.cnt[chan])
        self._record(tok, reads, writes)
        return tok

    def wait_tokens(self, eng, toks):
        self._new(eng, [], [(t, True) for t in toks], "wait")

    def barrier(self):
        for e in self.CE:
            assert not self.unsig[e], e
        toks = {e: ("op", self.last_sig[e]) for e in self.CE
                if e in self.last_sig and self.ops[self.last_sig[e]]["epoch"] == self.epoch}
        for e in ("pe", "act", "dve", "pool", "sp"):
            self.wait_tokens(e, [t for k, t in toks.items() if k != e])
        self.epoch += 1
        assert self.epoch < self.NEPOCH

    def finalize(self):
        for e in self.CE:
            assert not self.unsig[e], e

        def resolve(tok, rec, raw):
            if tok[0] == "dma":
                return tok
            i = self.redirect.get(tok[1], tok[1])
            d = self.ops[i]
            if d["epoch"] != rec["epoch"]:
                return None
            if d["eng"] == rec["eng"]:
                if rec["eng"] in ("pe", "sp"):
                    return None
                if not SAME_ENG and not raw:
                    return None
            return ("op", i)
        awaited = set()
        for rec in self.ops:
            r = []
            for tok, raw in rec["deps"]:
                t = resolve(tok, rec, raw)
                if t is not None:
                    r.append(t)
                    if t[0] == "op":
                        awaited.add(t[1])
            rec["rdeps"] = r
        val = {}
        cnt = {}
        for e in self.CE:
            for rec in self.byeng[e]:
                if rec["kind"] == "op" and rec["id"] in awaited:
                    k = f"{e}@{rec['epoch']}"
                    cnt[k] = cnt.get(k, 0) + 1
                    val[rec["id"]] = (k, cnt[k])
        self.prog = {}
        self.nsig = len(val)
        for e, recs in self.byeng.items():
            waited = {}
            out = []
            for rec in recs:
                w = {}
                for t in rec["rdeps"]:
                    k, v = val[t[1]] if t[0] == "op" else (t[1], t[2])
                    if w.get(k, 0) < v:
                        w[k] = v
                wl = []
                for k, v in w.items():
                    if waited.get(k, 0) < v:
                        waited[k] = v
                        wl.append((k, v))
                if rec["kind"] == "dma":
                    incs = [(rec["chan"], 16)] * len(rec["fns"])
                elif rec["id"] in val:
                    incs = [(val[rec["id"]][0], 1)]
                else:
                    incs = []
                out.append((wl, rec["fns"], incs))
            self.prog[e] = out

    def emit(self, eng, e):
        for waits, fns, incs in self.prog[eng]:
            for k, v in waits:
                e.wait_ge(self.sems[k], v)
            for i, fn in enumerate(fns):
                ins = fn(e)
                if i < len(incs):
                    ins.then_inc(self.sems[incs[i][0]], incs[i][1])


class Arena:
    def __init__(self, ap_f32, nbytes):
        self.ap = ap_f32
        self.n = nbytes
        self.off = 0

    def mark(self):
        return self.off

    def reset(self, m):
        self.off = m

    def alloc(self, shape, dt):
        es = 2 if dt == BF16 else 4
        free = 1
        for s in shape[1:]:
            free *= s
        nb = (free * es + 31) // 32 * 32
        assert self.off + nb <= self.n, ("arena overflow", self.off, nb, self.n)
        w0 = self.off // 4
        v = self.ap[:, w0:w0 + nb // 4]
        self.off += nb
        if dt == BF16:
            v = v.bitcast(BF16)
        v = v[:, 0:free]
        if len(shape) == 3:
            v = v.rearrange("p (a b) -> p a b", a=shape[1])
        elif len(shape) == 4:
            v = v.rearrange("p (a b c) -> p a b c", a=shape[1], b=shape[2])
        return v


def build_program(stop_after=None):
    nc = bass.Bass("TRN2", target_bir_lowering=False)
    dr = {}

    def din(name, shape):
        dr[name] = nc.dram_tensor(name, shape, F32, kind="ExternalInput").ap()
        return dr[name]

    xT = din("xT", [D, T])
    cT = din("cT", [D, CTX])
    prm_d = din("prm", [128, NPRM])
    cst_d = din("cst", [128, NCST])
    rope_d = din("rope", [2, 128, T])
    ada_w = din("ada_w", [2, D, 9 * D])
    f_wi = [din("ffn1_wi", [2, D, 2 * DFF]), din("ffn2_wi", [2, D, 2 * DFF])]
    f_wo = [din("ffn1_wo", [2, DFF, D]), din("ffn2_wo", [2, DFF, D])]
    pw1_w = din("pw1_w", [D, 2 * D])
    pw2_w = din("pw2_w", [D, D])
    watt = din("w_att", [D, 4 * 768])
    wv_d = din("w_v", [D, 256])
    wo_d = din("w_o", [D, D])
    outT = nc.dram_tensor("outT", [D, OWN], F32, kind="ExternalOutput").ap()

    semkeys = [f"{e}@{i}" for e in Sched.CE for i in range(Sched.NEPOCH)] + ["ldp", "ldc", "ldxc", "st0", "st1", "rp0", "rp1", "ad0", "ad1"] + \
              [f"ldx{i}" for i in range(5)] + [f"w{i}" for i in range(NSLOT)]

    import contextlib
    es = contextlib.ExitStack()
    with es:
        sems = {k: es.enter_context(nc.semaphore(k)) for k in semkeys}
        X = es.enter_context(nc.sbuf_tensor("X", [128, NCH, T], F32))
        XC = es.enter_context(nc.sbuf_tensor("XC", [128, NCH, CTX], F32))
        Wr = es.enter_context(nc.sbuf_tensor("Wr", [128, NSLOT, SLOT], BF16))
        prm = es.enter_context(nc.sbuf_tensor("prm_sb", [128, NPRM], F32))
        cst = es.enter_context(nc.sbuf_tensor("cst_sb", [128, NCST], BF16))
        mod = es.enter_context(nc.sbuf_tensor("mod", [128, 2, 72, 2], F32))
        tabA = es.enter_context(nc.sbuf_tensor("tabA", [128, 3, NCH, 2], F32))
        tabG = es.enter_context(nc.sbuf_tensor("tabG", [128, 3, NCH, 2], F32))
        tabGb = es.enter_context(nc.sbuf_tensor("tabGb", [128, NCH, 2], F32))
        scT = es.enter_context(nc.sbuf_tensor("scT", [128, NCH, 2], BF16))
        onesf = es.enter_context(nc.sbuf_tensor("onesf", [128, 128], F32))
        esink = es.enter_context(nc.sbuf_tensor("esink", [128, NCH], F32))
        epsb = es.enter_context(nc.sbuf_tensor("epsb", [128, 1], F32))
        ARENA_BYTES = 212863 - (NCH * T * 4 + NCH * CTX * 4 + NSLOT * SLOT * 2 + NPRM * 4 + NCST * 2
                                + 2 * 72 * 2 * 4 + 2 * 3 * NCH * 2 * 4 + NCH * 2 * 4 + NCH * 2 * 2
                                + 128 * 4 + NCH * 4) - 1024
        ARENA_BYTES = ARENA_BYTES // 64 * 64
        ar_t = es.enter_context(nc.sbuf_tensor("arena", [128, ARENA_BYTES // 4], F32))
        PSALL = es.enter_context(nc.psum_tensor("psall", [128, 4096], F32))
        PS = [PSALL[:, i * 512:(i + 1) * 512] for i in range(8)]
        S = Sched(nc, sems)
        A = Arena(ar_t, ARENA_BYTES)

        def P(name, a=0, b=None):
            o, w = PRM[name]
            b = w if b is None else b
            return prm[:, o + a:o + b]

        def xbuf(which):
            return X if which == 0 else XC

        def xk(which, c, a, n):
            if which == 1:
                return [("XC", c)]
            return [("X", c, s) for s in _segs(a, a + n)]

        def hk(c, a, n):
            if a >= T:
                return [("H", c, "c")]
            return [("H", c, s) for s in _segs(a, a + n)]

        S.dma("sp", [lambda e: e.dma_start(out=prm[:], in_=prm_d[:, :])], "ldp", writes=["prm"])
        S.dma("pool", [lambda e: e.dma_start(out=cst[:], in_=cst_d[:, :])], "ldc", writes=["cst"])
        xT_v = xT.rearrange("(c p) t -> p c t", p=128)
        cT_v = cT.rearrange("(c p) t -> p c t", p=128)
        for i in range(5):
            S.dma("sp", [lambda e, i=i: e.dma_start(out=X[:, :, i * 440:(i + 1) * 440],
                                                   in_=xT_v[:, :, i * 440:(i + 1) * 440])],
                  f"ldx{i}", writes=[k for c in range(NCH) for k in xk(0, c, i * 440, 440)])
        S.dma("sp", [lambda e: e.dma_start(out=XC[:], in_=cT_v)], "ldxc",
              writes=[("XC", c) for c in range(NCH)])
        S.op("dve", lambda e: e.memset(onesf[:], 1.0 / D), writes=["onesf"])
        S.op("dve", lambda e: e.memset(epsb[:], EPS), writes=["epsb"])
        S.op("act", lambda e: e.activation(out=scT[:], in_=P("cc").rearrange("p (k w) -> p k w", w=2),
                                           func=AF.Silu), reads=["prm"], writes=["scT"])
        S.op("act", lambda e: e.activation(out=esink[:], in_=P("sink"), func=AF.Exp),
             reads=["prm"], writes=["esink"])

        ring = {"next": 0}

        def slot_alloc():
            s = ring["next"]
            ring["next"] = (s + 1) % NSLOT
            return s

        def load_slot(s, fns):
            S.dma("pool", fns, f"w{s}", writes=[("W", s)])

        def ada_unit(l, u, stg, sb):
            aw = ada_w[l].rearrange("(k p) n -> p k n", p=128)
            S.dma("pool", [lambda e, u=u, stg=stg: e.dma_start(out=stg, in_=aw[:, :, u * 256:(u + 1) * 256])],
                  f"ad{sb}", writes=[("adst", sb)])

            def mm():
                pb = 6 + (u % 2)
                ps = PS[pb][:, 504:508].rearrange("p (a b) -> p a b", b=2)
                for oc in range(2):
                    for k in range(NCH):
                        S.op("pe", lambda e, oc=oc, k=k, ps=ps, stg=stg: e.matmul(
                            ps[:, oc, :], lhsT=stg[:, k, oc * 128:(oc + 1) * 128], rhs=scT[:, k, :],
                            start=(k == 0), stop=(k == NCH - 1)),
                            reads=[("adst", sb), "scT"], writes=[("ps", pb)], sig=(k == NCH - 1))
                ab = P("ada_b", l * 72 + u * 2, l * 72 + u * 2 + 2)
                S.op("dve", lambda e, ps=ps, ab=ab, u=u: e.tensor_tensor(
                    out=mod[:, l, u * 2:(u + 1) * 2, :], in0=ps,
                    in1=ab.unsqueeze(2).to_broadcast([128, 2, 2]), op=ALU.add),
                    reads=[("ps", pb), "prm"], writes=[("mod", l, u // 12)])
            return mm

        def ada_tables(l, j):
            g = P("norm_g", (l * 3 + j) * 8, (l * 3 + j) * 8 + 8)
            S.op("dve", lambda e, j=j, g=g: e.scalar_tensor_tensor(
                out=tabA[:, j], in0=mod[:, l, (3 * j + 1) * 8:(3 * j + 2) * 8, :], scalar=1.0,
                in1=g.unsqueeze(2).to_broadcast([128, NCH, 2]), op0=ALU.add, op1=ALU.mult),
                reads=[("mod", l, j), "prm"], writes=[("tabA", j)])
            S.op("dve", lambda e, j=j: e.tensor_scalar(
                out=tabG[:, j], in0=mod[:, l, (3 * j + 2) * 8:(3 * j + 3) * 8, :],
                scalar1=(1.0 if j == 1 else 0.5), scalar2=None, op0=ALU.mult),
                reads=[("mod", l, j)], writes=[("tabG", j)])
            if l == 0 and j == 1:
                S.op("dve", lambda e: e.tensor_tensor(
                    out=tabGb[:], in0=tabG[:, 1], in1=P("pw2_b").unsqueeze(2).to_broadcast([128, NCH, 2]),
                    op=ALU.mult), reads=[("tabG", 1), "prm"], writes=["tabGb"])

        def ada_run(l, units, adst):
            pend = []
            for i, u in enumerate(units):
                pend.append(ada_unit(l, u, adst[i % 2], i % 2))
                if len(pend) == 2:
                    pend.pop(0)()
            for f_ in pend:
                f_()

        def tabB(l, j):
            return mod[:, l, (3 * j) * 8:(3 * j + 1) * 8, :]

        def make_prepass(l, j, tiles, H, nbuf=3):
            sq = [A.alloc([128, 512], F32) for _ in range(nbuf)]
            rstd = [A.alloc([128, 512], F32) for _ in range(2)]
            rsq = [A.alloc([128, 512], F32) for _ in range(2)]
            tmp = [A.alloc([128, 512], F32) for _ in range(nbuf)]
            st_ = {"sq": 0, "tmp": 0}

            def pre_tile(ti):
                a, n, w = tiles[ti]
                xb = xbuf(w)
                xa = a - T if w == 1 else a
                msb = 6 + (ti % 2)
                ms = PS[msb][:, 0:n]
                for c in range(NCH):
                    q = st_["sq"] % nbuf
                    st_["sq"] += 1
                    S.op("act", lambda e, q=q, c=c, xb=xb, xa=xa, n=n: e.activation(
                        out=sq[q][:, 0:n], in_=xb[:, c, xa:xa + n], func=AF.Square),
                        reads=xk(w, c, xa, n), writes=[("sq", q)])
                    S.op("pe", lambda e, q=q, c=c, ms=ms, n=n: e.matmul(
                        ms, lhsT=onesf[:], rhs=sq[q][:, 0:n], start=(c == 0), stop=(c == NCH - 1)),
                        reads=[("sq", q), "onesf"], writes=[("ps", msb)], sig=True)
                r = ti % 2
                S.op("act", lambda e, r=r, ms=ms, n=n: e.activation(
                    out=rsq[r][:, 0:n], in_=ms, func=AF.Sqrt, bias=epsb[:, 0:1], scale=1.0),
                    reads=[("ps", msb), "epsb"], writes=[("rsq", r)])
                S.op("dve", lambda e, r=r, n=n: e.reciprocal(out=rstd[r][:, 0:n], in_=rsq[r][:, 0:n]),
                     reads=[("rsq", r)], writes=[("rstd", r)])
                for c in range(NCH):
                    q = st_["tmp"] % nbuf
                    st_["tmp"] += 1
                    S.op("dve", lambda e, q=q, c=c, xb=xb, xa=xa, n=n, r=r, w=w: e.scalar_tensor_tensor(
                        out=tmp[q][:, 0:n], in0=xb[:, c, xa:xa + n], scalar=tabA[:, j, c, w:w + 1],
                        in1=rstd[r][:, 0:n], op0=ALU.mult, op1=ALU.mult),
                        reads=xk(w, c, xa, n) + [("rstd", r), ("tabA", j)], writes=[("ptmp", q)])
                    S.op("act", lambda e, q=q, c=c, a=a, n=n, w=w: e.activation(
                        out=H[:, c, a:a + n], in_=tmp[q][:, 0:n], func=AF.Identity,
                        bias=tabB(l, j)[:, c, w:w + 1], scale=1.0),
                        reads=[("ptmp", q), ("mod", l, j)], writes=hk(c, a, n))
            return pre_tile

        def prepass(l, j, tiles, H):
            m = A.mark()
            pt = make_prepass(l, j, tiles, H)
            for ti in range(len(tiles)):
                pt(ti)
            S.barrier()
            A.reset(m)

        def ffn(l, f, tiles, H, ada_job=None):
            j = 0 if f == 0 else 2
            m = A.mark()
            pre_tile = make_prepass(l, j, tiles, H, nbuf=2)
            adst = [A.alloc([128, NCH, 256], BF16) for _ in range(2)]
            act = [A.alloc([128, 4, 512], BF16) for _ in range(2)]
            sg = [A.alloc([128, 512], F32) for _ in range(2)]
            wi = f_wi[f][l].rearrange("(k p) (two n) -> p k two n", p=128, two=2)
            wo = f_wo[f][l].rearrange("(j p) n -> p j n", p=128)
            sweeps = [[0, 1], [2, 3], [4, 5], [6, 7], [8, 9], [10]]
            unit_slot = {}

            def load_unit(u):
                s = slot_alloc()
                unit_slot[u] = s
                wiv = Wr[:, s, 0:4096].rearrange("p (k two n) -> p k two n", k=NCH, two=2)
                wov = Wr[:, s, 4096:6144].rearrange("p (j n) -> p j n", j=2)
                load_slot(s, [lambda e, wiv=wiv, u=u: e.dma_start(out=wiv[:, :, 0, :], in_=wi[:, :, 0, u * 256:(u + 1) * 256]),
                              lambda e, wiv=wiv, u=u: e.dma_start(out=wiv[:, :, 1, :], in_=wi[:, :, 1, u * 256:(u + 1) * 256]),
                              lambda e, wov=wov, u=u: e.dma_start(out=wov, in_=wo[:, 2 * u:2 * u + 2, :])])

            tasks = []
            for si, sw in enumerate(sweeps):
                for ti in range(len(tiles)):
                    tasks.append((si, ti))
            st = {"sg": 0, "gu": 0, "y": 0}

            def stage_a(k):
                si, ti = tasks[k]
                a, n, w = tiles[ti]
                ab = k % 2
                chunks = [(u, jj) for u in sweeps[si] for jj in range(2)]
                for ci, (u, jj) in enumerate(chunks):
                    s = unit_slot[u]
                    wiv = Wr[:, s, 0:4096].rearrange("p (k two n) -> p k two n", k=NCH, two=2)
                    gb = st["gu"] % 2
                    st["gu"] += 1
                    gps = PS[gb * 2][:, 0:n]
                    ups = PS[gb * 2 + 1][:, 0:n]
                    for two, pp, pb in ((0, gps, gb * 2), (1, ups, gb * 2 + 1)):
                        for kk in range(NCH):
                            S.op("pe", lambda e, pp=pp, wiv=wiv, kk=kk, two=two, jj=jj, a=a, n=n: e.matmul(
                                pp, lhsT=wiv[:, kk, two, jj * 128:(jj + 1) * 128], rhs=H[:, kk, a:a + n],
                                start=(kk == 0), stop=(kk == NCH - 1)),
                                reads=[("W", s)] + hk(kk, a, n), writes=[("ps", pb)], sig=(kk == NCH - 1))
                    q = st["sg"] % 2
                    st["sg"] += 1
                    S.op("act", lambda e, q=q, gps=gps, n=n: e.activation(out=sg[q][:, 0:n], in_=gps, func=AF.Silu),
                         reads=[("ps", gb * 2)], writes=[("sg", q)])
                    S.op("dve", lambda e, q=q, ups=ups, n=n, ab=ab, ci=ci: e.tensor_tensor(
                        out=act[ab][:, ci, 0:n], in0=sg[q][:, 0:n], in1=ups, op=ALU.mult),
                        reads=[("sg", q), ("ps", gb * 2 + 1)], writes=[("act", ab, ci)])

            def stage_b(k):
                si, ti = tasks[k]
                a, n, w = tiles[ti]
                xb = xbuf(w)
                xa = a - T if w == 1 else a
                ab = k % 2
                chunks = [(u, jj) for u in sweeps[si] for jj in range(2)]
                for d in range(NCH):
                    yb = 4 + st["y"] % 2
                    st["y"] += 1
                    yps = PS[yb][:, 0:n]
                    for ci, (u, jj) in enumerate(chunks):
                        s = unit_slot[u]
                        wov = Wr[:, s, 4096:6144].rearrange("p (j n) -> p j n", j=2)
                        S.op("pe", lambda e, yps=yps, wov=wov, jj=jj, d=d, ab=ab, ci=ci, n=n: e.matmul(
                            yps, lhsT=wov[:, jj, d * 128:(d + 1) * 128], rhs=act[ab][:, ci, 0:n],
                            start=(ci == 0), stop=(ci == len(chunks) - 1)),
                            reads=[("W", s), ("act", ab, ci)], writes=[("ps", yb)], sig=(ci == len(chunks) - 1))
                    S.op("dve", lambda e, yps=yps, d=d, xb=xb, xa=xa, n=n, w=w: e.scalar_tensor_tensor(
                        out=xb[:, d, xa:xa + n], in0=yps, scalar=tabG[:, j, d, w:w + 1], in1=xb[:, d, xa:xa + n],
                        op0=ALU.mult, op1=ALU.add),
                        reads=[("ps", yb), ("tabG", j)] + xk(w, d, xa, n), writes=xk(w, d, xa, n))

            loaded = 0

            def ensure_loaded(si):
                nonlocal loaded
                while loaded <= min(si, len(sweeps) - 1):
                    for u in sweeps[loaded]:
                        load_unit(u)
                    loaded += 1
            ensure_loaded(1)
            nt = len(tiles)
            pre_tile(0)
            if nt > 1:
                pre_tile(1)
            ada_l, ada_units = ada_job if ada_job else (0, [])
            ada_pend = []
            ada_i = 0
            for k in range(len(tasks) + 1):
                if k < len(tasks):
                    si, ti = tasks[k]
                    if ti == 1:
                        ensure_loaded(si + 1)
                    if si == 0 and ti + 2 < nt:
                        pre_tile(ti + 2)
                    if ada_i < len(ada_units):
                        ada_pend.append(ada_unit(ada_l, ada_units[ada_i], adst[ada_i % 2], ada_i % 2))
                        ada_i += 1
                        if len(ada_pend) == 2:
                            ada_pend.pop(0)()
                    stage_a(k)
                if k >= 1:
                    stage_b(k - 1)
            for f_ in ada_pend:
                f_()
            assert ada_i == len(ada_units)
            S.barrier()
            A.reset(m)

        def conv_phase(l, H):
            prepass(l, 1, TILES_A, H)
            m = A.mark()
            p1 = pw1_w.rearrange("(k p) n -> p k n", p=128)
            p2 = pw2_w.rearrange("(k p) n -> p k n", p=128)
            slots = []
            for i in range(4):
                s = slot_alloc()
                slots.append(s)
                v1 = Wr[:, s, 0:4096].rearrange("p (k two n) -> p k two n", k=NCH, two=2)
                v2 = Wr[:, s, 4096:6144].rearrange("p (k n) -> p k n", k=NCH)
                load_slot(s, [lambda e, v1=v1, i=i: e.dma_start(out=v1[:, :, 0, :], in_=p1[:, :, i * 256:(i + 1) * 256]),
                              lambda e, v1=v1, i=i: e.dma_start(out=v1[:, :, 1, :], in_=p1[:, :, D + i * 256:D + (i + 1) * 256]),
                              lambda e, v2=v2, i=i: e.dma_start(out=v2, in_=p2[:, :, i * 256:(i + 1) * 256])])
            UW = 472
            NPE = 16
            U = [A.alloc([128, UW], F32) for _ in range(2)]
            Ub = [A.alloc([128, UW], BF16) for _ in range(2)]
            Dg = A.alloc([128, 2, NPE, 128], BF16)
            V = A.alloc([128, NCH, 440], F32)
            sqv = [A.alloc([128, 440], F32) for _ in range(1)]
            varb = A.alloc([128, 440], F32)
            HN = A.alloc([128, NCH, 440], BF16)
            ident = cst[:, 704:832]
            dww = P("dw_w").rearrange("p (c k) -> p c k", k=CW)
            K1 = CW - NPE
            for ti, (a, n, w) in enumerate(TILES_A):
                xb = xbuf(w)
                xa = a - T if w == 1 else a
                s0, s1 = (0, T) if w == 0 else (T, T + CTX)
                lo, hi = max(a - 15, s0), min(a + n + 15, s1)
                ulo, uhi = lo - (a - 15), hi - (a - 15)
                nu = hi - lo
                for jp in range(4):
                    for jj in range(2):
                        j = 2 * jp + jj
                        s = slots[jp]
                        v1 = Wr[:, s, 0:4096].rearrange("p (k two n) -> p k two n", k=NCH, two=2)
                        for two in range(2):
                            pb = 2 * jj + two
                            pp = PS[pb][:, 0:nu]
                            for kk in range(NCH):
                                S.op("pe", lambda e, pp=pp, v1=v1, kk=kk, two=two, jj=jj, lo=lo, hi=hi: e.matmul(
                                    pp, lhsT=v1[:, kk, two, jj * 128:(jj + 1) * 128], rhs=H[:, kk, lo:hi],
                                    start=(kk == 0), stop=(kk == NCH - 1)),
                                    reads=[("W", s)] + hk(kk, lo, nu), writes=[("ps", pb)], sig=(kk == NCH - 1))
                        if ulo > 0:
                            S.op("dve", lambda e, jj=jj, ulo=ulo: e.memset(U[jj][:, 0:ulo], 0.0), writes=[("U", jj)])
                        if uhi < n + 30:
                            S.op("dve", lambda e, jj=jj, uhi=uhi, n=n: e.memset(U[jj][:, uhi:n + 30], 0.0), writes=[("U", jj)])
                        S.op("act", lambda e, jj=jj, j=j, nu=nu, ulo=ulo, uhi=uhi: e.activation(
                            out=U[jj][:, ulo:uhi], in_=PS[2 * jj + 1][:, 0:nu], func=AF.Sigmoid,
                            bias=P("pw1_b", 8 + j, 9 + j), scale=1.0),
                            reads=[("ps", 2 * jj + 1), "prm"], writes=[("U", jj)])
                        S.op("dve", lambda e, jj=jj, j=j, nu=nu, ulo=ulo, uhi=uhi: e.scalar_tensor_tensor(
                            out=U[jj][:, ulo:uhi], in0=PS[2 * jj][:, 0:nu], scalar=P("pw1_b", j, j + 1),
                            in1=U[jj][:, ulo:uhi], op0=ALU.add, op1=ALU.mult),
                            reads=[("ps", 2 * jj), "prm", ("U", jj)], writes=[("U", jj)])
                    for k in range(K1):
                        for jj in range(2):
                            j = 2 * jp + jj
                            if k == 0:
                                S.op("dve", lambda e, jj=jj, j=j, n=n: e.tensor_scalar(
                                    out=V[:, j, 0:n], in0=U[jj][:, 0:n], scalar1=dww[:, j, 0:1],
                                    scalar2=P("dw_b", j, j + 1), op0=ALU.mult, op1=ALU.add),
                                    reads=[("U", jj), "prm"], writes=[("V", j)])
                            else:
                                S.op("dve", lambda e, jj=jj, j=j, n=n, k=k: e.scalar_tensor_tensor(
                                    out=V[:, j, 0:n], in0=U[jj][:, k:k + n], scalar=dww[:, j, k:k + 1],
                                    in1=V[:, j, 0:n], op0=ALU.mult, op1=ALU.add),
                                    reads=[("U", jj), ("V", j), "prm"], writes=[("V", j)])
                    for jj in range(2):
                        j = 2 * jp + jj
                        S.op("act", lambda e, jj=jj, n=n: e.activation(
                            out=Ub[jj][:, 0:n + 30], in_=U[jj][:, 0:n + 30], func=AF.Identity),
                            reads=[("U", jj)], writes=[("Ub", jj)])
                        for kidx in range(NPE):
                            k = K1 + kidx
                            S.op("act", lambda e, jj=jj, j=j, k=k, kidx=kidx: e.activation(
                                out=Dg[:, jj, kidx, :], in_=ident, func=AF.Identity, scale=dww[:, j, k:k + 1]),
                                reads=["cst", "prm"], writes=[("Dg", jj, kidx)])
                    for jj in range(2):
                        j = 2 * jp + jj
                        for kidx in range(NPE):
                            k = K1 + kidx
                            S.op("pe", lambda e, jj=jj, k=k, kidx=kidx, n=n: e.matmul(
                                PS[4 + jj][:, 0:n], lhsT=Dg[:, jj, kidx, :], rhs=Ub[jj][:, k:k + n],
                                start=(kidx == 0), stop=(kidx == NPE - 1)),
                                reads=[("Dg", jj, kidx), ("Ub", jj)], writes=[("ps", 4 + jj)], sig=(kidx == NPE - 1))
                    for jj in range(2):
                        j = 2 * jp + jj
                        S.op("dve", lambda e, jj=jj, j=j, n=n: e.tensor_tensor(
                            out=V[:, j, 0:n], in0=V[:, j, 0:n], in1=PS[4 + jj][:, 0:n], op=ALU.add),
                            reads=[("V", j), ("ps", 4 + jj)], writes=[("V", j)])
                for j in range(NCH):
                    S.op("pe", lambda e, j=j, n=n: e.matmul(PS[4][:, 0:n], lhsT=onesf[:], rhs=V[:, j, 0:n],
                                                           start=(j == 0), stop=(j == NCH - 1)),
                         reads=[("V", j), "onesf"], writes=[("ps", 4)], sig=True)
                    S.op("act", lambda e, j=j, n=n: e.activation(out=sqv[0][:, 0:n], in_=V[:, j, 0:n], func=AF.Square),
                         reads=[("V", j)], writes=[("sqv", 0)])
                    S.op("pe", lambda e, j=j, n=n: e.matmul(PS[5][:, 0:n], lhsT=onesf[:], rhs=sqv[0][:, 0:n],
                                                           start=(j == 0), stop=(j == NCH - 1)),
                         reads=[("sqv", 0), "onesf"], writes=[("ps", 5)], sig=True)
                S.op("act", lambda e, n=n: e.activation(out=varb[:, 0:n], in_=PS[4][:, 0:n], func=AF.Square),
                     reads=[("ps", 4)], writes=["varb"])
                S.op("dve", lambda e, n=n: e.tensor_tensor(out=varb[:, 0:n], in0=PS[5][:, 0:n], in1=varb[:, 0:n],
                                                           op=ALU.subtract), reads=[("ps", 5), "varb"], writes=["varb"])
                S.op("act", lambda e, n=n: e.activation(out=varb[:, 0:n], in_=varb[:, 0:n], func=AF.Sqrt,
                                                        bias=epsb[:, 0:1], scale=1.0),
                     reads=["varb", "epsb"], writes=["varb"])
                S.op("dve", lambda e, n=n: e.reciprocal(out=varb[:, 0:n], in_=varb[:, 0:n]),
                     reads=["varb"], writes=["varb"])
                for j in range(NCH):
                    S.op("dve", lambda e, j=j, n=n: e.tensor_tensor(
                        out=V[:, j, 0:n], in0=V[:, j, 0:n], in1=PS[4][:, 0:n], op=ALU.subtract),
                        reads=[("V", j), ("ps", 4)], writes=[("V", j)])
                    S.op("dve", lambda e, j=j, n=n: e.tensor_tensor(
                        out=V[:, j, 0:n], in0=V[:, j, 0:n], in1=varb[:, 0:n], op=ALU.mult),
                        reads=[("V", j), "varb"], writes=[("V", j)])
                    S.op("act", lambda e, j=j, n=n: e.activation(
                        out=HN[:, j, 0:n], in_=V[:, j, 0:n], func=AF.Silu,
                        bias=P("ln_b", j, j + 1), scale=P("ln_g", j, j + 1)),
                        reads=[("V", j), "prm"], writes=[("HN", j)])
                for d in range(NCH):
                    s = slots[d // 2]
                    v2 = Wr[:, s, 4096:6144].rearrange("p (k n) -> p k n", k=NCH)
                    yb = 6 + d % 2
                    for kk in range(NCH):
                        S.op("pe", lambda e, v2=v2, kk=kk, d=d, yb=yb, n=n: e.matmul(
                            PS[yb][:, 0:n], lhsT=v2[:, kk, (d % 2) * 128:(d % 2 + 1) * 128], rhs=HN[:, kk, 0:n],
                            start=(kk == 0), stop=(kk == NCH - 1)),
                            reads=[("W", s), ("HN", kk)], writes=[("ps", yb)], sig=(kk == NCH - 1))
                    S.op("act", lambda e, d=d, n=n, w=w, yb=yb: e.activation(
                        out=sqv[0][:, 0:n], in_=PS[yb][:, 0:n], func=AF.Identity,
                        bias=tabGb[:, d, w:w + 1], scale=tabG[:, 1, d, w:w + 1]),
                        reads=[("ps", yb), ("tabG", 1), "tabGb"], writes=[("sqv", 0)])
                    S.op("dve", lambda e, d=d, xb=xb, xa=xa, n=n: e.tensor_tensor(
                        out=xb[:, d, xa:xa + n], in0=xb[:, d, xa:xa + n], in1=sqv[0][:, 0:n], op=ALU.add),
                        reads=[("sqv", 0)] + xk(w, d, xa, n), writes=xk(w, d, xa, n))
            S.barrier()
            A.reset(m)

        def attn_phase(l, H):
            prepass(l, 1, TILES_A, H)
            m = A.mark()
            Qg = A.alloc([128, 2, OWN], BF16)
            Kg = A.alloc([128, TK + CTX], BF16)
            Vpad = A.alloc([128, 19, 192], BF16)
            Og = [A.alloc([128, 2, 512], BF16) for _ in range(2)]
            PT = [A.alloc([128, 5, 2, 128] if ATT_PIPE else [128, 2, 5, 128], BF16) for _ in range(2)]
            cosb = A.alloc([128, 440], F32)
            sinb = A.alloc([128, 440], F32)
            t1 = A.alloc([128, 440], F32)
            t2 = A.alloc([128, 440], F32)
            rden = [A.alloc([128, 128], F32) for _ in range(2)]
            mask_lo = cst[:, 0:256].rearrange("p (h q) -> p h q", h=2)
            mask_hi = cst[:, 256:512].rearrange("p (h q) -> p h q", h=2)
            onespad = cst[:, 512:704]
            wa = watt.rearrange("(k p) n -> p k n", p=128)
            wvv = wv_d.rearrange("(k p) n -> p k n", p=128)
            wov = wo_d.rearrange("(j p) n -> p j n", p=128)
            S.op("pool", lambda e: e.memset(Vpad[:], 0.0), writes=["Vpad"])
            cnt = {"st": 0, "od": 0, "og": 0}
            for g in range(4):
                sA = slot_alloc()
                vA = Wr[:, sA, :].rearrange("p (k n) -> p k n", k=NCH)
                load_slot(sA, [lambda e, vA=vA, g=g: e.dma_start(out=vA, in_=wa[:, :, g * 768:(g + 1) * 768])])
                sB = slot_alloc()
                vBo = Wr[:, sB, 0:2048].rearrange("p (j n) -> p j n", j=2)
                vBv = Wr[:, sB, 2048:2560].rearrange("p (k n) -> p k n", k=NCH)
                load_slot(sB, [lambda e, vBo=vBo, g=g: e.dma_start(out=vBo, in_=wov[:, 2 * g:2 * g + 2, :]),
                               lambda e, vBv=vBv, g=g: e.dma_start(out=vBv, in_=wvv[:, :, 64 * g:64 * g + 64])])
                for ti in range(5):
                    a, n = 440 * ti, 440
                    S.dma("sp", [lambda e, a=a, n=n: e.dma_start(out=cosb[:, 0:n], in_=rope_d[0][:, a:a + n]),
                                 lambda e, a=a, n=n: e.dma_start(out=sinb[:, 0:n], in_=rope_d[1][:, a:a + n])],
                          "rp0", writes=["rope"])
                    items = [("q", 0), ("q", 1), ("k", 0)]
                    for ii, (kind, cc) in enumerate(items):
                        if kind == "q":
                            nn = max(0, min(a + n, OWN) - a)
                            base, bsw = cc * 128, 256 + cc * 128
                            dst = Qg[:, cc, a:a + nn] if nn > 0 else None
                            dkey = ("Qg", cc)
                        else:
                            nn = max(0, min(a + n, TK) - a)
                            base, bsw = 512, 640
                            dst = Kg[:, a:a + nn]
                            dkey = ("Kg",)
                        if nn == 0:
                            continue
                        pa, pb = 2 * (ii % 2), 2 * (ii % 2) + 1
                        for (pp, bb) in ((pa, base), (pb, bsw)):
                            for kk in range(NCH):
                                S.op("pe", lambda e, pp=pp, bb=bb, kk=kk, a=a, nn=nn, vA=vA: e.matmul(
                                    PS[pp][:, 0:nn], lhsT=vA[:, kk, bb:bb + 128], rhs=H[:, kk, a:a + nn],
                                    start=(kk == 0), stop=(kk == NCH - 1)),
                                    reads=[("W", sA)] + hk(kk, a, nn), writes=[("ps", pp)], sig=(kk == NCH - 1))
                        S.op("dve", lambda e, pa=pa, nn=nn: e.tensor_tensor(
                            out=t1[:, 0:nn], in0=PS[pa][:, 0:nn], in1=cosb[:, 0:nn], op=ALU.mult),
                            reads=[("ps", pa), "rope"], writes=["t1"])
                        S.op("dve", lambda e, pb=pb, nn=nn: e.tensor_tensor(
                            out=t2[:, 0:nn], in0=PS[pb][:, 0:nn], in1=sinb[:, 0:nn], op=ALU.mult),
                            reads=[("ps", pb), "rope"], writes=["t2"])
                        S.op("pool", lambda e, dst=dst, nn=nn: e.tensor_tensor(
                            out=dst, in0=t1[:, 0:nn], in1=t2[:, 0:nn], op=ALU.add),
                            reads=["t1", "t2"], writes=[dkey])
                for kk in range(NCH):
                    S.op("pe", lambda e, kk=kk, vA=vA: e.matmul(
                        PS[0][:, 0:CTX], lhsT=vA[:, kk, 512:640], rhs=H[:, kk, T:T + CTX],
                        start=(kk == 0), stop=(kk == NCH - 1)),
                        reads=[("W", sA)] + hk(kk, T, CTX), writes=[("ps", 0)], sig=(kk == NCH - 1))
                S.op("act", lambda e: e.activation(out=Kg[:, TK:TK + CTX], in_=PS[0][:, 0:CTX], func=AF.Identity),
                     reads=[("ps", 0)], writes=[("Kg",)])
                for b0 in range(0, 19, 8):
                    nb = min(8, 19 - b0)
                    vb = 6 + (b0 // 8) % 2
                    psv = PS[vb].rearrange("p (b d) -> p b d", d=64)
                    for bi in range(nb):
                        blk = b0 + bi
                        c0 = 128 * blk if blk < 17 else T + 128 * (blk - 17)
                        for kk in range(NCH):
                            S.op("pe", lambda e, psv=psv, bi=bi, kk=kk, c0=c0, vBv=vBv: e.matmul(
                                psv[:, bi, :], lhsT=H[:, kk, c0:c0 + 128], rhs=vBv[:, kk, :],
                                start=(kk == 0), stop=(kk == NCH - 1)),
                                reads=[("W", sB)] + hk(kk, c0, 128), writes=[("ps", vb)],
                                sig=(kk == NCH - 1 and bi == nb - 1))
                    S.op("act", lambda e, psv=psv, b0=b0, nb=nb: e.activation(
                        out=Vpad[:, b0:b0 + nb, 64:128], in_=psv[:, 0:nb, :], func=AF.Identity),
                        reads=[("ps", vb)], writes=["Vpad"])
                its = [(qt, qb, cc) for qt in range(4) for qb in range(4) for cc in range(2)]
                info = {}

                def stage_s(it):
                    qt, qb, cc = its[it]
                    i = 4 * qt + qb
                    sb = cnt["st"] % 2
                    cnt["st"] += 1
                    KBM = ATT_PIPE
                    if KBM:
                        stv = PSALL[:, sb * 1536:sb * 1536 + 1280].rearrange("p (k h q) -> p k h q", k=5, h=2)
                    else:
                        stv = PSALL[:, sb * 1536:sb * 1536 + 1280].rearrange("p (h k q) -> p h k q", h=2, k=5)
                    stkeys = [("ps", sb * 3 + z) for z in range(3)]
                    k0 = 1 if i == 0 else 0
                    srcs = []
                    for kbi in range(k0, 5):
                        if kbi < 3:
                            blk = i - 1 + kbi
                            srcs.append((kbi, 128 * blk, blk))
                        else:
                            srcs.append((kbi, TK + 128 * (kbi - 3), 17 + kbi - 3))
                    nmm = 2 * len(srcs)
                    z = 0
                    for h in range(2):
                        for (kbi, kc, vbk) in srcs:
                            z += 1
                            S.op("pe", lambda e, stv=stv, h=h, kbi=kbi, kc=kc, cc=cc, i=i, KBM=KBM: e.matmul(
                                stv[:, kbi, h, :] if KBM else stv[:, h, kbi, :],
                                lhsT=Kg[h * 64:(h + 1) * 64, kc:kc + 128],
                                rhs=Qg[h * 64:(h + 1) * 64, cc, 128 * i:128 * i + 128], start=True, stop=True),
                                reads=[("Kg",), ("Qg", cc)], writes=stkeys, sig=(z == nmm))
                    pb = sb
                    if KBM:
                        for (ka, kb_) in ((max(k0, 0), 2), (2, 4), (4, 5)):
                            if ka >= kb_:
                                continue
                            S.op("act", lambda e, stv=stv, pb=pb, ka=ka, kb_=kb_: e.activation(
                                out=PT[pb][:, ka:kb_, :, :], in_=stv[:, ka:kb_, :, :], func=AF.Exp, scale=0.125),
                                reads=stkeys, writes=[("PT", pb)])
                        if i > 0:
                            S.op(ATT_MASK, lambda e, pb=pb: e.tensor_tensor(
                                out=PT[pb][:, 0, :, :], in0=PT[pb][:, 0, :, :], in1=mask_lo, op=ALU.mult),
                                reads=[("PT", pb), "cst"], writes=[("PT", pb)])
                        S.op(ATT_MASK, lambda e, pb=pb: e.tensor_tensor(
                            out=PT[pb][:, 2, :, :], in0=PT[pb][:, 2, :, :], in1=mask_hi, op=ALU.mult),
                            reads=[("PT", pb), "cst"], writes=[("PT", pb)])
                    else:
                        S.op("act", lambda e, stv=stv, pb=pb, k0=k0: e.activation(
                            out=PT[pb][:, :, k0:5, :], in_=stv[:, :, k0:5, :], func=AF.Exp, scale=0.125),
                            reads=stkeys, writes=[("PT", pb)])
                        if i > 0:
                            S.op(ATT_MASK, lambda e, pb=pb: e.tensor_tensor(
                                out=PT[pb][:, :, 0, :], in0=PT[pb][:, :, 0, :], in1=mask_lo, op=ALU.mult),
                                reads=[("PT", pb), "cst"], writes=[("PT", pb)])
                        S.op(ATT_MASK, lambda e, pb=pb: e.tensor_tensor(
                            out=PT[pb][:, :, 2, :], in0=PT[pb][:, :, 2, :], in1=mask_hi, op=ALU.mult),
                            reads=[("PT", pb), "cst"], writes=[("PT", pb)])
                    info[it] = (pb, srcs, nmm)

                def stage_p(it):
                    qt, qb, cc = its[it]
                    pb, srcs, nmm = info.pop(it)
                    ob = (4 * g + qt) % 2
                    odb = cnt["od"] % 2
                    cnt["od"] += 1
                    KBM = ATT_PIPE
                    if KBM:
                        bo = (2 + 3 * odb) * 512 + 256
                        o_ps = PSALL[:, bo:bo + 128]
                        d_ps = PSALL[:, bo + 128:bo + 256]
                        odk = [("ps", 2 + 3 * odb)]
                    else:
                        o_ps = PS[6][:, (2 * odb) * 128:(2 * odb + 1) * 128]
                        d_ps = PS[6][:, (2 * odb + 1) * 128:(2 * odb + 2) * 128]
                        odk = []
                    for (tgt, is_den) in ((o_ps, False), (d_ps, True)):
                        z = 0
                        for h in range(2):
                            c_lo = 64 if h == 0 else 0
                            for (kbi, kc, vbk) in srcs:
                                z += 1
                                if is_den:
                                    lh = onespad[:, c_lo:c_lo + 128]
                                else:
                                    lh = Vpad[:, vbk, c_lo:c_lo + 128]
                                S.op("pe", lambda e, tgt=tgt, lh=lh, pb=pb, h=h, kbi=kbi, z=z, nmm=nmm, KBM=KBM: e.matmul(
                                    tgt, lhsT=lh, rhs=(PT[pb][:, kbi, h, :] if KBM else PT[pb][:, h, kbi, :]),
                                    start=(z == 1), stop=(z == nmm)),
                                    reads=[("PT", pb), "Vpad", "cst"], writes=[("ps6", odb, is_den)] + odk,
                                    sig=(z == nmm))
                    S.op("dve", lambda e, d_ps=d_ps, odb=odb, cc=cc, g=g: e.tensor_scalar(
                        out=rden[odb][:], in0=d_ps, scalar1=esink[:, 2 * g + cc:2 * g + cc + 1], scalar2=None,
                        op0=ALU.add), reads=[("ps6", odb, True), "esink"] + odk, writes=[("rden", odb)])
                    S.op("dve", lambda e, odb=odb: e.reciprocal(out=rden[odb][:], in_=rden[odb][:]),
                         reads=[("rden", odb)], writes=[("rden", odb)])
                    S.op("dve", lambda e, o_ps=o_ps, odb=odb, ob=ob, cc=cc, qb=qb: e.tensor_tensor(
                        out=Og[ob][:, cc, qb * 128:(qb + 1) * 128], in0=o_ps, in1=rden[odb][:], op=ALU.mult),
                        reads=[("ps6", odb, False), ("rden", odb)] + odk, writes=[("Og", ob, cc)])
                    if qb == 3 and cc == 1 and not ATT_EXP:
                        for d in range(NCH):
                            yb = 6 + (d % 2) if KBM else 7
                            for c2 in range(2):
                                S.op("pe", lambda e, d=d, c2=c2, ob=ob, vBo=vBo, yb=yb: e.matmul(
                                    PS[yb][:, 0:512], lhsT=vBo[:, c2, d * 128:(d + 1) * 128], rhs=Og[ob][:, c2, :],
                                    start=(c2 == 0), stop=(c2 == 1)),
                                    reads=[("W", sB), ("Og", ob, c2)], writes=[("ps", yb)], sig=(c2 == 1))
                            S.op("dve", lambda e, d=d, qt=qt, yb=yb: e.scalar_tensor_tensor(
                                out=X[:, d, qt * 512:(qt + 1) * 512], in0=PS[yb][:, 0:512], scalar=tabG[:, 1, d, 0:1],
                                in1=X[:, d, qt * 512:(qt + 1) * 512], op0=ALU.mult, op1=ALU.add),
                                reads=[("ps", yb), ("tabG", 1)] + xk(0, d, qt * 512, 512),
                                writes=xk(0, d, qt * 512, 512))

                if ATT_PIPE:
                    stage_s(0)
                    for it in range(len(its)):
                        if it + 1 < len(its):
                            stage_s(it + 1)
                        stage_p(it)
                else:
                    for it in range(len(its)):
                        stage_s(it)
                        stage_p(it)
            S.barrier()
            A.reset(m)

        H = A.alloc([128, NCH, TT], BF16)
        NOT = lambda *names: stop_after not in names
        if ONLY_ATTN:
            m0 = A.mark()
            ad0 = [A.alloc([128, NCH, 256], BF16) for _ in range(2)]
            ada_run(1, list(range(36)), ad0)
            for jj_ in range(3):
                ada_tables(1, jj_)
            S.barrier()
            A.reset(m0)
            attn_phase(1, H)
            stop_after = "x_ffn1_0"
        else:
            m0 = A.mark()
            ad0 = [A.alloc([128, NCH, 256], BF16) for _ in range(2)]
            ada_run(0, list(range(12)), ad0)
            ada_tables(0, 0)
            S.barrier()
            A.reset(m0)
            ffn(0, 0, TILES_A, H, ada_job=(0, list(range(12, 36))))
            ada_tables(0, 1)
            ada_tables(0, 2)
        if NOT("x_ffn1_0"):
            conv_phase(0, H)
        if NOT("x_ffn1_0", "x_mix_0"):
            ffn(0, 1, TILES_A, H, ada_job=(1, list(range(36))))
        if NOT("x_ffn1_0", "x_mix_0", "x_out_0"):
            for jj_ in range(3):
                ada_tables(1, jj_)
            ffn(1, 0, TILES_A, H)
        if NOT("x_ffn1_0", "x_mix_0", "x_out_0", "x_ffn1_1"):
            attn_phase(1, H)
        if NOT("x_ffn1_0", "x_mix_0", "x_out_0", "x_ffn1_1", "x_mix_1"):
            ffn(1, 1, TILES_OWN, H)

        def final_out(do_norm):
            S.barrier()
            m = A.mark()
            A.reset(0)
            stg = [A.alloc([128, NCH, 512], F32) for _ in range(2)]
            sq = [A.alloc([128, 512], F32) for _ in range(3)]
            rstd = [A.alloc([128, 512], F32) for _ in range(2)]
            rsq = [A.alloc([128, 512], F32) for _ in range(2)]
            oT_v = outT.rearrange("(c p) t -> p c t", p=128)
            toks = []
            n_sq = 0
            for ti, (a, n, w) in enumerate(TILES_OWN):
                b = ti % 2
                if do_norm:
                    msb = 6 + b
                    ms = PS[msb][:, 0:n]
                    for c in range(NCH):
                        q = n_sq % 3
                        n_sq += 1
                        S.op("act", lambda e, q=q, c=c, a=a, n=n: e.activation(
                            out=sq[q][:, 0:n], in_=X[:, c, a:a + n], func=AF.Square),
                            reads=xk(0, c, a, n), writes=[("sq", q)])
                        S.op("pe", lambda e, q=q, c=c, ms=ms, n=n: e.matmul(
                            ms, lhsT=onesf[:], rhs=sq[q][:, 0:n], start=(c == 0), stop=(c == NCH - 1)),
                            reads=[("sq", q), "onesf"], writes=[("ps", msb)], sig=True)
                    S.op("act", lambda e, b=b, ms=ms, n=n: e.activation(
                        out=rsq[b][:, 0:n], in_=ms, func=AF.Sqrt, bias=epsb[:, 0:1], scale=1.0),
                        reads=[("ps", msb), "epsb"], writes=[("rsq", b)])
                    S.op("dve", lambda e, b=b, n=n: e.reciprocal(out=rstd[b][:, 0:n], in_=rsq[b][:, 0:n]),
                         reads=[("rsq", b)], writes=[("rstd", b)])
                    for c in range(NCH):
                        S.op("dve", lambda e, c=c, a=a, n=n, b=b: e.scalar_tensor_tensor(
                            out=stg[b][:, c, 0:n], in0=X[:, c, a:a + n], scalar=P("final_g", c, c + 1),
                            in1=rstd[b][:, 0:n], op0=ALU.mult, op1=ALU.mult),
                            reads=xk(0, c, a, n) + [("rstd", b), "prm"], writes=[("stg", b)],
                            sig=True)
                else:
                    for c in range(NCH):
                        S.op("act", lambda e, c=c, a=a, n=n, b=b: e.activation(
                            out=stg[b][:, c, 0:n], in_=X[:, c, a:a + n], func=AF.Identity),
                            reads=xk(0, c, a, n), writes=[("stg", b)])
                toks.append(S.dma("sp", [lambda e, b=b, a=a, n=n: e.dma_start(
                    out=oT_v[:, :, a:a + n], in_=stg[b][:, :, 0:n])], f"st{b}", reads=[("stg", b)]))
            S.wait_tokens("sp", toks)
            A.reset(m)

        final_out(stop_after is None)

        S.finalize()
        with nc.Block() as block:
            @block.tensor
            def _(e):
                S.emit("pe", e)

            @block.scalar
            def _(e):
                S.emit("act", e)

            @block.vector
            def _(e):
                S.emit("dve", e)

            @block.gpsimd
            def _(e):
                S.emit("pool", e)

            @block.sync
            def _(e):
                S.emit("sp", e)
    return nc


def _partner_cols(n):
    j = np.arange(n)
    d = j % 64
    pd = np.where((d % 32) < 16, d + 16, d - 16)
    return j - d + pd


def prepare_inputs(inp):
    f = lambda a: np.ascontiguousarray(np.asarray(a, dtype=np.float32))
    x, c, ctx, c_ctx = f(inp["x"]), f(inp["c"]), f(inp["ctx"]), f(inp["c_ctx"])
    w_qkv = f(inp["attn_w_qkv"])[0]
    chunk = lambda v: np.ascontiguousarray(v.reshape(-1, 128).T)
    shared = {
        "ada_w": f(inp["ada_w"]), "ffn1_wi": f(inp["ffn1_wi"]), "ffn1_wo": f(inp["ffn1_wo"]),
        "ffn2_wi": f(inp["ffn2_wi"]), "ffn2_wo": f(inp["ffn2_wo"]),
        "pw1_w": f(inp["conv_pw1_w"])[0], "pw2_w": f(inp["conv_pw2_w"])[0],
        "w_v": np.ascontiguousarray(w_qkv[:, 1280:1536]), "w_o": f(inp["attn_w_o"])[0],
    }
    watt = np.zeros((D, 4 * 768), np.float32)
    for g in range(4):
        q = w_qkv[:, 256 * g:256 * g + 256]
        k = w_qkv[:, 1024 + 64 * g:1024 + 64 * g + 64]
        kd = np.concatenate([k, k], axis=1)
        o = 768 * g
        watt[:, o:o + 256] = q
        watt[:, o + 256:o + 512] = q[:, _partner_cols(256)]
        watt[:, o + 512:o + 640] = kd
        watt[:, o + 640:o + 768] = kd[:, _partner_cols(128)]
    shared["w_att"] = watt
    cst = np.zeros((128, NCST), np.float32)
    kk = np.arange(128)[:, None]
    qq = np.arange(128)[None, :]
    lo = (kk >= qq).astype(np.float32)
    hi = (kk <= qq).astype(np.float32)
    cst[:, 0:128] = lo
    cst[:, 128:256] = lo
    cst[:, 256:384] = hi
    cst[:, 384:512] = hi
    cst[:, 512 + 64:512 + 128] = 1.0
    cst[:, 704:832] = np.eye(128, dtype=np.float32)
    shared["cst"] = cst
    p = np.arange(128)
    d = p % 64
    ax = d // 32
    hh = (d % 32) // 16
    fr = d % 16
    inv = (np.float32(10000.0) ** (-np.arange(16, dtype=np.float32) / np.float32(16))).astype(np.float32)
    in_maps = []
    for core in range(8):
        b, hf = core // 2, core % 2
        idx = np.arange(T) if hf == 0 else (SEQ - 1 - np.arange(T))
        m = dict(shared)
        m["xT"] = np.ascontiguousarray(x[b, idx, :].T)
        cb = ctx[b] if hf == 0 else ctx[b, ::-1]
        m["cT"] = np.ascontiguousarray(cb.T)
        prm = np.zeros((128, NPRM), np.float32)

        def put(name, arr):
            o, w = PRM[name]
            assert arr.shape == (128, w), (name, arr.shape)
            prm[:, o:o + w] = arr
        cc = np.stack([chunk(c[b]), chunk(c_ctx)], axis=2).reshape(128, 16)
        put("cc", cc)
        put("ada_b", np.concatenate([chunk(f(inp["ada_b"])[l]) for l in range(2)], axis=1))
        put("norm_g", np.concatenate([chunk(f(inp["norm_g"])[l, j]) for l in range(2) for j in range(3)], axis=1))
        put("pw1_b", chunk(f(inp["conv_pw1_b"])[0]))
        dw = f(inp["conv_dw_w"])[0]
        if hf == 1:
            dw = dw[::-1]
        put("dw_w", np.ascontiguousarray(dw.T.reshape(NCH, 128, CW).transpose(1, 0, 2)).reshape(128, NCH * CW))
        put("dw_b", chunk(f(inp["conv_dw_b"])[0]))
        put("ln_g", chunk(f(inp["conv_ln_g"])[0]))
        put("ln_b", chunk(f(inp["conv_ln_b"])[0]))
        put("pw2_b", chunk(f(inp["conv_pw2_b"])[0]))
        sk = f(inp["attn_sink"])[0]
        put("sink", np.stack([np.where(p >= 64, sk[2 * cch + 1], sk[2 * cch]) for cch in range(NCH)], axis=1).astype(np.float32))
        put("final_g", chunk(f(inp["final_g"])))
        m["prm"] = prm
        row = (idx // 64).astype(np.float32)
        col = (idx % 64).astype(np.float32)
        pos = np.where(ax[:, None] == 0, row[None, :], col[None, :]).astype(np.float32)
        ang = (pos * inv[fr][:, None]).astype(np.float32)
        cs = np.cos(ang).astype(np.float32)
        sn = np.sin(ang).astype(np.float32)
        sn = np.where(hh[:, None] == 0, -sn, sn).astype(np.float32)
        m["rope"] = np.ascontiguousarray(np.stack([cs, sn], axis=0))
        in_maps.append(m)
    return in_maps


def assemble(results):
    out = np.zeros((4, SEQ, D), np.float32)
    for core in range(8):
        b, hf = core // 2, core % 2
        idx = np.arange(OWN) if hf == 0 else (SEQ - 1 - np.arange(OWN))
        out[b, idx, :] = np.asarray(results[core]["outT"]).T
    return out


def kernel(**inputs):
    nc = build_program()
    in_maps = prepare_inputs(inputs)
    res = run_bass_kernel_spmd(nc, in_maps, core_ids=list(range(8)))
    return assemble(res.results)
```

```python
import numpy as np
import concourse.bass as bass
import concourse.mybir as mybir
from concourse.bass_utils import run_bass_kernel_spmd

F32 = mybir.dt.float32
BF16 = mybir.dt.bfloat16
AF = mybir.ActivationFunctionType
ALU = mybir.AluOpType

D = 1024
NCH = 8
SEQ = 4096
OWN = 2048
T = 2200
TK = 2176
CTX = 256
TT = T + CTX
DFF = 2816
NFC = 22
CW = 31
EPS = 1e-6
SLOT = 6144
import os
ATT_PIPE = int(os.environ.get('ATT_PIPE', '1'))
ATT_MASK = os.environ.get('ATT_MASK', 'dve')
SAME_ENG = int(os.environ.get('SAME_ENG', '1'))
ONLY_ATTN = int(os.environ.get('ONLY_ATTN', '0'))
ATT_EXP = int(os.environ.get('ATT_EXP', '0'))
NSLOT = 4

TILES_A = [(i * 440, 440, 0) for i in range(5)] + [(T, 256, 1)]
TILES_OWN = [(i * 512, 512, 0) for i in range(4)]

PRM = {}
_o = 0
for _n, _w in [("cc", 16), ("ada_b", 144), ("norm_g", 48), ("pw1_b", 16), ("dw_w", 248), ("dw_b", 8),
               ("ln_g", 8), ("ln_b", 8), ("pw2_b", 8), ("sink", 8), ("final_g", 8)]:
    PRM[_n] = (_o, _w)
    _o += _w
NPRM = _o
NCST = 2 * 256 + 192 + 128 + 256

_XB = sorted(set([i * 440 for i in range(6)] + [i * 512 for i in range(5)] + [T]))


def _segs(a, b):
    return [i for i in range(len(_XB) - 1) if _XB[i] < b and _XB[i + 1] > a]


class Sched:
    CE = ("pe", "act", "dve", "pool")
    NEPOCH = 16

    def __init__(self, nc, sems):
        self.nc = nc
        self.sems = sems
        self.cnt = {k: 0 for k in sems}
        self.ops = []
        self.byeng = {e: [] for e in ("pe", "act", "dve", "pool", "sp")}
        self.last_w = {}
        self.readers = {}
        self.unsig = {e: [] for e in self.byeng}
        self.redirect = {}
        self.epoch = 0
        self.last_sig = {}

    def _deps(self, reads, writes):
        deps = []
        for b in reads:
            t = self.last_w.get(b)
            if t is not None:
                deps.append((t, True))
        for b in writes:
            t = self.last_w.get(b)
            if t is not None:
                deps.append((t, False))
            for t in self.readers.get(b, ()):
                deps.append((t, False))
        return deps

    def _record(self, tok, reads, writes):
        for b in reads:
            self.readers.setdefault(b, []).append(tok)
        for b in writes:
            self.last_w[b] = tok
            self.readers[b] = []

    def _new(self, eng, fns, deps, kind, sig=True, chan=None):
        rec = dict(id=len(self.ops), eng=eng, epoch=self.epoch, fns=fns, deps=deps, kind=kind, sig=sig, chan=chan)
        self.ops.append(rec)
        self.byeng[eng].append(rec)
        return rec

    def op(self, eng, fn, reads=(), writes=(), sig=True):
        rec = self._new(eng, [fn], self._deps(reads, writes), "op", sig=sig)
        if sig:
            for i in self.unsig[eng]:
                self.redirect[i] = rec["id"]
            self.unsig[eng] = []
            self.last_sig[eng] = rec["id"]
        else:
            self.unsig[eng].append(rec["id"])
        self._record(("op", rec["id"]), reads, writes)

    def dma(self, eng, fns, chan, reads=(), writes=()):
        rec = self._new(eng, list(fns), self._deps(reads, writes), "dma", chan=chan)
        self.cnt[chan] += 16 * len(fns)
        tok = ("dma", chan, self.cnt[chan])
        self._record(tok, reads, writes)
        return tok

    def wait_tokens(self, eng, toks):
        self._new(eng, [], [(t, True) for t in toks], "wait")

    def barrier(self):
        for e in self.CE:
            assert not self.unsig[e], e
        toks = {e: ("op", self.last_sig[e]) for e in self.CE
                if e in self.last_sig and self.ops[self.last_sig[e]]["epoch"] == self.epoch}
        for e in ("pe", "act", "dve", "pool", "sp"):
            self.wait_tokens(e, [t for k, t in toks.items() if k != e])
        self.epoch += 1
        assert self.epoch < self.NEPOCH

    def finalize(self):
        for e in self.CE:
            assert not self.unsig[e], e

        def resolve(tok, rec, raw):
            if tok[0] == "dma":
                return tok
            i = self.redirect.get(tok[1], tok[1])
            d = self.ops[i]
            if d["epoch"] != rec["epoch"]:
                return None
            if d["eng"] == rec["eng"]:
                if rec["eng"] in ("pe", "sp"):
                    return None
                if not SAME_ENG and not raw:
                    return None
            return ("op", i)
        awaited = set()
        for rec in self.ops:
            r = []
            for tok, raw in rec["deps"]:
                t = resolve(tok, rec, raw)
                if t is not None:
                    r.append(t)
                    if t[0] == "op":
                        awaited.add(t[1])
            rec["rdeps"] = r
        val = {}
        cnt = {}
        for e in self.CE:
            for rec in self.byeng[e]:
                if rec["kind"] == "op" and rec["id"] in awaited:
                    k = f"{e}@{rec['epoch']}"
                    cnt[k] = cnt.get(k, 0) + 1
                    val[rec["id"]] = (k, cnt[k])
        self.prog = {}
        self.nsig = len(val)
        for e, recs in self.byeng.items():
            waited = {}
            out = []
            for rec in recs:
                w = {}
                for t in rec["rdeps"]:
                    k, v = val[t[1]] if t[0] == "op" else (t[1], t[2])
                    if w.get(k, 0) < v:
                        w[k] = v
                wl = []
                for k, v in w.items():
                    if waited.get(k, 0) < v:
                        waited[k] = v
                        wl.append((k, v))
                if rec["kind"] == "dma":
                    incs = [(rec["chan"], 16)] * len(rec["fns"])
                elif rec["id"] in val:
                    incs = [(val[rec["id"]][0], 1)]
                else:
                    incs = []
                out.append((wl, rec["fns"], incs))
            self.prog[e] = out

    def emit(self, eng, e):
        for waits, fns, incs in self.prog[eng]:
            for k, v in waits:
                e.wait_ge(self.sems[k], v)
            for i, fn in enumerate(fns):
                ins = fn(e)
                if i < len(incs):
                    ins.then_inc(self.sems[incs[i][0]], incs[i][1])


class Arena:
    def __init__(self, ap_f32, nbytes):
        self.ap = ap_f32
        self.n = nbytes
        self.off = 0

    def mark(self):
        return self.off

    def reset(self, m):
        self.off = m

    def alloc(self, shape, dt):
        es = 2 if dt == BF16 else 4
        free = 1
        for s in shape[1:]:
            free *= s
        nb = (free * es + 31) // 32 * 32
        assert self.off + nb <= self.n, ("arena overflow", self.off, nb, self.n)
        w0 = self.off // 4
        v = self.ap[:, w0:w0 + nb // 4]
        self.off += nb
        if dt == BF16:
            v = v.bitcast(BF16)
        v = v[:, 0:free]
        if len(shape) == 3:
            v = v.rearrange("p (a b) -> p a b", a=shape[1])
        elif len(shape) == 4:
            v = v.rearrange("p (a b c) -> p a b c", a=shape[1], b=shape[2])
        return v


def build_program(stop_after=None):
    nc = bass.Bass("TRN2", target_bir_lowering=False)
    dr = {}

    def din(name, shape):
        dr[name] = nc.dram_tensor(name, shape, F32, kind="ExternalInput").ap()
        return dr[name]

    xT = din("xT", [D, T])
    cT = din("cT", [D, CTX])
    prm_d = din("prm", [128, NPRM])
    cst_d = din("cst", [128, NCST])
    rope_d = din("rope", [2, 128, T])
    ada_w = din("ada_w", [2, D, 9 * D])
    f_wi = [din("ffn1_wi", [2, D, 2 * DFF]), din("ffn2_wi", [2, D, 2 * DFF])]
    f_wo = [din("ffn1_wo", [2, DFF, D]), din("ffn2_wo", [2, DFF, D])]
    pw1_w = din("pw1_w", [D, 2 * D])
    pw2_w = din("pw2_w", [D, D])
    watt = din("w_att", [D, 4 * 768])
    wv_d = din("w_v", [D, 256])
    wo_d = din("w_o", [D, D])
    outT = nc.dram_tensor("outT", [D, OWN], F32, kind="ExternalOutput").ap()

    semkeys = [f"{e}@{i}" for e in Sched.CE for i in range(Sched.NEPOCH)] + ["ldp", "ldc", "ldxc", "st0", "st1", "rp0", "rp1", "ad0", "ad1"] + \
              [f"ldx{i}" for i in range(5)] + [f"w{i}" for i in range(NSLOT)]

    import contextlib
    es = contextlib.ExitStack()
    with es:
        sems = {k: es.enter_context(nc.semaphore(k)) for k in semkeys}
        X = es.enter_context(nc.sbuf_tensor("X", [128, NCH, T], F32))
        XC = es.enter_context(nc.sbuf_tensor("XC", [128, NCH, CTX], F32))
        Wr = es.enter_context(nc.sbuf_tensor("Wr", [128, NSLOT, SLOT], BF16))
        prm = es.enter_context(nc.sbuf_tensor("prm_sb", [128, NPRM], F32))
        cst = es.enter_context(nc.sbuf_tensor("cst_sb", [128, NCST], BF16))
        mod = es.enter_context(nc.sbuf_tensor("mod", [128, 2, 72, 2], F32))
        tabA = es.enter_context(nc.sbuf_tensor("tabA", [128, 3, NCH, 2], F32))
        tabG = es.enter_context(nc.sbuf_tensor("tabG", [128, 3, NCH, 2], F32))
        tabGb = es.enter_context(nc.sbuf_tensor("tabGb", [128, NCH, 2], F32))
        scT = es.enter_context(nc.sbuf_tensor("scT", [128, NCH, 2], BF16))
        onesf = es.enter_context(nc.sbuf_tensor("onesf", [128, 128], F32))
        esink = es.enter_context(nc.sbuf_tensor("esink", [128, NCH], F32))
        epsb = es.enter_context(nc.sbuf_tensor("epsb", [128, 1], F32))
        ARENA_BYTES = 212863 - (NCH * T * 4 + NCH * CTX * 4 + NSLOT * SLOT * 2 + NPRM * 4 + NCST * 2
                                + 2 * 72 * 2 * 4 + 2 * 3 * NCH * 2 * 4 + NCH * 2 * 4 + NCH * 2 * 2
                                + 128 * 4 + NCH * 4) - 384
        ARENA_BYTES = ARENA_BYTES // 64 * 64
        ar_t = es.enter_context(nc.sbuf_tensor("arena", [128, ARENA_BYTES // 4], F32))
        PSALL = es.enter_context(nc.psum_tensor("psall", [128, 4096], F32))
        PS = [PSALL[:, i * 512:(i + 1) * 512] for i in range(8)]
        S = Sched(nc, sems)
        A = Arena(ar_t, ARENA_BYTES)

        def P(name, a=0, b=None):
            o, w = PRM[name]
            b = w if b is None else b
            return prm[:, o + a:o + b]

        def xbuf(which):
            return X if which == 0 else XC

        def xk(which, c, a, n):
            if which == 1:
                return [("XC", c)]
            return [("X", c, s) for s in _segs(a, a + n)]

        def hk(c, a, n):
            if a >= T:
                return [("H", c, "c")]
            return [("H", c, s) for s in _segs(a, a + n)]

        S.dma("sp", [lambda e: e.dma_start(out=prm[:], in_=prm_d[:, :])], "ldp", writes=["prm"])
        S.dma("pool", [lambda e: e.dma_start(out=cst[:], in_=cst_d[:, :])], "ldc", writes=["cst"])
        xT_v = xT.rearrange("(c p) t -> p c t", p=128)
        cT_v = cT.rearrange("(c p) t -> p c t", p=128)
        for i in range(5):
            S.dma("sp", [lambda e, i=i: e.dma_start(out=X[:, :, i * 440:(i + 1) * 440],
                                                   in_=xT_v[:, :, i * 440:(i + 1) * 440])],
                  f"ldx{i}", writes=[k for c in range(NCH) for k in xk(0, c, i * 440, 440)])
        S.dma("sp", [lambda e: e.dma_start(out=XC[:], in_=cT_v)], "ldxc",
              writes=[("XC", c) for c in range(NCH)])
        S.op("dve", lambda e: e.memset(onesf[:], 1.0 / D), writes=["onesf"])
        S.op("dve", lambda e: e.memset(epsb[:], EPS), writes=["epsb"])
        S.op("act", lambda e: e.activation(out=scT[:], in_=P("cc").rearrange("p (k w) -> p k w", w=2),
                                           func=AF.Silu), reads=["prm"], writes=["scT"])
        S.op("act", lambda e: e.activation(out=esink[:], in_=P("sink"), func=AF.Exp),
             reads=["prm"], writes=["esink"])

        ring = {"next": 0}

        def slot_alloc():
            s = ring["next"]
            ring["next"] = (s + 1) % NSLOT
            return s

        def load_slot(s, fns):
            S.dma("pool", fns, f"w{s}", writes=[("W", s)])

        def ada_unit(l, u, stg, sb):
            aw = ada_w[l].rearrange("(k p) n -> p k n", p=128)
            S.dma("pool", [lambda e, u=u, stg=stg: e.dma_start(out=stg, in_=aw[:, :, u * 256:(u + 1) * 256])],
                  f"ad{sb}", writes=[("adst", sb)])

            def mm():
                pb = 6 + (u % 2)
                ps = PS[pb][:, 504:508].rearrange("p (a b) -> p a b", b=2)
                for oc in range(2):
                    for k in range(NCH):
                        S.op("pe", lambda e, oc=oc, k=k, ps=ps, stg=stg: e.matmul(
                            ps[:, oc, :], lhsT=stg[:, k, oc * 128:(oc + 1) * 128], rhs=scT[:, k, :],
                            start=(k == 0), stop=(k == NCH - 1)),
                            reads=[("adst", sb), "scT"], writes=[("ps", pb)], sig=(k == NCH - 1))
                ab = P("ada_b", l * 72 + u * 2, l * 72 + u * 2 + 2)
                S.op("dve", lambda e, ps=ps, ab=ab, u=u: e.tensor_tensor(
                    out=mod[:, l, u * 2:(u + 1) * 2, :], in0=ps,
                    in1=ab.unsqueeze(2).to_broadcast([128, 2, 2]), op=ALU.add),
                    reads=[("ps", pb), "prm"], writes=[("mod", l, u // 12)])
            return mm

        def ada_tables(l, j):
            g = P("norm_g", (l * 3 + j) * 8, (l * 3 + j) * 8 + 8)
            S.op("dve", lambda e, j=j, g=g: e.scalar_tensor_tensor(
                out=tabA[:, j], in0=mod[:, l, (3 * j + 1) * 8:(3 * j + 2) * 8, :], scalar=1.0,
                in1=g.unsqueeze(2).to_broadcast([128, NCH, 2]), op0=ALU.add, op1=ALU.mult),
                reads=[("mod", l, j), "prm"], writes=[("tabA", j)])
            S.op("dve", lambda e, j=j: e.tensor_scalar(
                out=tabG[:, j], in0=mod[:, l, (3 * j + 2) * 8:(3 * j + 3) * 8, :],
                scalar1=(1.0 if j == 1 else 0.5), scalar2=None, op0=ALU.mult),
                reads=[("mod", l, j)], writes=[("tabG", j)])
            if l == 0 and j == 1:
                S.op("dve", lambda e: e.tensor_tensor(
                    out=tabGb[:], in0=tabG[:, 1], in1=P("pw2_b").unsqueeze(2).to_broadcast([128, NCH, 2]),
                    op=ALU.mult), reads=[("tabG", 1), "prm"], writes=["tabGb"])

        def ada_run(l, units, adst):
            pend = []
            for i, u in enumerate(units):
                pend.append(ada_unit(l, u, adst[i % 2], i % 2))
                if len(pend) == 2:
                    pend.pop(0)()
            for f_ in pend:
                f_()

        def tabB(l, j):
            return mod[:, l, (3 * j) * 8:(3 * j + 1) * 8, :]

        def make_prepass(l, j, tiles, H, nbuf=3):
            sq = [A.alloc([128, 512], F32) for _ in range(nbuf)]
            rstd = [A.alloc([128, 512], F32) for _ in range(2)]
            rsq = [A.alloc([128, 512], F32) for _ in range(2)]
            tmp = [A.alloc([128, 512], F32) for _ in range(nbuf)]
            st_ = {"sq": 0, "tmp": 0}

            def pre_tile(ti):
                a, n, w = tiles[ti]
                xb = xbuf(w)
                xa = a - T if w == 1 else a
                msb = 6 + (ti % 2)
                ms = PS[msb][:, 0:n]
                for c in range(NCH):
                    q = st_["sq"] % nbuf
                    st_["sq"] += 1
                    S.op("act", lambda e, q=q, c=c, xb=xb, xa=xa, n=n: e.activation(
                        out=sq[q][:, 0:n], in_=xb[:, c, xa:xa + n], func=AF.Square),
                        reads=xk(w, c, xa, n), writes=[("sq", q)])
                    S.op("pe", lambda e, q=q, c=c, ms=ms, n=n: e.matmul(
                        ms, lhsT=onesf[:], rhs=sq[q][:, 0:n], start=(c == 0), stop=(c == NCH - 1)),
                        reads=[("sq", q), "onesf"], writes=[("ps", msb)], sig=True)
                r = ti % 2
                S.op("act", lambda e, r=r, ms=ms, n=n: e.activation(
                    out=rsq[r][:, 0:n], in_=ms, func=AF.Sqrt, bias=epsb[:, 0:1], scale=1.0),
                    reads=[("ps", msb), "epsb"], writes=[("rsq", r)])
                S.op("dve", lambda e, r=r, n=n: e.reciprocal(out=rstd[r][:, 0:n], in_=rsq[r][:, 0:n]),
                     reads=[("rsq", r)], writes=[("rstd", r)])
                for c in range(NCH):
                    q = st_["tmp"] % nbuf
                    st_["tmp"] += 1
                    S.op("dve", lambda e, q=q, c=c, xb=xb, xa=xa, n=n, r=r, w=w: e.scalar_tensor_tensor(
                        out=tmp[q][:, 0:n], in0=xb[:, c, xa:xa + n], scalar=tabA[:, j, c, w:w + 1],
                        in1=rstd[r][:, 0:n], op0=ALU.mult, op1=ALU.mult),
                        reads=xk(w, c, xa, n) + [("rstd", r), ("tabA", j)], writes=[("ptmp", q)])
                    S.op("act", lambda e, q=q, c=c, a=a, n=n, w=w: e.activation(
                        out=H[:, c, a:a + n], in_=tmp[q][:, 0:n], func=AF.Identity,
                        bias=tabB(l, j)[:, c, w:w + 1], scale=1.0),
                        reads=[("ptmp", q), ("mod", l, j)], writes=hk(c, a, n))
            return pre_tile

        def prepass(l, j, tiles, H):
            m = A.mark()
            pt = make_prepass(l, j, tiles, H)
            for ti in range(len(tiles)):
                pt(ti)
            S.barrier()
            A.reset(m)

        def ffn(l, f, tiles, H, ada_job=None):
            j = 0 if f == 0 else 2
            m = A.mark()
            pre_tile = make_prepass(l, j, tiles, H, nbuf=2)
            adst = [A.alloc([128, NCH, 256], BF16) for _ in range(2)]
            act = [A.alloc([128, 4, 512], BF16) for _ in range(2)]
            sg = [A.alloc([128, 512], F32) for _ in range(2)]
            wi = f_wi[f][l].rearrange("(k p) (two n) -> p k two n", p=128, two=2)
            wo = f_wo[f][l].rearrange("(j p) n -> p j n", p=128)
            sweeps = [[0, 1], [2, 3], [4, 5], [6, 7], [8, 9], [10]]
            unit_slot = {}

            def load_unit(u):
                s = slot_alloc()
                unit_slot[u] = s
                wiv = Wr[:, s, 0:4096].rearrange("p (k two n) -> p k two n", k=NCH, two=2)
                wov = Wr[:, s, 4096:6144].rearrange("p (j n) -> p j n", j=2)
                load_slot(s, [lambda e, wiv=wiv, u=u: e.dma_start(out=wiv[:, :, 0, :], in_=wi[:, :, 0, u * 256:(u + 1) * 256]),
                              lambda e, wiv=wiv, u=u: e.dma_start(out=wiv[:, :, 1, :], in_=wi[:, :, 1, u * 256:(u + 1) * 256]),
                              lambda e, wov=wov, u=u: e.dma_start(out=wov, in_=wo[:, 2 * u:2 * u + 2, :])])

            tasks = []
            for si, sw in enumerate(sweeps):
                for ti in range(len(tiles)):
                    tasks.append((si, ti))
            st = {"sg": 0, "gu": 0, "y": 0}

            def stage_a(k):
                si, ti = tasks[k]
                a, n, w = tiles[ti]
                ab = k % 2
                chunks = [(u, jj) for u in sweeps[si] for jj in range(2)]
                for ci, (u, jj) in enumerate(chunks):
                    s = unit_slot[u]
                    wiv = Wr[:, s, 0:4096].rearrange("p (k two n) -> p k two n", k=NCH, two=2)
                    gb = st["gu"] % 2
                    st["gu"] += 1
                    gps = PS[gb * 2][:, 0:n]
                    ups = PS[gb * 2 + 1][:, 0:n]
                    for two, pp, pb in ((0, gps, gb * 2), (1, ups, gb * 2 + 1)):
                        for kk in range(NCH):
                            S.op("pe", lambda e, pp=pp, wiv=wiv, kk=kk, two=two, jj=jj, a=a, n=n: e.matmul(
                                pp, lhsT=wiv[:, kk, two, jj * 128:(jj + 1) * 128], rhs=H[:, kk, a:a + n],
                                start=(kk == 0), stop=(kk == NCH - 1)),
                                reads=[("W", s)] + hk(kk, a, n), writes=[("ps", pb)], sig=(kk == NCH - 1))
                    q = st["sg"] % 2
                    st["sg"] += 1
                    S.op("act", lambda e, q=q, gps=gps, n=n: e.activation(out=sg[q][:, 0:n], in_=gps, func=AF.Silu),
                         reads=[("ps", gb * 2)], writes=[("sg", q)])
                    S.op("dve", lambda e, q=q, ups=ups, n=n, ab=ab, ci=ci: e.tensor_tensor(
                        out=act[ab][:, ci, 0:n], in0=sg[q][:, 0:n], in1=ups, op=ALU.mult),
                        reads=[("sg", q), ("ps", gb * 2 + 1)], writes=[("act", ab, ci)])

            def stage_b(k):
                si, ti = tasks[k]
                a, n, w = tiles[ti]
                xb = xbuf(w)
                xa = a - T if w == 1 else a
                ab = k % 2
                chunks = [(u, jj) for u in sweeps[si] for jj in range(2)]
                for d in range(NCH):
                    yb = 4 + st["y"] % 2
                    st["y"] += 1
                    yps = PS[yb][:, 0:n]
                    for ci, (u, jj) in enumerate(chunks):
                        s = unit_slot[u]
                        wov = Wr[:, s, 4096:6144].rearrange("p (j n) -> p j n", j=2)
                        S.op("pe", lambda e, yps=yps, wov=wov, jj=jj, d=d, ab=ab, ci=ci, n=n: e.matmul(
                            yps, lhsT=wov[:, jj, d * 128:(d + 1) * 128], rhs=act[ab][:, ci, 0:n],
                            start=(ci == 0), stop=(ci == len(chunks) - 1)),
                            reads=[("W", s), ("act", ab, ci)], writes=[("ps", yb)], sig=(ci == len(chunks) - 1))
                    S.op("dve", lambda e, yps=yps, d=d, xb=xb, xa=xa, n=n, w=w: e.scalar_tensor_tensor(
                        out=xb[:, d, xa:xa + n], in0=yps, scalar=tabG[:, j, d, w:w + 1], in1=xb[:, d, xa:xa + n],
                        op0=ALU.mult, op1=ALU.add),
                        reads=[("ps", yb), ("tabG", j)] + xk(w, d, xa, n), writes=xk(w, d, xa, n))

            loaded = 0

            def ensure_loaded(si):
                nonlocal loaded
                while loaded <= min(si, len(sweeps) - 1):
                    for u in sweeps[loaded]:
                        load_unit(u)
                    loaded += 1
            ensure_loaded(1)
            nt = len(tiles)
            pre_tile(0)
            if nt > 1:
                pre_tile(1)
            ada_l, ada_units = ada_job if ada_job else (0, [])
            ada_pend = []
            ada_i = 0
            for k in range(len(tasks) + 1):
                if k < len(tasks):
                    si, ti = tasks[k]
                    if ti == 1:
                        ensure_loaded(si + 1)
                    if si == 0 and ti + 2 < nt:
                        pre_tile(ti + 2)
                    if ada_i < len(ada_units):
                        ada_pend.append(ada_unit(ada_l, ada_units[ada_i], adst[ada_i % 2], ada_i % 2))
                        ada_i += 1
                        if len(ada_pend) == 2:
                            ada_pend.pop(0)()
                    stage_a(k)
                if k >= 1:
                    stage_b(k - 1)
            for f_ in ada_pend:
                f_()
            assert ada_i == len(ada_units)
            S.barrier()
            A.reset(m)

        def conv_phase(l, H):
            prepass(l, 1, TILES_A, H)
            m = A.mark()
            p1 = pw1_w.rearrange("(k p) n -> p k n", p=128)
            p2 = pw2_w.rearrange("(k p) n -> p k n", p=128)
            slots = []
            for i in range(4):
                s = slot_alloc()
                slots.append(s)
                v1 = Wr[:, s, 0:4096].rearrange("p (k two n) -> p k two n", k=NCH, two=2)
                v2 = Wr[:, s, 4096:6144].rearrange("p (k n) -> p k n", k=NCH)
                load_slot(s, [lambda e, v1=v1, i=i: e.dma_start(out=v1[:, :, 0, :], in_=p1[:, :, i * 256:(i + 1) * 256]),
                              lambda e, v1=v1, i=i: e.dma_start(out=v1[:, :, 1, :], in_=p1[:, :, D + i * 256:D + (i + 1) * 256]),
                              lambda e, v2=v2, i=i: e.dma_start(out=v2, in_=p2[:, :, i * 256:(i + 1) * 256])])
            UW = 472
            NPE = 16
            U = [A.alloc([128, UW], F32) for _ in range(2)]
            Ub = [A.alloc([128, UW], BF16) for _ in range(2)]
            Dg = A.alloc([128, 2, NPE, 128], BF16)
            V = A.alloc([128, NCH, 440], F32)
            sqv = [A.alloc([128, 440], F32) for _ in range(1)]
            varb = A.alloc([128, 440], F32)
            HN = A.alloc([128, NCH, 440], BF16)
            ident = cst[:, 704:832]
            dww = P("dw_w").rearrange("p (c k) -> p c k", k=CW)
            K1 = CW - NPE
            for ti, (a, n, w) in enumerate(TILES_A):
                xb = xbuf(w)
                xa = a - T if w == 1 else a
                s0, s1 = (0, T) if w == 0 else (T, T + CTX)
                lo, hi = max(a - 15, s0), min(a + n + 15, s1)
                ulo, uhi = lo - (a - 15), hi - (a - 15)
                nu = hi - lo
                for jp in range(4):
                    for jj in range(2):
                        j = 2 * jp + jj
                        s = slots[jp]
                        v1 = Wr[:, s, 0:4096].rearrange("p (k two n) -> p k two n", k=NCH, two=2)
                        for two in range(2):
                            pb = 2 * jj + two
                            pp = PS[pb][:, 0:nu]
                            for kk in range(NCH):
                                S.op("pe", lambda e, pp=pp, v1=v1, kk=kk, two=two, jj=jj, lo=lo, hi=hi: e.matmul(
                                    pp, lhsT=v1[:, kk, two, jj * 128:(jj + 1) * 128], rhs=H[:, kk, lo:hi],
                                    start=(kk == 0), stop=(kk == NCH - 1)),
                                    reads=[("W", s)] + hk(kk, lo, nu), writes=[("ps", pb)], sig=(kk == NCH - 1))
                        if ulo > 0:
                            S.op("dve", lambda e, jj=jj, ulo=ulo: e.memset(U[jj][:, 0:ulo], 0.0), writes=[("U", jj)])
                        if uhi < n + 30:
                            S.op("dve", lambda e, jj=jj, uhi=uhi, n=n: e.memset(U[jj][:, uhi:n + 30], 0.0), writes=[("U", jj)])
                        S.op("act", lambda e, jj=jj, j=j, nu=nu, ulo=ulo, uhi=uhi: e.activation(
                            out=U[jj][:, ulo:uhi], in_=PS[2 * jj + 1][:, 0:nu], func=AF.Sigmoid,
                            bias=P("pw1_b", 8 + j, 9 + j), scale=1.0),
                            reads=[("ps", 2 * jj + 1), "prm"], writes=[("U", jj)])
                        S.op("dve", lambda e, jj=jj, j=j, nu=nu, ulo=ulo, uhi=uhi: e.scalar_tensor_tensor(
                            out=U[jj][:, ulo:uhi], in0=PS[2 * jj][:, 0:nu], scalar=P("pw1_b", j, j + 1),
                            in1=U[jj][:, ulo:uhi], op0=ALU.add, op1=ALU.mult),
                            reads=[("ps", 2 * jj), "prm", ("U", jj)], writes=[("U", jj)])
                    for k in range(K1):
                        for jj in range(2):
                            j = 2 * jp + jj
                            if k == 0:
                                S.op("dve", lambda e, jj=jj, j=j, n=n: e.tensor_scalar(
                                    out=V[:, j, 0:n], in0=U[jj][:, 0:n], scalar1=dww[:, j, 0:1],
                                    scalar2=P("dw_b", j, j + 1), op0=ALU.mult, op1=ALU.add),
                                    reads=[("U", jj), "prm"], writes=[("V", j)])
                            else:
                                S.op("dve", lambda e, jj=jj, j=j, n=n, k=k: e.scalar_tensor_tensor(
                                    out=V[:, j, 0:n], in0=U[jj][:, k:k + n], scalar=dww[:, j, k:k + 1],
                                    in1=V[:, j, 0:n], op0=ALU.mult, op1=ALU.add),
                                    reads=[("U", jj), ("V", j), "prm"], writes=[("V", j)])
                    for jj in range(2):
                        j = 2 * jp + jj
                        S.op("act", lambda e, jj=jj, n=n: e.activation(
                            out=Ub[jj][:, 0:n + 30], in_=U[jj][:, 0:n + 30], func=AF.Identity),
                            reads=[("U", jj)], writes=[("Ub", jj)])
                        for kidx in range(NPE):
                            k = K1 + kidx
                            S.op("act", lambda e, jj=jj, j=j, k=k, kidx=kidx: e.activation(
                                out=Dg[:, jj, kidx, :], in_=ident, func=AF.Identity, scale=dww[:, j, k:k + 1]),
                                reads=["cst", "prm"], writes=[("Dg", jj, kidx)])
                    for jj in range(2):
                        j = 2 * jp + jj
                        for kidx in range(NPE):
                            k = K1 + kidx
                            S.op("pe", lambda e, jj=jj, k=k, kidx=kidx, n=n: e.matmul(
                                PS[4 + jj][:, 0:n], lhsT=Dg[:, jj, kidx, :], rhs=Ub[jj][:, k:k + n],
                                start=(kidx == 0), stop=(kidx == NPE - 1)),
                                reads=[("Dg", jj, kidx), ("Ub", jj)], writes=[("ps", 4 + jj)], sig=(kidx == NPE - 1))
                    for jj in range(2):
                        j = 2 * jp + jj
                        S.op("dve", lambda e, jj=jj, j=j, n=n: e.tensor_tensor(
                            out=V[:, j, 0:n], in0=V[:, j, 0:n], in1=PS[4 + jj][:, 0:n], op=ALU.add),
                            reads=[("V", j), ("ps", 4 + jj)], writes=[("V", j)])
                for j in range(NCH):
                    S.op("pe", lambda e, j=j, n=n: e.matmul(PS[4][:, 0:n], lhsT=onesf[:], rhs=V[:, j, 0:n],
                                                           start=(j == 0), stop=(j == NCH - 1)),
                         reads=[("V", j), "onesf"], writes=[("ps", 4)], sig=True)
                    S.op("act", lambda e, j=j, n=n: e.activation(out=sqv[0][:, 0:n], in_=V[:, j, 0:n], func=AF.Square),
                         reads=[("V", j)], writes=[("sqv", 0)])
                    S.op("pe", lambda e, j=j, n=n: e.matmul(PS[5][:, 0:n], lhsT=onesf[:], rhs=sqv[0][:, 0:n],
                                                           start=(j == 0), stop=(j == NCH - 1)),
                         reads=[("sqv", 0), "onesf"], writes=[("ps", 5)], sig=True)
                S.op("act", lambda e, n=n: e.activation(out=varb[:, 0:n], in_=PS[4][:, 0:n], func=AF.Square),
                     reads=[("ps", 4)], writes=["varb"])
                S.op("dve", lambda e, n=n: e.tensor_tensor(out=varb[:, 0:n], in0=PS[5][:, 0:n], in1=varb[:, 0:n],
                                                           op=ALU.subtract), reads=[("ps", 5), "varb"], writes=["varb"])
                S.op("act", lambda e, n=n: e.activation(out=varb[:, 0:n], in_=varb[:, 0:n], func=AF.Sqrt,
                                                        bias=epsb[:, 0:1], scale=1.0),
                     reads=["varb", "epsb"], writes=["varb"])
                S.op("dve", lambda e, n=n: e.reciprocal(out=varb[:, 0:n], in_=varb[:, 0:n]),
                     reads=["varb"], writes=["varb"])
                for j in range(NCH):
                    S.op("dve", lambda e, j=j, n=n: e.tensor_tensor(
                        out=V[:, j, 0:n], in0=V[:, j, 0:n], in1=PS[4][:, 0:n], op=ALU.subtract),
                        reads=[("V", j), ("ps", 4)], writes=[("V", j)])
                    S.op("dve", lambda e, j=j, n=n: e.tensor_tensor(
                        out=V[:, j, 0:n], in0=V[:, j, 0:n], in1=varb[:, 0:n], op=ALU.mult),
                        reads=[("V", j), "varb"], writes=[("V", j)])
                    S.op("act", lambda e, j=j, n=n: e.activation(
                        out=HN[:, j, 0:n], in_=V[:, j, 0:n], func=AF.Silu,
                        bias=P("ln_b", j, j + 1), scale=P("ln_g", j, j + 1)),
                        reads=[("V", j), "prm"], writes=[("HN", j)])
                for d in range(NCH):
                    s = slots[d // 2]
                    v2 = Wr[:, s, 4096:6144].rearrange("p (k n) -> p k n", k=NCH)
                    yb = 6 + d % 2
                    for kk in range(NCH):
                        S.op("pe", lambda e, v2=v2, kk=kk, d=d, yb=yb, n=n: e.matmul(
                            PS[yb][:, 0:n], lhsT=v2[:, kk, (d % 2) * 128:(d % 2 + 1) * 128], rhs=HN[:, kk, 0:n],
                            start=(kk == 0), stop=(kk == NCH - 1)),
                            reads=[("W", s), ("HN", kk)], writes=[("ps", yb)], sig=(kk == NCH - 1))
                    S.op("act", lambda e, d=d, n=n, w=w, yb=yb: e.activation(
                        out=sqv[0][:, 0:n], in_=PS[yb][:, 0:n], func=AF.Identity,
                        bias=tabGb[:, d, w:w + 1], scale=tabG[:, 1, d, w:w + 1]),
                        reads=[("ps", yb), ("tabG", 1), "tabGb"], writes=[("sqv", 0)])
                    S.op("dve", lambda e, d=d, xb=xb, xa=xa, n=n: e.tensor_tensor(
                        out=xb[:, d, xa:xa + n], in0=xb[:, d, xa:xa + n], in1=sqv[0][:, 0:n], op=ALU.add),
                        reads=[("sqv", 0)] + xk(w, d, xa, n), writes=xk(w, d, xa, n))
            S.barrier()
            A.reset(m)

        def attn_phase(l, H):
            prepass(l, 1, TILES_A, H)
            m = A.mark()
            Qg = A.alloc([128, 2, OWN], BF16)
            KgA = A.alloc([128, TK + CTX], BF16)
            KgB = A.alloc([128, TK + CTX], BF16)
            Vpad = A.alloc([128, 19, 192], BF16)
            Og = [A.alloc([128, 2, 512], BF16) for _ in range(1)]
            PT = [A.alloc([128, 5, 2, 128] if ATT_PIPE else [128, 2, 5, 128], BF16) for _ in range(2)]
            cosb = A.alloc([128, 440], F32)
            sinb = A.alloc([128, 440], F32)
            t1 = A.alloc([128, 440], F32)
            rden = [A.alloc([128, 128], F32) for _ in range(2)]
            mask_lo = cst[:, 0:256].rearrange("p (h q) -> p h q", h=2)
            mask_hi = cst[:, 256:512].rearrange("p (h q) -> p h q", h=2)
            onespad = cst[:, 512:704]
            wa = watt.rearrange("(k p) n -> p k n", p=128)
            wvv = wv_d.rearrange("(k p) n -> p k n", p=128)
            wov = wo_d.rearrange("(j p) n -> p j n", p=128)
            S.op("pool", lambda e: e.memset(Vpad[:], 0.0), writes=["Vpad"])
            S.op("dve", lambda e: e.memset(KgA[64:128, :], 0.0), writes=[("Kg",)])
            S.op("dve", lambda e: e.memset(KgB[0:64, :], 0.0), writes=[("Kg",)])
            cnt = {"st": 0, "od": 0, "og": 0}
            for g in range(4):
                sA = slot_alloc()
                vA = Wr[:, sA, :].rearrange("p (k n) -> p k n", k=NCH)
                load_slot(sA, [lambda e, vA=vA, g=g: e.dma_start(out=vA, in_=wa[:, :, g * 768:(g + 1) * 768])])
                sB = slot_alloc()
                vBo = Wr[:, sB, 0:2048].rearrange("p (j n) -> p j n", j=2)
                vBv = Wr[:, sB, 2048:2560].rearrange("p (k n) -> p k n", k=NCH)
                load_slot(sB, [lambda e, vBo=vBo, g=g: e.dma_start(out=vBo, in_=wov[:, 2 * g:2 * g + 2, :]),
                               lambda e, vBv=vBv, g=g: e.dma_start(out=vBv, in_=wvv[:, :, 64 * g:64 * g + 64])])
                for ti in range(5):
                    a, n = 440 * ti, 440
                    S.dma("sp", [lambda e, a=a, n=n: e.dma_start(out=cosb[:, 0:n], in_=rope_d[0][:, a:a + n]),
                                 lambda e, a=a, n=n: e.dma_start(out=sinb[:, 0:n], in_=rope_d[1][:, a:a + n])],
                          "rp0", writes=["rope"])
                    items = [("q", 0), ("q", 1), ("k", 0)]
                    for ii, (kind, cc) in enumerate(items):
                        if kind == "q":
                            nn = max(0, min(a + n, OWN) - a)
                            base, bsw = cc * 128, 256 + cc * 128
                            dkey = ("Qg", cc)
                        else:
                            nn = max(0, min(a + n, TK) - a)
                            base, bsw = 512, 640
                            dkey = ("Kg",)
                        if nn == 0:
                            continue
                        pa, pb = 2 * (ii % 2), 2 * (ii % 2) + 1
                        for (pp, bb) in ((pa, base), (pb, bsw)):
                            for kk in range(NCH):
                                S.op("pe", lambda e, pp=pp, bb=bb, kk=kk, a=a, nn=nn, vA=vA: e.matmul(
                                    PS[pp][:, 0:nn], lhsT=vA[:, kk, bb:bb + 128], rhs=H[:, kk, a:a + nn],
                                    start=(kk == 0), stop=(kk == NCH - 1)),
                                    reads=[("W", sA)] + hk(kk, a, nn), writes=[("ps", pp)], sig=(kk == NCH - 1))
                        S.op("dve", lambda e, pa=pa, nn=nn: e.tensor_tensor(
                            out=PS[pa][:, 0:nn], in0=PS[pa][:, 0:nn], in1=cosb[:, 0:nn], op=ALU.mult),
                            reads=[("ps", pa), "rope"], writes=[("ps", pa)])
                        S.op("dve", lambda e, pb=pb, nn=nn: e.tensor_tensor(
                            out=t1[:, 0:nn], in0=PS[pb][:, 0:nn], in1=sinb[:, 0:nn], op=ALU.mult),
                            reads=[("ps", pb), "rope"], writes=["t1"])
                        if kind == "q":
                            S.op("dve", lambda e, pa=pa, nn=nn, cc=cc, a=a: e.tensor_tensor(
                                out=Qg[:, cc, a:a + nn], in0=PS[pa][:, 0:nn], in1=t1[:, 0:nn], op=ALU.add),
                                reads=[("ps", pa), "t1"], writes=[dkey])
                        else:
                            S.op("dve", lambda e, pa=pa, nn=nn, a=a: e.tensor_tensor(
                                out=KgA[0:64, a:a + nn], in0=PS[pa][0:64, 0:nn], in1=t1[0:64, 0:nn], op=ALU.add),
                                reads=[("ps", pa), "t1"], writes=[dkey])
                            S.op("dve", lambda e, pa=pa, nn=nn, a=a: e.tensor_tensor(
                                out=KgB[64:128, a:a + nn], in0=PS[pa][64:128, 0:nn], in1=t1[64:128, 0:nn], op=ALU.add),
                                reads=[("ps", pa), "t1"], writes=[dkey])
                for kk in range(NCH):
                    S.op("pe", lambda e, kk=kk, vA=vA: e.matmul(
                        PS[0][:, 0:CTX], lhsT=vA[:, kk, 512:640], rhs=H[:, kk, T:T + CTX],
                        start=(kk == 0), stop=(kk == NCH - 1)),
                        reads=[("W", sA)] + hk(kk, T, CTX), writes=[("ps", 0)], sig=(kk == NCH - 1))
                S.op("act", lambda e: e.activation(out=KgA[0:64, TK:TK + CTX], in_=PS[0][0:64, 0:CTX], func=AF.Identity),
                     reads=[("ps", 0)], writes=[("Kg",)])
                S.op("act", lambda e: e.activation(out=KgB[64:128, TK:TK + CTX], in_=PS[0][64:128, 0:CTX], func=AF.Identity),
                     reads=[("ps", 0)], writes=[("Kg",)])
                for b0 in range(0, 19, 8):
                    nb = min(8, 19 - b0)
                    vb = 6 + (b0 // 8) % 2
                    psv = PS[vb].rearrange("p (b d) -> p b d", d=64)
                    for bi in range(nb):
                        blk = b0 + bi
                        c0 = 128 * blk if blk < 17 else T + 128 * (blk - 17)
                        for kk in range(NCH):
                            S.op("pe", lambda e, psv=psv, bi=bi, kk=kk, c0=c0, vBv=vBv: e.matmul(
                                psv[:, bi, :], lhsT=H[:, kk, c0:c0 + 128], rhs=vBv[:, kk, :],
                                start=(kk == 0), stop=(kk == NCH - 1)),
                                reads=[("W", sB)] + hk(kk, c0, 128), writes=[("ps", vb)],
                                sig=(kk == NCH - 1 and bi == nb - 1))
                    S.op("act", lambda e, psv=psv, b0=b0, nb=nb: e.activation(
                        out=Vpad[:, b0:b0 + nb, 64:128], in_=psv[:, 0:nb, :], func=AF.Identity),
                        reads=[("ps", vb)], writes=["Vpad"])
                its = [(qt, qb, cc) for qt in range(4) for qb in range(4) for cc in range(2)]
                info = {}

                def stage_s(it):
                    qt, qb, cc = its[it]
                    i = 4 * qt + qb
                    sb = cnt["st"] % 2
                    cnt["st"] += 1
                    KBM = ATT_PIPE
                    if KBM:
                        stv = PSALL[:, sb * 1536:sb * 1536 + 1280].rearrange("p (k h q) -> p k h q", k=5, h=2)
                    else:
                        stv = PSALL[:, sb * 1536:sb * 1536 + 1280].rearrange("p (h k q) -> p h k q", h=2, k=5)
                    stkeys = [("ps", sb * 3 + z) for z in range(3)]
                    k0 = 1 if i == 0 else 0
                    srcs = []
                    for kbi in range(k0, 5):
                        if kbi < 3:
                            blk = i - 1 + kbi
                            srcs.append((kbi, 128 * blk, blk))
                        else:
                            srcs.append((kbi, TK + 128 * (kbi - 3), 17 + kbi - 3))
                    nmm = 2 * len(srcs)
                    PEM = (ATT_MASK == "pe")
                    z = 0
                    for h in (1, 0):
                        for (kbi, kc, vbk) in srcs:
                            z += 1
                            msk = PEM and kbi in (0, 2)
                            S.op("pe", lambda e, stv=stv, h=h, kbi=kbi, kc=kc, cc=cc, i=i, KBM=KBM, msk=msk: e.matmul(
                                stv[:, kbi, h, :] if KBM else stv[:, h, kbi, :],
                                lhsT=(KgA if h == 0 else KgB)[:, kc:kc + 128],
                                rhs=Qg[:, cc, 128 * i:128 * i + 128], start=True, stop=(not msk)),
                                reads=[("Kg",), ("Qg", cc)], writes=stkeys, sig=(z == nmm and not msk))
                            if msk:
                                mb = cst[:, 832:960] if kbi == 0 else cst[:, 960:1088]
                                S.op("pe", lambda e, stv=stv, h=h, kbi=kbi, KBM=KBM, mb=mb: e.matmul(
                                    stv[:, kbi, h, :] if KBM else stv[:, h, kbi, :],
                                    lhsT=cst[:, 704:832], rhs=mb, start=False, stop=True),
                                    reads=["cst"], writes=stkeys, sig=(z == nmm))
                    pb = sb
                    if KBM:
                        for (ka, kb_) in ((max(k0, 0), 2), (2, 4), (4, 5)):
                            if ka >= kb_:
                                continue
                            S.op("act", lambda e, stv=stv, pb=pb, ka=ka, kb_=kb_: e.activation(
                                out=PT[pb][:, ka:kb_, :, :], in_=stv[:, ka:kb_, :, :], func=AF.Exp, scale=0.125),
                                reads=stkeys, writes=[("PT", pb)])
                        if i > 0 and not PEM:
                            S.op(ATT_MASK, lambda e, pb=pb: e.tensor_tensor(
                                out=PT[pb][:, 0, :, :], in0=PT[pb][:, 0, :, :], in1=mask_lo, op=ALU.mult),
                                reads=[("PT", pb), "cst"], writes=[("PT", pb)])
                        if not PEM:
                            S.op(ATT_MASK, lambda e, pb=pb: e.tensor_tensor(
                                out=PT[pb][:, 2, :, :], in0=PT[pb][:, 2, :, :], in1=mask_hi, op=ALU.mult),
                                reads=[("PT", pb), "cst"], writes=[("PT", pb)])
                    else:
                        S.op("act", lambda e, stv=stv, pb=pb, k0=k0: e.activation(
                            out=PT[pb][:, :, k0:5, :], in_=stv[:, :, k0:5, :], func=AF.Exp, scale=0.125),
                            reads=stkeys, writes=[("PT", pb)])
                        if i > 0 and not PEM:
                            S.op(ATT_MASK, lambda e, pb=pb: e.tensor_tensor(
                                out=PT[pb][:, :, 0, :], in0=PT[pb][:, :, 0, :], in1=mask_lo, op=ALU.mult),
                                reads=[("PT", pb), "cst"], writes=[("PT", pb)])
                        if not PEM:
                            S.op(ATT_MASK, lambda e, pb=pb: e.tensor_tensor(
                                out=PT[pb][:, :, 2, :], in0=PT[pb][:, :, 2, :], in1=mask_hi, op=ALU.mult),
                                reads=[("PT", pb), "cst"], writes=[("PT", pb)])
                    info[it] = (pb, srcs, nmm)

                def stage_p(it):
                    qt, qb, cc = its[it]
                    pb, srcs, nmm = info.pop(it)
                    ob = 0
                    odb = cnt["od"] % 2
                    cnt["od"] += 1
                    KBM = ATT_PIPE
                    if KBM:
                        bo = (2 + 3 * odb) * 512 + 256
                        o_ps = PSALL[:, bo:bo + 128]
                        d_ps = PSALL[:, bo + 128:bo + 256]
                        odk = [("ps", 2 + 3 * odb)]
                    else:
                        o_ps = PS[6][:, (2 * odb) * 128:(2 * odb + 1) * 128]
                        d_ps = PS[6][:, (2 * odb + 1) * 128:(2 * odb + 2) * 128]
                        odk = []
                    for (tgt, is_den) in ((o_ps, False), (d_ps, True)):
                        z = 0
                        for h in range(2):
                            c_lo = 64 if h == 0 else 0
                            for (kbi, kc, vbk) in srcs:
                                z += 1
                                if is_den:
                                    lh = onespad[:, c_lo:c_lo + 128]
                                else:
                                    lh = Vpad[:, vbk, c_lo:c_lo + 128]
                                S.op("pe", lambda e, tgt=tgt, lh=lh, pb=pb, h=h, kbi=kbi, z=z, nmm=nmm, KBM=KBM: e.matmul(
                                    tgt, lhsT=lh, rhs=(PT[pb][:, kbi, h, :] if KBM else PT[pb][:, h, kbi, :]),
                                    start=(z == 1), stop=(z == nmm)),
                                    reads=[("PT", pb), "Vpad", "cst"], writes=[("ps6", odb, is_den)] + odk,
                                    sig=(z == nmm))
                    S.op("dve", lambda e, d_ps=d_ps, odb=odb, cc=cc, g=g: e.tensor_scalar(
                        out=rden[odb][:], in0=d_ps, scalar1=esink[:, 2 * g + cc:2 * g + cc + 1], scalar2=None,
                        op0=ALU.add), reads=[("ps6", odb, True), "esink"] + odk, writes=[("rden", odb)])
                    S.op("dve", lambda e, odb=odb: e.reciprocal(out=rden[odb][:], in_=rden[odb][:]),
                         reads=[("rden", odb)], writes=[("rden", odb)])
                    S.op("dve", lambda e, o_ps=o_ps, odb=odb, ob=ob, cc=cc, qb=qb: e.tensor_tensor(
                        out=Og[ob][:, cc, qb * 128:(qb + 1) * 128], in0=o_ps, in1=rden[odb][:], op=ALU.mult),
                        reads=[("ps6", odb, False), ("rden", odb)] + odk, writes=[("Og", ob, cc)])
                    if qb == 3 and cc == 1 and not ATT_EXP:
                        for d in range(NCH):
                            yb = 6 + (d % 2) if KBM else 7
                            for c2 in range(2):
                                S.op("pe", lambda e, d=d, c2=c2, ob=ob, vBo=vBo, yb=yb: e.matmul(
                                    PS[yb][:, 0:512], lhsT=vBo[:, c2, d * 128:(d + 1) * 128], rhs=Og[ob][:, c2, :],
                                    start=(c2 == 0), stop=(c2 == 1)),
                                    reads=[("W", sB), ("Og", ob, c2)], writes=[("ps", yb)], sig=(c2 == 1))
                            S.op("dve", lambda e, d=d, qt=qt, yb=yb: e.scalar_tensor_tensor(
                                out=X[:, d, qt * 512:(qt + 1) * 512], in0=PS[yb][:, 0:512], scalar=tabG[:, 1, d, 0:1],
                                in1=X[:, d, qt * 512:(qt + 1) * 512], op0=ALU.mult, op1=ALU.add),
                                reads=[("ps", yb), ("tabG", 1)] + xk(0, d, qt * 512, 512),
                                writes=xk(0, d, qt * 512, 512))

                if ATT_PIPE:
                    stage_s(0)
                    for it in range(len(its)):
                        if it + 1 < len(its):
                            stage_s(it + 1)
                        stage_p(it)
                else:
                    for it in range(len(its)):
                        stage_s(it)
                        stage_p(it)
            S.barrier()
            A.reset(m)

        H = A.alloc([128, NCH, TT], BF16)
        NOT = lambda *names: stop_after not in names
        if ONLY_ATTN:
            m0 = A.mark()
            ad0 = [A.alloc([128, NCH, 256], BF16) for _ in range(2)]
            ada_run(1, list(range(36)), ad0)
            for jj_ in range(3):
                ada_tables(1, jj_)
            S.barrier()
            A.reset(m0)
            attn_phase(1, H)
            stop_after = "x_ffn1_0"
        else:
            m0 = A.mark()
            ad0 = [A.alloc([128, NCH, 256], BF16) for _ in range(2)]
            ada_run(0, list(range(12)), ad0)
            ada_tables(0, 0)
            S.barrier()
            A.reset(m0)
            ffn(0, 0, TILES_A, H, ada_job=(0, list(range(12, 36))))
            ada_tables(0, 1)
            ada_tables(0, 2)
        if NOT("x_ffn1_0"):
            conv_phase(0, H)
        if NOT("x_ffn1_0", "x_mix_0"):
            ffn(0, 1, TILES_A, H, ada_job=(1, list(range(36))))
        if NOT("x_ffn1_0", "x_mix_0", "x_out_0"):
            for jj_ in range(3):
                ada_tables(1, jj_)
            ffn(1, 0, TILES_A, H)
        if NOT("x_ffn1_0", "x_mix_0", "x_out_0", "x_ffn1_1"):
            attn_phase(1, H)
        if NOT("x_ffn1_0", "x_mix_0", "x_out_0", "x_ffn1_1", "x_mix_1"):
            ffn(1, 1, TILES_OWN, H)

        def final_out(do_norm):
            S.barrier()
            m = A.mark()
            A.reset(0)
            stg = [A.alloc([128, NCH, 512], F32) for _ in range(2)]
            sq = [A.alloc([128, 512], F32) for _ in range(3)]
            rstd = [A.alloc([128, 512], F32) for _ in range(2)]
            rsq = [A.alloc([128, 512], F32) for _ in range(2)]
            oT_v = outT.rearrange("(c p) t -> p c t", p=128)
            toks = []
            n_sq = 0
            for ti, (a, n, w) in enumerate(TILES_OWN):
                b = ti % 2
                if do_norm:
                    msb = 6 + b
                    ms = PS[msb][:, 0:n]
                    for c in range(NCH):
                        q = n_sq % 3
                        n_sq += 1
                        S.op("act", lambda e, q=q, c=c, a=a, n=n: e.activation(
                            out=sq[q][:, 0:n], in_=X[:, c, a:a + n], func=AF.Square),
                            reads=xk(0, c, a, n), writes=[("sq", q)])
                        S.op("pe", lambda e, q=q, c=c, ms=ms, n=n: e.matmul(
                            ms, lhsT=onesf[:], rhs=sq[q][:, 0:n], start=(c == 0), stop=(c == NCH - 1)),
                            reads=[("sq", q), "onesf"], writes=[("ps", msb)], sig=True)
                    S.op("act", lambda e, b=b, ms=ms, n=n: e.activation(
                        out=rsq[b][:, 0:n], in_=ms, func=AF.Sqrt, bias=epsb[:, 0:1], scale=1.0),
                        reads=[("ps", msb), "epsb"], writes=[("rsq", b)])
                    S.op("dve", lambda e, b=b, n=n: e.reciprocal(out=rstd[b][:, 0:n], in_=rsq[b][:, 0:n]),
                         reads=[("rsq", b)], writes=[("rstd", b)])
                    for c in range(NCH):
                        S.op("dve", lambda e, c=c, a=a, n=n, b=b: e.scalar_tensor_tensor(
                            out=stg[b][:, c, 0:n], in0=X[:, c, a:a + n], scalar=P("final_g", c, c + 1),
                            in1=rstd[b][:, 0:n], op0=ALU.mult, op1=ALU.mult),
                            reads=xk(0, c, a, n) + [("rstd", b), "prm"], writes=[("stg", b)],
                            sig=True)
                else:
                    for c in range(NCH):
                        S.op("act", lambda e, c=c, a=a, n=n, b=b: e.activation(
                            out=stg[b][:, c, 0:n], in_=X[:, c, a:a + n], func=AF.Identity),
                            reads=xk(0, c, a, n), writes=[("stg", b)])
                toks.append(S.dma("sp", [lambda e, b=b, a=a, n=n: e.dma_start(
                    out=oT_v[:, :, a:a + n], in_=stg[b][:, :, 0:n])], f"st{b}", reads=[("stg", b)]))
            S.wait_tokens("sp", toks)
            A.reset(m)

        final_out(stop_after is None)

        S.finalize()
        with nc.Block() as block:
            @block.tensor
            def _(e):
                S.emit("pe", e)

            @block.scalar
            def _(e):
                S.emit("act", e)

            @block.vector
            def _(e):
                S.emit("dve", e)

            @block.gpsimd
            def _(e):
                S.emit("pool", e)

            @block.sync
            def _(e):
                S.emit("sp", e)
    return nc


def _partner_cols(n):
    j = np.arange(n)
    d = j % 64
    pd = np.where((d % 32) < 16, d + 16, d - 16)
    return j - d + pd


def prepare_inputs(inp):
    f = lambda a: np.ascontiguousarray(np.asarray(a, dtype=np.float32))
    x, c, ctx, c_ctx = f(inp["x"]), f(inp["c"]), f(inp["ctx"]), f(inp["c_ctx"])
    w_qkv = f(inp["attn_w_qkv"])[0]
    chunk = lambda v: np.ascontiguousarray(v.reshape(-1, 128).T)
    shared = {
        "ada_w": f(inp["ada_w"]), "ffn1_wi": f(inp["ffn1_wi"]), "ffn1_wo": f(inp["ffn1_wo"]),
        "ffn2_wi": f(inp["ffn2_wi"]), "ffn2_wo": f(inp["ffn2_wo"]),
        "pw1_w": f(inp["conv_pw1_w"])[0], "pw2_w": f(inp["conv_pw2_w"])[0],
        "w_v": np.ascontiguousarray(w_qkv[:, 1280:1536]), "w_o": f(inp["attn_w_o"])[0],
    }
    watt = np.zeros((D, 4 * 768), np.float32)
    for g in range(4):
        q = w_qkv[:, 256 * g:256 * g + 256]
        k = w_qkv[:, 1024 + 64 * g:1024 + 64 * g + 64]
        kd = np.concatenate([k, k], axis=1)
        o = 768 * g
        watt[:, o:o + 256] = q
        watt[:, o + 256:o + 512] = q[:, _partner_cols(256)]
        watt[:, o + 512:o + 640] = kd
        watt[:, o + 640:o + 768] = kd[:, _partner_cols(128)]
    shared["w_att"] = watt
    cst = np.zeros((128, NCST), np.float32)
    kk = np.arange(128)[:, None]
    qq = np.arange(128)[None, :]
    lo = (kk >= qq).astype(np.float32)
    hi = (kk <= qq).astype(np.float32)
    cst[:, 0:128] = lo
    cst[:, 128:256] = lo
    cst[:, 256:384] = hi
    cst[:, 384:512] = hi
    cst[:, 512 + 64:512 + 128] = 1.0
    cst[:, 704:832] = np.eye(128, dtype=np.float32)
    cst[:, 832:960] = (lo - 1.0) * 30000.0
    cst[:, 960:1088] = (hi - 1.0) * 30000.0
    shared["cst"] = cst
    p = np.arange(128)
    d = p % 64
    ax = d // 32
    hh = (d % 32) // 16
    fr = d % 16
    inv = (np.float32(10000.0) ** (-np.arange(16, dtype=np.float32) / np.float32(16))).astype(np.float32)
    in_maps = []
    for core in range(8):
        b, hf = core // 2, core % 2
        idx = np.arange(T) if hf == 0 else (SEQ - 1 - np.arange(T))
        m = dict(shared)
        m["xT"] = np.ascontiguousarray(x[b, idx, :].T)
        cb = ctx[b] if hf == 0 else ctx[b, ::-1]
        m["cT"] = np.ascontiguousarray(cb.T)
        prm = np.zeros((128, NPRM), np.float32)

        def put(name, arr):
            o, w = PRM[name]
            assert arr.shape == (128, w), (name, arr.shape)
            prm[:, o:o + w] = arr
        cc = np.stack([chunk(c[b]), chunk(c_ctx)], axis=2).reshape(128, 16)
        put("cc", cc)
        put("ada_b", np.concatenate([chunk(f(inp["ada_b"])[l]) for l in range(2)], axis=1))
        put("norm_g", np.concatenate([chunk(f(inp["norm_g"])[l, j]) for l in range(2) for j in range(3)], axis=1))
        put("pw1_b", chunk(f(inp["conv_pw1_b"])[0]))
        dw = f(inp["conv_dw_w"])[0]
        if hf == 1:
            dw = dw[::-1]
        put("dw_w", np.ascontiguousarray(dw.T.reshape(NCH, 128, CW).transpose(1, 0, 2)).reshape(128, NCH * CW))
        put("dw_b", chunk(f(inp["conv_dw_b"])[0]))
        put("ln_g", chunk(f(inp["conv_ln_g"])[0]))
        put("ln_b", chunk(f(inp["conv_ln_b"])[0]))
        put("pw2_b", chunk(f(inp["conv_pw2_b"])[0]))
        sk = f(inp["attn_sink"])[0]
        put("sink", np.stack([np.where(p >= 64, sk[2 * cch + 1], sk[2 * cch]) for cch in range(NCH)], axis=1).astype(np.float32))
        put("final_g", chunk(f(inp["final_g"])))
        m["prm"] = prm
        row = (idx // 64).astype(np.float32)
        col = (idx % 64).astype(np.float32)
        pos = np.where(ax[:, None] == 0, row[None, :], col[None, :]).astype(np.float32)
        ang = (pos * inv[fr][:, None]).astype(np.float32)
        cs = np.cos(ang).astype(np.float32)
        sn = np.sin(ang).astype(np.float32)
        sn = np.where(hh[:, None] == 0, -sn, sn).astype(np.float32)
        m["rope"] = np.ascontiguousarray(np.stack([cs, sn], axis=0))
        in_maps.append(m)
    return in_maps


def assemble(results):
    out = np.zeros((4, SEQ, D), np.float32)
    for core in range(8):
        b, hf = core // 2, core % 2
        idx = np.arange(OWN) if hf == 0 else (SEQ - 1 - np.arange(OWN))
        out[b, idx, :] = np.asarray(results[core]["outT"]).T
    return out


def kernel(**inputs):
    nc = build_program()
    in_maps = prepare_inputs(inputs)
    res = run_bass_kernel_spmd(nc, in_maps, core_ids=list(range(8)))
    return assemble(res.results)
```

```python
import numpy as np
import concourse.bass as bass
import concourse.mybir as mybir
from concourse.bass_utils import run_bass_kernel_spmd

F32 = mybir.dt.float32
BF16 = mybir.dt.bfloat16
AF = mybir.ActivationFunctionType
ALU = mybir.AluOpType

D = 1024
NCH = 8
SEQ = 4096
OWN = 2048
T = 2200
TK = 2176
CTX = 256
TT = T + CTX
DFF = 2816
NFC = 22
CW = 31
EPS = 1e-6
SLOT = 6144
import os
ATT_PIPE = int(os.environ.get('ATT_PIPE', '1'))
ATT_MASK = os.environ.get('ATT_MASK', 'dve')
SAME_ENG = int(os.environ.get('SAME_ENG', '1'))
ONLY_ATTN = int(os.environ.get('ONLY_ATTN', '0'))
ATT_EXP = int(os.environ.get('ATT_EXP', '0'))
ATT_DIV = int(os.environ.get('ATT_DIV', '0'))
NSLOT = 4

TILES_A = [(i * 440, 440, 0) for i in range(5)] + [(T, 256, 1)]
TILES_OWN = [(i * 512, 512, 0) for i in range(4)]

PRM = {}
_o = 0
for _n, _w in [("cc", 16), ("ada_b", 144), ("norm_g", 48), ("pw1_b", 16), ("dw_w", 248), ("dw_b", 8),
               ("ln_g", 8), ("ln_b", 8), ("pw2_b", 8), ("sink", 8), ("final_g", 8)]:
    PRM[_n] = (_o, _w)
    _o += _w
NPRM = _o
NCST = 2 * 256 + 192 + 128 + 256

_XB = sorted(set([i * 440 for i in range(6)] + [i * 512 for i in range(5)] + [T]))


def _segs(a, b):
    return [i for i in range(len(_XB) - 1) if _XB[i] < b and _XB[i + 1] > a]


class Sched:
    CE = ("pe", "act", "dve", "pool")
    NEPOCH = 16

    def __init__(self, nc, sems):
        self.nc = nc
        self.sems = sems
        self.cnt = {k: 0 for k in sems}
        self.ops = []
        self.byeng = {e: [] for e in ("pe", "act", "dve", "pool", "sp")}
        self.last_w = {}
        self.readers = {}
        self.unsig = {e: [] for e in self.byeng}
        self.redirect = {}
        self.epoch = 0
        self.last_sig = {}

    def _deps(self, reads, writes):
        deps = []
        for b in reads:
            t = self.last_w.get(b)
            if t is not None:
                deps.append((t, True))
        for b in writes:
            t = self.last_w.get(b)
            if t is not None:
                deps.append((t, False))
            for t in self.readers.get(b, ()):
                deps.append((t, False))
        return deps

    def _record(self, tok, reads, writes):
        for b in reads:
            self.readers.setdefault(b, []).append(tok)
        for b in writes:
            self.last_w[b] = tok
            self.readers[b] = []

    def _new(self, eng, fns, deps, kind, sig=True, chan=None):
        rec = dict(id=len(self.ops), eng=eng, epoch=self.epoch, fns=fns, deps=deps, kind=kind, sig=sig, chan=chan)
        self.ops.append(rec)
        self.byeng[eng].append(rec)
        return rec

    def op(self, eng, fn, reads=(), writes=(), sig=True):
        rec = self._new(eng, [fn], self._deps(reads, writes), "op", sig=sig)
        if sig:
            for i in self.unsig[eng]:
                self.redirect[i] = rec["id"]
            self.unsig[eng] = []
            self.last_sig[eng] = rec["id"]
        else:
            self.unsig[eng].append(rec["id"])
        self._record(("op", rec["id"]), reads, writes)

    def dma(self, eng, fns, chan, reads=(), writes=()):
        rec = self._new(eng, list(fns), self._deps(reads, writes), "dma", chan=chan)
        self.cnt[chan] += 16 * len(fns)
        tok = ("dma", chan, self.cnt[chan])
        self._record(tok, reads, writes)
        return tok

    def wait_tokens(self, eng, toks):
        self._new(eng, [], [(t, True) for t in toks], "wait")

    def barrier(self):
        for e in self.CE:
            assert not self.unsig[e], e
        toks = {e: ("op", self.last_sig[e]) for e in self.CE
                if e in self.last_sig and self.ops[self.last_sig[e]]["epoch"] == self.epoch}
        for e in ("pe", "act", "dve", "pool", "sp"):
            self.wait_tokens(e, [t for k, t in toks.items() if k != e])
        self.epoch += 1
        assert self.epoch < self.NEPOCH

    def finalize(self):
        for e in self.CE:
            assert not self.unsig[e], e

        def resolve(tok, rec, raw):
            if tok[0] == "dma":
                return tok
            i = self.redirect.get(tok[1], tok[1])
            d = self.ops[i]
            if d["epoch"] != rec["epoch"]:
                return None
            if d["eng"] == rec["eng"]:
                if rec["eng"] in ("pe", "sp"):
                    return None
                if not SAME_ENG and not raw:
                    return None
            return ("op", i)
        awaited = set()
        for rec in self.ops:
            r = []
            for tok, raw in rec["deps"]:
                t = resolve(tok, rec, raw)
                if t is not None:
                    r.append(t)
                    if t[0] == "op":
                        awaited.add(t[1])
            rec["rdeps"] = r
        val = {}
        cnt = {}
        for e in self.CE:
            for rec in self.byeng[e]:
                if rec["kind"] == "op" and rec["id"] in awaited:
                    k = f"{e}@{rec['epoch']}"
                    cnt[k] = cnt.get(k, 0) + 1
                    val[rec["id"]] = (k, cnt[k])
        self.prog = {}
        self.nsig = len(val)
        for e, recs in self.byeng.items():
            waited = {}
            out = []
            for rec in recs:
                w = {}
                for t in rec["rdeps"]:
                    k, v = val[t[1]] if t[0] == "op" else (t[1], t[2])
                    if w.get(k, 0) < v:
                        w[k] = v
                wl = []
                for k, v in w.items():
                    if waited.get(k, 0) < v:
                        waited[k] = v
                        wl.append((k, v))
                if rec["kind"] == "dma":
                    incs = [(rec["chan"], 16)] * len(rec["fns"])
                elif rec["id"] in val:
                    incs = [(val[rec["id"]][0], 1)]
                else:
                    incs = []
                out.append((wl, rec["fns"], incs))
            self.prog[e] = out

    def emit(self, eng, e):
        for waits, fns, incs in self.prog[eng]:
            for k, v in waits:
                e.wait_ge(self.sems[k], v)
            for i, fn in enumerate(fns):
                ins = fn(e)
                if i < len(incs):
                    ins.then_inc(self.sems[incs[i][0]], incs[i][1])


class Arena:
    def __init__(self, ap_f32, nbytes):
        self.ap = ap_f32
        self.n = nbytes
        self.off = 0

    def mark(self):
        return self.off

    def reset(self, m):
        self.off = m

    def alloc(self, shape, dt):
        es = 2 if dt == BF16 else 4
        free = 1
        for s in shape[1:]:
            free *= s
        nb = (free * es + 31) // 32 * 32
        assert self.off + nb <= self.n, ("arena overflow", self.off, nb, self.n)
        w0 = self.off // 4
        v = self.ap[:, w0:w0 + nb // 4]
        self.off += nb
        if dt == BF16:
            v = v.bitcast(BF16)
        v = v[:, 0:free]
        if len(shape) == 3:
            v = v.rearrange("p (a b) -> p a b", a=shape[1])
        elif len(shape) == 4:
            v = v.rearrange("p (a b c) -> p a b c", a=shape[1], b=shape[2])
        return v


def build_program(stop_after=None):
    nc = bass.Bass("TRN2", target_bir_lowering=False)
    dr = {}

    def din(name, shape):
        dr[name] = nc.dram_tensor(name, shape, F32, kind="ExternalInput").ap()
        return dr[name]

    xT = din("xT", [D, T])
    cT = din("cT", [D, CTX])
    prm_d = din("prm", [128, NPRM])
    cst_d = din("cst", [128, NCST])
    rope_d = din("rope", [2, 128, T])
    ada_w = din("ada_w", [2, D, 9 * D])
    f_wi = [din("ffn1_wi", [2, D, 2 * DFF]), din("ffn2_wi", [2, D, 2 * DFF])]
    f_wo = [din("ffn1_wo", [2, DFF, D]), din("ffn2_wo", [2, DFF, D])]
    pw1_w = din("pw1_w", [D, 2 * D])
    pw2_w = din("pw2_w", [D, D])
    watt = din("w_att", [D, 4 * 768])
    wv_d = din("w_v", [D, 256])
    wo_d = din("w_o", [D, D])
    outT = nc.dram_tensor("outT", [D, OWN], F32, kind="ExternalOutput").ap()

    semkeys = [f"{e}@{i}" for e in Sched.CE for i in range(Sched.NEPOCH)] + ["ldp", "ldc", "ldxc", "st0", "st1", "rp0", "rp1", "ad0", "ad1"] + \
              [f"ldx{i}" for i in range(5)] + [f"w{i}" for i in range(NSLOT)]

    import contextlib
    es = contextlib.ExitStack()
    with es:
        sems = {k: es.enter_context(nc.semaphore(k)) for k in semkeys}
        X = es.enter_context(nc.sbuf_tensor("X", [128, NCH, T], F32))
        XC = es.enter_context(nc.sbuf_tensor("XC", [128, NCH, CTX], F32))
        Wr = es.enter_context(nc.sbuf_tensor("Wr", [128, NSLOT, SLOT], BF16))
        prm = es.enter_context(nc.sbuf_tensor("prm_sb", [128, NPRM], F32))
        cst = es.enter_context(nc.sbuf_tensor("cst_sb", [128, NCST], BF16))
        mod = es.enter_context(nc.sbuf_tensor("mod", [128, 2, 72, 2], F32))
        tabA = es.enter_context(nc.sbuf_tensor("tabA", [128, 3, NCH, 2], F32))
        tabG = es.enter_context(nc.sbuf_tensor("tabG", [128, 3, NCH, 2], F32))
        tabGb = es.enter_context(nc.sbuf_tensor("tabGb", [128, NCH, 2], F32))
        scT = es.enter_context(nc.sbuf_tensor("scT", [128, NCH, 2], BF16))
        onesf = es.enter_context(nc.sbuf_tensor("onesf", [128, 128], F32))
        esink = es.enter_context(nc.sbuf_tensor("esink", [128, NCH], F32))
        epsb = es.enter_context(nc.sbuf_tensor("epsb", [128, 1], F32))
        ARENA_BYTES = 212863 - (NCH * T * 4 + NCH * CTX * 4 + NSLOT * SLOT * 2 + NPRM * 4 + NCST * 2
                                + 2 * 72 * 2 * 4 + 2 * 3 * NCH * 2 * 4 + NCH * 2 * 4 + NCH * 2 * 2
                                + 128 * 4 + NCH * 4) - 384
        ARENA_BYTES = ARENA_BYTES // 64 * 64
        ar_t = es.enter_context(nc.sbuf_tensor("arena", [128, ARENA_BYTES // 4], F32))
        PSALL = es.enter_context(nc.psum_tensor("psall", [128, 4096], F32))
        PS = [PSALL[:, i * 512:(i + 1) * 512] for i in range(8)]
        S = Sched(nc, sems)
        A = Arena(ar_t, ARENA_BYTES)

        def P(name, a=0, b=None):
            o, w = PRM[name]
            b = w if b is None else b
            return prm[:, o + a:o + b]

        def xbuf(which):
            return X if which == 0 else XC

        def xk(which, c, a, n):
            if which == 1:
                return [("XC", c)]
            return [("X", c, s) for s in _segs(a, a + n)]

        def hk(c, a, n):
            if a >= T:
                return [("H", c, "c")]
            return [("H", c, s) for s in _segs(a, a + n)]

        S.dma("sp", [lambda e: e.dma_start(out=prm[:], in_=prm_d[:, :])], "ldp", writes=["prm"])
        S.dma("pool", [lambda e: e.dma_start(out=cst[:], in_=cst_d[:, :])], "ldc", writes=["cst"])
        xT_v = xT.rearrange("(c p) t -> p c t", p=128)
        cT_v = cT.rearrange("(c p) t -> p c t", p=128)
        for i in range(5):
            S.dma("sp", [lambda e, i=i: e.dma_start(out=X[:, :, i * 440:(i + 1) * 440],
                                                   in_=xT_v[:, :, i * 440:(i + 1) * 440])],
                  f"ldx{i}", writes=[k for c in range(NCH) for k in xk(0, c, i * 440, 440)])
        S.dma("sp", [lambda e: e.dma_start(out=XC[:], in_=cT_v)], "ldxc",
              writes=[("XC", c) for c in range(NCH)])
        S.op("dve", lambda e: e.memset(onesf[:], 1.0 / D), writes=["onesf"])
        S.op("dve", lambda e: e.memset(epsb[:], EPS), writes=["epsb"])
        S.op("act", lambda e: e.activation(out=scT[:], in_=P("cc").rearrange("p (k w) -> p k w", w=2),
                                           func=AF.Silu), reads=["prm"], writes=["scT"])
        S.op("act", lambda e: e.activation(out=esink[:], in_=P("sink"), func=AF.Exp),
             reads=["prm"], writes=["esink"])

        ring = {"next": 0}

        def slot_alloc():
            s = ring["next"]
            ring["next"] = (s + 1) % NSLOT
            return s

        def load_slot(s, fns):
            S.dma("pool", fns, f"w{s}", writes=[("W", s)])

        def ada_unit(l, u, stg, sb):
            aw = ada_w[l].rearrange("(k p) n -> p k n", p=128)
            S.dma("pool", [lambda e, u=u, stg=stg: e.dma_start(out=stg, in_=aw[:, :, u * 256:(u + 1) * 256])],
                  f"ad{sb}", writes=[("adst", sb)])

            def mm():
                pb = 6 + (u % 2)
                ps = PS[pb][:, 504:508].rearrange("p (a b) -> p a b", b=2)
                for oc in range(2):
                    for k in range(NCH):
                        S.op("pe", lambda e, oc=oc, k=k, ps=ps, stg=stg: e.matmul(
                            ps[:, oc, :], lhsT=stg[:, k, oc * 128:(oc + 1) * 128], rhs=scT[:, k, :],
                            start=(k == 0), stop=(k == NCH - 1)),
                            reads=[("adst", sb), "scT"], writes=[("ps", pb)], sig=(k == NCH - 1))
                ab = P("ada_b", l * 72 + u * 2, l * 72 + u * 2 + 2)
                S.op("dve", lambda e, ps=ps, ab=ab, u=u: e.tensor_tensor(
                    out=mod[:, l, u * 2:(u + 1) * 2, :], in0=ps,
                    in1=ab.unsqueeze(2).to_broadcast([128, 2, 2]), op=ALU.add),
                    reads=[("ps", pb), "prm"], writes=[("mod", l, u // 12)])
            return mm

        def ada_tables(l, j):
            g = P("norm_g", (l * 3 + j) * 8, (l * 3 + j) * 8 + 8)
            S.op("dve", lambda e, j=j, g=g: e.scalar_tensor_tensor(
                out=tabA[:, j], in0=mod[:, l, (3 * j + 1) * 8:(3 * j + 2) * 8, :], scalar=1.0,
                in1=g.unsqueeze(2).to_broadcast([128, NCH, 2]), op0=ALU.add, op1=ALU.mult),
                reads=[("mod", l, j), "prm"], writes=[("tabA", j)])
            S.op("dve", lambda e, j=j: e.tensor_scalar(
                out=tabG[:, j], in0=mod[:, l, (3 * j + 2) * 8:(3 * j + 3) * 8, :],
                scalar1=(1.0 if j == 1 else 0.5), scalar2=None, op0=ALU.mult),
                reads=[("mod", l, j)], writes=[("tabG", j)])
            if l == 0 and j == 1:
                S.op("dve", lambda e: e.tensor_tensor(
                    out=tabGb[:], in0=tabG[:, 1], in1=P("pw2_b").unsqueeze(2).to_broadcast([128, NCH, 2]),
                    op=ALU.mult), reads=[("tabG", 1), "prm"], writes=["tabGb"])

        def ada_run(l, units, adst):
            pend = []
            for i, u in enumerate(units):
                pend.append(ada_unit(l, u, adst[i % 2], i % 2))
                if len(pend) == 2:
                    pend.pop(0)()
            for f_ in pend:
                f_()

        def tabB(l, j):
            return mod[:, l, (3 * j) * 8:(3 * j + 1) * 8, :]

        def make_prepass(l, j, tiles, H, nbuf=3):
            sq = [A.alloc([128, 512], F32) for _ in range(nbuf)]
            rstd = [A.alloc([128, 512], F32) for _ in range(2)]
            rsq = [A.alloc([128, 512], F32) for _ in range(2)]
            tmp = [A.alloc([128, 512], F32) for _ in range(nbuf)]
            st_ = {"sq": 0, "tmp": 0}

            def pre_tile(ti):
                a, n, w = tiles[ti]
                xb = xbuf(w)
                xa = a - T if w == 1 else a
                msb = 6 + (ti % 2)
                ms = PS[msb][:, 0:n]
                for c in range(NCH):
                    q = st_["sq"] % nbuf
                    st_["sq"] += 1
                    S.op("act", lambda e, q=q, c=c, xb=xb, xa=xa, n=n: e.activation(
                        out=sq[q][:, 0:n], in_=xb[:, c, xa:xa + n], func=AF.Square),
                        reads=xk(w, c, xa, n), writes=[("sq", q)])
                    S.op("pe", lambda e, q=q, c=c, ms=ms, n=n: e.matmul(
                        ms, lhsT=onesf[:], rhs=sq[q][:, 0:n], start=(c == 0), stop=(c == NCH - 1)),
                        reads=[("sq", q), "onesf"], writes=[("ps", msb)], sig=True)
                r = ti % 2
                S.op("act", lambda e, r=r, ms=ms, n=n: e.activation(
                    out=rsq[r][:, 0:n], in_=ms, func=AF.Sqrt, bias=epsb[:, 0:1], scale=1.0),
                    reads=[("ps", msb), "epsb"], writes=[("rsq", r)])
                S.op("dve", lambda e, r=r, n=n: e.reciprocal(out=rstd[r][:, 0:n], in_=rsq[r][:, 0:n]),
                     reads=[("rsq", r)], writes=[("rstd", r)])
                for c in range(NCH):
                    q = st_["tmp"] % nbuf
                    st_["tmp"] += 1
                    S.op("dve", lambda e, q=q, c=c, xb=xb, xa=xa, n=n, r=r, w=w: e.scalar_tensor_tensor(
                        out=tmp[q][:, 0:n], in0=xb[:, c, xa:xa + n], scalar=tabA[:, j, c, w:w + 1],
                        in1=rstd[r][:, 0:n], op0=ALU.mult, op1=ALU.mult),
                        reads=xk(w, c, xa, n) + [("rstd", r), ("tabA", j)], writes=[("ptmp", q)])
                    S.op("act", lambda e, q=q, c=c, a=a, n=n, w=w: e.activation(
                        out=H[:, c, a:a + n], in_=tmp[q][:, 0:n], func=AF.Identity,
                        bias=tabB(l, j)[:, c, w:w + 1], scale=1.0),
                        reads=[("ptmp", q), ("mod", l, j)], writes=hk(c, a, n))
            return pre_tile

        def prepass(l, j, tiles, H):
            m = A.mark()
            pt = make_prepass(l, j, tiles, H)
            for ti in range(len(tiles)):
                pt(ti)
            S.barrier()
            A.reset(m)

        def ffn(l, f, tiles, H, ada_job=None):
            j = 0 if f == 0 else 2
            m = A.mark()
            pre_tile = make_prepass(l, j, tiles, H, nbuf=2)
            adst = [A.alloc([128, NCH, 256], BF16) for _ in range(2)]
            act = [A.alloc([128, 4, 512], BF16) for _ in range(2)]
            sg = [A.alloc([128, 512], F32) for _ in range(2)]
            wi = f_wi[f][l].rearrange("(k p) (two n) -> p k two n", p=128, two=2)
            wo = f_wo[f][l].rearrange("(j p) n -> p j n", p=128)
            sweeps = [[0, 1], [2, 3], [4, 5], [6, 7], [8, 9], [10]]
            unit_slot = {}

            def load_unit(u):
                s = slot_alloc()
                unit_slot[u] = s
                wiv = Wr[:, s, 0:4096].rearrange("p (k two n) -> p k two n", k=NCH, two=2)
                wov = Wr[:, s, 4096:6144].rearrange("p (j n) -> p j n", j=2)
                load_slot(s, [lambda e, wiv=wiv, u=u: e.dma_start(out=wiv[:, :, 0, :], in_=wi[:, :, 0, u * 256:(u + 1) * 256]),
                              lambda e, wiv=wiv, u=u: e.dma_start(out=wiv[:, :, 1, :], in_=wi[:, :, 1, u * 256:(u + 1) * 256]),
                              lambda e, wov=wov, u=u: e.dma_start(out=wov, in_=wo[:, 2 * u:2 * u + 2, :])])

            tasks = []
            for si, sw in enumerate(sweeps):
                for ti in range(len(tiles)):
                    tasks.append((si, ti))
            st = {"sg": 0, "gu": 0, "y": 0}

            def stage_a(k):
                si, ti = tasks[k]
                a, n, w = tiles[ti]
                ab = k % 2
                chunks = [(u, jj) for u in sweeps[si] for jj in range(2)]
                for ci, (u, jj) in enumerate(chunks):
                    s = unit_slot[u]
                    wiv = Wr[:, s, 0:4096].rearrange("p (k two n) -> p k two n", k=NCH, two=2)
                    gb = st["gu"] % 2
                    st["gu"] += 1
                    gps = PS[gb * 2][:, 0:n]
                    ups = PS[gb * 2 + 1][:, 0:n]
                    for two, pp, pb in ((0, gps, gb * 2), (1, ups, gb * 2 + 1)):
                        for kk in range(NCH):
                            S.op("pe", lambda e, pp=pp, wiv=wiv, kk=kk, two=two, jj=jj, a=a, n=n: e.matmul(
                                pp, lhsT=wiv[:, kk, two, jj * 128:(jj + 1) * 128], rhs=H[:, kk, a:a + n],
                                start=(kk == 0), stop=(kk == NCH - 1)),
                                reads=[("W", s)] + hk(kk, a, n), writes=[("ps", pb)], sig=(kk == NCH - 1))
                    q = st["sg"] % 2
                    st["sg"] += 1
                    S.op("act", lambda e, q=q, gps=gps, n=n: e.activation(out=sg[q][:, 0:n], in_=gps, func=AF.Silu),
                         reads=[("ps", gb * 2)], writes=[("sg", q)])
                    S.op("dve", lambda e, q=q, ups=ups, n=n, ab=ab, ci=ci: e.tensor_tensor(
                        out=act[ab][:, ci, 0:n], in0=sg[q][:, 0:n], in1=ups, op=ALU.mult),
                        reads=[("sg", q), ("ps", gb * 2 + 1)], writes=[("act", ab, ci)])

            def stage_b(k):
                si, ti = tasks[k]
                a, n, w = tiles[ti]
                xb = xbuf(w)
                xa = a - T if w == 1 else a
                ab = k % 2
                chunks = [(u, jj) for u in sweeps[si] for jj in range(2)]
                for d in range(NCH):
                    yb = 4 + st["y"] % 2
                    st["y"] += 1
                    yps = PS[yb][:, 0:n]
                    for ci, (u, jj) in enumerate(chunks):
                        s = unit_slot[u]
                        wov = Wr[:, s, 4096:6144].rearrange("p (j n) -> p j n", j=2)
                        S.op("pe", lambda e, yps=yps, wov=wov, jj=jj, d=d, ab=ab, ci=ci, n=n: e.matmul(
                            yps, lhsT=wov[:, jj, d * 128:(d + 1) * 128], rhs=act[ab][:, ci, 0:n],
                            start=(ci == 0), stop=(ci == len(chunks) - 1)),
                            reads=[("W", s), ("act", ab, ci)], writes=[("ps", yb)], sig=(ci == len(chunks) - 1))
                    S.op("dve", lambda e, yps=yps, d=d, xb=xb, xa=xa, n=n, w=w: e.scalar_tensor_tensor(
                        out=xb[:, d, xa:xa + n], in0=yps, scalar=tabG[:, j, d, w:w + 1], in1=xb[:, d, xa:xa + n],
                        op0=ALU.mult, op1=ALU.add),
                        reads=[("ps", yb), ("tabG", j)] + xk(w, d, xa, n), writes=xk(w, d, xa, n))

            loaded = 0

            def ensure_loaded(si):
                nonlocal loaded
                while loaded <= min(si, len(sweeps) - 1):
                    for u in sweeps[loaded]:
                        load_unit(u)
                    loaded += 1
            ensure_loaded(1)
            nt = len(tiles)
            pre_tile(0)
            if nt > 1:
                pre_tile(1)
            ada_l, ada_units = ada_job if ada_job else (0, [])
            ada_pend = []
            ada_i = 0
            for k in range(len(tasks) + 1):
                if k < len(tasks):
                    si, ti = tasks[k]
                    if ti == 1:
                        ensure_loaded(si + 1)
                    if si == 0 and ti + 2 < nt:
                        pre_tile(ti + 2)
                    if ada_i < len(ada_units):
                        ada_pend.append(ada_unit(ada_l, ada_units[ada_i], adst[ada_i % 2], ada_i % 2))
                        ada_i += 1
                        if len(ada_pend) == 2:
                            ada_pend.pop(0)()
                    stage_a(k)
                if k >= 1:
                    stage_b(k - 1)
            for f_ in ada_pend:
                f_()
            assert ada_i == len(ada_units)
            S.barrier()
            A.reset(m)

        def conv_phase(l, H):
            prepass(l, 1, TILES_A, H)
            m = A.mark()
            p1 = pw1_w.rearrange("(k p) n -> p k n", p=128)
            p2 = pw2_w.rearrange("(k p) n -> p k n", p=128)
            slots = []
            for i in range(4):
                s = slot_alloc()
                slots.append(s)
                v1 = Wr[:, s, 0:4096].rearrange("p (k two n) -> p k two n", k=NCH, two=2)
                v2 = Wr[:, s, 4096:6144].rearrange("p (k n) -> p k n", k=NCH)
                load_slot(s, [lambda e, v1=v1, i=i: e.dma_start(out=v1[:, :, 0, :], in_=p1[:, :, i * 256:(i + 1) * 256]),
                              lambda e, v1=v1, i=i: e.dma_start(out=v1[:, :, 1, :], in_=p1[:, :, D + i * 256:D + (i + 1) * 256]),
                              lambda e, v2=v2, i=i: e.dma_start(out=v2, in_=p2[:, :, i * 256:(i + 1) * 256])])
            UW = 472
            NPE = 16
            U = [A.alloc([128, UW], F32) for _ in range(2)]
            Ub = [A.alloc([128, UW], BF16) for _ in range(2)]
            Dg = A.alloc([128, 2, NPE, 128], BF16)
            V = A.alloc([128, NCH, 440], F32)
            sqv = [A.alloc([128, 440], F32) for _ in range(1)]
            varb = A.alloc([128, 440], F32)
            HN = A.alloc([128, NCH, 440], BF16)
            ident = cst[:, 704:832]
            dww = P("dw_w").rearrange("p (c k) -> p c k", k=CW)
            K1 = CW - NPE
            for ti, (a, n, w) in enumerate(TILES_A):
                xb = xbuf(w)
                xa = a - T if w == 1 else a
                s0, s1 = (0, T) if w == 0 else (T, T + CTX)
                lo, hi = max(a - 15, s0), min(a + n + 15, s1)
                ulo, uhi = lo - (a - 15), hi - (a - 15)
                nu = hi - lo
                for jp in range(4):
                    for jj in range(2):
                        j = 2 * jp + jj
                        s = slots[jp]
                        v1 = Wr[:, s, 0:4096].rearrange("p (k two n) -> p k two n", k=NCH, two=2)
                        for two in range(2):
                            pb = 2 * jj + two
                            pp = PS[pb][:, 0:nu]
                            for kk in range(NCH):
                                S.op("pe", lambda e, pp=pp, v1=v1, kk=kk, two=two, jj=jj, lo=lo, hi=hi: e.matmul(
                                    pp, lhsT=v1[:, kk, two, jj * 128:(jj + 1) * 128], rhs=H[:, kk, lo:hi],
                                    start=(kk == 0), stop=(kk == NCH - 1)),
                                    reads=[("W", s)] + hk(kk, lo, nu), writes=[("ps", pb)], sig=(kk == NCH - 1))
                        if ulo > 0:
                            S.op("dve", lambda e, jj=jj, ulo=ulo: e.memset(U[jj][:, 0:ulo], 0.0), writes=[("U", jj)])
                        if uhi < n + 30:
                            S.op("dve", lambda e, jj=jj, uhi=uhi, n=n: e.memset(U[jj][:, uhi:n + 30], 0.0), writes=[("U", jj)])
                        S.op("act", lambda e, jj=jj, j=j, nu=nu, ulo=ulo, uhi=uhi: e.activation(
                            out=U[jj][:, ulo:uhi], in_=PS[2 * jj + 1][:, 0:nu], func=AF.Sigmoid,
                            bias=P("pw1_b", 8 + j, 9 + j), scale=1.0),
                            reads=[("ps", 2 * jj + 1), "prm"], writes=[("U", jj)])
                        S.op("dve", lambda e, jj=jj, j=j, nu=nu, ulo=ulo, uhi=uhi: e.scalar_tensor_tensor(
                            out=U[jj][:, ulo:uhi], in0=PS[2 * jj][:, 0:nu], scalar=P("pw1_b", j, j + 1),
                            in1=U[jj][:, ulo:uhi], op0=ALU.add, op1=ALU.mult),
                            reads=[("ps", 2 * jj), "prm", ("U", jj)], writes=[("U", jj)])
                    for k in range(K1):
                        for jj in range(2):
                            j = 2 * jp + jj
                            if k == 0:
                                S.op("dve", lambda e, jj=jj, j=j, n=n: e.tensor_scalar(
                                    out=V[:, j, 0:n], in0=U[jj][:, 0:n], scalar1=dww[:, j, 0:1],
                                    scalar2=P("dw_b", j, j + 1), op0=ALU.mult, op1=ALU.add),
                                    reads=[("U", jj), "prm"], writes=[("V", j)])
                            else:
                                S.op("dve", lambda e, jj=jj, j=j, n=n, k=k: e.scalar_tensor_tensor(
                                    out=V[:, j, 0:n], in0=U[jj][:, k:k + n], scalar=dww[:, j, k:k + 1],
                                    in1=V[:, j, 0:n], op0=ALU.mult, op1=ALU.add),
                                    reads=[("U", jj), ("V", j), "prm"], writes=[("V", j)])
                    for jj in range(2):
                        j = 2 * jp + jj
                        S.op("act", lambda e, jj=jj, n=n: e.activation(
                            out=Ub[jj][:, 0:n + 30], in_=U[jj][:, 0:n + 30], func=AF.Identity),
                            reads=[("U", jj)], writes=[("Ub", jj)])
                        for kidx in range(NPE):
                            k = K1 + kidx
                            S.op("act", lambda e, jj=jj, j=j, k=k, kidx=kidx: e.activation(
                                out=Dg[:, jj, kidx, :], in_=ident, func=AF.Identity, scale=dww[:, j, k:k + 1]),
                                reads=["cst", "prm"], writes=[("Dg", jj, kidx)])
                    for jj in range(2):
                        j = 2 * jp + jj
                        for kidx in range(NPE):
                            k = K1 + kidx
                            S.op("pe", lambda e, jj=jj, k=k, kidx=kidx, n=n: e.matmul(
                                PS[4 + jj][:, 0:n], lhsT=Dg[:, jj, kidx, :], rhs=Ub[jj][:, k:k + n],
                                start=(kidx == 0), stop=(kidx == NPE - 1)),
                                reads=[("Dg", jj, kidx), ("Ub", jj)], writes=[("ps", 4 + jj)], sig=(kidx == NPE - 1))
                    for jj in range(2):
                        j = 2 * jp + jj
                        S.op("dve", lambda e, jj=jj, j=j, n=n: e.tensor_tensor(
                            out=V[:, j, 0:n], in0=V[:, j, 0:n], in1=PS[4 + jj][:, 0:n], op=ALU.add),
                            reads=[("V", j), ("ps", 4 + jj)], writes=[("V", j)])
                for j in range(NCH):
                    S.op("pe", lambda e, j=j, n=n: e.matmul(PS[4][:, 0:n], lhsT=onesf[:], rhs=V[:, j, 0:n],
                                                           start=(j == 0), stop=(j == NCH - 1)),
                         reads=[("V", j), "onesf"], writes=[("ps", 4)], sig=True)
                    S.op("act", lambda e, j=j, n=n: e.activation(out=sqv[0][:, 0:n], in_=V[:, j, 0:n], func=AF.Square),
                         reads=[("V", j)], writes=[("sqv", 0)])
                    S.op("pe", lambda e, j=j, n=n: e.matmul(PS[5][:, 0:n], lhsT=onesf[:], rhs=sqv[0][:, 0:n],
                                                           start=(j == 0), stop=(j == NCH - 1)),
                         reads=[("sqv", 0), "onesf"], writes=[("ps", 5)], sig=True)
                S.op("act", lambda e, n=n: e.activation(out=varb[:, 0:n], in_=PS[4][:, 0:n], func=AF.Square),
                     reads=[("ps", 4)], writes=["varb"])
                S.op("dve", lambda e, n=n: e.tensor_tensor(out=varb[:, 0:n], in0=PS[5][:, 0:n], in1=varb[:, 0:n],
                                                           op=ALU.subtract), reads=[("ps", 5), "varb"], writes=["varb"])
                S.op("act", lambda e, n=n: e.activation(out=varb[:, 0:n], in_=varb[:, 0:n], func=AF.Sqrt,
                                                        bias=epsb[:, 0:1], scale=1.0),
                     reads=["varb", "epsb"], writes=["varb"])
                S.op("dve", lambda e, n=n: e.reciprocal(out=varb[:, 0:n], in_=varb[:, 0:n]),
                     reads=["varb"], writes=["varb"])
                for j in range(NCH):
                    S.op("dve", lambda e, j=j, n=n: e.tensor_tensor(
                        out=V[:, j, 0:n], in0=V[:, j, 0:n], in1=PS[4][:, 0:n], op=ALU.subtract),
                        reads=[("V", j), ("ps", 4)], writes=[("V", j)])
                    S.op("dve", lambda e, j=j, n=n: e.tensor_tensor(
                        out=V[:, j, 0:n], in0=V[:, j, 0:n], in1=varb[:, 0:n], op=ALU.mult),
                        reads=[("V", j), "varb"], writes=[("V", j)])
                    S.op("act", lambda e, j=j, n=n: e.activation(
                        out=HN[:, j, 0:n], in_=V[:, j, 0:n], func=AF.Silu,
                        bias=P("ln_b", j, j + 1), scale=P("ln_g", j, j + 1)),
                        reads=[("V", j), "prm"], writes=[("HN", j)])
                for d in range(NCH):
                    s = slots[d // 2]
                    v2 = Wr[:, s, 4096:6144].rearrange("p (k n) -> p k n", k=NCH)
                    yb = 6 + d % 2
                    for kk in range(NCH):
                        S.op("pe", lambda e, v2=v2, kk=kk, d=d, yb=yb, n=n: e.matmul(
                            PS[yb][:, 0:n], lhsT=v2[:, kk, (d % 2) * 128:(d % 2 + 1) * 128], rhs=HN[:, kk, 0:n],
                            start=(kk == 0), stop=(kk == NCH - 1)),
                            reads=[("W", s), ("HN", kk)], writes=[("ps", yb)], sig=(kk == NCH - 1))
                    S.op("act", lambda e, d=d, n=n, w=w, yb=yb: e.activation(
                        out=sqv[0][:, 0:n], in_=PS[yb][:, 0:n], func=AF.Identity,
                        bias=tabGb[:, d, w:w + 1], scale=tabG[:, 1, d, w:w + 1]),
                        reads=[("ps", yb), ("tabG", 1), "tabGb"], writes=[("sqv", 0)])
                    S.op("dve", lambda e, d=d, xb=xb, xa=xa, n=n: e.tensor_tensor(
                        out=xb[:, d, xa:xa + n], in0=xb[:, d, xa:xa + n], in1=sqv[0][:, 0:n], op=ALU.add),
                        reads=[("sqv", 0)] + xk(w, d, xa, n), writes=xk(w, d, xa, n))
            S.barrier()
            A.reset(m)

        def attn_phase(l, H):
            prepass(l, 1, TILES_A, H)
            m = A.mark()
            Qg = A.alloc([128, 2, OWN], BF16)
            KgA = A.alloc([128, TK + CTX], BF16)
            KgB = A.alloc([128, TK + CTX], BF16)
            Vpad = A.alloc([128, 19, 192], BF16)
            Og = [A.alloc([128, 2, 512], BF16) for _ in range(1)]
            PT = [A.alloc([128, 5, 2, 128] if ATT_PIPE else [128, 2, 5, 128], BF16) for _ in range(2)]
            cosb = A.alloc([128, 440], F32)
            sinb = A.alloc([128, 440], F32)
            t1 = A.alloc([128, 440], F32)
            rden = [A.alloc([128, 128], F32) for _ in range(2)]
            mask_lo = cst[:, 0:256].rearrange("p (h q) -> p h q", h=2)
            mask_hi = cst[:, 256:512].rearrange("p (h q) -> p h q", h=2)
            onespad = cst[:, 512:704]
            wa = watt.rearrange("(k p) n -> p k n", p=128)
            wvv = wv_d.rearrange("(k p) n -> p k n", p=128)
            wov = wo_d.rearrange("(j p) n -> p j n", p=128)
            S.op("pool", lambda e: e.memset(Vpad[:], 0.0), writes=["Vpad"])
            S.op("dve", lambda e: e.memset(KgA[64:128, :], 0.0), writes=[("Kg",)])
            S.op("dve", lambda e: e.memset(KgB[0:64, :], 0.0), writes=[("Kg",)])
            cnt = {"st": 0, "od": 0, "og": 0}
            for g in range(4):
                sA = slot_alloc()
                vA = Wr[:, sA, :].rearrange("p (k n) -> p k n", k=NCH)
                load_slot(sA, [lambda e, vA=vA, g=g: e.dma_start(out=vA, in_=wa[:, :, g * 768:(g + 1) * 768])])
                sB = slot_alloc()
                vBo = Wr[:, sB, 0:2048].rearrange("p (j n) -> p j n", j=2)
                vBv = Wr[:, sB, 2048:2560].rearrange("p (k n) -> p k n", k=NCH)
                load_slot(sB, [lambda e, vBo=vBo, g=g: e.dma_start(out=vBo, in_=wov[:, 2 * g:2 * g + 2, :]),
                               lambda e, vBv=vBv, g=g: e.dma_start(out=vBv, in_=wvv[:, :, 64 * g:64 * g + 64])])
                for ti in range(5):
                    a, n = 440 * ti, 440
                    S.dma("sp", [lambda e, a=a, n=n: e.dma_start(out=cosb[:, 0:n], in_=rope_d[0][:, a:a + n]),
                                 lambda e, a=a, n=n: e.dma_start(out=sinb[:, 0:n], in_=rope_d[1][:, a:a + n])],
                          "rp0", writes=["rope"])
                    items = [("q", 0), ("q", 1), ("k", 0)]
                    for ii, (kind, cc) in enumerate(items):
                        if kind == "q":
                            nn = max(0, min(a + n, OWN) - a)
                            base, bsw = cc * 128, 256 + cc * 128
                            dkey = ("Qg", cc)
                        else:
                            nn = max(0, min(a + n, TK) - a)
                            base, bsw = 512, 640
                            dkey = ("Kg",)
                        if nn == 0:
                            continue
                        pa, pb = 2 * (ii % 2), 2 * (ii % 2) + 1
                        for (pp, bb) in ((pa, base), (pb, bsw)):
                            for kk in range(NCH):
                                S.op("pe", lambda e, pp=pp, bb=bb, kk=kk, a=a, nn=nn, vA=vA: e.matmul(
                                    PS[pp][:, 0:nn], lhsT=vA[:, kk, bb:bb + 128], rhs=H[:, kk, a:a + nn],
                                    start=(kk == 0), stop=(kk == NCH - 1)),
                                    reads=[("W", sA)] + hk(kk, a, nn), writes=[("ps", pp)], sig=(kk == NCH - 1))
                        S.op("dve", lambda e, pa=pa, nn=nn: e.tensor_tensor(
                            out=PS[pa][:, 0:nn], in0=PS[pa][:, 0:nn], in1=cosb[:, 0:nn], op=ALU.mult),
                            reads=[("ps", pa), "rope"], writes=[("ps", pa)])
                        S.op("dve", lambda e, pb=pb, nn=nn: e.tensor_tensor(
                            out=t1[:, 0:nn], in0=PS[pb][:, 0:nn], in1=sinb[:, 0:nn], op=ALU.mult),
                            reads=[("ps", pb), "rope"], writes=["t1"])
                        if kind == "q":
                            S.op("dve", lambda e, pa=pa, nn=nn, cc=cc, a=a: e.tensor_tensor(
                                out=Qg[:, cc, a:a + nn], in0=PS[pa][:, 0:nn], in1=t1[:, 0:nn], op=ALU.add),
                                reads=[("ps", pa), "t1"], writes=[dkey])
                        else:
                            S.op("dve", lambda e, pa=pa, nn=nn, a=a: e.tensor_tensor(
                                out=KgA[0:64, a:a + nn], in0=PS[pa][0:64, 0:nn], in1=t1[0:64, 0:nn], op=ALU.add),
                                reads=[("ps", pa), "t1"], writes=[dkey])
                            S.op("dve", lambda e, pa=pa, nn=nn, a=a: e.tensor_tensor(
                                out=KgB[64:128, a:a + nn], in0=PS[pa][64:128, 0:nn], in1=t1[64:128, 0:nn], op=ALU.add),
                                reads=[("ps", pa), "t1"], writes=[dkey])
                for kk in range(NCH):
                    S.op("pe", lambda e, kk=kk, vA=vA: e.matmul(
                        PS[0][:, 0:CTX], lhsT=vA[:, kk, 512:640], rhs=H[:, kk, T:T + CTX],
                        start=(kk == 0), stop=(kk == NCH - 1)),
                        reads=[("W", sA)] + hk(kk, T, CTX), writes=[("ps", 0)], sig=(kk == NCH - 1))
                S.op("act", lambda e: e.activation(out=KgA[0:64, TK:TK + CTX], in_=PS[0][0:64, 0:CTX], func=AF.Identity),
                     reads=[("ps", 0)], writes=[("Kg",)])
                S.op("act", lambda e: e.activation(out=KgB[64:128, TK:TK + CTX], in_=PS[0][64:128, 0:CTX], func=AF.Identity),
                     reads=[("ps", 0)], writes=[("Kg",)])
                for b0 in range(0, 19, 8):
                    nb = min(8, 19 - b0)
                    vb = 6 + (b0 // 8) % 2
                    psv = PS[vb].rearrange("p (b d) -> p b d", d=64)
                    for bi in range(nb):
                        blk = b0 + bi
                        c0 = 128 * blk if blk < 17 else T + 128 * (blk - 17)
                        for kk in range(NCH):
                            S.op("pe", lambda e, psv=psv, bi=bi, kk=kk, c0=c0, vBv=vBv: e.matmul(
                                psv[:, bi, :], lhsT=H[:, kk, c0:c0 + 128], rhs=vBv[:, kk, :],
                                start=(kk == 0), stop=(kk == NCH - 1)),
                                reads=[("W", sB)] + hk(kk, c0, 128), writes=[("ps", vb)],
                                sig=(kk == NCH - 1 and bi == nb - 1))
                    S.op("act", lambda e, psv=psv, b0=b0, nb=nb: e.activation(
                        out=Vpad[:, b0:b0 + nb, 64:128], in_=psv[:, 0:nb, :], func=AF.Identity),
                        reads=[("ps", vb)], writes=["Vpad"])
                its = [(qt, qb, cc) for qt in range(4) for qb in range(4) for cc in range(2)]
                info = {}

                def stage_s(it):
                    qt, qb, cc = its[it]
                    i = 4 * qt + qb
                    sb = cnt["st"] % 2
                    cnt["st"] += 1
                    KBM = ATT_PIPE
                    if KBM:
                        stv = PSALL[:, sb * 1536:sb * 1536 + 1280].rearrange("p (k h q) -> p k h q", k=5, h=2)
                    else:
                        stv = PSALL[:, sb * 1536:sb * 1536 + 1280].rearrange("p (h k q) -> p h k q", h=2, k=5)
                    stkeys = [("ps", sb * 3 + z) for z in range(3)]
                    k0 = 1 if i == 0 else 0
                    srcs = []
                    for kbi in range(k0, 5):
                        if kbi < 3:
                            blk = i - 1 + kbi
                            srcs.append((kbi, 128 * blk, blk))
                        else:
                            srcs.append((kbi, TK + 128 * (kbi - 3), 17 + kbi - 3))
                    nmm = 2 * len(srcs)
                    PEM = (ATT_MASK == "pe")
                    z = 0
                    for h in (1, 0):
                        for (kbi, kc, vbk) in srcs:
                            z += 1
                            msk = PEM and kbi in (0, 2)
                            S.op("pe", lambda e, stv=stv, h=h, kbi=kbi, kc=kc, cc=cc, i=i, KBM=KBM, msk=msk: e.matmul(
                                stv[:, kbi, h, :] if KBM else stv[:, h, kbi, :],
                                lhsT=(KgA if h == 0 else KgB)[:, kc:kc + 128],
                                rhs=Qg[:, cc, 128 * i:128 * i + 128], start=True, stop=(not msk)),
                                reads=[("Kg",), ("Qg", cc)], writes=stkeys, sig=(z == nmm and not msk))
                            if msk:
                                mb = cst[:, 832:960] if kbi == 0 else cst[:, 960:1088]
                                S.op("pe", lambda e, stv=stv, h=h, kbi=kbi, KBM=KBM, mb=mb: e.matmul(
                                    stv[:, kbi, h, :] if KBM else stv[:, h, kbi, :],
                                    lhsT=cst[:, 704:832], rhs=mb, start=False, stop=True),
                                    reads=["cst"], writes=stkeys, sig=(z == nmm))
                    pb = sb
                    if KBM:
                        for (ka, kb_) in ((max(k0, 0), 2), (2, 4), (4, 5)):
                            if ka >= kb_:
                                continue
                            S.op("act", lambda e, stv=stv, pb=pb, ka=ka, kb_=kb_: e.activation(
                                out=PT[pb][:, ka:kb_, :, :], in_=stv[:, ka:kb_, :, :], func=AF.Exp, scale=0.125),
                                reads=stkeys, writes=[("PT", pb)])
                        if i > 0 and not PEM:
                            S.op(ATT_MASK, lambda e, pb=pb: e.tensor_tensor(
                                out=PT[pb][:, 0, :, :], in0=PT[pb][:, 0, :, :], in1=mask_lo, op=ALU.mult),
                                reads=[("PT", pb), "cst"], writes=[("PT", pb)])
                        if not PEM:
                            S.op(ATT_MASK, lambda e, pb=pb: e.tensor_tensor(
                                out=PT[pb][:, 2, :, :], in0=PT[pb][:, 2, :, :], in1=mask_hi, op=ALU.mult),
                                reads=[("PT", pb), "cst"], writes=[("PT", pb)])
                    else:
                        S.op("act", lambda e, stv=stv, pb=pb, k0=k0: e.activation(
                            out=PT[pb][:, :, k0:5, :], in_=stv[:, :, k0:5, :], func=AF.Exp, scale=0.125),
                            reads=stkeys, writes=[("PT", pb)])
                        if i > 0 and not PEM:
                            S.op(ATT_MASK, lambda e, pb=pb: e.tensor_tensor(
                                out=PT[pb][:, :, 0, :], in0=PT[pb][:, :, 0, :], in1=mask_lo, op=ALU.mult),
                                reads=[("PT", pb), "cst"], writes=[("PT", pb)])
                        if not PEM:
                            S.op(ATT_MASK, lambda e, pb=pb: e.tensor_tensor(
                                out=PT[pb][:, :, 2, :], in0=PT[pb][:, :, 2, :], in1=mask_hi, op=ALU.mult),
                                reads=[("PT", pb), "cst"], writes=[("PT", pb)])
                    info[it] = (pb, srcs, nmm)

                def stage_p(it):
                    qt, qb, cc = its[it]
                    pb, srcs, nmm = info.pop(it)
                    ob = 0
                    odb = cnt["od"] % 2
                    cnt["od"] += 1
                    KBM = ATT_PIPE
                    if KBM:
                        o_ps = PS[6 + odb][:, 0:128]
                        d_ps = PS[6 + odb][:, 128:256]
                        odk = [("ps", 6 + odb)]
                    else:
                        o_ps = PS[6][:, (2 * odb) * 128:(2 * odb + 1) * 128]
                        d_ps = PS[6][:, (2 * odb + 1) * 128:(2 * odb + 2) * 128]
                        odk = []
                    for (tgt, is_den) in ((o_ps, False), (d_ps, True)):
                        z = 0
                        for h in range(2):
                            c_lo = 64 if h == 0 else 0
                            for (kbi, kc, vbk) in srcs:
                                z += 1
                                if is_den:
                                    lh = onespad[:, c_lo:c_lo + 128]
                                else:
                                    lh = Vpad[:, vbk, c_lo:c_lo + 128]
                                S.op("pe", lambda e, tgt=tgt, lh=lh, pb=pb, h=h, kbi=kbi, z=z, nmm=nmm, KBM=KBM: e.matmul(
                                    tgt, lhsT=lh, rhs=(PT[pb][:, kbi, h, :] if KBM else PT[pb][:, h, kbi, :]),
                                    start=(z == 1), stop=(z == nmm)),
                                    reads=[("PT", pb), "Vpad", "cst"], writes=[("ps6", odb, is_den)] + odk,
                                    sig=(z == nmm))
                    S.op("dve", lambda e, d_ps=d_ps, odb=odb, cc=cc, g=g: e.tensor_scalar(
                        out=rden[odb][:], in0=d_ps, scalar1=esink[:, 2 * g + cc:2 * g + cc + 1], scalar2=None,
                        op0=ALU.add), reads=[("ps6", odb, True), "esink"] + odk, writes=[("rden", odb)])
                    if ATT_DIV:
                        S.op("dve", lambda e, o_ps=o_ps, odb=odb, ob=ob, cc=cc, qb=qb: e.tensor_tensor(
                            out=Og[ob][:, cc, qb * 128:(qb + 1) * 128], in0=o_ps, in1=rden[odb][:], op=ALU.divide),
                            reads=[("ps6", odb, False), ("rden", odb)] + odk, writes=[("Og", ob, cc)])
                    else:
                        S.op("dve", lambda e, odb=odb: e.reciprocal(out=rden[odb][:], in_=rden[odb][:]),
                             reads=[("rden", odb)], writes=[("rden", odb)])
                        S.op("dve", lambda e, o_ps=o_ps, odb=odb, ob=ob, cc=cc, qb=qb: e.tensor_tensor(
                            out=Og[ob][:, cc, qb * 128:(qb + 1) * 128], in0=o_ps, in1=rden[odb][:], op=ALU.mult),
                            reads=[("ps6", odb, False), ("rden", odb)] + odk, writes=[("Og", ob, cc)])
                    if qb == 3 and cc == 1 and not ATT_EXP:
                        for d in range(NCH):
                            if KBM:
                                for hf in range(2):
                                    ybk = 2 + 3 * hf
                                    yv = PSALL[:, ybk * 512 + 256:ybk * 512 + 512]
                                    for c2 in range(2):
                                        S.op("pe", lambda e, d=d, c2=c2, ob=ob, vBo=vBo, yv=yv, hf=hf: e.matmul(
                                            yv, lhsT=vBo[:, c2, d * 128:(d + 1) * 128],
                                            rhs=Og[ob][:, c2, hf * 256:(hf + 1) * 256],
                                            start=(c2 == 0), stop=(c2 == 1)),
                                            reads=[("W", sB), ("Og", ob, c2)], writes=[("ps", ybk)], sig=(c2 == 1))
                                    x0 = qt * 512 + hf * 256
                                    S.op("dve", lambda e, d=d, x0=x0, yv=yv: e.scalar_tensor_tensor(
                                        out=X[:, d, x0:x0 + 256], in0=yv, scalar=tabG[:, 1, d, 0:1],
                                        in1=X[:, d, x0:x0 + 256], op0=ALU.mult, op1=ALU.add),
                                        reads=[("ps", ybk), ("tabG", 1)] + xk(0, d, x0, 256),
                                        writes=xk(0, d, x0, 256))
                                continue
                            yb = 7
                            for c2 in range(2):
                                S.op("pe", lambda e, d=d, c2=c2, ob=ob, vBo=vBo, yb=yb: e.matmul(
                                    PS[yb][:, 0:512], lhsT=vBo[:, c2, d * 128:(d + 1) * 128], rhs=Og[ob][:, c2, :],
                                    start=(c2 == 0), stop=(c2 == 1)),
                                    reads=[("W", sB), ("Og", ob, c2)], writes=[("ps", yb)], sig=(c2 == 1))
                            S.op("dve", lambda e, d=d, qt=qt, yb=yb: e.scalar_tensor_tensor(
                                out=X[:, d, qt * 512:(qt + 1) * 512], in0=PS[yb][:, 0:512], scalar=tabG[:, 1, d, 0:1],
                                in1=X[:, d, qt * 512:(qt + 1) * 512], op0=ALU.mult, op1=ALU.add),
                                reads=[("ps", yb), ("tabG", 1)] + xk(0, d, qt * 512, 512),
                                writes=xk(0, d, qt * 512, 512))

                if ATT_PIPE:
                    stage_s(0)
                    for it in range(len(its)):
                        if it + 1 < len(its):
                            stage_s(it + 1)
                        stage_p(it)
                else:
                    for it in range(len(its)):
                        stage_s(it)
                        stage_p(it)
            S.barrier()
            A.reset(m)

        H = A.alloc([128, NCH, TT], BF16)
        NOT = lambda *names: stop_after not in names
        if ONLY_ATTN:
            m0 = A.mark()
            ad0 = [A.alloc([128, NCH, 256], BF16) for _ in range(2)]
            ada_run(1, list(range(36)), ad0)
            for jj_ in range(3):
                ada_tables(1, jj_)
            S.barrier()
            A.reset(m0)
            attn_phase(1, H)
            stop_after = "x_ffn1_0"
        else:
            m0 = A.mark()
            ad0 = [A.alloc([128, NCH, 256], BF16) for _ in range(2)]
            ada_run(0, list(range(12)), ad0)
            ada_tables(0, 0)
            S.barrier()
            A.reset(m0)
            ffn(0, 0, TILES_A, H, ada_job=(0, list(range(12, 36))))
            ada_tables(0, 1)
            ada_tables(0, 2)
        if NOT("x_ffn1_0"):
            conv_phase(0, H)
        if NOT("x_ffn1_0", "x_mix_0"):
            ffn(0, 1, TILES_A, H, ada_job=(1, list(range(36))))
        if NOT("x_ffn1_0", "x_mix_0", "x_out_0"):
            for jj_ in range(3):
                ada_tables(1, jj_)
            ffn(1, 0, TILES_A, H)
        if NOT("x_ffn1_0", "x_mix_0", "x_out_0", "x_ffn1_1"):
            attn_phase(1, H)
        if NOT("x_ffn1_0", "x_mix_0", "x_out_0", "x_ffn1_1", "x_mix_1"):
            ffn(1, 1, TILES_OWN, H)

        def final_out(do_norm):
            S.barrier()
            m = A.mark()
            A.reset(0)
            stg = [A.alloc([128, NCH, 512], F32) for _ in range(2)]
            sq = [A.alloc([128, 512], F32) for _ in range(3)]
            rstd = [A.alloc([128, 512], F32) for _ in range(2)]
            rsq = [A.alloc([128, 512], F32) for _ in range(2)]
            oT_v = outT.rearrange("(c p) t -> p c t", p=128)
            toks = []
            n_sq = 0
            for ti, (a, n, w) in enumerate(TILES_OWN):
                b = ti % 2
                if do_norm:
                    msb = 6 + b
                    ms = PS[msb][:, 0:n]
                    for c in range(NCH):
                        q = n_sq % 3
                        n_sq += 1
                        S.op("act", lambda e, q=q, c=c, a=a, n=n: e.activation(
                            out=sq[q][:, 0:n], in_=X[:, c, a:a + n], func=AF.Square),
                            reads=xk(0, c, a, n), writes=[("sq", q)])
                        S.op("pe", lambda e, q=q, c=c, ms=ms, n=n: e.matmul(
                            ms, lhsT=onesf[:], rhs=sq[q][:, 0:n], start=(c == 0), stop=(c == NCH - 1)),
                            reads=[("sq", q), "onesf"], writes=[("ps", msb)], sig=True)
                    S.op("act", lambda e, b=b, ms=ms, n=n: e.activation(
                        out=rsq[b][:, 0:n], in_=ms, func=AF.Sqrt, bias=epsb[:, 0:1], scale=1.0),
                        reads=[("ps", msb), "epsb"], writes=[("rsq", b)])
                    S.op("dve", lambda e, b=b, n=n: e.reciprocal(out=rstd[b][:, 0:n], in_=rsq[b][:, 0:n]),
                         reads=[("rsq", b)], writes=[("rstd", b)])
                    for c in range(NCH):
                        S.op("dve", lambda e, c=c, a=a, n=n, b=b: e.scalar_tensor_tensor(
                            out=stg[b][:, c, 0:n], in0=X[:, c, a:a + n], scalar=P("final_g", c, c + 1),
                            in1=rstd[b][:, 0:n], op0=ALU.mult, op1=ALU.mult),
                            reads=xk(0, c, a, n) + [("rstd", b), "prm"], writes=[("stg", b)],
                            sig=True)
                else:
                    for c in range(NCH):
                        S.op("act", lambda e, c=c, a=a, n=n, b=b: e.activation(
                            out=stg[b][:, c, 0:n], in_=X[:, c, a:a + n], func=AF.Identity),
                            reads=xk(0, c, a, n), writes=[("stg", b)])
                toks.append(S.dma("sp", [lambda e, b=b, a=a, n=n: e.dma_start(
                    out=oT_v[:, :, a:a + n], in_=stg[b][:, :, 0:n])], f"st{b}", reads=[("stg", b)]))
            S.wait_tokens("sp", toks)
            A.reset(m)

        final_out(stop_after is None)

        S.finalize()
        with nc.Block() as block:
            @block.tensor
            def _(e):
                S.emit("pe", e)

            @block.scalar
            def _(e):
                S.emit("act", e)

            @block.vector
            def _(e):
                S.emit("dve", e)

            @block.gpsimd
            def _(e):
                S.emit("pool", e)

            @block.sync
            def _(e):
                S.emit("sp", e)
    return nc


def _partner_cols(n):
    j = np.arange(n)
    d = j % 64
    pd = np.where((d % 32) < 16, d + 16, d - 16)
    return j - d + pd


def prepare_inputs(inp):
    f = lambda a: np.ascontiguousarray(np.asarray(a, dtype=np.float32))
    x, c, ctx, c_ctx = f(inp["x"]), f(inp["c"]), f(inp["ctx"]), f(inp["c_ctx"])
    w_qkv = f(inp["attn_w_qkv"])[0]
    chunk = lambda v: np.ascontiguousarray(v.reshape(-1, 128).T)
    shared = {
        "ada_w": f(inp["ada_w"]), "ffn1_wi": f(inp["ffn1_wi"]), "ffn1_wo": f(inp["ffn1_wo"]),
        "ffn2_wi": f(inp["ffn2_wi"]), "ffn2_wo": f(inp["ffn2_wo"]),
        "pw1_w": f(inp["conv_pw1_w"])[0], "pw2_w": f(inp["conv_pw2_w"])[0],
        "w_v": np.ascontiguousarray(w_qkv[:, 1280:1536]), "w_o": f(inp["attn_w_o"])[0],
    }
    watt = np.zeros((D, 4 * 768), np.float32)
    for g in range(4):
        q = w_qkv[:, 256 * g:256 * g + 256]
        k = w_qkv[:, 1024 + 64 * g:1024 + 64 * g + 64]
        kd = np.concatenate([k, k], axis=1)
        o = 768 * g
        watt[:, o:o + 256] = q
        watt[:, o + 256:o + 512] = q[:, _partner_cols(256)]
        watt[:, o + 512:o + 640] = kd
        watt[:, o + 640:o + 768] = kd[:, _partner_cols(128)]
    shared["w_att"] = watt
    cst = np.zeros((128, NCST), np.float32)
    kk = np.arange(128)[:, None]
    qq = np.arange(128)[None, :]
    lo = (kk >= qq).astype(np.float32)
    hi = (kk <= qq).astype(np.float32)
    cst[:, 0:128] = lo
    cst[:, 128:256] = lo
    cst[:, 256:384] = hi
    cst[:, 384:512] = hi
    cst[:, 512 + 64:512 + 128] = 1.0
    cst[:, 704:832] = np.eye(128, dtype=np.float32)
    cst[:, 832:960] = (lo - 1.0) * 30000.0
    cst[:, 960:1088] = (hi - 1.0) * 30000.0
    shared["cst"] = cst
    p = np.arange(128)
    d = p % 64
    ax = d // 32
    hh = (d % 32) // 16
    fr = d % 16
    inv = (np.float32(10000.0) ** (-np.arange(16, dtype=np.float32) / np.float32(16))).astype(np.float32)
    in_maps = []
    for core in range(8):
        b, hf = core // 2, core % 2
        idx = np.arange(T) if hf == 0 else (SEQ - 1 - np.arange(T))
        m = dict(shared)
        m["xT"] = np.ascontiguousarray(x[b, idx, :].T)
        cb = ctx[b] if hf == 0 else ctx[b, ::-1]
        m["cT"] = np.ascontiguousarray(cb.T)
        prm = np.zeros((128, NPRM), np.float32)

        def put(name, arr):
            o, w = PRM[name]
            assert arr.shape == (128, w), (name, arr.shape)
            prm[:, o:o + w] = arr
        cc = np.stack([chunk(c[b]), chunk(c_ctx)], axis=2).reshape(128, 16)
        put("cc", cc)
        put("ada_b", np.concatenate([chunk(f(inp["ada_b"])[l]) for l in range(2)], axis=1))
        put("norm_g", np.concatenate([chunk(f(inp["norm_g"])[l, j]) for l in range(2) for j in range(3)], axis=1))
        put("pw1_b", chunk(f(inp["conv_pw1_b"])[0]))
        dw = f(inp["conv_dw_w"])[0]
        if hf == 1:
            dw = dw[::-1]
        put("dw_w", np.ascontiguousarray(dw.T.reshape(NCH, 128, CW).transpose(1, 0, 2)).reshape(128, NCH * CW))
        put("dw_b", chunk(f(inp["conv_dw_b"])[0]))
        put("ln_g", chunk(f(inp["conv_ln_g"])[0]))
        put("ln_b", chunk(f(inp["conv_ln_b"])[0]))
        put("pw2_b", chunk(f(inp["conv_pw2_b"])[0]))
        sk = f(inp["attn_sink"])[0]
        put("sink", np.stack([np.where(p >= 64, sk[2 * cch + 1], sk[2 * cch]) for cch in range(NCH)], axis=1).astype(np.float32))
        put("final_g", chunk(f(inp["final_g"])))
        m["prm"] = prm
        row = (idx // 64).astype(np.float32)
        col = (idx % 64).astype(np.float32)
        pos = np.where(ax[:, None] == 0, row[None, :], col[None, :]).astype(np.float32)
        ang = (pos * inv[fr][:, None]).astype(np.float32)
        cs = np.cos(ang).astype(np.float32)
        sn = np.sin(ang).astype(np.float32)
        sn = np.where(hh[:, None] == 0, -sn, sn).astype(np.float32)
        m["rope"] = np.ascontiguousarray(np.stack([cs, sn], axis=0))
        in_maps.append(m)
    return in_maps


def assemble(results):
    out = np.zeros((4, SEQ, D), np.float32)
    for core in range(8):
        b, hf = core // 2, core % 2
        idx = np.arange(OWN) if hf == 0 else (SEQ - 1 - np.arange(OWN))
        out[b, idx, :] = np.asarray(results[core]["outT"]).T
    return out


def kernel(**inputs):
    nc = build_program()
    in_maps = prepare_inputs(inputs)
    res = run_bass_kernel_spmd(nc, in_maps, core_ids=list(range(8)))
    return assemble(res.results)
```

```python
import numpy as np
import concourse.bass as bass
import concourse.mybir as mybir
from concourse.bass_utils import run_bass_kernel_spmd

F32 = mybir.dt.float32
BF16 = mybir.dt.bfloat16
AF = mybir.ActivationFunctionType
ALU = mybir.AluOpType

D = 1024
NCH = 8
SEQ = 4096
OWN = 2048
T = 2200
TK = 2176
CTX = 256
TT = T + CTX
DFF = 2816
NFC = 22
CW = 31
EPS = 1e-6
SLOT = 6144
import os
ATT_PIPE = int(os.environ.get('ATT_PIPE', '1'))
ATT_MASK = os.environ.get('ATT_MASK', 'dve')
SAME_ENG = int(os.environ.get('SAME_ENG', '1'))
ONLY_ATTN = int(os.environ.get('ONLY_ATTN', '0'))
ATT_EXP = int(os.environ.get('ATT_EXP', '0'))
ATT_DIV = int(os.environ.get('ATT_DIV', '0'))
NSLOT = 4

TILES_A = [(i * 440, 440, 0) for i in range(5)] + [(T, 256, 1)]
TILES_OWN = [(i * 512, 512, 0) for i in range(4)]

PRM = {}
_o = 0
for _n, _w in [("cc", 16), ("ada_b", 144), ("norm_g", 48), ("pw1_b", 16), ("dw_w", 248), ("dw_b", 8),
               ("ln_g", 8), ("ln_b", 8), ("pw2_b", 8), ("sink", 8), ("final_g", 8)]:
    PRM[_n] = (_o, _w)
    _o += _w
NPRM = _o
NCST = 2 * 256 + 192 + 128 + 256

_XB = sorted(set([i * 440 for i in range(6)] + [i * 512 for i in range(5)] + [T]))


def _segs(a, b):
    return [i for i in range(len(_XB) - 1) if _XB[i] < b and _XB[i + 1] > a]


class Sched:
    CE = ("pe", "act", "dve", "pool")
    NEPOCH = 16

    def __init__(self, nc, sems):
        self.nc = nc
        self.sems = sems
        self.cnt = {k: 0 for k in sems}
        self.ops = []
        self.byeng = {e: [] for e in ("pe", "act", "dve", "pool", "sp")}
        self.last_w = {}
        self.readers = {}
        self.unsig = {e: [] for e in self.byeng}
        self.redirect = {}
        self.epoch = 0
        self.last_sig = {}

    def _deps(self, reads, writes):
        deps = []
        for b in reads:
            t = self.last_w.get(b)
            if t is not None:
                deps.append((t, True))
        for b in writes:
            t = self.last_w.get(b)
            if t is not None:
                deps.append((t, False))
            for t in self.readers.get(b, ()):
                deps.append((t, False))
        return deps

    def _record(self, tok, reads, writes):
        for b in reads:
            self.readers.setdefault(b, []).append(tok)
        for b in writes:
            self.last_w[b] = tok
            self.readers[b] = []

    def _new(self, eng, fns, deps, kind, sig=True, chan=None):
        rec = dict(id=len(self.ops), eng=eng, epoch=self.epoch, fns=fns, deps=deps, kind=kind, sig=sig, chan=chan)
        self.ops.append(rec)
        self.byeng[eng].append(rec)
        return rec

    def op(self, eng, fn, reads=(), writes=(), sig=True):
        rec = self._new(eng, [fn], self._deps(reads, writes), "op", sig=sig)
        if sig:
            for i in self.unsig[eng]:
                self.redirect[i] = rec["id"]
            self.unsig[eng] = []
            self.last_sig[eng] = rec["id"]
        else:
            self.unsig[eng].append(rec["id"])
        self._record(("op", rec["id"]), reads, writes)

    def dma(self, eng, fns, chan, reads=(), writes=()):
        rec = self._new(eng, list(fns), self._deps(reads, writes), "dma", chan=chan)
        self.cnt[chan] += 16 * len(fns)
        tok = ("dma", chan, self.cnt[chan])
        self._record(tok, reads, writes)
        return tok

    def wait_tokens(self, eng, toks):
        self._new(eng, [], [(t, True) for t in toks], "wait")

    def barrier(self):
        for e in self.CE:
            assert not self.unsig[e], e
        toks = {e: ("op", self.last_sig[e]) for e in self.CE
                if e in self.last_sig and self.ops[self.last_sig[e]]["epoch"] == self.epoch}
        for e in ("pe", "act", "dve", "pool", "sp"):
            self.wait_tokens(e, [t for k, t in toks.items() if k != e])
        self.epoch += 1
        assert self.epoch < self.NEPOCH

    def finalize(self):
        for e in self.CE:
            assert not self.unsig[e], e

        def resolve(tok, rec, raw):
            if tok[0] == "dma":
                return tok
            i = self.redirect.get(tok[1], tok[1])
            d = self.ops[i]
            if d["epoch"] != rec["epoch"]:
                return None
            if d["eng"] == rec["eng"]:
                if rec["eng"] in ("pe", "sp"):
                    return None
                if not SAME_ENG and not raw:
                    return None
            return ("op", i)
        awaited = set()
        for rec in self.ops:
            r = []
            for tok, raw in rec["deps"]:
                t = resolve(tok, rec, raw)
                if t is not None:
                    r.append(t)
                    if t[0] == "op":
                        awaited.add(t[1])
            rec["rdeps"] = r
        val = {}
        cnt = {}
        for e in self.CE:
            for rec in self.byeng[e]:
                if rec["kind"] == "op" and rec["id"] in awaited:
                    k = f"{e}@{rec['epoch']}"
                    cnt[k] = cnt.get(k, 0) + 1
                    val[rec["id"]] = (k, cnt[k])
        self.prog = {}
        self.nsig = len(val)
        for e, recs in self.byeng.items():
            waited = {}
            out = []
            for rec in recs:
                w = {}
                for t in rec["rdeps"]:
                    k, v = val[t[1]] if t[0] == "op" else (t[1], t[2])
                    if w.get(k, 0) < v:
                        w[k] = v
                wl = []
                for k, v in w.items():
                    if waited.get(k, 0) < v:
                        waited[k] = v
                        wl.append((k, v))
                if rec["kind"] == "dma":
                    incs = [(rec["chan"], 16)] * len(rec["fns"])
                elif rec["id"] in val:
                    incs = [(val[rec["id"]][0], 1)]
                else:
                    incs = []
                out.append((wl, rec["fns"], incs))
            self.prog[e] = out

    def emit(self, eng, e):
        for waits, fns, incs in self.prog[eng]:
            for k, v in waits:
                e.wait_ge(self.sems[k], v)
            for i, fn in enumerate(fns):
                ins = fn(e)
                if i < len(incs):
                    ins.then_inc(self.sems[incs[i][0]], incs[i][1])


class Arena:
    def __init__(self, ap_f32, nbytes):
        self.ap = ap_f32
        self.n = nbytes
        self.off = 0

    def mark(self):
        return self.off

    def reset(self, m):
        self.off = m

    def alloc(self, shape, dt):
        es = 2 if dt == BF16 else 4
        free = 1
        for s in shape[1:]:
            free *= s
        nb = (free * es + 31) // 32 * 32
        assert self.off + nb <= self.n, ("arena overflow", self.off, nb, self.n)
        w0 = self.off // 4
        v = self.ap[:, w0:w0 + nb // 4]
        self.off += nb
        if dt == BF16:
            v = v.bitcast(BF16)
        v = v[:, 0:free]
        if len(shape) == 3:
            v = v.rearrange("p (a b) -> p a b", a=shape[1])
        elif len(shape) == 4:
            v = v.rearrange("p (a b c) -> p a b c", a=shape[1], b=shape[2])
        return v


def build_program(stop_after=None):
    nc = bass.Bass("TRN2", target_bir_lowering=False)
    dr = {}

    def din(name, shape):
        dr[name] = nc.dram_tensor(name, shape, F32, kind="ExternalInput").ap()
        return dr[name]

    xT = din("xT", [D, T])
    cT = din("cT", [D, CTX])
    prm_d = din("prm", [128, NPRM])
    cst_d = din("cst", [128, NCST])
    rope_d = din("rope", [2, 128, T])
    ada_w = din("ada_w", [2, D, 9 * D])
    f_wi = [din("ffn1_wi", [2, D, 2 * DFF]), din("ffn2_wi", [2, D, 2 * DFF])]
    f_wo = [din("ffn1_wo", [2, DFF, D]), din("ffn2_wo", [2, DFF, D])]
    pw1_w = din("pw1_w", [D, 2 * D])
    pw2_w = din("pw2_w", [D, D])
    watt = din("w_att", [D, 4 * 768])
    wv_d = din("w_v", [D, 256])
    wo_d = din("w_o", [D, D])
    outT = nc.dram_tensor("outT", [D, OWN], F32, kind="ExternalOutput").ap()

    semkeys = [f"{e}@{i}" for e in Sched.CE for i in range(Sched.NEPOCH)] + ["ldp", "ldc", "ldxc", "st0", "st1", "rp0", "rp1", "ad0", "ad1"] + \
              [f"ldx{i}" for i in range(5)] + [f"w{i}" for i in range(NSLOT)]

    import contextlib
    es = contextlib.ExitStack()
    with es:
        sems = {k: es.enter_context(nc.semaphore(k)) for k in semkeys}
        X = es.enter_context(nc.sbuf_tensor("X", [128, NCH, T], F32))
        XC = es.enter_context(nc.sbuf_tensor("XC", [128, NCH, CTX], F32))
        Wr = es.enter_context(nc.sbuf_tensor("Wr", [128, NSLOT, SLOT], BF16))
        prm = es.enter_context(nc.sbuf_tensor("prm_sb", [128, NPRM], F32))
        cst = es.enter_context(nc.sbuf_tensor("cst_sb", [128, NCST], BF16))
        mod = es.enter_context(nc.sbuf_tensor("mod", [128, 2, 72, 2], F32))
        tabA = es.enter_context(nc.sbuf_tensor("tabA", [128, 3, NCH, 2], F32))
        tabG = es.enter_context(nc.sbuf_tensor("tabG", [128, 3, NCH, 2], F32))
        tabGb = es.enter_context(nc.sbuf_tensor("tabGb", [128, NCH, 2], F32))
        scT = es.enter_context(nc.sbuf_tensor("scT", [128, NCH, 2], BF16))
        onesf = es.enter_context(nc.sbuf_tensor("onesf", [128, 128], F32))
        esink = es.enter_context(nc.sbuf_tensor("esink", [128, NCH], F32))
        epsb = es.enter_context(nc.sbuf_tensor("epsb", [128, 1], F32))
        ARENA_BYTES = 212863 - (NCH * T * 4 + NCH * CTX * 4 + NSLOT * SLOT * 2 + NPRM * 4 + NCST * 2
                                + 2 * 72 * 2 * 4 + 2 * 3 * NCH * 2 * 4 + NCH * 2 * 4 + NCH * 2 * 2
                                + 128 * 4 + NCH * 4) - 384
        ARENA_BYTES = ARENA_BYTES // 64 * 64
        ar_t = es.enter_context(nc.sbuf_tensor("arena", [128, ARENA_BYTES // 4], F32))
        PSALL = es.enter_context(nc.psum_tensor("psall", [128, 4096], F32))
        PS = [PSALL[:, i * 512:(i + 1) * 512] for i in range(8)]
        S = Sched(nc, sems)
        A = Arena(ar_t, ARENA_BYTES)

        def P(name, a=0, b=None):
            o, w = PRM[name]
            b = w if b is None else b
            return prm[:, o + a:o + b]

        def xbuf(which):
            return X if which == 0 else XC

        def xk(which, c, a, n):
            if which == 1:
                return [("XC", c)]
            return [("X", c, s) for s in _segs(a, a + n)]

        def hk(c, a, n):
            if a >= T:
                return [("H", c, "c")]
            return [("H", c, s) for s in _segs(a, a + n)]

        S.dma("sp", [lambda e: e.dma_start(out=prm[:], in_=prm_d[:, :])], "ldp", writes=["prm"])
        S.dma("pool", [lambda e: e.dma_start(out=cst[:], in_=cst_d[:, :])], "ldc", writes=["cst"])
        xT_v = xT.rearrange("(c p) t -> p c t", p=128)
        cT_v = cT.rearrange("(c p) t -> p c t", p=128)
        for i in range(5):
            S.dma("sp", [lambda e, i=i: e.dma_start(out=X[:, :, i * 440:(i + 1) * 440],
                                                   in_=xT_v[:, :, i * 440:(i + 1) * 440])],
                  f"ldx{i}", writes=[k for c in range(NCH) for k in xk(0, c, i * 440, 440)])
        S.dma("sp", [lambda e: e.dma_start(out=XC[:], in_=cT_v)], "ldxc",
              writes=[("XC", c) for c in range(NCH)])
        S.op("dve", lambda e: e.memset(onesf[:], 1.0 / D), writes=["onesf"])
        S.op("dve", lambda e: e.memset(epsb[:], EPS), writes=["epsb"])
        S.op("act", lambda e: e.activation(out=scT[:], in_=P("cc").rearrange("p (k w) -> p k w", w=2),
                                           func=AF.Silu), reads=["prm"], writes=["scT"])
        S.op("act", lambda e: e.activation(out=esink[:], in_=P("sink"), func=AF.Exp),
             reads=["prm"], writes=["esink"])

        ring = {"next": 0}

        def slot_alloc():
            s = ring["next"]
            ring["next"] = (s + 1) % NSLOT
            return s

        def load_slot(s, fns):
            S.dma("pool", fns, f"w{s}", writes=[("W", s)])

        def ada_unit(l, u, stg, sb):
            aw = ada_w[l].rearrange("(k p) n -> p k n", p=128)
            S.dma("pool", [lambda e, u=u, stg=stg: e.dma_start(out=stg, in_=aw[:, :, u * 256:(u + 1) * 256])],
                  f"ad{sb}", writes=[("adst", sb)])

            def mm():
                pb = 6 + (u % 2)
                ps = PS[pb][:, 504:508].rearrange("p (a b) -> p a b", b=2)
                for oc in range(2):
                    for k in range(NCH):
                        S.op("pe", lambda e, oc=oc, k=k, ps=ps, stg=stg: e.matmul(
                            ps[:, oc, :], lhsT=stg[:, k, oc * 128:(oc + 1) * 128], rhs=scT[:, k, :],
                            start=(k == 0), stop=(k == NCH - 1)),
                            reads=[("adst", sb), "scT"], writes=[("ps", pb)], sig=(k == NCH - 1))
                ab = P("ada_b", l * 72 + u * 2, l * 72 + u * 2 + 2)
                S.op("dve", lambda e, ps=ps, ab=ab, u=u: e.tensor_tensor(
                    out=mod[:, l, u * 2:(u + 1) * 2, :], in0=ps,
                    in1=ab.unsqueeze(2).to_broadcast([128, 2, 2]), op=ALU.add),
                    reads=[("ps", pb), "prm"], writes=[("mod", l, u // 12)])
            return mm

        def ada_tables(l, j):
            g = P("norm_g", (l * 3 + j) * 8, (l * 3 + j) * 8 + 8)
            S.op("dve", lambda e, j=j, g=g: e.scalar_tensor_tensor(
                out=tabA[:, j], in0=mod[:, l, (3 * j + 1) * 8:(3 * j + 2) * 8, :], scalar=1.0,
                in1=g.unsqueeze(2).to_broadcast([128, NCH, 2]), op0=ALU.add, op1=ALU.mult),
                reads=[("mod", l, j), "prm"], writes=[("tabA", j)])
            S.op("dve", lambda e, j=j: e.tensor_scalar(
                out=tabG[:, j], in0=mod[:, l, (3 * j + 2) * 8:(3 * j + 3) * 8, :],
                scalar1=(1.0 if j == 1 else 0.5), scalar2=None, op0=ALU.mult),
                reads=[("mod", l, j)], writes=[("tabG", j)])
            if l == 0 and j == 1:
                S.op("dve", lambda e: e.tensor_tensor(
                    out=tabGb[:], in0=tabG[:, 1], in1=P("pw2_b").unsqueeze(2).to_broadcast([128, NCH, 2]),
                    op=ALU.mult), reads=[("tabG", 1), "prm"], writes=["tabGb"])

        def ada_run(l, units, adst):
            pend = []
            for i, u in enumerate(units):
                pend.append(ada_unit(l, u, adst[i % 2], i % 2))
                if len(pend) == 2:
                    pend.pop(0)()
            for f_ in pend:
                f_()

        def tabB(l, j):
            return mod[:, l, (3 * j) * 8:(3 * j + 1) * 8, :]

        def make_prepass(l, j, tiles, H, nbuf=3):
            sq = [A.alloc([128, 512], F32) for _ in range(nbuf)]
            rstd = [A.alloc([128, 512], F32) for _ in range(2)]
            rsq = [A.alloc([128, 512], F32) for _ in range(2)]
            tmp = [A.alloc([128, 512], F32) for _ in range(nbuf)]
            st_ = {"sq": 0, "tmp": 0}

            def pre_tile(ti):
                a, n, w = tiles[ti]
                xb = xbuf(w)
                xa = a - T if w == 1 else a
                msb = 6 + (ti % 2)
                ms = PS[msb][:, 0:n]
                for c in range(NCH):
                    q = st_["sq"] % nbuf
                    st_["sq"] += 1
                    S.op("act", lambda e, q=q, c=c, xb=xb, xa=xa, n=n: e.activation(
                        out=sq[q][:, 0:n], in_=xb[:, c, xa:xa + n], func=AF.Square),
                        reads=xk(w, c, xa, n), writes=[("sq", q)])
                    S.op("pe", lambda e, q=q, c=c, ms=ms, n=n: e.matmul(
                        ms, lhsT=onesf[:], rhs=sq[q][:, 0:n], start=(c == 0), stop=(c == NCH - 1)),
                        reads=[("sq", q), "onesf"], writes=[("ps", msb)], sig=True)
                r = ti % 2
                S.op("act", lambda e, r=r, ms=ms, n=n: e.activation(
                    out=rsq[r][:, 0:n], in_=ms, func=AF.Sqrt, bias=epsb[:, 0:1], scale=1.0),
                    reads=[("ps", msb), "epsb"], writes=[("rsq", r)])
                S.op("dve", lambda e, r=r, n=n: e.reciprocal(out=rstd[r][:, 0:n], in_=rsq[r][:, 0:n]),
                     reads=[("rsq", r)], writes=[("rstd", r)])
                for c in range(NCH):
                    q = st_["tmp"] % nbuf
                    st_["tmp"] += 1
                    S.op("dve", lambda e, q=q, c=c, xb=xb, xa=xa, n=n, r=r, w=w: e.scalar_tensor_tensor(
                        out=tmp[q][:, 0:n], in0=xb[:, c, xa:xa + n], scalar=tabA[:, j, c, w:w + 1],
                        in1=rstd[r][:, 0:n], op0=ALU.mult, op1=ALU.mult),
                        reads=xk(w, c, xa, n) + [("rstd", r), ("tabA", j)], writes=[("ptmp", q)])
                    S.op("act", lambda e, q=q, c=c, a=a, n=n, w=w: e.activation(
                        out=H[:, c, a:a + n], in_=tmp[q][:, 0:n], func=AF.Identity,
                        bias=tabB(l, j)[:, c, w:w + 1], scale=1.0),
                        reads=[("ptmp", q), ("mod", l, j)], writes=hk(c, a, n))
            return pre_tile

        def prepass(l, j, tiles, H):
            m = A.mark()
            pt = make_prepass(l, j, tiles, H)
            for ti in range(len(tiles)):
                pt(ti)
            S.barrier()
            A.reset(m)

        def ffn(l, f, tiles, H, ada_job=None):
            j = 0 if f == 0 else 2
            m = A.mark()
            pre_tile = make_prepass(l, j, tiles, H, nbuf=2)
            adst = [A.alloc([128, NCH, 256], BF16) for _ in range(2)]
            act = [A.alloc([128, 4, 512], BF16) for _ in range(2)]
            sg = [A.alloc([128, 512], F32) for _ in range(2)]
            wi = f_wi[f][l].rearrange("(k p) (two n) -> p k two n", p=128, two=2)
            wo = f_wo[f][l].rearrange("(j p) n -> p j n", p=128)
            sweeps = [[0, 1], [2, 3], [4, 5], [6, 7], [8, 9], [10]]
            unit_slot = {}

            def load_unit(u):
                s = slot_alloc()
                unit_slot[u] = s
                wiv = Wr[:, s, 0:4096].rearrange("p (k two n) -> p k two n", k=NCH, two=2)
                wov = Wr[:, s, 4096:6144].rearrange("p (j n) -> p j n", j=2)
                load_slot(s, [lambda e, wiv=wiv, u=u: e.dma_start(out=wiv[:, :, 0, :], in_=wi[:, :, 0, u * 256:(u + 1) * 256]),
                              lambda e, wiv=wiv, u=u: e.dma_start(out=wiv[:, :, 1, :], in_=wi[:, :, 1, u * 256:(u + 1) * 256]),
                              lambda e, wov=wov, u=u: e.dma_start(out=wov, in_=wo[:, 2 * u:2 * u + 2, :])])

            tasks = []
            for si, sw in enumerate(sweeps):
                for ti in range(len(tiles)):
                    tasks.append((si, ti))
            st = {"sg": 0, "gu": 0, "y": 0}

            def stage_a(k):
                si, ti = tasks[k]
                a, n, w = tiles[ti]
                ab = k % 2
                chunks = [(u, jj) for u in sweeps[si] for jj in range(2)]
                for ci, (u, jj) in enumerate(chunks):
                    s = unit_slot[u]
                    wiv = Wr[:, s, 0:4096].rearrange("p (k two n) -> p k two n", k=NCH, two=2)
                    gb = st["gu"] % 2
                    st["gu"] += 1
                    gps = PS[gb * 2][:, 0:n]
                    ups = PS[gb * 2 + 1][:, 0:n]
                    for two, pp, pb in ((0, gps, gb * 2), (1, ups, gb * 2 + 1)):
                        for kk in range(NCH):
                            S.op("pe", lambda e, pp=pp, wiv=wiv, kk=kk, two=two, jj=jj, a=a, n=n: e.matmul(
                                pp, lhsT=wiv[:, kk, two, jj * 128:(jj + 1) * 128], rhs=H[:, kk, a:a + n],
                                start=(kk == 0), stop=(kk == NCH - 1)),
                                reads=[("W", s)] + hk(kk, a, n), writes=[("ps", pb)], sig=(kk == NCH - 1))
                    q = st["sg"] % 2
                    st["sg"] += 1
                    S.op("act", lambda e, q=q, gps=gps, n=n: e.activation(out=sg[q][:, 0:n], in_=gps, func=AF.Silu),
                         reads=[("ps", gb * 2)], writes=[("sg", q)])
                    S.op("dve", lambda e, q=q, ups=ups, n=n, ab=ab, ci=ci: e.tensor_tensor(
                        out=act[ab][:, ci, 0:n], in0=sg[q][:, 0:n], in1=ups, op=ALU.mult),
                        reads=[("sg", q), ("ps", gb * 2 + 1)], writes=[("act", ab, ci)])

            def stage_b(k):
                si, ti = tasks[k]
                a, n, w = tiles[ti]
                xb = xbuf(w)
                xa = a - T if w == 1 else a
                ab = k % 2
                chunks = [(u, jj) for u in sweeps[si] for jj in range(2)]
                for d in range(NCH):
                    yb = 4 + st["y"] % 2
                    st["y"] += 1
                    yps = PS[yb][:, 0:n]
                    for ci, (u, jj) in enumerate(chunks):
                        s = unit_slot[u]
                        wov = Wr[:, s, 4096:6144].rearrange("p (j n) -> p j n", j=2)
                        S.op("pe", lambda e, yps=yps, wov=wov, jj=jj, d=d, ab=ab, ci=ci, n=n: e.matmul(
                            yps, lhsT=wov[:, jj, d * 128:(d + 1) * 128], rhs=act[ab][:, ci, 0:n],
                            start=(ci == 0), stop=(ci == len(chunks) - 1)),
                            reads=[("W", s), ("act", ab, ci)], writes=[("ps", yb)], sig=(ci == len(chunks) - 1))
                    S.op("dve", lambda e, yps=yps, d=d, xb=xb, xa=xa, n=n, w=w: e.scalar_tensor_tensor(
                        out=xb[:, d, xa:xa + n], in0=yps, scalar=tabG[:, j, d, w:w + 1], in1=xb[:, d, xa:xa + n],
                        op0=ALU.mult, op1=ALU.add),
                        reads=[("ps", yb), ("tabG", j)] + xk(w, d, xa, n), writes=xk(w, d, xa, n))

            loaded = 0

            def ensure_loaded(si):
                nonlocal loaded
                while loaded <= min(si, len(sweeps) - 1):
                    for u in sweeps[loaded]:
                        load_unit(u)
                    loaded += 1
            ensure_loaded(1)
            nt = len(tiles)
            pre_tile(0)
            if nt > 1:
                pre_tile(1)
            ada_l, ada_units = ada_job if ada_job else (0, [])
            ada_pend = []
            ada_i = 0
            for k in range(len(tasks) + 1):
                if k < len(tasks):
                    si, ti = tasks[k]
                    if ti == 1:
                        ensure_loaded(si + 1)
                    if si == 0 and ti + 2 < nt:
                        pre_tile(ti + 2)
                    if ada_i < len(ada_units):
                        ada_pend.append(ada_unit(ada_l, ada_units[ada_i], adst[ada_i % 2], ada_i % 2))
                        ada_i += 1
                        if len(ada_pend) == 2:
                            ada_pend.pop(0)()
                    stage_a(k)
                if k >= 1:
                    stage_b(k - 1)
            for f_ in ada_pend:
                f_()
            assert ada_i == len(ada_units)
            S.barrier()
            A.reset(m)

        def conv_phase(l, H):
            prepass(l, 1, TILES_A, H)
            m = A.mark()
            p1 = pw1_w.rearrange("(k p) n -> p k n", p=128)
            p2 = pw2_w.rearrange("(k p) n -> p k n", p=128)
            slots = []
            for i in range(4):
                s = slot_alloc()
                slots.append(s)
                v1 = Wr[:, s, 0:4096].rearrange("p (k two n) -> p k two n", k=NCH, two=2)
                v2 = Wr[:, s, 4096:6144].rearrange("p (k n) -> p k n", k=NCH)
                load_slot(s, [lambda e, v1=v1, i=i: e.dma_start(out=v1[:, :, 0, :], in_=p1[:, :, i * 256:(i + 1) * 256]),
                              lambda e, v1=v1, i=i: e.dma_start(out=v1[:, :, 1, :], in_=p1[:, :, D + i * 256:D + (i + 1) * 256]),
                              lambda e, v2=v2, i=i: e.dma_start(out=v2, in_=p2[:, :, i * 256:(i + 1) * 256])])
            UW = 472
            NPE = 16
            U = [A.alloc([128, UW], F32) for _ in range(2)]
            Ub = [A.alloc([128, UW], BF16) for _ in range(2)]
            Dg = A.alloc([128, 2, NPE, 128], BF16)
            V = A.alloc([128, NCH, 440], F32)
            sqv = [A.alloc([128, 440], F32) for _ in range(1)]
            varb = A.alloc([128, 440], F32)
            HN = A.alloc([128, NCH, 440], BF16)
            ident = cst[:, 704:832]
            dww = P("dw_w").rearrange("p (c k) -> p c k", k=CW)
            K1 = CW - NPE
            pend_d = []
            for ti, (a, n, w) in enumerate(TILES_A):
                xb = xbuf(w)
                xa = a - T if w == 1 else a
                s0, s1 = (0, T) if w == 0 else (T, T + CTX)
                lo, hi = max(a - 15, s0), min(a + n + 15, s1)
                ulo, uhi = lo - (a - 15), hi - (a - 15)
                nu = hi - lo
                for jp in range(4):
                    for jj in range(2):
                        j = 2 * jp + jj
                        s = slots[jp]
                        v1 = Wr[:, s, 0:4096].rearrange("p (k two n) -> p k two n", k=NCH, two=2)
                        for two in range(2):
                            pb = 2 * jj + two
                            pp = PS[pb][:, 0:nu]
                            for kk in range(NCH):
                                S.op("pe", lambda e, pp=pp, v1=v1, kk=kk, two=two, jj=jj, lo=lo, hi=hi: e.matmul(
                                    pp, lhsT=v1[:, kk, two, jj * 128:(jj + 1) * 128], rhs=H[:, kk, lo:hi],
                                    start=(kk == 0), stop=(kk == NCH - 1)),
                                    reads=[("W", s)] + hk(kk, lo, nu), writes=[("ps", pb)], sig=(kk == NCH - 1))
                        if ulo > 0:
                            S.op("dve", lambda e, jj=jj, ulo=ulo: e.memset(U[jj][:, 0:ulo], 0.0), writes=[("U", jj)])
                        if uhi < n + 30:
                            S.op("dve", lambda e, jj=jj, uhi=uhi, n=n: e.memset(U[jj][:, uhi:n + 30], 0.0), writes=[("U", jj)])
                        S.op("act", lambda e, jj=jj, j=j, nu=nu, ulo=ulo, uhi=uhi: e.activation(
                            out=U[jj][:, ulo:uhi], in_=PS[2 * jj + 1][:, 0:nu], func=AF.Sigmoid,
                            bias=P("pw1_b", 8 + j, 9 + j), scale=1.0),
                            reads=[("ps", 2 * jj + 1), "prm"], writes=[("U", jj)])
                        S.op("dve", lambda e, jj=jj, j=j, nu=nu, ulo=ulo, uhi=uhi: e.scalar_tensor_tensor(
                            out=U[jj][:, ulo:uhi], in0=PS[2 * jj][:, 0:nu], scalar=P("pw1_b", j, j + 1),
                            in1=U[jj][:, ulo:uhi], op0=ALU.add, op1=ALU.mult),
                            reads=[("ps", 2 * jj), "prm", ("U", jj)], writes=[("U", jj)])
                    for k in range(K1):
                        for jj in range(2):
                            j = 2 * jp + jj
                            if k == 0:
                                S.op("dve", lambda e, jj=jj, j=j, n=n: e.tensor_scalar(
                                    out=V[:, j, 0:n], in0=U[jj][:, 0:n], scalar1=dww[:, j, 0:1],
                                    scalar2=P("dw_b", j, j + 1), op0=ALU.mult, op1=ALU.add),
                                    reads=[("U", jj), "prm"], writes=[("V", j)])
                            else:
                                S.op("dve", lambda e, jj=jj, j=j, n=n, k=k: e.scalar_tensor_tensor(
                                    out=V[:, j, 0:n], in0=U[jj][:, k:k + n], scalar=dww[:, j, k:k + 1],
                                    in1=V[:, j, 0:n], op0=ALU.mult, op1=ALU.add),
                                    reads=[("U", jj), ("V", j), "prm"], writes=[("V", j)])
                    for jj in range(2):
                        j = 2 * jp + jj
                        S.op("act", lambda e, jj=jj, n=n: e.activation(
                            out=Ub[jj][:, 0:n + 30], in_=U[jj][:, 0:n + 30], func=AF.Identity),
                            reads=[("U", jj)], writes=[("Ub", jj)])
                        for kidx in range(NPE):
                            k = K1 + kidx
                            S.op("act", lambda e, jj=jj, j=j, k=k, kidx=kidx: e.activation(
                                out=Dg[:, jj, kidx, :], in_=ident, func=AF.Identity, scale=dww[:, j, k:k + 1]),
                                reads=["cst", "prm"], writes=[("Dg", jj, kidx)])
                    for jj in range(2):
                        j = 2 * jp + jj
                        for kidx in range(NPE):
                            k = K1 + kidx
                            S.op("pe", lambda e, jj=jj, k=k, kidx=kidx, n=n: e.matmul(
                                PS[4 + jj][:, 0:n], lhsT=Dg[:, jj, kidx, :], rhs=Ub[jj][:, k:k + n],
                                start=(kidx == 0), stop=(kidx == NPE - 1)),
                                reads=[("Dg", jj, kidx), ("Ub", jj)], writes=[("ps", 4 + jj)], sig=(kidx == NPE - 1))
                    for jj in range(2):
                        j = 2 * jp + jj
                        S.op("dve", lambda e, jj=jj, j=j, n=n: e.tensor_tensor(
                            out=V[:, j, 0:n], in0=V[:, j, 0:n], in1=PS[4 + jj][:, 0:n], op=ALU.add),
                            reads=[("V", j), ("ps", 4 + jj)], writes=[("V", j)])
                while pend_d:
                    pend_d.pop(0)()
                for j in range(NCH):
                    S.op("pe", lambda e, j=j, n=n: e.matmul(PS[4][:, 0:n], lhsT=onesf[:], rhs=V[:, j, 0:n],
                                                           start=(j == 0), stop=(j == NCH - 1)),
                         reads=[("V", j), "onesf"], writes=[("ps", 4)], sig=True)
                    S.op("act", lambda e, j=j, n=n: e.activation(out=sqv[0][:, 0:n], in_=V[:, j, 0:n], func=AF.Square),
                         reads=[("V", j)], writes=[("sqv", 0)])
                    S.op("pe", lambda e, j=j, n=n: e.matmul(PS[5][:, 0:n], lhsT=onesf[:], rhs=sqv[0][:, 0:n],
                                                           start=(j == 0), stop=(j == NCH - 1)),
                         reads=[("sqv", 0), "onesf"], writes=[("ps", 5)], sig=True)
                S.op("act", lambda e, n=n: e.activation(out=varb[:, 0:n], in_=PS[4][:, 0:n], func=AF.Square),
                     reads=[("ps", 4)], writes=["varb"])
                S.op("dve", lambda e, n=n: e.tensor_tensor(out=varb[:, 0:n], in0=PS[5][:, 0:n], in1=varb[:, 0:n],
                                                           op=ALU.subtract), reads=[("ps", 5), "varb"], writes=["varb"])
                S.op("act", lambda e, n=n: e.activation(out=varb[:, 0:n], in_=varb[:, 0:n], func=AF.Sqrt,
                                                        bias=epsb[:, 0:1], scale=1.0),
                     reads=["varb", "epsb"], writes=["varb"])
                S.op("dve", lambda e, n=n: e.reciprocal(out=varb[:, 0:n], in_=varb[:, 0:n]),
                     reads=["varb"], writes=["varb"])
                for j in range(NCH):
                    S.op("dve", lambda e, j=j, n=n: e.tensor_tensor(
                        out=V[:, j, 0:n], in0=V[:, j, 0:n], in1=PS[4][:, 0:n], op=ALU.subtract),
                        reads=[("V", j), ("ps", 4)], writes=[("V", j)])
                    S.op("dve", lambda e, j=j, n=n: e.tensor_tensor(
                        out=V[:, j, 0:n], in0=V[:, j, 0:n], in1=varb[:, 0:n], op=ALU.mult),
                        reads=[("V", j), "varb"], writes=[("V", j)])
                    S.op("act", lambda e, j=j, n=n: e.activation(
                        out=HN[:, j, 0:n], in_=V[:, j, 0:n], func=AF.Silu,
                        bias=P("ln_b", j, j + 1), scale=P("ln_g", j, j + 1)),
                        reads=[("V", j), "prm"], writes=[("HN", j)])
                def stage_d(a=a, n=n, w=w, xb=xb, xa=xa):
                    for d in range(NCH):
                        s = slots[d // 2]
                        v2 = Wr[:, s, 4096:6144].rearrange("p (k n) -> p k n", k=NCH)
                        yb = 6 + d % 2
                        for kk in range(NCH):
                            S.op("pe", lambda e, v2=v2, kk=kk, d=d, yb=yb, n=n: e.matmul(
                                PS[yb][:, 0:n], lhsT=v2[:, kk, (d % 2) * 128:(d % 2 + 1) * 128], rhs=HN[:, kk, 0:n],
                                start=(kk == 0), stop=(kk == NCH - 1)),
                                reads=[("W", s), ("HN", kk)], writes=[("ps", yb)], sig=(kk == NCH - 1))
                        S.op("act", lambda e, d=d, n=n, w=w, yb=yb: e.activation(
                            out=sqv[0][:, 0:n], in_=PS[yb][:, 0:n], func=AF.Identity,
                            bias=tabGb[:, d, w:w + 1], scale=tabG[:, 1, d, w:w + 1]),
                            reads=[("ps", yb), ("tabG", 1), "tabGb"], writes=[("sqv", 0)])
                        S.op("dve", lambda e, d=d, xb=xb, xa=xa, n=n: e.tensor_tensor(
                            out=xb[:, d, xa:xa + n], in0=xb[:, d, xa:xa + n], in1=sqv[0][:, 0:n], op=ALU.add),
                            reads=[("sqv", 0)] + xk(w, d, xa, n), writes=xk(w, d, xa, n))
                pend_d.append(stage_d)
            while pend_d:
                pend_d.pop(0)()
            S.barrier()
            A.reset(m)

        def attn_phase(l, H):
            prepass(l, 1, TILES_A, H)
            m = A.mark()
            Qg = A.alloc([128, 2, OWN], BF16)
            KgA = A.alloc([128, TK + CTX], BF16)
            KgB = A.alloc([128, TK + CTX], BF16)
            Vpad = A.alloc([128, 19, 192], BF16)
            Og = [A.alloc([128, 2, 512], BF16) for _ in range(1)]
            PT = [A.alloc([128, 5, 2, 128] if ATT_PIPE else [128, 2, 5, 128], BF16) for _ in range(2)]
            cosb = A.alloc([128, 440], F32)
            sinb = A.alloc([128, 440], F32)
            t1 = A.alloc([128, 440], F32)
            rden = [A.alloc([128, 128], F32) for _ in range(2)]
            mask_lo = cst[:, 0:256].rearrange("p (h q) -> p h q", h=2)
            mask_hi = cst[:, 256:512].rearrange("p (h q) -> p h q", h=2)
            onespad = cst[:, 512:704]
            wa = watt.rearrange("(k p) n -> p k n", p=128)
            wvv = wv_d.rearrange("(k p) n -> p k n", p=128)
            wov = wo_d.rearrange("(j p) n -> p j n", p=128)
            S.op("pool", lambda e: e.memset(Vpad[:], 0.0), writes=["Vpad"])
            S.op("dve", lambda e: e.memset(KgA[64:128, :], 0.0), writes=[("Kg",)])
            S.op("dve", lambda e: e.memset(KgB[0:64, :], 0.0), writes=[("Kg",)])
            cnt = {"st": 0, "od": 0, "og": 0}
            for g in range(4):
                sA = slot_alloc()
                vA = Wr[:, sA, :].rearrange("p (k n) -> p k n", k=NCH)
                load_slot(sA, [lambda e, vA=vA, g=g: e.dma_start(out=vA, in_=wa[:, :, g * 768:(g + 1) * 768])])
                sB = slot_alloc()
                vBo = Wr[:, sB, 0:2048].rearrange("p (j n) -> p j n", j=2)
                vBv = Wr[:, sB, 2048:2560].rearrange("p (k n) -> p k n", k=NCH)
                load_slot(sB, [lambda e, vBo=vBo, g=g: e.dma_start(out=vBo, in_=wov[:, 2 * g:2 * g + 2, :]),
                               lambda e, vBv=vBv, g=g: e.dma_start(out=vBv, in_=wvv[:, :, 64 * g:64 * g + 64])])
                for ti in range(5):
                    a, n = 440 * ti, 440
                    S.dma("sp", [lambda e, a=a, n=n: e.dma_start(out=cosb[:, 0:n], in_=rope_d[0][:, a:a + n]),
                                 lambda e, a=a, n=n: e.dma_start(out=sinb[:, 0:n], in_=rope_d[1][:, a:a + n])],
                          "rp0", writes=["rope"])
                    items = [("q", 0), ("q", 1), ("k", 0)]
                    for ii, (kind, cc) in enumerate(items):
                        if kind == "q":
                            nn = max(0, min(a + n, OWN) - a)
                            base, bsw = cc * 128, 256 + cc * 128
                            dkey = ("Qg", cc)
                        else:
                            nn = max(0, min(a + n, TK) - a)
                            base, bsw = 512, 640
                            dkey = ("Kg",)
                        if nn == 0:
                            continue
                        pa, pb = 2 * (ii % 2), 2 * (ii % 2) + 1
                        for (pp, bb) in ((pa, base), (pb, bsw)):
                            for kk in range(NCH):
                                S.op("pe", lambda e, pp=pp, bb=bb, kk=kk, a=a, nn=nn, vA=vA: e.matmul(
                                    PS[pp][:, 0:nn], lhsT=vA[:, kk, bb:bb + 128], rhs=H[:, kk, a:a + nn],
                                    start=(kk == 0), stop=(kk == NCH - 1)),
                                    reads=[("W", sA)] + hk(kk, a, nn), writes=[("ps", pp)], sig=(kk == NCH - 1))
                        S.op("dve", lambda e, pa=pa, nn=nn: e.tensor_tensor(
                            out=PS[pa][:, 0:nn], in0=PS[pa][:, 0:nn], in1=cosb[:, 0:nn], op=ALU.mult),
                            reads=[("ps", pa), "rope"], writes=[("ps", pa)])
                        S.op("dve", lambda e, pb=pb, nn=nn: e.tensor_tensor(
                            out=t1[:, 0:nn], in0=PS[pb][:, 0:nn], in1=sinb[:, 0:nn], op=ALU.mult),
                            reads=[("ps", pb), "rope"], writes=["t1"])
                        if kind == "q":
                            S.op("dve", lambda e, pa=pa, nn=nn, cc=cc, a=a: e.tensor_tensor(
                                out=Qg[:, cc, a:a + nn], in0=PS[pa][:, 0:nn], in1=t1[:, 0:nn], op=ALU.add),
                                reads=[("ps", pa), "t1"], writes=[dkey])
                        else:
                            S.op("dve", lambda e, pa=pa, nn=nn, a=a: e.tensor_tensor(
                                out=KgA[0:64, a:a + nn], in0=PS[pa][0:64, 0:nn], in1=t1[0:64, 0:nn], op=ALU.add),
                                reads=[("ps", pa), "t1"], writes=[dkey])
                            S.op("dve", lambda e, pa=pa, nn=nn, a=a: e.tensor_tensor(
                                out=KgB[64:128, a:a + nn], in0=PS[pa][64:128, 0:nn], in1=t1[64:128, 0:nn], op=ALU.add),
                                reads=[("ps", pa), "t1"], writes=[dkey])
                for kk in range(NCH):
                    S.op("pe", lambda e, kk=kk, vA=vA: e.matmul(
                        PS[0][:, 0:CTX], lhsT=vA[:, kk, 512:640], rhs=H[:, kk, T:T + CTX],
                        start=(kk == 0), stop=(kk == NCH - 1)),
                        reads=[("W", sA)] + hk(kk, T, CTX), writes=[("ps", 0)], sig=(kk == NCH - 1))
                S.op("act", lambda e: e.activation(out=KgA[0:64, TK:TK + CTX], in_=PS[0][0:64, 0:CTX], func=AF.Identity),
                     reads=[("ps", 0)], writes=[("Kg",)])
                S.op("act", lambda e: e.activation(out=KgB[64:128, TK:TK + CTX], in_=PS[0][64:128, 0:CTX], func=AF.Identity),
                     reads=[("ps", 0)], writes=[("Kg",)])
                for b0 in range(0, 19, 8):
                    nb = min(8, 19 - b0)
                    vb = 6 + (b0 // 8) % 2
                    psv = PS[vb].rearrange("p (b d) -> p b d", d=64)
                    for bi in range(nb):
                        blk = b0 + bi
                        c0 = 128 * blk if blk < 17 else T + 128 * (blk - 17)
                        for kk in range(NCH):
                            S.op("pe", lambda e, psv=psv, bi=bi, kk=kk, c0=c0, vBv=vBv: e.matmul(
                                psv[:, bi, :], lhsT=H[:, kk, c0:c0 + 128], rhs=vBv[:, kk, :],
                                start=(kk == 0), stop=(kk == NCH - 1)),
                                reads=[("W", sB)] + hk(kk, c0, 128), writes=[("ps", vb)],
                                sig=(kk == NCH - 1 and bi == nb - 1))
                    S.op("act", lambda e, psv=psv, b0=b0, nb=nb: e.activation(
                        out=Vpad[:, b0:b0 + nb, 64:128], in_=psv[:, 0:nb, :], func=AF.Identity),
                        reads=[("ps", vb)], writes=["Vpad"])
                its = [(qt, qb, cc) for qt in range(4) for qb in range(4) for cc in range(2)]
                info = {}

                def stage_s(it):
                    qt, qb, cc = its[it]
                    i = 4 * qt + qb
                    sb = cnt["st"] % 2
                    cnt["st"] += 1
                    KBM = ATT_PIPE
                    if KBM:
                        stv = PSALL[:, sb * 1536:sb * 1536 + 1280].rearrange("p (k h q) -> p k h q", k=5, h=2)
                    else:
                        stv = PSALL[:, sb * 1536:sb * 1536 + 1280].rearrange("p (h k q) -> p h k q", h=2, k=5)
                    stkeys = [("ps", sb * 3 + z) for z in range(3)]
                    k0 = 1 if i == 0 else 0
                    srcs = []
                    for kbi in range(k0, 5):
                        if kbi < 3:
                            blk = i - 1 + kbi
                            srcs.append((kbi, 128 * blk, blk))
                        else:
                            srcs.append((kbi, TK + 128 * (kbi - 3), 17 + kbi - 3))
                    nmm = 2 * len(srcs)
                    PEM = (ATT_MASK == "pe")
                    z = 0
                    for h in (1, 0):
                        for (kbi, kc, vbk) in srcs:
                            z += 1
                            msk = PEM and kbi in (0, 2)
                            S.op("pe", lambda e, stv=stv, h=h, kbi=kbi, kc=kc, cc=cc, i=i, KBM=KBM, msk=msk: e.matmul(
                                stv[:, kbi, h, :] if KBM else stv[:, h, kbi, :],
                                lhsT=(KgA if h == 0 else KgB)[:, kc:kc + 128],
                                rhs=Qg[:, cc, 128 * i:128 * i + 128], start=True, stop=(not msk)),
                                reads=[("Kg",), ("Qg", cc)], writes=stkeys, sig=(z == nmm and not msk))
                            if msk:
                                mb = cst[:, 832:960] if kbi == 0 else cst[:, 960:1088]
                                S.op("pe", lambda e, stv=stv, h=h, kbi=kbi, KBM=KBM, mb=mb: e.matmul(
                                    stv[:, kbi, h, :] if KBM else stv[:, h, kbi, :],
                                    lhsT=cst[:, 704:832], rhs=mb, start=False, stop=True),
                                    reads=["cst"], writes=stkeys, sig=(z == nmm))
                    pb = sb
                    if KBM:
                        for (ka, kb_) in ((max(k0, 0), 2), (2, 4), (4, 5)):
                            if ka >= kb_:
                                continue
                            S.op("act", lambda e, stv=stv, pb=pb, ka=ka, kb_=kb_: e.activation(
                                out=PT[pb][:, ka:kb_, :, :], in_=stv[:, ka:kb_, :, :], func=AF.Exp, scale=0.125),
                                reads=stkeys, writes=[("PT", pb)])
                        if i > 0 and not PEM:
                            S.op(ATT_MASK, lambda e, pb=pb: e.tensor_tensor(
                                out=PT[pb][:, 0, :, :], in0=PT[pb][:, 0, :, :], in1=mask_lo, op=ALU.mult),
                                reads=[("PT", pb), "cst"], writes=[("PT", pb)])
                        if not PEM:
                            S.op(ATT_MASK, lambda e, pb=pb: e.tensor_tensor(
                                out=PT[pb][:, 2, :, :], in0=PT[pb][:, 2, :, :], in1=mask_hi, op=ALU.mult),
                                reads=[("PT", pb), "cst"], writes=[("PT", pb)])
                    else:
                        S.op("act", lambda e, stv=stv, pb=pb, k0=k0: e.activation(
                            out=PT[pb][:, :, k0:5, :], in_=stv[:, :, k0:5, :], func=AF.Exp, scale=0.125),
                            reads=stkeys, writes=[("PT", pb)])
                        if i > 0 and not PEM:
                            S.op(ATT_MASK, lambda e, pb=pb: e.tensor_tensor(
                                out=PT[pb][:, :, 0, :], in0=PT[pb][:, :, 0, :], in1=mask_lo, op=ALU.mult),
                                reads=[("PT", pb), "cst"], writes=[("PT", pb)])
                        if not PEM:
                            S.op(ATT_MASK, lambda e, pb=pb: e.tensor_tensor(
                                out=PT[pb][:, :, 2, :], in0=PT[pb][:, :, 2, :], in1=mask_hi, op=ALU.mult),
                                reads=[("PT", pb), "cst"], writes=[("PT", pb)])
                    info[it] = (pb, srcs, nmm)

                def stage_p(it):
                    qt, qb, cc = its[it]
                    pb, srcs, nmm = info.pop(it)
                    ob = 0
                    odb = cnt["od"] % 2
                    cnt["od"] += 1
                    KBM = ATT_PIPE
                    if KBM:
                        o_ps = PS[6 + odb][:, 0:128]
                        d_ps = PS[6 + odb][:, 128:256]
                        odk = [("ps", 6 + odb)]
                    else:
                        o_ps = PS[6][:, (2 * odb) * 128:(2 * odb + 1) * 128]
                        d_ps = PS[6][:, (2 * odb + 1) * 128:(2 * odb + 2) * 128]
                        odk = []
                    for (tgt, is_den) in ((o_ps, False), (d_ps, True)):
                        z = 0
                        for h in range(2):
                            c_lo = 64 if h == 0 else 0
                            for (kbi, kc, vbk) in srcs:
                                z += 1
                                if is_den:
                                    lh = onespad[:, c_lo:c_lo + 128]
                                else:
                                    lh = Vpad[:, vbk, c_lo:c_lo + 128]
                                S.op("pe", lambda e, tgt=tgt, lh=lh, pb=pb, h=h, kbi=kbi, z=z, nmm=nmm, KBM=KBM: e.matmul(
                                    tgt, lhsT=lh, rhs=(PT[pb][:, kbi, h, :] if KBM else PT[pb][:, h, kbi, :]),
                                    start=(z == 1), stop=(z == nmm)),
                                    reads=[("PT", pb), "Vpad", "cst"], writes=[("ps6", odb, is_den)] + odk,
                                    sig=(z == nmm))
                    S.op("dve", lambda e, d_ps=d_ps, odb=odb, cc=cc, g=g: e.tensor_scalar(
                        out=rden[odb][:], in0=d_ps, scalar1=esink[:, 2 * g + cc:2 * g + cc + 1], scalar2=None,
                        op0=ALU.add), reads=[("ps6", odb, True), "esink"] + odk, writes=[("rden", odb)])
                    if ATT_DIV:
                        S.op("dve", lambda e, o_ps=o_ps, odb=odb, ob=ob, cc=cc, qb=qb: e.tensor_tensor(
                            out=Og[ob][:, cc, qb * 128:(qb + 1) * 128], in0=o_ps, in1=rden[odb][:], op=ALU.divide),
                            reads=[("ps6", odb, False), ("rden", odb)] + odk, writes=[("Og", ob, cc)])
                    else:
                        S.op("dve", lambda e, odb=odb: e.reciprocal(out=rden[odb][:], in_=rden[odb][:]),
                             reads=[("rden", odb)], writes=[("rden", odb)])
                        S.op("dve", lambda e, o_ps=o_ps, odb=odb, ob=ob, cc=cc, qb=qb: e.tensor_tensor(
                            out=Og[ob][:, cc, qb * 128:(qb + 1) * 128], in0=o_ps, in1=rden[odb][:], op=ALU.mult),
                            reads=[("ps6", odb, False), ("rden", odb)] + odk, writes=[("Og", ob, cc)])
                    if qb == 3 and cc == 1 and not ATT_EXP:
                        for d in range(NCH):
                            if KBM:
                                for hf in range(2):
                                    ybk = 2 + 3 * hf
                                    yv = PSALL[:, ybk * 512 + 256:ybk * 512 + 512]
                                    for c2 in range(2):
                                        S.op("pe", lambda e, d=d, c2=c2, ob=ob, vBo=vBo, yv=yv, hf=hf: e.matmul(
                                            yv, lhsT=vBo[:, c2, d * 128:(d + 1) * 128],
                                            rhs=Og[ob][:, c2, hf * 256:(hf + 1) * 256],
                                            start=(c2 == 0), stop=(c2 == 1)),
                                            reads=[("W", sB), ("Og", ob, c2)], writes=[("ps", ybk)], sig=(c2 == 1))
                                    x0 = qt * 512 + hf * 256
                                    S.op("dve", lambda e, d=d, x0=x0, yv=yv: e.scalar_tensor_tensor(
                                        out=X[:, d, x0:x0 + 256], in0=yv, scalar=tabG[:, 1, d, 0:1],
                                        in1=X[:, d, x0:x0 + 256], op0=ALU.mult, op1=ALU.add),
                                        reads=[("ps", ybk), ("tabG", 1)] + xk(0, d, x0, 256),
                                        writes=xk(0, d, x0, 256))
                                continue
                            yb = 7
                            for c2 in range(2):
                                S.op("pe", lambda e, d=d, c2=c2, ob=ob, vBo=vBo, yb=yb: e.matmul(
                                    PS[yb][:, 0:512], lhsT=vBo[:, c2, d * 128:(d + 1) * 128], rhs=Og[ob][:, c2, :],
                                    start=(c2 == 0), stop=(c2 == 1)),
                                    reads=[("W", sB), ("Og", ob, c2)], writes=[("ps", yb)], sig=(c2 == 1))
                            S.op("dve", lambda e, d=d, qt=qt, yb=yb: e.scalar_tensor_tensor(
                                out=X[:, d, qt * 512:(qt + 1) * 512], in0=PS[yb][:, 0:512], scalar=tabG[:, 1, d, 0:1],
                                in1=X[:, d, qt * 512:(qt + 1) * 512], op0=ALU.mult, op1=ALU.add),
                                reads=[("ps", yb), ("tabG", 1)] + xk(0, d, qt * 512, 512),
                                writes=xk(0, d, qt * 512, 512))

                if ATT_PIPE:
                    stage_s(0)
                    for it in range(len(its)):
                        if it + 1 < len(its):
                            stage_s(it + 1)
                        stage_p(it)
                else:
                    for it in range(len(its)):
                        stage_s(it)
                        stage_p(it)
            S.barrier()
            A.reset(m)

        H = A.alloc([128, NCH, TT], BF16)
        NOT = lambda *names: stop_after not in names
        if ONLY_ATTN:
            m0 = A.mark()
            ad0 = [A.alloc([128, NCH, 256], BF16) for _ in range(2)]
            ada_run(1, list(range(36)), ad0)
            for jj_ in range(3):
                ada_tables(1, jj_)
            S.barrier()
            A.reset(m0)
            attn_phase(1, H)
            stop_after = "x_ffn1_0"
        else:
            m0 = A.mark()
            ad0 = [A.alloc([128, NCH, 256], BF16) for _ in range(2)]
            ada_run(0, list(range(12)), ad0)
            ada_tables(0, 0)
            S.barrier()
            A.reset(m0)
            ffn(0, 0, TILES_A, H, ada_job=(0, list(range(12, 36))))
            ada_tables(0, 1)
            ada_tables(0, 2)
        if NOT("x_ffn1_0"):
            conv_phase(0, H)
        if NOT("x_ffn1_0", "x_mix_0"):
            ffn(0, 1, TILES_A, H, ada_job=(1, list(range(36))))
        if NOT("x_ffn1_0", "x_mix_0", "x_out_0"):
            for jj_ in range(3):
                ada_tables(1, jj_)
            ffn(1, 0, TILES_A, H)
        if NOT("x_ffn1_0", "x_mix_0", "x_out_0", "x_ffn1_1"):
            attn_phase(1, H)
        if NOT("x_ffn1_0", "x_mix_0", "x_out_0", "x_ffn1_1", "x_mix_1"):
            ffn(1, 1, TILES_OWN, H)

        def final_out(do_norm):
            S.barrier()
            m = A.mark()
            A.reset(0)
            stg = [A.alloc([128, NCH, 512], F32) for _ in range(2)]
            sq = [A.alloc([128, 512], F32) for _ in range(3)]
            rstd = [A.alloc([128, 512], F32) for _ in range(2)]
            rsq = [A.alloc([128, 512], F32) for _ in range(2)]
            oT_v = outT.rearrange("(c p) t -> p c t", p=128)
            toks = []
            n_sq = 0
            for ti, (a, n, w) in enumerate(TILES_OWN):
                b = ti % 2
                if do_norm:
                    msb = 6 + b
                    ms = PS[msb][:, 0:n]
                    for c in range(NCH):
                        q = n_sq % 3
                        n_sq += 1
                        S.op("act", lambda e, q=q, c=c, a=a, n=n: e.activation(
                            out=sq[q][:, 0:n], in_=X[:, c, a:a + n], func=AF.Square),
                            reads=xk(0, c, a, n), writes=[("sq", q)])
                        S.op("pe", lambda e, q=q, c=c, ms=ms, n=n: e.matmul(
                            ms, lhsT=onesf[:], rhs=sq[q][:, 0:n], start=(c == 0), stop=(c == NCH - 1)),
                            reads=[("sq", q), "onesf"], writes=[("ps", msb)], sig=True)
                    S.op("act", lambda e, b=b, ms=ms, n=n: e.activation(
                        out=rsq[b][:, 0:n], in_=ms, func=AF.Sqrt, bias=epsb[:, 0:1], scale=1.0),
                        reads=[("ps", msb), "epsb"], writes=[("rsq", b)])
                    S.op("dve", lambda e, b=b, n=n: e.reciprocal(out=rstd[b][:, 0:n], in_=rsq[b][:, 0:n]),
                         reads=[("rsq", b)], writes=[("rstd", b)])
                    for c in range(NCH):
                        S.op("dve", lambda e, c=c, a=a, n=n, b=b: e.scalar_tensor_tensor(
                            out=stg[b][:, c, 0:n], in0=X[:, c, a:a + n], scalar=P("final_g", c, c + 1),
                            in1=rstd[b][:, 0:n], op0=ALU.mult, op1=ALU.mult),
                            reads=xk(0, c, a, n) + [("rstd", b), "prm"], writes=[("stg", b)],
                            sig=True)
                else:
                    for c in range(NCH):
                        S.op("act", lambda e, c=c, a=a, n=n, b=b: e.activation(
                            out=stg[b][:, c, 0:n], in_=X[:, c, a:a + n], func=AF.Identity),
                            reads=xk(0, c, a, n), writes=[("stg", b)])
                toks.append(S.dma("sp", [lambda e, b=b, a=a, n=n: e.dma_start(
                    out=oT_v[:, :, a:a + n], in_=stg[b][:, :, 0:n])], f"st{b}", reads=[("stg", b)]))
            S.wait_tokens("sp", toks)
            A.reset(m)

        final_out(stop_after is None)

        S.finalize()
        with nc.Block() as block:
            @block.tensor
            def _(e):
                S.emit("pe", e)

            @block.scalar
            def _(e):
                S.emit("act", e)

            @block.vector
            def _(e):
                S.emit("dve", e)

            @block.gpsimd
            def _(e):
                S.emit("pool", e)

            @block.sync
            def _(e):
                S.emit("sp", e)
    return nc


def _partner_cols(n):
    j = np.arange(n)
    d = j % 64
    pd = np.where((d % 32) < 16, d + 16, d - 16)
    return j - d + pd


def prepare_inputs(inp):
    f = lambda a: np.ascontiguousarray(np.asarray(a, dtype=np.float32))
    x, c, ctx, c_ctx = f(inp["x"]), f(inp["c"]), f(inp["ctx"]), f(inp["c_ctx"])
    w_qkv = f(inp["attn_w_qkv"])[0]
    chunk = lambda v: np.ascontiguousarray(v.reshape(-1, 128).T)
    shared = {
        "ada_w": f(inp["ada_w"]), "ffn1_wi": f(inp["ffn1_wi"]), "ffn1_wo": f(inp["ffn1_wo"]),
        "ffn2_wi": f(inp["ffn2_wi"]), "ffn2_wo": f(inp["ffn2_wo"]),
        "pw1_w": f(inp["conv_pw1_w"])[0], "pw2_w": f(inp["conv_pw2_w"])[0],
        "w_v": np.ascontiguousarray(w_qkv[:, 1280:1536]), "w_o": f(inp["attn_w_o"])[0],
    }
    watt = np.zeros((D, 4 * 768), np.float32)
    for g in range(4):
        q = w_qkv[:, 256 * g:256 * g + 256]
        k = w_qkv[:, 1024 + 64 * g:1024 + 64 * g + 64]
        kd = np.concatenate([k, k], axis=1)
        o = 768 * g
        watt[:, o:o + 256] = q
        watt[:, o + 256:o + 512] = q[:, _partner_cols(256)]
        watt[:, o + 512:o + 640] = kd
        watt[:, o + 640:o + 768] = kd[:, _partner_cols(128)]
    shared["w_att"] = watt
    cst = np.zeros((128, NCST), np.float32)
    kk = np.arange(128)[:, None]
    qq = np.arange(128)[None, :]
    lo = (kk >= qq).astype(np.float32)
    hi = (kk <= qq).astype(np.float32)
    cst[:, 0:128] = lo
    cst[:, 128:256] = lo
    cst[:, 256:384] = hi
    cst[:, 384:512] = hi
    cst[:, 512 + 64:512 + 128] = 1.0
    cst[:, 704:832] = np.eye(128, dtype=np.float32)
    cst[:, 832:960] = (lo - 1.0) * 30000.0
    cst[:, 960:1088] = (hi - 1.0) * 30000.0
    shared["cst"] = cst
    p = np.arange(128)
    d = p % 64
    ax = d // 32
    hh = (d % 32) // 16
    fr = d % 16
    inv = (np.float32(10000.0) ** (-np.arange(16, dtype=np.float32) / np.float32(16))).astype(np.float32)
    in_maps = []
    for core in range(8):
        b, hf = core // 2, core % 2
        idx = np.arange(T) if hf == 0 else (SEQ - 1 - np.arange(T))
        m = dict(shared)
        m["xT"] = np.ascontiguousarray(x[b, idx, :].T)
        cb = ctx[b] if hf == 0 else ctx[b, ::-1]
        m["cT"] = np.ascontiguousarray(cb.T)
        prm = np.zeros((128, NPRM), np.float32)

        def put(name, arr):
            o, w = PRM[name]
            assert arr.shape == (128, w), (name, arr.shape)
            prm[:, o:o + w] = arr
        cc = np.stack([chunk(c[b]), chunk(c_ctx)], axis=2).reshape(128, 16)
        put("cc", cc)
        put("ada_b", np.concatenate([chunk(f(inp["ada_b"])[l]) for l in range(2)], axis=1))
        put("norm_g", np.concatenate([chunk(f(inp["norm_g"])[l, j]) for l in range(2) for j in range(3)], axis=1))
        put("pw1_b", chunk(f(inp["conv_pw1_b"])[0]))
        dw = f(inp["conv_dw_w"])[0]
        if hf == 1:
            dw = dw[::-1]
        put("dw_w", np.ascontiguousarray(dw.T.reshape(NCH, 128, CW).transpose(1, 0, 2)).reshape(128, NCH * CW))
        put("dw_b", chunk(f(inp["conv_dw_b"])[0]))
        put("ln_g", chunk(f(inp["conv_ln_g"])[0]))
        put("ln_b", chunk(f(inp["conv_ln_b"])[0]))
        put("pw2_b", chunk(f(inp["conv_pw2_b"])[0]))
        sk = f(inp["attn_sink"])[0]
        put("sink", np.stack([np.where(p >= 64, sk[2 * cch + 1], sk[2 * cch]) for cch in range(NCH)], axis=1).astype(np.float32))
        put("final_g", chunk(f(inp["final_g"])))
        m["prm"] = prm
        row = (idx // 64).astype(np.float32)
        col = (idx % 64).astype(np.float32)
        pos = np.where(ax[:, None] == 0, row[None, :], col[None, :]).astype(np.float32)
        ang = (pos * inv[fr][:, None]).astype(np.float32)
        cs = np.cos(ang).astype(np.float32)
        sn = np.sin(ang).astype(np.float32)
        sn = np.where(hh[:, None] == 0, -sn, sn).astype(np.float32)
        m["rope"] = np.ascontiguousarray(np.stack([cs, sn], axis=0))
        in_maps.append(m)
    return in_maps


def assemble(results):
    out = np.zeros((4, SEQ, D), np.float32)
    for core in range(8):
        b, hf = core // 2, core % 2
        idx = np.arange(OWN) if hf == 0 else (SEQ - 1 - np.arange(OWN))
        out[b, idx, :] = np.asarray(results[core]["outT"]).T
    return out


def kernel(**inputs):
    nc = build_program()
    in_maps = prepare_inputs(inputs)
    res = run_bass_kernel_spmd(nc, in_maps, core_ids=list(range(8)))
    return assemble(res.results)
```

```python
import numpy as np
import concourse.bass as bass
import concourse.mybir as mybir
from concourse.bass_utils import run_bass_kernel_spmd

F32 = mybir.dt.float32
BF16 = mybir.dt.bfloat16
AF = mybir.ActivationFunctionType
ALU = mybir.AluOpType

D = 1024
NCH = 8
SEQ = 4096
OWN = 2048
T = 2200
TK = 2176
CTX = 256
TT = T + CTX
DFF = 2816
NFC = 22
CW = 31
EPS = 1e-6
SLOT = 6144
import os
ATT_PIPE = int(os.environ.get('ATT_PIPE', '1'))
ATT_MASK = os.environ.get('ATT_MASK', 'dve')
SAME_ENG = int(os.environ.get('SAME_ENG', '1'))
ONLY_ATTN = int(os.environ.get('ONLY_ATTN', '0'))
ATT_EXP = int(os.environ.get('ATT_EXP', '0'))
ATT_DIV = int(os.environ.get('ATT_DIV', '0'))
NSLOT = 4

TILES_A = [(i * 440, 440, 0) for i in range(5)] + [(T, 256, 1)]
TILES_OWN = [(i * 512, 512, 0) for i in range(4)]

PRM = {}
_o = 0
for _n, _w in [("cc", 16), ("ada_b", 144), ("norm_g", 48), ("pw1_b", 16), ("dw_w", 248), ("dw_b", 8),
               ("ln_g", 8), ("ln_b", 8), ("pw2_b", 8), ("sink", 8), ("final_g", 8)]:
    PRM[_n] = (_o, _w)
    _o += _w
NPRM = _o
NCST = 2 * 256 + 192 + 128 + 256

_XB = sorted(set([i * 440 for i in range(6)] + [i * 512 for i in range(5)] + [T]))


def _segs(a, b):
    return [i for i in range(len(_XB) - 1) if _XB[i] < b and _XB[i + 1] > a]


class Sched:
    CE = ("pe", "act", "dve", "pool")
    NEPOCH = 16

    def __init__(self, nc, sems):
        self.nc = nc
        self.sems = sems
        self.cnt = {k: 0 for k in sems}
        self.ops = []
        self.byeng = {e: [] for e in ("pe", "act", "dve", "pool", "sp")}
        self.last_w = {}
        self.readers = {}
        self.unsig = {e: [] for e in self.byeng}
        self.redirect = {}
        self.epoch = 0
        self.last_sig = {}

    def _deps(self, reads, writes):
        deps = []
        for b in reads:
            t = self.last_w.get(b)
            if t is not None:
                deps.append((t, True))
        for b in writes:
            t = self.last_w.get(b)
            if t is not None:
                deps.append((t, False))
            for t in self.readers.get(b, ()):
                deps.append((t, False))
        return deps

    def _record(self, tok, reads, writes):
        for b in reads:
            self.readers.setdefault(b, []).append(tok)
        for b in writes:
            self.last_w[b] = tok
            self.readers[b] = []

    def _new(self, eng, fns, deps, kind, sig=True, chan=None):
        rec = dict(id=len(self.ops), eng=eng, epoch=self.epoch, fns=fns, deps=deps, kind=kind, sig=sig, chan=chan)
        self.ops.append(rec)
        self.byeng[eng].append(rec)
        return rec

    def op(self, eng, fn, reads=(), writes=(), sig=True):
        rec = self._new(eng, [fn], self._deps(reads, writes), "op", sig=sig)
        if sig:
            for i in self.unsig[eng]:
                self.redirect[i] = rec["id"]
            self.unsig[eng] = []
            self.last_sig[eng] = rec["id"]
        else:
            self.unsig[eng].append(rec["id"])
        self._record(("op", rec["id"]), reads, writes)

    def dma(self, eng, fns, chan, reads=(), writes=()):
        rec = self._new(eng, list(fns), self._deps(reads, writes), "dma", chan=chan)
        self.cnt[chan] += 16 * len(fns)
        tok = ("dma", chan, self.cnt[chan])
        self._record(tok, reads, writes)
        return tok

    def wait_tokens(self, eng, toks):
        self._new(eng, [], [(t, True) for t in toks], "wait")

    def barrier(self):
        for e in self.CE:
            assert not self.unsig[e], e
        toks = {e: ("op", self.last_sig[e]) for e in self.CE
                if e in self.last_sig and self.ops[self.last_sig[e]]["epoch"] == self.epoch}
        for e in ("pe", "act", "dve", "pool", "sp"):
            self.wait_tokens(e, [t for k, t in toks.items() if k != e])
        self.epoch += 1
        assert self.epoch < self.NEPOCH

    def finalize(self):
        for e in self.CE:
            assert not self.unsig[e], e

        def resolve(tok, rec, raw):
            if tok[0] == "dma":
                return tok
            i = self.redirect.get(tok[1], tok[1])
            d = self.ops[i]
            if d["epoch"] != rec["epoch"]:
                return None
            if d["eng"] == rec["eng"]:
                if rec["eng"] in ("pe", "sp"):
                    return None
                if not SAME_ENG and not raw:
                    return None
            return ("op", i)
        awaited = set()
        for rec in self.ops:
            r = []
            for tok, raw in rec["deps"]:
                t = resolve(tok, rec, raw)
                if t is not None:
                    r.append(t)
                    if t[0] == "op":
                        awaited.add(t[1])
            rec["rdeps"] = r
        val = {}
        cnt = {}
        for e in self.CE:
            for rec in self.byeng[e]:
                if rec["kind"] == "op" and rec["id"] in awaited:
                    k = f"{e}@{rec['epoch']}"
                    cnt[k] = cnt.get(k, 0) + 1
                    val[rec["id"]] = (k, cnt[k])
        self.prog = {}
        self.nsig = len(val)
        for e, recs in self.byeng.items():
            waited = {}
            out = []
            for rec in recs:
                w = {}
                for t in rec["rdeps"]:
                    k, v = val[t[1]] if t[0] == "op" else (t[1], t[2])
                    if w.get(k, 0) < v:
                        w[k] = v
                wl = []
                for k, v in w.items():
                    if waited.get(k, 0) < v:
                        waited[k] = v
                        wl.append((k, v))
                if rec["kind"] == "dma":
                    incs = [(rec["chan"], 16)] * len(rec["fns"])
                elif rec["id"] in val:
                    incs = [(val[rec["id"]][0], 1)]
                else:
                    incs = []
                out.append((wl, rec["fns"], incs))
            self.prog[e] = out

    def emit(self, eng, e):
        for waits, fns, incs in self.prog[eng]:
            for k, v in waits:
                e.wait_ge(self.sems[k], v)
            for i, fn in enumerate(fns):
                ins = fn(e)
                if i < len(incs):
                    ins.then_inc(self.sems[incs[i][0]], incs[i][1])


class Arena:
    def __init__(self, ap_f32, nbytes):
        self.ap = ap_f32
        self.n = nbytes
        self.off = 0

    def mark(self):
        return self.off

    def reset(self, m):
        self.off = m

    def alloc(self, shape, dt):
        es = 2 if dt == BF16 else 4
        free = 1
        for s in shape[1:]:
            free *= s
        nb = (free * es + 31) // 32 * 32
        assert self.off + nb <= self.n, ("arena overflow", self.off, nb, self.n)
        w0 = self.off // 4
        v = self.ap[:, w0:w0 + nb // 4]
        self.off += nb
        if dt == BF16:
            v = v.bitcast(BF16)
        v = v[:, 0:free]
        if len(shape) == 3:
            v = v.rearrange("p (a b) -> p a b", a=shape[1])
        elif len(shape) == 4:
            v = v.rearrange("p (a b c) -> p a b c", a=shape[1], b=shape[2])
        return v


def build_program(stop_after=None):
    nc = bass.Bass("TRN2", target_bir_lowering=False)
    dr = {}

    def din(name, shape):
        dr[name] = nc.dram_tensor(name, shape, F32, kind="ExternalInput").ap()
        return dr[name]

    xT = din("xT", [D, T])
    cT = din("cT", [D, CTX])
    prm_d = din("prm", [128, NPRM])
    cst_d = din("cst", [128, NCST])
    rope_d = din("rope", [2, 128, T])
    ada_w = din("ada_w", [2, D, 9 * D])
    f_wi = [din("ffn1_wi", [2, D, 2 * DFF]), din("ffn2_wi", [2, D, 2 * DFF])]
    f_wo = [din("ffn1_wo", [2, DFF, D]), din("ffn2_wo", [2, DFF, D])]
    pw1_w = din("pw1_w", [D, 2 * D])
    pw2_w = din("pw2_w", [D, D])
    watt = din("w_att", [D, 4 * 768])
    wv_d = din("w_v", [D, 256])
    wo_d = din("w_o", [D, D])
    outT = nc.dram_tensor("outT", [D, OWN], F32, kind="ExternalOutput").ap()

    semkeys = [f"{e}@{i}" for e in Sched.CE for i in range(Sched.NEPOCH)] + ["ldp", "ldc", "ldxc", "st0", "st1", "rp0", "rp1", "ad0", "ad1"] + \
              [f"ldx{i}" for i in range(5)] + [f"w{i}" for i in range(NSLOT)]

    import contextlib
    es = contextlib.ExitStack()
    with es:
        sems = {k: es.enter_context(nc.semaphore(k)) for k in semkeys}
        X = es.enter_context(nc.sbuf_tensor("X", [128, NCH, T], F32))
        XC = es.enter_context(nc.sbuf_tensor("XC", [128, NCH, CTX], F32))
        Wr = es.enter_context(nc.sbuf_tensor("Wr", [128, NSLOT, SLOT], BF16))
        prm = es.enter_context(nc.sbuf_tensor("prm_sb", [128, NPRM], F32))
        cst = es.enter_context(nc.sbuf_tensor("cst_sb", [128, NCST], BF16))
        mod = es.enter_context(nc.sbuf_tensor("mod", [128, 2, 72, 2], F32))
        tabA = es.enter_context(nc.sbuf_tensor("tabA", [128, 3, NCH, 2], F32))
        tabG = es.enter_context(nc.sbuf_tensor("tabG", [128, 3, NCH, 2], F32))
        tabGb = es.enter_context(nc.sbuf_tensor("tabGb", [128, NCH, 2], F32))
        scT = es.enter_context(nc.sbuf_tensor("scT", [128, NCH, 2], BF16))
        onesf = es.enter_context(nc.sbuf_tensor("onesf", [128, 128], F32))
        esink = es.enter_context(nc.sbuf_tensor("esink", [128, NCH], F32))
        epsb = es.enter_context(nc.sbuf_tensor("epsb", [128, 1], F32))
        ARENA_BYTES = 212863 - (NCH * T * 4 + NCH * CTX * 4 + NSLOT * SLOT * 2 + NPRM * 4 + NCST * 2
                                + 2 * 72 * 2 * 4 + 2 * 3 * NCH * 2 * 4 + NCH * 2 * 4 + NCH * 2 * 2
                                + 128 * 4 + NCH * 4) - 384
        ARENA_BYTES = ARENA_BYTES // 64 * 64
        ar_t = es.enter_context(nc.sbuf_tensor("arena", [128, ARENA_BYTES // 4], F32))
        PSALL = es.enter_context(nc.psum_tensor("psall", [128, 4096], F32))
        PS = [PSALL[:, i * 512:(i + 1) * 512] for i in range(8)]
        S = Sched(nc, sems)
        A = Arena(ar_t, ARENA_BYTES)

        def P(name, a=0, b=None):
            o, w = PRM[name]
            b = w if b is None else b
            return prm[:, o + a:o + b]

        def xbuf(which):
            return X if which == 0 else XC

        def xk(which, c, a, n):
            if which == 1:
                return [("XC", c)]
            return [("X", c, s) for s in _segs(a, a + n)]

        def hk(c, a, n):
            if a >= T:
                return [("H", c, "c")]
            return [("H", c, s) for s in _segs(a, a + n)]

        S.dma("sp", [lambda e: e.dma_start(out=prm[:], in_=prm_d[:, :])], "ldp", writes=["prm"])
        S.dma("pool", [lambda e: e.dma_start(out=cst[:], in_=cst_d[:, :])], "ldc", writes=["cst"])
        xT_v = xT.rearrange("(c p) t -> p c t", p=128)
        cT_v = cT.rearrange("(c p) t -> p c t", p=128)
        for i in range(5):
            S.dma("sp", [lambda e, i=i: e.dma_start(out=X[:, :, i * 440:(i + 1) * 440],
                                                   in_=xT_v[:, :, i * 440:(i + 1) * 440])],
                  f"ldx{i}", writes=[k for c in range(NCH) for k in xk(0, c, i * 440, 440)])
        S.dma("sp", [lambda e: e.dma_start(out=XC[:], in_=cT_v)], "ldxc",
              writes=[("XC", c) for c in range(NCH)])
        S.op("dve", lambda e: e.memset(onesf[:], 1.0 / D), writes=["onesf"])
        S.op("dve", lambda e: e.memset(epsb[:], EPS), writes=["epsb"])
        S.op("act", lambda e: e.activation(out=scT[:], in_=P("cc").rearrange("p (k w) -> p k w", w=2),
                                           func=AF.Silu), reads=["prm"], writes=["scT"])
        S.op("act", lambda e: e.activation(out=esink[:], in_=P("sink"), func=AF.Exp),
             reads=["prm"], writes=["esink"])

        ring = {"next": 0}

        def slot_alloc():
            s = ring["next"]
            ring["next"] = (s + 1) % NSLOT
            return s

        def load_slot(s, fns):
            S.dma("pool", fns, f"w{s}", writes=[("W", s)])

        def ada_unit(l, u, stg, sb):
            aw = ada_w[l].rearrange("(k p) n -> p k n", p=128)
            S.dma("pool", [lambda e, u=u, stg=stg: e.dma_start(out=stg, in_=aw[:, :, u * 256:(u + 1) * 256])],
                  f"ad{sb}", writes=[("adst", sb)])

            def mm():
                pb = 6 + (u % 2)
                ps = PS[pb][:, 504:508].rearrange("p (a b) -> p a b", b=2)
                for oc in range(2):
                    for k in range(NCH):
                        S.op("pe", lambda e, oc=oc, k=k, ps=ps, stg=stg: e.matmul(
                            ps[:, oc, :], lhsT=stg[:, k, oc * 128:(oc + 1) * 128], rhs=scT[:, k, :],
                            start=(k == 0), stop=(k == NCH - 1)),
                            reads=[("adst", sb), "scT"], writes=[("ps", pb)], sig=(k == NCH - 1))
                ab = P("ada_b", l * 72 + u * 2, l * 72 + u * 2 + 2)
                S.op("dve", lambda e, ps=ps, ab=ab, u=u: e.tensor_tensor(
                    out=mod[:, l, u * 2:(u + 1) * 2, :], in0=ps,
                    in1=ab.unsqueeze(2).to_broadcast([128, 2, 2]), op=ALU.add),
                    reads=[("ps", pb), "prm"], writes=[("mod", l, u // 12)])
            return mm

        def ada_tables(l, j):
            g = P("norm_g", (l * 3 + j) * 8, (l * 3 + j) * 8 + 8)
            S.op("dve", lambda e, j=j, g=g: e.scalar_tensor_tensor(
                out=tabA[:, j], in0=mod[:, l, (3 * j + 1) * 8:(3 * j + 2) * 8, :], scalar=1.0,
                in1=g.unsqueeze(2).to_broadcast([128, NCH, 2]), op0=ALU.add, op1=ALU.mult),
                reads=[("mod", l, j), "prm"], writes=[("tabA", j)])
            S.op("dve", lambda e, j=j: e.tensor_scalar(
                out=tabG[:, j], in0=mod[:, l, (3 * j + 2) * 8:(3 * j + 3) * 8, :],
                scalar1=(1.0 if j == 1 else 0.5), scalar2=None, op0=ALU.mult),
                reads=[("mod", l, j)], writes=[("tabG", j)])
            if l == 0 and j == 1:
                S.op("dve", lambda e: e.tensor_tensor(
                    out=tabGb[:], in0=tabG[:, 1], in1=P("pw2_b").unsqueeze(2).to_broadcast([128, NCH, 2]),
                    op=ALU.mult), reads=[("tabG", 1), "prm"], writes=["tabGb"])

        def ada_run(l, units, adst):
            pend = []
            for i, u in enumerate(units):
                pend.append(ada_unit(l, u, adst[i % 2], i % 2))
                if len(pend) == 2:
                    pend.pop(0)()
            for f_ in pend:
                f_()

        def tabB(l, j):
            return mod[:, l, (3 * j) * 8:(3 * j + 1) * 8, :]

        def make_prepass(l, j, tiles, H, nbuf=3):
            sq = [A.alloc([128, 512], F32) for _ in range(nbuf)]
            rstd = [A.alloc([128, 512], F32) for _ in range(2)]
            rsq = [A.alloc([128, 512], F32) for _ in range(2)]
            tmp = [A.alloc([128, 512], F32) for _ in range(nbuf)]
            st_ = {"sq": 0, "tmp": 0}

            def pre_tile(ti):
                a, n, w = tiles[ti]
                xb = xbuf(w)
                xa = a - T if w == 1 else a
                msb = 6 + (ti % 2)
                ms = PS[msb][:, 0:n]
                for c in range(NCH):
                    q = st_["sq"] % nbuf
                    st_["sq"] += 1
                    S.op("act", lambda e, q=q, c=c, xb=xb, xa=xa, n=n: e.activation(
                        out=sq[q][:, 0:n], in_=xb[:, c, xa:xa + n], func=AF.Square),
                        reads=xk(w, c, xa, n), writes=[("sq", q)])
                    S.op("pe", lambda e, q=q, c=c, ms=ms, n=n: e.matmul(
                        ms, lhsT=onesf[:], rhs=sq[q][:, 0:n], start=(c == 0), stop=(c == NCH - 1)),
                        reads=[("sq", q), "onesf"], writes=[("ps", msb)], sig=True)
                r = ti % 2
                S.op("act", lambda e, r=r, ms=ms, n=n: e.activation(
                    out=rsq[r][:, 0:n], in_=ms, func=AF.Sqrt, bias=epsb[:, 0:1], scale=1.0),
                    reads=[("ps", msb), "epsb"], writes=[("rsq", r)])
                S.op("dve", lambda e, r=r, n=n: e.reciprocal(out=rstd[r][:, 0:n], in_=rsq[r][:, 0:n]),
                     reads=[("rsq", r)], writes=[("rstd", r)])
                for c in range(NCH):
                    q = st_["tmp"] % nbuf
                    st_["tmp"] += 1
                    S.op("dve", lambda e, q=q, c=c, xb=xb, xa=xa, n=n, r=r, w=w: e.scalar_tensor_tensor(
                        out=tmp[q][:, 0:n], in0=xb[:, c, xa:xa + n], scalar=tabA[:, j, c, w:w + 1],
                        in1=rstd[r][:, 0:n], op0=ALU.mult, op1=ALU.mult),
                        reads=xk(w, c, xa, n) + [("rstd", r), ("tabA", j)], writes=[("ptmp", q)])
                    S.op("act", lambda e, q=q, c=c, a=a, n=n, w=w: e.activation(
                        out=H[:, c, a:a + n], in_=tmp[q][:, 0:n], func=AF.Identity,
                        bias=tabB(l, j)[:, c, w:w + 1], scale=1.0),
                        reads=[("ptmp", q), ("mod", l, j)], writes=hk(c, a, n))
            return pre_tile

        def prepass(l, j, tiles, H):
            m = A.mark()
            pt = make_prepass(l, j, tiles, H)
            for ti in range(len(tiles)):
                pt(ti)
            S.barrier()
            A.reset(m)

        def ffn(l, f, tiles, H, ada_job=None):
            j = 0 if f == 0 else 2
            m = A.mark()
            pre_tile = make_prepass(l, j, tiles, H, nbuf=2)
            adst = [A.alloc([128, NCH, 256], BF16) for _ in range(2)]
            act = [A.alloc([128, 4, 512], BF16) for _ in range(2)]
            sg = [A.alloc([128, 512], F32) for _ in range(2)]
            wi = f_wi[f][l].rearrange("(k p) (two n) -> p k two n", p=128, two=2)
            wo = f_wo[f][l].rearrange("(j p) n -> p j n", p=128)
            sweeps = [[0, 1], [2, 3], [4, 5], [6, 7], [8, 9], [10]]
            unit_slot = {}

            def load_unit(u):
                s = slot_alloc()
                unit_slot[u] = s
                wiv = Wr[:, s, 0:4096].rearrange("p (k two n) -> p k two n", k=NCH, two=2)
                wov = Wr[:, s, 4096:6144].rearrange("p (j n) -> p j n", j=2)
                load_slot(s, [lambda e, wiv=wiv, u=u: e.dma_start(out=wiv[:, :, 0, :], in_=wi[:, :, 0, u * 256:(u + 1) * 256]),
                              lambda e, wiv=wiv, u=u: e.dma_start(out=wiv[:, :, 1, :], in_=wi[:, :, 1, u * 256:(u + 1) * 256]),
                              lambda e, wov=wov, u=u: e.dma_start(out=wov, in_=wo[:, 2 * u:2 * u + 2, :])])

            tasks = []
            for si, sw in enumerate(sweeps):
                for ti in range(len(tiles)):
                    tasks.append((si, ti))
            st = {"sg": 0, "gu": 0, "y": 0}

            def stage_a(k):
                si, ti = tasks[k]
                a, n, w = tiles[ti]
                ab = k % 2
                chunks = [(u, jj) for u in sweeps[si] for jj in range(2)]
                for ci, (u, jj) in enumerate(chunks):
                    s = unit_slot[u]
                    wiv = Wr[:, s, 0:4096].rearrange("p (k two n) -> p k two n", k=NCH, two=2)
                    gb = st["gu"] % 2
                    st["gu"] += 1
                    gps = PS[gb * 2][:, 0:n]
                    ups = PS[gb * 2 + 1][:, 0:n]
                    for two, pp, pb in ((0, gps, gb * 2), (1, ups, gb * 2 + 1)):
                        for kk in range(NCH):
                            S.op("pe", lambda e, pp=pp, wiv=wiv, kk=kk, two=two, jj=jj, a=a, n=n: e.matmul(
                                pp, lhsT=wiv[:, kk, two, jj * 128:(jj + 1) * 128], rhs=H[:, kk, a:a + n],
                                start=(kk == 0), stop=(kk == NCH - 1)),
                                reads=[("W", s)] + hk(kk, a, n), writes=[("ps", pb)], sig=(kk == NCH - 1))
                    q = st["sg"] % 2
                    st["sg"] += 1
                    S.op("act", lambda e, q=q, gps=gps, n=n: e.activation(out=sg[q][:, 0:n], in_=gps, func=AF.Silu),
                         reads=[("ps", gb * 2)], writes=[("sg", q)])
                    S.op("dve", lambda e, q=q, ups=ups, n=n, ab=ab, ci=ci: e.tensor_tensor(
                        out=act[ab][:, ci, 0:n], in0=sg[q][:, 0:n], in1=ups, op=ALU.mult),
                        reads=[("sg", q), ("ps", gb * 2 + 1)], writes=[("act", ab, ci)])

            def stage_b(k):
                si, ti = tasks[k]
                a, n, w = tiles[ti]
                xb = xbuf(w)
                xa = a - T if w == 1 else a
                ab = k % 2
                chunks = [(u, jj) for u in sweeps[si] for jj in range(2)]
                for d in range(NCH):
                    yb = 4 + st["y"] % 2
                    st["y"] += 1
                    yps = PS[yb][:, 0:n]
                    for ci, (u, jj) in enumerate(chunks):
                        s = unit_slot[u]
                        wov = Wr[:, s, 4096:6144].rearrange("p (j n) -> p j n", j=2)
                        S.op("pe", lambda e, yps=yps, wov=wov, jj=jj, d=d, ab=ab, ci=ci, n=n: e.matmul(
                            yps, lhsT=wov[:, jj, d * 128:(d + 1) * 128], rhs=act[ab][:, ci, 0:n],
                            start=(ci == 0), stop=(ci == len(chunks) - 1)),
                            reads=[("W", s), ("act", ab, ci)], writes=[("ps", yb)], sig=(ci == len(chunks) - 1))
                    S.op("dve", lambda e, yps=yps, d=d, xb=xb, xa=xa, n=n, w=w: e.scalar_tensor_tensor(
                        out=xb[:, d, xa:xa + n], in0=yps, scalar=tabG[:, j, d, w:w + 1], in1=xb[:, d, xa:xa + n],
                        op0=ALU.mult, op1=ALU.add),
                        reads=[("ps", yb), ("tabG", j)] + xk(w, d, xa, n), writes=xk(w, d, xa, n))

            loaded = 0

            def ensure_loaded(si):
                nonlocal loaded
                while loaded <= min(si, len(sweeps) - 1):
                    for u in sweeps[loaded]:
                        load_unit(u)
                    loaded += 1
            ensure_loaded(1)
            nt = len(tiles)
            pre_tile(0)
            if nt > 1:
                pre_tile(1)
            ada_l, ada_units = ada_job if ada_job else (0, [])
            ada_pend = []
            ada_i = 0
            for k in range(len(tasks) + 1):
                if k < len(tasks):
                    si, ti = tasks[k]
                    if ti == 1:
                        ensure_loaded(si + 1)
                    if si == 0 and ti + 2 < nt:
                        pre_tile(ti + 2)
                    if ada_i < len(ada_units):
                        ada_pend.append(ada_unit(ada_l, ada_units[ada_i], adst[ada_i % 2], ada_i % 2))
                        ada_i += 1
                        if len(ada_pend) == 2:
                            ada_pend.pop(0)()
                    stage_a(k)
                if k >= 1:
                    stage_b(k - 1)
            for f_ in ada_pend:
                f_()
            assert ada_i == len(ada_units)
            S.barrier()
            A.reset(m)

        def conv_phase(l, H):
            prepass(l, 1, TILES_A, H)
            m = A.mark()
            p1 = pw1_w.rearrange("(k p) n -> p k n", p=128)
            p2 = pw2_w.rearrange("(k p) n -> p k n", p=128)
            slots = []
            for i in range(4):
                s = slot_alloc()
                slots.append(s)
                v1 = Wr[:, s, 0:4096].rearrange("p (k two n) -> p k two n", k=NCH, two=2)
                v2 = Wr[:, s, 4096:6144].rearrange("p (k n) -> p k n", k=NCH)
                load_slot(s, [lambda e, v1=v1, i=i: e.dma_start(out=v1[:, :, 0, :], in_=p1[:, :, i * 256:(i + 1) * 256]),
                              lambda e, v1=v1, i=i: e.dma_start(out=v1[:, :, 1, :], in_=p1[:, :, D + i * 256:D + (i + 1) * 256]),
                              lambda e, v2=v2, i=i: e.dma_start(out=v2, in_=p2[:, :, i * 256:(i + 1) * 256])])
            UW = 472
            NPE = 16
            U = [A.alloc([128, UW], F32) for _ in range(2)]
            Ub = [A.alloc([128, UW], BF16) for _ in range(2)]
            Dg = A.alloc([128, 2, NPE, 128], BF16)
            V = A.alloc([128, NCH, 440], F32)
            sqv = [A.alloc([128, 440], F32) for _ in range(1)]
            varb = A.alloc([128, 440], F32)
            HN = A.alloc([128, NCH, 440], BF16)
            ident = cst[:, 704:832]
            dww = P("dw_w").rearrange("p (c k) -> p c k", k=CW)
            K1 = CW - NPE
            pend_d = []
            for ti, (a, n, w) in enumerate(TILES_A):
                xb = xbuf(w)
                xa = a - T if w == 1 else a
                s0, s1 = (0, T) if w == 0 else (T, T + CTX)
                lo, hi = max(a - 15, s0), min(a + n + 15, s1)
                ulo, uhi = lo - (a - 15), hi - (a - 15)
                nu = hi - lo
                for jp in range(4):
                    for jj in range(2):
                        j = 2 * jp + jj
                        s = slots[jp]
                        v1 = Wr[:, s, 0:4096].rearrange("p (k two n) -> p k two n", k=NCH, two=2)
                        for two in range(2):
                            pb = 2 * jj + two
                            pp = PS[pb][:, 0:nu]
                            for kk in range(NCH):
                                S.op("pe", lambda e, pp=pp, v1=v1, kk=kk, two=two, jj=jj, lo=lo, hi=hi: e.matmul(
                                    pp, lhsT=v1[:, kk, two, jj * 128:(jj + 1) * 128], rhs=H[:, kk, lo:hi],
                                    start=(kk == 0), stop=(kk == NCH - 1)),
                                    reads=[("W", s)] + hk(kk, lo, nu), writes=[("ps", pb)], sig=(kk == NCH - 1))
                        if ulo > 0:
                            S.op("dve", lambda e, jj=jj, ulo=ulo: e.memset(U[jj][:, 0:ulo], 0.0), writes=[("U", jj)])
                        if uhi < n + 30:
                            S.op("dve", lambda e, jj=jj, uhi=uhi, n=n: e.memset(U[jj][:, uhi:n + 30], 0.0), writes=[("U", jj)])
                        S.op("act", lambda e, jj=jj, j=j, nu=nu, ulo=ulo, uhi=uhi: e.activation(
                            out=U[jj][:, ulo:uhi], in_=PS[2 * jj + 1][:, 0:nu], func=AF.Sigmoid,
                            bias=P("pw1_b", 8 + j, 9 + j), scale=1.0),
                            reads=[("ps", 2 * jj + 1), "prm"], writes=[("U", jj)])
                        S.op("dve", lambda e, jj=jj, j=j, nu=nu, ulo=ulo, uhi=uhi: e.scalar_tensor_tensor(
                            out=U[jj][:, ulo:uhi], in0=PS[2 * jj][:, 0:nu], scalar=P("pw1_b", j, j + 1),
                            in1=U[jj][:, ulo:uhi], op0=ALU.add, op1=ALU.mult),
                            reads=[("ps", 2 * jj), "prm", ("U", jj)], writes=[("U", jj)])
                    for k in range(K1):
                        for jj in range(2):
                            j = 2 * jp + jj
                            if k == 0:
                                S.op("dve", lambda e, jj=jj, j=j, n=n: e.tensor_scalar(
                                    out=V[:, j, 0:n], in0=U[jj][:, 0:n], scalar1=dww[:, j, 0:1],
                                    scalar2=P("dw_b", j, j + 1), op0=ALU.mult, op1=ALU.add),
                                    reads=[("U", jj), "prm"], writes=[("V", j)])
                            else:
                                S.op("dve", lambda e, jj=jj, j=j, n=n, k=k: e.scalar_tensor_tensor(
                                    out=V[:, j, 0:n], in0=U[jj][:, k:k + n], scalar=dww[:, j, k:k + 1],
                                    in1=V[:, j, 0:n], op0=ALU.mult, op1=ALU.add),
                                    reads=[("U", jj), ("V", j), "prm"], writes=[("V", j)])
                    for jj in range(2):
                        j = 2 * jp + jj
                        S.op("act", lambda e, jj=jj, n=n: e.activation(
                            out=Ub[jj][:, 0:n + 30], in_=U[jj][:, 0:n + 30], func=AF.Identity),
                            reads=[("U", jj)], writes=[("Ub", jj)])
                        for kidx in range(NPE):
                            k = K1 + kidx
                            S.op("act", lambda e, jj=jj, j=j, k=k, kidx=kidx: e.activation(
                                out=Dg[:, jj, kidx, :], in_=ident, func=AF.Identity, scale=dww[:, j, k:k + 1]),
                                reads=["cst", "prm"], writes=[("Dg", jj, kidx)])
                    for jj in range(2):
                        j = 2 * jp + jj
                        for kidx in range(NPE):
                            k = K1 + kidx
                            S.op("pe", lambda e, jj=jj, k=k, kidx=kidx, n=n: e.matmul(
                                PS[4 + jj][:, 0:n], lhsT=Dg[:, jj, kidx, :], rhs=Ub[jj][:, k:k + n],
                                start=(kidx == 0), stop=(kidx == NPE - 1)),
                                reads=[("Dg", jj, kidx), ("Ub", jj)], writes=[("ps", 4 + jj)], sig=(kidx == NPE - 1))
                    for jj in range(2):
                        j = 2 * jp + jj
                        S.op("dve", lambda e, jj=jj, j=j, n=n: e.tensor_tensor(
                            out=V[:, j, 0:n], in0=V[:, j, 0:n], in1=PS[4 + jj][:, 0:n], op=ALU.add),
                            reads=[("V", j), ("ps", 4 + jj)], writes=[("V", j)])
                while pend_d:
                    pend_d.pop(0)()
                for j in range(NCH):
                    S.op("pe", lambda e, j=j, n=n: e.matmul(PS[4][:, 0:n], lhsT=onesf[:], rhs=V[:, j, 0:n],
                                                           start=(j == 0), stop=(j == NCH - 1)),
                         reads=[("V", j), "onesf"], writes=[("ps", 4)], sig=True)
                    S.op("act", lambda e, j=j, n=n: e.activation(out=sqv[0][:, 0:n], in_=V[:, j, 0:n], func=AF.Square),
                         reads=[("V", j)], writes=[("sqv", 0)])
                    S.op("pe", lambda e, j=j, n=n: e.matmul(PS[5][:, 0:n], lhsT=onesf[:], rhs=sqv[0][:, 0:n],
                                                           start=(j == 0), stop=(j == NCH - 1)),
                         reads=[("sqv", 0), "onesf"], writes=[("ps", 5)], sig=True)
                S.op("act", lambda e, n=n: e.activation(out=varb[:, 0:n], in_=PS[4][:, 0:n], func=AF.Square),
                     reads=[("ps", 4)], writes=["varb"])
                S.op("dve", lambda e, n=n: e.tensor_tensor(out=varb[:, 0:n], in0=PS[5][:, 0:n], in1=varb[:, 0:n],
                                                           op=ALU.subtract), reads=[("ps", 5), "varb"], writes=["varb"])
                S.op("act", lambda e, n=n: e.activation(out=varb[:, 0:n], in_=varb[:, 0:n], func=AF.Sqrt,
                                                        bias=epsb[:, 0:1], scale=1.0),
                     reads=["varb", "epsb"], writes=["varb"])
                S.op("dve", lambda e, n=n: e.reciprocal(out=varb[:, 0:n], in_=varb[:, 0:n]),
                     reads=["varb"], writes=["varb"])
                for j in range(NCH):
                    S.op("dve", lambda e, j=j, n=n: e.tensor_tensor(
                        out=V[:, j, 0:n], in0=V[:, j, 0:n], in1=PS[4][:, 0:n], op=ALU.subtract),
                        reads=[("V", j), ("ps", 4)], writes=[("V", j)])
                    S.op("dve", lambda e, j=j, n=n: e.tensor_tensor(
                        out=V[:, j, 0:n], in0=V[:, j, 0:n], in1=varb[:, 0:n], op=ALU.mult),
                        reads=[("V", j), "varb"], writes=[("V", j)])
                    S.op("act", lambda e, j=j, n=n: e.activation(
                        out=HN[:, j, 0:n], in_=V[:, j, 0:n], func=AF.Silu,
                        bias=P("ln_b", j, j + 1), scale=P("ln_g", j, j + 1)),
                        reads=[("V", j), "prm"], writes=[("HN", j)])
                def stage_d(a=a, n=n, w=w, xb=xb, xa=xa):
                    for d in range(NCH):
                        s = slots[d // 2]
                        v2 = Wr[:, s, 4096:6144].rearrange("p (k n) -> p k n", k=NCH)
                        yb = 6 + d % 2
                        for kk in range(NCH):
                            S.op("pe", lambda e, v2=v2, kk=kk, d=d, yb=yb, n=n: e.matmul(
                                PS[yb][:, 0:n], lhsT=v2[:, kk, (d % 2) * 128:(d % 2 + 1) * 128], rhs=HN[:, kk, 0:n],
                                start=(kk == 0), stop=(kk == NCH - 1)),
                                reads=[("W", s), ("HN", kk)], writes=[("ps", yb)], sig=(kk == NCH - 1))
                        S.op("act", lambda e, d=d, n=n, w=w, yb=yb: e.activation(
                            out=sqv[0][:, 0:n], in_=PS[yb][:, 0:n], func=AF.Identity,
                            bias=tabGb[:, d, w:w + 1], scale=tabG[:, 1, d, w:w + 1]),
                            reads=[("ps", yb), ("tabG", 1), "tabGb"], writes=[("sqv", 0)])
                        S.op("dve", lambda e, d=d, xb=xb, xa=xa, n=n: e.tensor_tensor(
                            out=xb[:, d, xa:xa + n], in0=xb[:, d, xa:xa + n], in1=sqv[0][:, 0:n], op=ALU.add),
                            reads=[("sqv", 0)] + xk(w, d, xa, n), writes=xk(w, d, xa, n))
                pend_d.append(stage_d)
            while pend_d:
                pend_d.pop(0)()
            S.barrier()
            A.reset(m)

        def attn_phase(l, H):
            prepass(l, 1, TILES_A, H)
            m = A.mark()
            Qg = A.alloc([128, 2, OWN], BF16)
            KgA = A.alloc([128, TK + CTX], BF16)
            KgB = A.alloc([128, TK + CTX], BF16)
            Vpad = A.alloc([128, 19, 192], BF16)
            Og = [A.alloc([128, 2, 512], BF16) for _ in range(1)]
            PT = [A.alloc([128, 5, 2, 128] if ATT_PIPE else [128, 2, 5, 128], BF16) for _ in range(2)]
            cosb = A.alloc([128, 440], F32)
            sinb = A.alloc([128, 440], F32)
            t1 = A.alloc([128, 440], F32)
            rden = [A.alloc([128, 128], F32) for _ in range(2)]
            mask_lo = cst[:, 0:256].rearrange("p (h q) -> p h q", h=2)
            mask_hi = cst[:, 256:512].rearrange("p (h q) -> p h q", h=2)
            onespad = cst[:, 512:704]
            wa = watt.rearrange("(k p) n -> p k n", p=128)
            wvv = wv_d.rearrange("(k p) n -> p k n", p=128)
            wov = wo_d.rearrange("(j p) n -> p j n", p=128)
            S.op("pool", lambda e: e.memset(Vpad[:], 0.0), writes=["Vpad"])
            S.op("dve", lambda e: e.memset(KgA[64:128, :], 0.0), writes=[("Kg",)])
            S.op("dve", lambda e: e.memset(KgB[0:64, :], 0.0), writes=[("Kg",)])
            cnt = {"st": 0, "od": 0, "og": 0}
            for g in range(4):
                sA = slot_alloc()
                vA = Wr[:, sA, :].rearrange("p (k n) -> p k n", k=NCH)
                load_slot(sA, [lambda e, vA=vA, g=g: e.dma_start(out=vA, in_=wa[:, :, g * 768:(g + 1) * 768])])
                sB = slot_alloc()
                vBo = Wr[:, sB, 0:2048].rearrange("p (j n) -> p j n", j=2)
                vBv = Wr[:, sB, 2048:2560].rearrange("p (k n) -> p k n", k=NCH)
                load_slot(sB, [lambda e, vBo=vBo, g=g: e.dma_start(out=vBo, in_=wov[:, 2 * g:2 * g + 2, :]),
                               lambda e, vBv=vBv, g=g: e.dma_start(out=vBv, in_=wvv[:, :, 64 * g:64 * g + 64])])
                for ti in range(5):
                    a, n = 440 * ti, 440
                    S.dma("sp", [lambda e, a=a, n=n: e.dma_start(out=cosb[:, 0:n], in_=rope_d[0][:, a:a + n]),
                                 lambda e, a=a, n=n: e.dma_start(out=sinb[:, 0:n], in_=rope_d[1][:, a:a + n])],
                          "rp0", writes=["rope"])
                    items = [("q", 0), ("q", 1), ("k", 0)]
                    for ii, (kind, cc) in enumerate(items):
                        if kind == "q":
                            nn = max(0, min(a + n, OWN) - a)
                            base, bsw = cc * 128, 256 + cc * 128
                            dkey = ("Qg", cc)
                        else:
                            nn = max(0, min(a + n, TK) - a)
                            base, bsw = 512, 640
                            dkey = ("Kg",)
                        if nn == 0:
                            continue
                        pa, pb = 2 * (ii % 2), 2 * (ii % 2) + 1
                        for (pp, bb) in ((pa, base), (pb, bsw)):
                            for kk in range(NCH):
                                S.op("pe", lambda e, pp=pp, bb=bb, kk=kk, a=a, nn=nn, vA=vA: e.matmul(
                                    PS[pp][:, 0:nn], lhsT=vA[:, kk, bb:bb + 128], rhs=H[:, kk, a:a + nn],
                                    start=(kk == 0), stop=(kk == NCH - 1)),
                                    reads=[("W", sA)] + hk(kk, a, nn), writes=[("ps", pp)], sig=(kk == NCH - 1))
                        S.op("dve", lambda e, pa=pa, nn=nn: e.tensor_tensor(
                            out=PS[pa][:, 0:nn], in0=PS[pa][:, 0:nn], in1=cosb[:, 0:nn], op=ALU.mult),
                            reads=[("ps", pa), "rope"], writes=[("ps", pa)])
                        S.op("dve", lambda e, pb=pb, nn=nn: e.tensor_tensor(
                            out=t1[:, 0:nn], in0=PS[pb][:, 0:nn], in1=sinb[:, 0:nn], op=ALU.mult),
                            reads=[("ps", pb), "rope"], writes=["t1"])
                        if kind == "q":
                            S.op("dve", lambda e, pa=pa, nn=nn, cc=cc, a=a: e.tensor_tensor(
                                out=Qg[:, cc, a:a + nn], in0=PS[pa][:, 0:nn], in1=t1[:, 0:nn], op=ALU.add),
                                reads=[("ps", pa), "t1"], writes=[dkey])
                        else:
                            S.op("dve", lambda e, pa=pa, nn=nn, a=a: e.tensor_tensor(
                                out=KgA[0:64, a:a + nn], in0=PS[pa][0:64, 0:nn], in1=t1[0:64, 0:nn], op=ALU.add),
                                reads=[("ps", pa), "t1"], writes=[dkey])
                            S.op("dve", lambda e, pa=pa, nn=nn, a=a: e.tensor_tensor(
                                out=KgB[64:128, a:a + nn], in0=PS[pa][64:128, 0:nn], in1=t1[64:128, 0:nn], op=ALU.add),
                                reads=[("ps", pa), "t1"], writes=[dkey])
                for kk in range(NCH):
                    S.op("pe", lambda e, kk=kk, vA=vA: e.matmul(
                        PS[0][:, 0:CTX], lhsT=vA[:, kk, 512:640], rhs=H[:, kk, T:T + CTX],
                        start=(kk == 0), stop=(kk == NCH - 1)),
                        reads=[("W", sA)] + hk(kk, T, CTX), writes=[("ps", 0)], sig=(kk == NCH - 1))
                S.op("act", lambda e: e.activation(out=KgA[0:64, TK:TK + CTX], in_=PS[0][0:64, 0:CTX], func=AF.Identity),
                     reads=[("ps", 0)], writes=[("Kg",)])
                S.op("act", lambda e: e.activation(out=KgB[64:128, TK:TK + CTX], in_=PS[0][64:128, 0:CTX], func=AF.Identity),
                     reads=[("ps", 0)], writes=[("Kg",)])
                for b0 in range(0, 19, 8):
                    nb = min(8, 19 - b0)
                    vb = 6 + (b0 // 8) % 2
                    psv = PS[vb].rearrange("p (b d) -> p b d", d=64)
                    for bi in range(nb):
                        blk = b0 + bi
                        c0 = 128 * blk if blk < 17 else T + 128 * (blk - 17)
                        for kk in range(NCH):
                            S.op("pe", lambda e, psv=psv, bi=bi, kk=kk, c0=c0, vBv=vBv: e.matmul(
                                psv[:, bi, :], lhsT=H[:, kk, c0:c0 + 128], rhs=vBv[:, kk, :],
                                start=(kk == 0), stop=(kk == NCH - 1)),
                                reads=[("W", sB)] + hk(kk, c0, 128), writes=[("ps", vb)],
                                sig=(kk == NCH - 1 and bi == nb - 1))
                    S.op("act", lambda e, psv=psv, b0=b0, nb=nb: e.activation(
                        out=Vpad[:, b0:b0 + nb, 64:128], in_=psv[:, 0:nb, :], func=AF.Identity),
                        reads=[("ps", vb)], writes=["Vpad"])
                its = [(qt, qb, cc) for qt in range(4) for qb in range(4) for cc in range(2)]
                info = {}

                def stage_s(it):
                    qt, qb, cc = its[it]
                    i = 4 * qt + qb
                    sb = cnt["st"] % 2
                    cnt["st"] += 1
                    KBM = ATT_PIPE
                    if KBM:
                        stv = PSALL[:, sb * 1536:sb * 1536 + 1280].rearrange("p (k h q) -> p k h q", k=5, h=2)
                    else:
                        stv = PSALL[:, sb * 1536:sb * 1536 + 1280].rearrange("p (h k q) -> p h k q", h=2, k=5)
                    stkeys = [("ps", sb * 3 + z) for z in range(3)]
                    k0 = 1 if i == 0 else 0
                    srcs = []
                    for kbi in range(k0, 5):
                        if kbi < 3:
                            blk = i - 1 + kbi
                            srcs.append((kbi, 128 * blk, blk))
                        else:
                            srcs.append((kbi, TK + 128 * (kbi - 3), 17 + kbi - 3))
                    nmm = 2 * len(srcs)
                    PEM = (ATT_MASK == "pe")
                    z = 0
                    for h in (1, 0):
                        for (kbi, kc, vbk) in srcs:
                            z += 1
                            msk = PEM and kbi in (0, 2)
                            S.op("pe", lambda e, stv=stv, h=h, kbi=kbi, kc=kc, cc=cc, i=i, KBM=KBM, msk=msk: e.matmul(
                                stv[:, kbi, h, :] if KBM else stv[:, h, kbi, :],
                                lhsT=(KgA if h == 0 else KgB)[:, kc:kc + 128],
                                rhs=Qg[:, cc, 128 * i:128 * i + 128], start=True, stop=(not msk)),
                                reads=[("Kg",), ("Qg", cc)], writes=stkeys, sig=(z == nmm and not msk))
                            if msk:
                                mb = cst[:, 832:960] if kbi == 0 else cst[:, 960:1088]
                                S.op("pe", lambda e, stv=stv, h=h, kbi=kbi, KBM=KBM, mb=mb: e.matmul(
                                    stv[:, kbi, h, :] if KBM else stv[:, h, kbi, :],
                                    lhsT=cst[:, 704:832], rhs=mb, start=False, stop=True),
                                    reads=["cst"], writes=stkeys, sig=(z == nmm))
                    pb = sb
                    if KBM:
                        for (ka, kb_) in ((k0, 5),):
                            S.op("act", lambda e, stv=stv, pb=pb, ka=ka, kb_=kb_: e.activation(
                                out=PT[pb][:, ka:kb_, :, :], in_=stv[:, ka:kb_, :, :], func=AF.Exp, scale=0.125),
                                reads=stkeys, writes=[("PT", pb)])
                        if i > 0 and not PEM:
                            S.op(ATT_MASK, lambda e, pb=pb: e.tensor_tensor(
                                out=PT[pb][:, 0, :, :], in0=PT[pb][:, 0, :, :], in1=mask_lo, op=ALU.mult),
                                reads=[("PT", pb), "cst"], writes=[("PT", pb)])
                        if not PEM:
                            S.op(ATT_MASK, lambda e, pb=pb: e.tensor_tensor(
                                out=PT[pb][:, 2, :, :], in0=PT[pb][:, 2, :, :], in1=mask_hi, op=ALU.mult),
                                reads=[("PT", pb), "cst"], writes=[("PT", pb)])
                    else:
                        S.op("act", lambda e, stv=stv, pb=pb, k0=k0: e.activation(
                            out=PT[pb][:, :, k0:5, :], in_=stv[:, :, k0:5, :], func=AF.Exp, scale=0.125),
                            reads=stkeys, writes=[("PT", pb)])
                        if i > 0 and not PEM:
                            S.op(ATT_MASK, lambda e, pb=pb: e.tensor_tensor(
                                out=PT[pb][:, :, 0, :], in0=PT[pb][:, :, 0, :], in1=mask_lo, op=ALU.mult),
                                reads=[("PT", pb), "cst"], writes=[("PT", pb)])
                        if not PEM:
                            S.op(ATT_MASK, lambda e, pb=pb: e.tensor_tensor(
                                out=PT[pb][:, :, 2, :], in0=PT[pb][:, :, 2, :], in1=mask_hi, op=ALU.mult),
                                reads=[("PT", pb), "cst"], writes=[("PT", pb)])
                    info[it] = (pb, srcs, nmm)

                def stage_p(it):
                    qt, qb, cc = its[it]
                    pb, srcs, nmm = info.pop(it)
                    ob = 0
                    odb = cnt["od"] % 2
                    cnt["od"] += 1
                    KBM = ATT_PIPE
                    if KBM:
                        o_ps = PS[6 + odb][:, 0:128]
                        d_ps = PS[6 + odb][:, 128:256]
                        odk = [("ps", 6 + odb)]
                    else:
                        o_ps = PS[6][:, (2 * odb) * 128:(2 * odb + 1) * 128]
                        d_ps = PS[6][:, (2 * odb + 1) * 128:(2 * odb + 2) * 128]
                        odk = []
                    for (tgt, is_den) in ((o_ps, False), (d_ps, True)):
                        z = 0
                        for h in range(2):
                            c_lo = 64 if h == 0 else 0
                            for (kbi, kc, vbk) in srcs:
                                z += 1
                                if is_den:
                                    lh = onespad[:, c_lo:c_lo + 128]
                                else:
                                    lh = Vpad[:, vbk, c_lo:c_lo + 128]
                                S.op("pe", lambda e, tgt=tgt, lh=lh, pb=pb, h=h, kbi=kbi, z=z, nmm=nmm, KBM=KBM: e.matmul(
                                    tgt, lhsT=lh, rhs=(PT[pb][:, kbi, h, :] if KBM else PT[pb][:, h, kbi, :]),
                                    start=(z == 1), stop=(z == nmm)),
                                    reads=[("PT", pb), "Vpad", "cst"], writes=[("ps6", odb, is_den)] + odk,
                                    sig=(z == nmm))
                    S.op("dve", lambda e, d_ps=d_ps, odb=odb, cc=cc, g=g: e.tensor_scalar(
                        out=rden[odb][:], in0=d_ps, scalar1=esink[:, 2 * g + cc:2 * g + cc + 1], scalar2=None,
                        op0=ALU.add), reads=[("ps6", odb, True), "esink"] + odk, writes=[("rden", odb)])
                    if ATT_DIV:
                        S.op("dve", lambda e, o_ps=o_ps, odb=odb, ob=ob, cc=cc, qb=qb: e.tensor_tensor(
                            out=Og[ob][:, cc, qb * 128:(qb + 1) * 128], in0=o_ps, in1=rden[odb][:], op=ALU.divide),
                            reads=[("ps6", odb, False), ("rden", odb)] + odk, writes=[("Og", ob, cc)])
                    else:
                        S.op("dve", lambda e, odb=odb: e.reciprocal(out=rden[odb][:], in_=rden[odb][:]),
                             reads=[("rden", odb)], writes=[("rden", odb)])
                        S.op("dve", lambda e, o_ps=o_ps, odb=odb, ob=ob, cc=cc, qb=qb: e.tensor_tensor(
                            out=Og[ob][:, cc, qb * 128:(qb + 1) * 128], in0=o_ps, in1=rden[odb][:], op=ALU.mult),
                            reads=[("ps6", odb, False), ("rden", odb)] + odk, writes=[("Og", ob, cc)])
                    if qb == 3 and cc == 1 and not ATT_EXP:
                        for d in range(NCH):
                            if KBM:
                                for hf in range(2):
                                    ybk = 2 + 3 * hf
                                    yv = PSALL[:, ybk * 512 + 256:ybk * 512 + 512]
                                    for c2 in range(2):
                                        S.op("pe", lambda e, d=d, c2=c2, ob=ob, vBo=vBo, yv=yv, hf=hf: e.matmul(
                                            yv, lhsT=vBo[:, c2, d * 128:(d + 1) * 128],
                                            rhs=Og[ob][:, c2, hf * 256:(hf + 1) * 256],
                                            start=(c2 == 0), stop=(c2 == 1)),
                                            reads=[("W", sB), ("Og", ob, c2)], writes=[("ps", ybk)], sig=(c2 == 1))
                                    x0 = qt * 512 + hf * 256
                                    S.op("dve", lambda e, d=d, x0=x0, yv=yv: e.scalar_tensor_tensor(
                                        out=X[:, d, x0:x0 + 256], in0=yv, scalar=tabG[:, 1, d, 0:1],
                                        in1=X[:, d, x0:x0 + 256], op0=ALU.mult, op1=ALU.add),
                                        reads=[("ps", ybk), ("tabG", 1)] + xk(0, d, x0, 256),
                                        writes=xk(0, d, x0, 256))
                                continue
                            yb = 7
                            for c2 in range(2):
                                S.op("pe", lambda e, d=d, c2=c2, ob=ob, vBo=vBo, yb=yb: e.matmul(
                                    PS[yb][:, 0:512], lhsT=vBo[:, c2, d * 128:(d + 1) * 128], rhs=Og[ob][:, c2, :],
                                    start=(c2 == 0), stop=(c2 == 1)),
                                    reads=[("W", sB), ("Og", ob, c2)], writes=[("ps", yb)], sig=(c2 == 1))
                            S.op("dve", lambda e, d=d, qt=qt, yb=yb: e.scalar_tensor_tensor(
                                out=X[:, d, qt * 512:(qt + 1) * 512], in0=PS[yb][:, 0:512], scalar=tabG[:, 1, d, 0:1],
                                in1=X[:, d, qt * 512:(qt + 1) * 512], op0=ALU.mult, op1=ALU.add),
                                reads=[("ps", yb), ("tabG", 1)] + xk(0, d, qt * 512, 512),
                                writes=xk(0, d, qt * 512, 512))

                if ATT_PIPE:
                    stage_s(0)
                    for it in range(len(its)):
                        if it + 1 < len(its):
                            stage_s(it + 1)
                        stage_p(it)
                else:
                    for it in range(len(its)):
                        stage_s(it)
                        stage_p(it)
            S.barrier()
            A.reset(m)

        H = A.alloc([128, NCH, TT], BF16)
        NOT = lambda *names: stop_after not in names
        if ONLY_ATTN:
            m0 = A.mark()
            ad0 = [A.alloc([128, NCH, 256], BF16) for _ in range(2)]
            ada_run(1, list(range(36)), ad0)
            for jj_ in range(3):
                ada_tables(1, jj_)
            S.barrier()
            A.reset(m0)
            attn_phase(1, H)
            stop_after = "x_ffn1_0"
        else:
            m0 = A.mark()
            ad0 = [A.alloc([128, NCH, 256], BF16) for _ in range(2)]
            ada_run(0, list(range(12)), ad0)
            ada_tables(0, 0)
            S.barrier()
            A.reset(m0)
            ffn(0, 0, TILES_A, H, ada_job=(0, list(range(12, 36))))
            ada_tables(0, 1)
            ada_tables(0, 2)
        if NOT("x_ffn1_0"):
            conv_phase(0, H)
        if NOT("x_ffn1_0", "x_mix_0"):
            ffn(0, 1, TILES_A, H, ada_job=(1, list(range(36))))
        if NOT("x_ffn1_0", "x_mix_0", "x_out_0"):
            for jj_ in range(3):
                ada_tables(1, jj_)
            ffn(1, 0, TILES_A, H)
        if NOT("x_ffn1_0", "x_mix_0", "x_out_0", "x_ffn1_1"):
            attn_phase(1, H)
        if NOT("x_ffn1_0", "x_mix_0", "x_out_0", "x_ffn1_1", "x_mix_1"):
            ffn(1, 1, TILES_OWN, H)

        def final_out(do_norm):
            S.barrier()
            m = A.mark()
            A.reset(0)
            stg = [A.alloc([128, NCH, 512], F32) for _ in range(2)]
            sq = [A.alloc([128, 512], F32) for _ in range(3)]
            rstd = [A.alloc([128, 512], F32) for _ in range(2)]
            rsq = [A.alloc([128, 512], F32) for _ in range(2)]
            oT_v = outT.rearrange("(c p) t -> p c t", p=128)
            toks = []
            n_sq = 0
            for ti, (a, n, w) in enumerate(TILES_OWN):
                b = ti % 2
                if do_norm:
                    msb = 6 + b
                    ms = PS[msb][:, 0:n]
                    for c in range(NCH):
                        q = n_sq % 3
                        n_sq += 1
                        S.op("act", lambda e, q=q, c=c, a=a, n=n: e.activation(
                            out=sq[q][:, 0:n], in_=X[:, c, a:a + n], func=AF.Square),
                            reads=xk(0, c, a, n), writes=[("sq", q)])
                        S.op("pe", lambda e, q=q, c=c, ms=ms, n=n: e.matmul(
                            ms, lhsT=onesf[:], rhs=sq[q][:, 0:n], start=(c == 0), stop=(c == NCH - 1)),
                            reads=[("sq", q), "onesf"], writes=[("ps", msb)], sig=True)
                    S.op("act", lambda e, b=b, ms=ms, n=n: e.activation(
                        out=rsq[b][:, 0:n], in_=ms, func=AF.Sqrt, bias=epsb[:, 0:1], scale=1.0),
                        reads=[("ps", msb), "epsb"], writes=[("rsq", b)])
                    S.op("dve", lambda e, b=b, n=n: e.reciprocal(out=rstd[b][:, 0:n], in_=rsq[b][:, 0:n]),
                         reads=[("rsq", b)], writes=[("rstd", b)])
                    for c in range(NCH):
                        S.op("dve", lambda e, c=c, a=a, n=n, b=b: e.scalar_tensor_tensor(
                            out=stg[b][:, c, 0:n], in0=X[:, c, a:a + n], scalar=P("final_g", c, c + 1),
                            in1=rstd[b][:, 0:n], op0=ALU.mult, op1=ALU.mult),
                            reads=xk(0, c, a, n) + [("rstd", b), "prm"], writes=[("stg", b)],
                            sig=True)
                else:
                    for c in range(NCH):
                        S.op("act", lambda e, c=c, a=a, n=n, b=b: e.activation(
                            out=stg[b][:, c, 0:n], in_=X[:, c, a:a + n], func=AF.Identity),
                            reads=xk(0, c, a, n), writes=[("stg", b)])
                toks.append(S.dma("sp", [lambda e, b=b, a=a, n=n: e.dma_start(
                    out=oT_v[:, :, a:a + n], in_=stg[b][:, :, 0:n])], f"st{b}", reads=[("stg", b)]))
            S.wait_tokens("sp", toks)
            A.reset(m)

        final_out(stop_after is None)

        S.finalize()
        with nc.Block() as block:
            @block.tensor
            def _(e):
                S.emit("pe", e)

            @block.scalar
            def _(e):
                S.emit("act", e)

            @block.vector
            def _(e):
                S.emit("dve", e)

            @block.gpsimd
            def _(e):
                S.emit("pool", e)

            @block.sync
            def _(e):
                S.emit("sp", e)
    return nc


def _partner_cols(n):
    j = np.arange(n)
    d = j % 64
    pd = np.where((d % 32) < 16, d + 16, d - 16)
    return j - d + pd


def prepare_inputs(inp):
    f = lambda a: np.ascontiguousarray(np.asarray(a, dtype=np.float32))
    x, c, ctx, c_ctx = f(inp["x"]), f(inp["c"]), f(inp["ctx"]), f(inp["c_ctx"])
    w_qkv = f(inp["attn_w_qkv"])[0]
    chunk = lambda v: np.ascontiguousarray(v.reshape(-1, 128).T)
    shared = {
        "ada_w": f(inp["ada_w"]), "ffn1_wi": f(inp["ffn1_wi"]), "ffn1_wo": f(inp["ffn1_wo"]),
        "ffn2_wi": f(inp["ffn2_wi"]), "ffn2_wo": f(inp["ffn2_wo"]),
        "pw1_w": f(inp["conv_pw1_w"])[0], "pw2_w": f(inp["conv_pw2_w"])[0],
        "w_v": np.ascontiguousarray(w_qkv[:, 1280:1536]), "w_o": f(inp["attn_w_o"])[0],
    }
    watt = np.zeros((D, 4 * 768), np.float32)
    for g in range(4):
        q = w_qkv[:, 256 * g:256 * g + 256]
        k = w_qkv[:, 1024 + 64 * g:1024 + 64 * g + 64]
        kd = np.concatenate([k, k], axis=1)
        o = 768 * g
        watt[:, o:o + 256] = q
        watt[:, o + 256:o + 512] = q[:, _partner_cols(256)]
        watt[:, o + 512:o + 640] = kd
        watt[:, o + 640:o + 768] = kd[:, _partner_cols(128)]
    shared["w_att"] = watt
    cst = np.zeros((128, NCST), np.float32)
    kk = np.arange(128)[:, None]
    qq = np.arange(128)[None, :]
    lo = (kk >= qq).astype(np.float32)
    hi = (kk <= qq).astype(np.float32)
    cst[:, 0:128] = lo
    cst[:, 128:256] = lo
    cst[:, 256:384] = hi
    cst[:, 384:512] = hi
    cst[:, 512 + 64:512 + 128] = 1.0
    cst[:, 704:832] = np.eye(128, dtype=np.float32)
    cst[:, 832:960] = (lo - 1.0) * 30000.0
    cst[:, 960:1088] = (hi - 1.0) * 30000.0
    shared["cst"] = cst
    p = np.arange(128)
    d = p % 64
    ax = d // 32
    hh = (d % 32) // 16
    fr = d % 16
    inv = (np.float32(10000.0) ** (-np.arange(16, dtype=np.float32) / np.float32(16))).astype(np.float32)
    in_maps = []
    for core in range(8):
        b, hf = core // 2, core % 2
        idx = np.arange(T) if hf == 0 else (SEQ - 1 - np.arange(T))
        m = dict(shared)
        m["xT"] = np.ascontiguousarray(x[b, idx, :].T)
        cb = ctx[b] if hf == 0 else ctx[b, ::-1]
        m["cT"] = np.ascontiguousarray(cb.T)
        prm = np.zeros((128, NPRM), np.float32)

        def put(name, arr):
            o, w = PRM[name]
            assert arr.shape == (128, w), (name, arr.shape)
            prm[:, o:o + w] = arr
        cc = np.stack([chunk(c[b]), chunk(c_ctx)], axis=2).reshape(128, 16)
        put("cc", cc)
        put("ada_b", np.concatenate([chunk(f(inp["ada_b"])[l]) for l in range(2)], axis=1))
        put("norm_g", np.concatenate([chunk(f(inp["norm_g"])[l, j]) for l in range(2) for j in range(3)], axis=1))
        put("pw1_b", chunk(f(inp["conv_pw1_b"])[0]))
        dw = f(inp["conv_dw_w"])[0]
        if hf == 1:
            dw = dw[::-1]
        put("dw_w", np.ascontiguousarray(dw.T.reshape(NCH, 128, CW).transpose(1, 0, 2)).reshape(128, NCH * CW))
        put("dw_b", chunk(f(inp["conv_dw_b"])[0]))
        put("ln_g", chunk(f(inp["conv_ln_g"])[0]))
        put("ln_b", chunk(f(inp["conv_ln_b"])[0]))
        put("pw2_b", chunk(f(inp["conv_pw2_b"])[0]))
        sk = f(inp["attn_sink"])[0]
        put("sink", np.stack([np.where(p >= 64, sk[2 * cch + 1], sk[2 * cch]) for cch in range(NCH)], axis=1).astype(np.float32))
        put("final_g", chunk(f(inp["final_g"])))
        m["prm"] = prm
        row = (idx // 64).astype(np.float32)
        col = (idx % 64).astype(np.float32)
        pos = np.where(ax[:, None] == 0, row[None, :], col[None, :]).astype(np.float32)
        ang = (pos * inv[fr][:, None]).astype(np.float32)
        cs = np.cos(ang).astype(np.float32)
        sn = np.sin(ang).astype(np.float32)
        sn = np.where(hh[:, None] == 0, -sn, sn).astype(np.float32)
        m["rope"] = np.ascontiguousarray(np.stack([cs, sn], axis=0))
        in_maps.append(m)
    return in_maps


def assemble(results):
    out = np.zeros((4, SEQ, D), np.float32)
    for core in range(8):
        b, hf = core // 2, core % 2
        idx = np.arange(OWN) if hf == 0 else (SEQ - 1 - np.arange(OWN))
        out[b, idx, :] = np.asarray(results[core]["outT"]).T
    return out


def kernel(**inputs):
    nc = build_program()
    in_maps = prepare_inputs(inputs)
    res = run_bass_kernel_spmd(nc, in_maps, core_ids=list(range(8)))
    return assemble(res.results)
```

```python
import numpy as np
import concourse.bass as bass
import concourse.mybir as mybir
from concourse.bass_utils import run_bass_kernel_spmd

F32 = mybir.dt.float32
BF16 = mybir.dt.bfloat16
AF = mybir.ActivationFunctionType
ALU = mybir.AluOpType

D = 1024
NCH = 8
SEQ = 4096
OWN = 2048
T = 2200
TK = 2176
CTX = 256
TT = T + CTX
DFF = 2816
NFC = 22
CW = 31
EPS = 1e-6
SLOT = 6144
import os
ATT_PIPE = int(os.environ.get('ATT_PIPE', '1'))
ATT_MASK = os.environ.get('ATT_MASK', 'dve')
SAME_ENG = int(os.environ.get('SAME_ENG', '1'))
ONLY_ATTN = int(os.environ.get('ONLY_ATTN', '0'))
ATT_EXP = int(os.environ.get('ATT_EXP', '0'))
ATT_DIV = int(os.environ.get('ATT_DIV', '0'))
NSLOT = 4

TILES_A = [(i * 440, 440, 0) for i in range(5)] + [(T, 256, 1)]
TILES_OWN = [(i * 512, 512, 0) for i in range(4)]

PRM = {}
_o = 0
for _n, _w in [("cc", 16), ("ada_b", 144), ("norm_g", 48), ("pw1_b", 16), ("dw_w", 248), ("dw_b", 8),
               ("ln_g", 8), ("ln_b", 8), ("pw2_b", 8), ("sink", 8), ("final_g", 8)]:
    PRM[_n] = (_o, _w)
    _o += _w
NPRM = _o
NCST = 2 * 256 + 192 + 128 + 256

_XB = sorted(set([i * 440 for i in range(6)] + [i * 512 for i in range(5)] + [T]))


def _segs(a, b):
    return [i for i in range(len(_XB) - 1) if _XB[i] < b and _XB[i + 1] > a]


class Sched:
    CE = ("pe", "act", "dve", "pool")
    NEPOCH = 16

    def __init__(self, nc, sems):
        self.nc = nc
        self.sems = sems
        self.cnt = {k: 0 for k in sems}
        self.ops = []
        self.byeng = {e: [] for e in ("pe", "act", "dve", "pool", "sp")}
        self.last_w = {}
        self.readers = {}
        self.unsig = {e: [] for e in self.byeng}
        self.redirect = {}
        self.epoch = 0
        self.last_sig = {}

    def _deps(self, reads, writes):
        deps = []
        for b in reads:
            t = self.last_w.get(b)
            if t is not None:
                deps.append((t, True))
        for b in writes:
            t = self.last_w.get(b)
            if t is not None:
                deps.append((t, False))
            for t in self.readers.get(b, ()):
                deps.append((t, False))
        return deps

    def _record(self, tok, reads, writes):
        for b in reads:
            self.readers.setdefault(b, []).append(tok)
        for b in writes:
            self.last_w[b] = tok
            self.readers[b] = []

    def _new(self, eng, fns, deps, kind, sig=True, chan=None):
        rec = dict(id=len(self.ops), eng=eng, epoch=self.epoch, fns=fns, deps=deps, kind=kind, sig=sig, chan=chan)
        self.ops.append(rec)
        self.byeng[eng].append(rec)
        return rec

    def op(self, eng, fn, reads=(), writes=(), sig=True):
        rec = self._new(eng, [fn], self._deps(reads, writes), "op", sig=sig)
        if sig:
            for i in self.unsig[eng]:
                self.redirect[i] = rec["id"]
            self.unsig[eng] = []
            self.last_sig[eng] = rec["id"]
        else:
            self.unsig[eng].append(rec["id"])
        self._record(("op", rec["id"]), reads, writes)

    def dma(self, eng, fns, chan, reads=(), writes=()):
        rec = self._new(eng, list(fns), self._deps(reads, writes), "dma", chan=chan)
        self.cnt[chan] += 16 * len(fns)
        tok = ("dma", chan, self.cnt[chan])
        self._record(tok, reads, writes)
        return tok

    def wait_tokens(self, eng, toks):
        self._new(eng, [], [(t, True) for t in toks], "wait")

    def barrier(self):
        for e in self.CE:
            assert not self.unsig[e], e
        toks = {e: ("op", self.last_sig[e]) for e in self.CE
                if e in self.last_sig and self.ops[self.last_sig[e]]["epoch"] == self.epoch}
        for e in ("pe", "act", "dve", "pool", "sp"):
            self.wait_tokens(e, [t for k, t in toks.items() if k != e or e != "pe"])
        self.epoch += 1
        assert self.epoch < self.NEPOCH

    def finalize(self):
        for e in self.CE:
            assert not self.unsig[e], e

        def resolve(tok, rec, raw):
            if tok[0] == "dma":
                return tok
            i = self.redirect.get(tok[1], tok[1])
            d = self.ops[i]
            if d["epoch"] != rec["epoch"]:
                return None
            if d["eng"] == rec["eng"]:
                if rec["eng"] in ("pe", "sp"):
                    return None
                if not SAME_ENG and not raw:
                    return None
            return ("op", i)
        awaited = set()
        for rec in self.ops:
            r = []
            for tok, raw in rec["deps"]:
                t = resolve(tok, rec, raw)
                if t is not None:
                    r.append(t)
                    if t[0] == "op":
                        awaited.add(t[1])
            rec["rdeps"] = r
        val = {}
        cnt = {}
        for e in self.CE:
            for rec in self.byeng[e]:
                if rec["kind"] == "op" and rec["id"] in awaited:
                    k = f"{e}@{rec['epoch']}"
                    cnt[k] = cnt.get(k, 0) + 1
                    val[rec["id"]] = (k, cnt[k])
        self.prog = {}
        self.nsig = len(val)
        for e, recs in self.byeng.items():
            waited = {}
            out = []
            for rec in recs:
                w = {}
                for t in rec["rdeps"]:
                    k, v = val[t[1]] if t[0] == "op" else (t[1], t[2])
                    if w.get(k, 0) < v:
                        w[k] = v
                wl = []
                for k, v in w.items():
                    if waited.get(k, 0) < v:
                        waited[k] = v
                        wl.append((k, v))
                if rec["kind"] == "dma":
                    incs = [(rec["chan"], 16)] * len(rec["fns"])
                elif rec["id"] in val:
                    incs = [(val[rec["id"]][0], 1)]
                else:
                    incs = []
                out.append((wl, rec["fns"], incs))
            self.prog[e] = out

    def emit(self, eng, e):
        for waits, fns, incs in self.prog[eng]:
            for k, v in waits:
                e.wait_ge(self.sems[k], v)
            for i, fn in enumerate(fns):
                ins = fn(e)
                if i < len(incs):
                    ins.then_inc(self.sems[incs[i][0]], incs[i][1])


class Arena:
    def __init__(self, ap_f32, nbytes):
        self.ap = ap_f32
        self.n = nbytes
        self.off = 0

    def mark(self):
        return self.off

    def reset(self, m):
        self.off = m

    def alloc(self, shape, dt):
        es = 2 if dt == BF16 else 4
        free = 1
        for s in shape[1:]:
            free *= s
        nb = (free * es + 31) // 32 * 32
        assert self.off + nb <= self.n, ("arena overflow", self.off, nb, self.n)
        w0 = self.off // 4
        v = self.ap[:, w0:w0 + nb // 4]
        self.off += nb
        if dt == BF16:
            v = v.bitcast(BF16)
        v = v[:, 0:free]
        if len(shape) == 3:
            v = v.rearrange("p (a b) -> p a b", a=shape[1])
        elif len(shape) == 4:
            v = v.rearrange("p (a b c) -> p a b c", a=shape[1], b=shape[2])
        return v


def build_program(stop_after=None):
    nc = bass.Bass("TRN2", target_bir_lowering=False)
    dr = {}

    def din(name, shape):
        dr[name] = nc.dram_tensor(name, shape, F32, kind="ExternalInput").ap()
        return dr[name]

    xT = din("xT", [D, T])
    cT = din("cT", [D, CTX])
    prm_d = din("prm", [128, NPRM])
    cst_d = din("cst", [128, NCST])
    rope_d = din("rope", [2, 128, T])
    ada_w = din("ada_w", [2, D, 9 * D])
    f_wi = [din("ffn1_wi", [2, D, 2 * DFF]), din("ffn2_wi", [2, D, 2 * DFF])]
    f_wo = [din("ffn1_wo", [2, DFF, D]), din("ffn2_wo", [2, DFF, D])]
    pw1_w = din("pw1_w", [D, 2 * D])
    pw2_w = din("pw2_w", [D, D])
    watt = din("w_att", [D, 4 * 768])
    wv_d = din("w_v", [D, 256])
    wo_d = din("w_o", [D, D])
    outT = nc.dram_tensor("outT", [D, OWN], F32, kind="ExternalOutput").ap()

    semkeys = [f"{e}@{i}" for e in Sched.CE for i in range(Sched.NEPOCH)] + ["ldp", "ldc", "ldxc", "st0", "st1", "rp0", "rp1", "ad0", "ad1"] + \
              [f"ldx{i}" for i in range(5)] + [f"w{i}" for i in range(NSLOT)]

    import contextlib
    es = contextlib.ExitStack()
    with es:
        sems = {k: es.enter_context(nc.semaphore(k)) for k in semkeys}
        X = es.enter_context(nc.sbuf_tensor("X", [128, NCH, T], F32))
        XC = es.enter_context(nc.sbuf_tensor("XC", [128, NCH, CTX], F32))
        Wr = es.enter_context(nc.sbuf_tensor("Wr", [128, NSLOT, SLOT], BF16))
        prm = es.enter_context(nc.sbuf_tensor("prm_sb", [128, NPRM], F32))
        cst = es.enter_context(nc.sbuf_tensor("cst_sb", [128, NCST], BF16))
        mod = es.enter_context(nc.sbuf_tensor("mod", [128, 2, 72, 2], F32))
        tabA = es.enter_context(nc.sbuf_tensor("tabA", [128, 3, NCH, 2], F32))
        tabG = es.enter_context(nc.sbuf_tensor("tabG", [128, 3, NCH, 2], F32))
        tabGb = es.enter_context(nc.sbuf_tensor("tabGb", [128, NCH, 2], F32))
        scT = es.enter_context(nc.sbuf_tensor("scT", [128, NCH, 2], BF16))
        onesf = es.enter_context(nc.sbuf_tensor("onesf", [128, 128], F32))
        esink = es.enter_context(nc.sbuf_tensor("esink", [128, NCH], F32))
        epsb = es.enter_context(nc.sbuf_tensor("epsb", [128, 1], F32))
        ARENA_BYTES = 212863 - (NCH * T * 4 + NCH * CTX * 4 + NSLOT * SLOT * 2 + NPRM * 4 + NCST * 2
                                + 2 * 72 * 2 * 4 + 2 * 3 * NCH * 2 * 4 + NCH * 2 * 4 + NCH * 2 * 2
                                + 128 * 4 + NCH * 4) - 384
        ARENA_BYTES = ARENA_BYTES // 64 * 64
        ar_t = es.enter_context(nc.sbuf_tensor("arena", [128, ARENA_BYTES // 4], F32))
        PSALL = es.enter_context(nc.psum_tensor("psall", [128, 4096], F32))
        PS = [PSALL[:, i * 512:(i + 1) * 512] for i in range(8)]
        S = Sched(nc, sems)
        A = Arena(ar_t, ARENA_BYTES)

        def P(name, a=0, b=None):
            o, w = PRM[name]
            b = w if b is None else b
            return prm[:, o + a:o + b]

        def xbuf(which):
            return X if which == 0 else XC

        def xk(which, c, a, n):
            if which == 1:
                return [("XC", c)]
            return [("X", c, s) for s in _segs(a, a + n)]

        def hk(c, a, n):
            if a >= T:
                return [("H", c, "c")]
            return [("H", c, s) for s in _segs(a, a + n)]

        S.dma("sp", [lambda e: e.dma_start(out=prm[:], in_=prm_d[:, :])], "ldp", writes=["prm"])
        S.dma("pool", [lambda e: e.dma_start(out=cst[:], in_=cst_d[:, :])], "ldc", writes=["cst"])
        xT_v = xT.rearrange("(c p) t -> p c t", p=128)
        cT_v = cT.rearrange("(c p) t -> p c t", p=128)
        for i in range(5):
            S.dma("sp", [lambda e, i=i: e.dma_start(out=X[:, :, i * 440:(i + 1) * 440],
                                                   in_=xT_v[:, :, i * 440:(i + 1) * 440])],
                  f"ldx{i}", writes=[k for c in range(NCH) for k in xk(0, c, i * 440, 440)])
        S.dma("sp", [lambda e: e.dma_start(out=XC[:], in_=cT_v)], "ldxc",
              writes=[("XC", c) for c in range(NCH)])
        S.op("dve", lambda e: e.memset(onesf[:], 1.0 / D), writes=["onesf"])
        S.op("dve", lambda e: e.memset(epsb[:], EPS), writes=["epsb"])
        S.op("act", lambda e: e.activation(out=scT[:], in_=P("cc").rearrange("p (k w) -> p k w", w=2),
                                           func=AF.Silu), reads=["prm"], writes=["scT"])
        S.op("act", lambda e: e.activation(out=esink[:], in_=P("sink"), func=AF.Exp),
             reads=["prm"], writes=["esink"])

        ring = {"next": 0}

        def slot_alloc():
            s = ring["next"]
            ring["next"] = (s + 1) % NSLOT
            return s

        def load_slot(s, fns):
            S.dma("pool", fns, f"w{s}", writes=[("W", s)])

        def ada_unit(l, u, stg, sb):
            aw = ada_w[l].rearrange("(k p) n -> p k n", p=128)
            S.dma("pool", [lambda e, u=u, stg=stg: e.dma_start(out=stg, in_=aw[:, :, u * 256:(u + 1) * 256])],
                  f"ad{sb}", writes=[("adst", sb)])

            def mm():
                pb = 6 + (u % 2)
                ps = PS[pb][:, 504:508].rearrange("p (a b) -> p a b", b=2)
                for oc in range(2):
                    for k in range(NCH):
                        S.op("pe", lambda e, oc=oc, k=k, ps=ps, stg=stg: e.matmul(
                            ps[:, oc, :], lhsT=stg[:, k, oc * 128:(oc + 1) * 128], rhs=scT[:, k, :],
                            start=(k == 0), stop=(k == NCH - 1)),
                            reads=[("adst", sb), "scT"], writes=[("ps", pb)], sig=(k == NCH - 1))
                ab = P("ada_b", l * 72 + u * 2, l * 72 + u * 2 + 2)
                S.op("dve", lambda e, ps=ps, ab=ab, u=u: e.tensor_tensor(
                    out=mod[:, l, u * 2:(u + 1) * 2, :], in0=ps,
                    in1=ab.unsqueeze(2).to_broadcast([128, 2, 2]), op=ALU.add),
                    reads=[("ps", pb), "prm"], writes=[("mod", l, u // 12)])
            return mm

        def ada_tables(l, j):
            g = P("norm_g", (l * 3 + j) * 8, (l * 3 + j) * 8 + 8)
            S.op("dve", lambda e, j=j, g=g: e.scalar_tensor_tensor(
                out=tabA[:, j], in0=mod[:, l, (3 * j + 1) * 8:(3 * j + 2) * 8, :], scalar=1.0,
                in1=g.unsqueeze(2).to_broadcast([128, NCH, 2]), op0=ALU.add, op1=ALU.mult),
                reads=[("mod", l, j), "prm"], writes=[("tabA", j)])
            S.op("dve", lambda e, j=j: e.tensor_scalar(
                out=tabG[:, j], in0=mod[:, l, (3 * j + 2) * 8:(3 * j + 3) * 8, :],
                scalar1=(1.0 if j == 1 else 0.5), scalar2=None, op0=ALU.mult),
                reads=[("mod", l, j)], writes=[("tabG", j)])
            if l == 0 and j == 1:
                S.op("dve", lambda e: e.tensor_tensor(
                    out=tabGb[:], in0=tabG[:, 1], in1=P("pw2_b").unsqueeze(2).to_broadcast([128, NCH, 2]),
                    op=ALU.mult), reads=[("tabG", 1), "prm"], writes=["tabGb"])

        def ada_run(l, units, adst):
            pend = []
            for i, u in enumerate(units):
                pend.append(ada_unit(l, u, adst[i % 2], i % 2))
                if len(pend) == 2:
                    pend.pop(0)()
            for f_ in pend:
                f_()

        def tabB(l, j):
            return mod[:, l, (3 * j) * 8:(3 * j + 1) * 8, :]

        def make_prepass(l, j, tiles, H, nbuf=3):
            sq = [A.alloc([128, 512], F32) for _ in range(nbuf)]
            rstd = [A.alloc([128, 512], F32) for _ in range(2)]
            rsq = [A.alloc([128, 512], F32) for _ in range(2)]
            tmp = [A.alloc([128, 512], F32) for _ in range(nbuf)]
            st_ = {"sq": 0, "tmp": 0}

            def pre_tile(ti):
                a, n, w = tiles[ti]
                xb = xbuf(w)
                xa = a - T if w == 1 else a
                msb = 6 + (ti % 2)
                ms = PS[msb][:, 0:n]
                for c in range(NCH):
                    q = st_["sq"] % nbuf
                    st_["sq"] += 1
                    S.op("act", lambda e, q=q, c=c, xb=xb, xa=xa, n=n: e.activation(
                        out=sq[q][:, 0:n], in_=xb[:, c, xa:xa + n], func=AF.Square),
                        reads=xk(w, c, xa, n), writes=[("sq", q)])
                    S.op("pe", lambda e, q=q, c=c, ms=ms, n=n: e.matmul(
                        ms, lhsT=onesf[:], rhs=sq[q][:, 0:n], start=(c == 0), stop=(c == NCH - 1)),
                        reads=[("sq", q), "onesf"], writes=[("ps", msb)], sig=True)
                r = ti % 2
                S.op("act", lambda e, r=r, ms=ms, n=n: e.activation(
                    out=rsq[r][:, 0:n], in_=ms, func=AF.Sqrt, bias=epsb[:, 0:1], scale=1.0),
                    reads=[("ps", msb), "epsb"], writes=[("rsq", r)])
                S.op("dve", lambda e, r=r, n=n: e.reciprocal(out=rstd[r][:, 0:n], in_=rsq[r][:, 0:n]),
                     reads=[("rsq", r)], writes=[("rstd", r)])
                for c in range(NCH):
                    q = st_["tmp"] % nbuf
                    st_["tmp"] += 1
                    S.op("dve", lambda e, q=q, c=c, xb=xb, xa=xa, n=n, r=r, w=w: e.scalar_tensor_tensor(
                        out=tmp[q][:, 0:n], in0=xb[:, c, xa:xa + n], scalar=tabA[:, j, c, w:w + 1],
                        in1=rstd[r][:, 0:n], op0=ALU.mult, op1=ALU.mult),
                        reads=xk(w, c, xa, n) + [("rstd", r), ("tabA", j)], writes=[("ptmp", q)])
                    S.op("act", lambda e, q=q, c=c, a=a, n=n, w=w: e.activation(
                        out=H[:, c, a:a + n], in_=tmp[q][:, 0:n], func=AF.Identity,
                        bias=tabB(l, j)[:, c, w:w + 1], scale=1.0),
                        reads=[("ptmp", q), ("mod", l, j)], writes=hk(c, a, n))
            return pre_tile

        def prepass(l, j, tiles, H):
            m = A.mark()
            pt = make_prepass(l, j, tiles, H)
            for ti in range(len(tiles)):
                pt(ti)
            S.barrier()
            A.reset(m)

        def ffn(l, f, tiles, H, ada_job=None):
            j = 0 if f == 0 else 2
            m = A.mark()
            pre_tile = make_prepass(l, j, tiles, H, nbuf=2)
            adst = [A.alloc([128, NCH, 256], BF16) for _ in range(2)]
            act = [A.alloc([128, 4, 512], BF16) for _ in range(2)]
            sg = [A.alloc([128, 512], F32) for _ in range(2)]
            wi = f_wi[f][l].rearrange("(k p) (two n) -> p k two n", p=128, two=2)
            wo = f_wo[f][l].rearrange("(j p) n -> p j n", p=128)
            sweeps = [[0, 1], [2, 3], [4, 5], [6, 7], [8, 9], [10]]
            unit_slot = {}

            def load_unit(u):
                s = slot_alloc()
                unit_slot[u] = s
                wiv = Wr[:, s, 0:4096].rearrange("p (k two n) -> p k two n", k=NCH, two=2)
                wov = Wr[:, s, 4096:6144].rearrange("p (j n) -> p j n", j=2)
                load_slot(s, [lambda e, wiv=wiv, u=u: e.dma_start(out=wiv[:, :, 0, :], in_=wi[:, :, 0, u * 256:(u + 1) * 256]),
                              lambda e, wiv=wiv, u=u: e.dma_start(out=wiv[:, :, 1, :], in_=wi[:, :, 1, u * 256:(u + 1) * 256]),
                              lambda e, wov=wov, u=u: e.dma_start(out=wov, in_=wo[:, 2 * u:2 * u + 2, :])])

            tasks = []
            for si, sw in enumerate(sweeps):
                for ti in range(len(tiles)):
                    tasks.append((si, ti))
            st = {"sg": 0, "gu": 0, "y": 0}

            def stage_a(k):
                si, ti = tasks[k]
                a, n, w = tiles[ti]
                ab = k % 2
                chunks = [(u, jj) for u in sweeps[si] for jj in range(2)]
                for ci, (u, jj) in enumerate(chunks):
                    s = unit_slot[u]
                    wiv = Wr[:, s, 0:4096].rearrange("p (k two n) -> p k two n", k=NCH, two=2)
                    gb = st["gu"] % 2
                    st["gu"] += 1
                    gps = PS[gb * 2][:, 0:n]
                    ups = PS[gb * 2 + 1][:, 0:n]
                    for two, pp, pb in ((0, gps, gb * 2), (1, ups, gb * 2 + 1)):
                        for kk in range(NCH):
                            S.op("pe", lambda e, pp=pp, wiv=wiv, kk=kk, two=two, jj=jj, a=a, n=n: e.matmul(
                                pp, lhsT=wiv[:, kk, two, jj * 128:(jj + 1) * 128], rhs=H[:, kk, a:a + n],
                                start=(kk == 0), stop=(kk == NCH - 1)),
                                reads=[("W", s)] + hk(kk, a, n), writes=[("ps", pb)], sig=(kk == NCH - 1))
                    q = st["sg"] % 2
                    st["sg"] += 1
                    S.op("act", lambda e, q=q, gps=gps, n=n: e.activation(out=sg[q][:, 0:n], in_=gps, func=AF.Silu),
                         reads=[("ps", gb * 2)], writes=[("sg", q)])
                    S.op("dve", lambda e, q=q, ups=ups, n=n, ab=ab, ci=ci: e.tensor_tensor(
                        out=act[ab][:, ci, 0:n], in0=sg[q][:, 0:n], in1=ups, op=ALU.mult),
                        reads=[("sg", q), ("ps", gb * 2 + 1)], writes=[("act", ab, ci)])

            def stage_b(k):
                si, ti = tasks[k]
                a, n, w = tiles[ti]
                xb = xbuf(w)
                xa = a - T if w == 1 else a
                ab = k % 2
                chunks = [(u, jj) for u in sweeps[si] for jj in range(2)]
                for d in range(NCH):
                    yb = 4 + st["y"] % 2
                    st["y"] += 1
                    yps = PS[yb][:, 0:n]
                    for ci, (u, jj) in enumerate(chunks):
                        s = unit_slot[u]
                        wov = Wr[:, s, 4096:6144].rearrange("p (j n) -> p j n", j=2)
                        S.op("pe", lambda e, yps=yps, wov=wov, jj=jj, d=d, ab=ab, ci=ci, n=n: e.matmul(
                            yps, lhsT=wov[:, jj, d * 128:(d + 1) * 128], rhs=act[ab][:, ci, 0:n],
                            start=(ci == 0), stop=(ci == len(chunks) - 1)),
                            reads=[("W", s), ("act", ab, ci)], writes=[("ps", yb)], sig=(ci == len(chunks) - 1))
                    S.op("dve", lambda e, yps=yps, d=d, xb=xb, xa=xa, n=n, w=w: e.scalar_tensor_tensor(
                        out=xb[:, d, xa:xa + n], in0=yps, scalar=tabG[:, j, d, w:w + 1], in1=xb[:, d, xa:xa + n],
                        op0=ALU.mult, op1=ALU.add),
                        reads=[("ps", yb), ("tabG", j)] + xk(w, d, xa, n), writes=xk(w, d, xa, n))

            loaded = 0

            def ensure_loaded(si):
                nonlocal loaded
                while loaded <= min(si, len(sweeps) - 1):
                    for u in sweeps[loaded]:
                        load_unit(u)
                    loaded += 1
            ensure_loaded(1)
            nt = len(tiles)
            pre_tile(0)
            if nt > 1:
                pre_tile(1)
            ada_l, ada_units = ada_job if ada_job else (0, [])
            ada_pend = []
            ada_i = 0
            for k in range(len(tasks) + 1):
                if k < len(tasks):
                    si, ti = tasks[k]
                    if ti == 1:
                        ensure_loaded(si + 1)
                    if si == 0 and ti + 2 < nt:
                        pre_tile(ti + 2)
                    if ada_i < len(ada_units):
                        ada_pend.append(ada_unit(ada_l, ada_units[ada_i], adst[ada_i % 2], ada_i % 2))
                        ada_i += 1
                        if len(ada_pend) == 2:
                            ada_pend.pop(0)()
                    stage_a(k)
                if k >= 1:
                    stage_b(k - 1)
            for f_ in ada_pend:
                f_()
            assert ada_i == len(ada_units)
            S.barrier()
            A.reset(m)

        def conv_phase(l, H):
            prepass(l, 1, TILES_A, H)
            m = A.mark()
            p1 = pw1_w.rearrange("(k p) n -> p k n", p=128)
            p2 = pw2_w.rearrange("(k p) n -> p k n", p=128)
            slots = []
            for i in range(4):
                s = slot_alloc()
                slots.append(s)
                v1 = Wr[:, s, 0:4096].rearrange("p (k two n) -> p k two n", k=NCH, two=2)
                v2 = Wr[:, s, 4096:6144].rearrange("p (k n) -> p k n", k=NCH)
                load_slot(s, [lambda e, v1=v1, i=i: e.dma_start(out=v1[:, :, 0, :], in_=p1[:, :, i * 256:(i + 1) * 256]),
                              lambda e, v1=v1, i=i: e.dma_start(out=v1[:, :, 1, :], in_=p1[:, :, D + i * 256:D + (i + 1) * 256]),
                              lambda e, v2=v2, i=i: e.dma_start(out=v2, in_=p2[:, :, i * 256:(i + 1) * 256])])
            UW = 472
            NPE = 16
            U = [A.alloc([128, UW], F32) for _ in range(2)]
            Ub = [A.alloc([128, UW], BF16) for _ in range(2)]
            Dg = A.alloc([128, 2, NPE, 128], BF16)
            V = A.alloc([128, NCH, 440], F32)
            sqv = [A.alloc([128, 440], F32) for _ in range(1)]
            varb = A.alloc([128, 440], F32)
            HN = A.alloc([128, NCH, 440], BF16)
            ident = cst[:, 704:832]
            dww = P("dw_w").rearrange("p (c k) -> p c k", k=CW)
            K1 = CW - NPE
            pend_d = []
            for ti, (a, n, w) in enumerate(TILES_A):
                xb = xbuf(w)
                xa = a - T if w == 1 else a
                s0, s1 = (0, T) if w == 0 else (T, T + CTX)
                lo, hi = max(a - 15, s0), min(a + n + 15, s1)
                ulo, uhi = lo - (a - 15), hi - (a - 15)
                nu = hi - lo
                for jp in range(4):
                    for jj in range(2):
                        j = 2 * jp + jj
                        s = slots[jp]
                        v1 = Wr[:, s, 0:4096].rearrange("p (k two n) -> p k two n", k=NCH, two=2)
                        for two in range(2):
                            pb = 2 * jj + two
                            pp = PS[pb][:, 0:nu]
                            for kk in range(NCH):
                                S.op("pe", lambda e, pp=pp, v1=v1, kk=kk, two=two, jj=jj, lo=lo, hi=hi: e.matmul(
                                    pp, lhsT=v1[:, kk, two, jj * 128:(jj + 1) * 128], rhs=H[:, kk, lo:hi],
                                    start=(kk == 0), stop=(kk == NCH - 1)),
                                    reads=[("W", s)] + hk(kk, lo, nu), writes=[("ps", pb)], sig=(kk == NCH - 1))
                        if ulo > 0:
                            S.op("dve", lambda e, jj=jj, ulo=ulo: e.memset(U[jj][:, 0:ulo], 0.0), writes=[("U", jj)])
                        if uhi < n + 30:
                            S.op("dve", lambda e, jj=jj, uhi=uhi, n=n: e.memset(U[jj][:, uhi:n + 30], 0.0), writes=[("U", jj)])
                        S.op("act", lambda e, jj=jj, j=j, nu=nu, ulo=ulo, uhi=uhi: e.activation(
                            out=U[jj][:, ulo:uhi], in_=PS[2 * jj + 1][:, 0:nu], func=AF.Sigmoid,
                            bias=P("pw1_b", 8 + j, 9 + j), scale=1.0),
                            reads=[("ps", 2 * jj + 1), "prm"], writes=[("U", jj)])
                        S.op("dve", lambda e, jj=jj, j=j, nu=nu, ulo=ulo, uhi=uhi: e.scalar_tensor_tensor(
                            out=U[jj][:, ulo:uhi], in0=PS[2 * jj][:, 0:nu], scalar=P("pw1_b", j, j + 1),
                            in1=U[jj][:, ulo:uhi], op0=ALU.add, op1=ALU.mult),
                            reads=[("ps", 2 * jj), "prm", ("U", jj)], writes=[("U", jj)])
                    for k in range(K1):
                        for jj in range(2):
                            j = 2 * jp + jj
                            if k == 0:
                                S.op("dve", lambda e, jj=jj, j=j, n=n: e.tensor_scalar(
                                    out=V[:, j, 0:n], in0=U[jj][:, 0:n], scalar1=dww[:, j, 0:1],
                                    scalar2=P("dw_b", j, j + 1), op0=ALU.mult, op1=ALU.add),
                                    reads=[("U", jj), "prm"], writes=[("V", j)])
                            else:
                                S.op("dve", lambda e, jj=jj, j=j, n=n, k=k: e.scalar_tensor_tensor(
                                    out=V[:, j, 0:n], in0=U[jj][:, k:k + n], scalar=dww[:, j, k:k + 1],
                                    in1=V[:, j, 0:n], op0=ALU.mult, op1=ALU.add),
                                    reads=[("U", jj), ("V", j), "prm"], writes=[("V", j)])
                    for jj in range(2):
                        j = 2 * jp + jj
                        S.op("act", lambda e, jj=jj, n=n: e.activation(
                            out=Ub[jj][:, 0:n + 30], in_=U[jj][:, 0:n + 30], func=AF.Identity),
                            reads=[("U", jj)], writes=[("Ub", jj)])
                        for kidx in range(NPE):
                            k = K1 + kidx
                            S.op("act", lambda e, jj=jj, j=j, k=k, kidx=kidx: e.activation(
                                out=Dg[:, jj, kidx, :], in_=ident, func=AF.Identity, scale=dww[:, j, k:k + 1]),
                                reads=["cst", "prm"], writes=[("Dg", jj, kidx)])
                    for jj in range(2):
                        j = 2 * jp + jj
                        for kidx in range(NPE):
                            k = K1 + kidx
                            S.op("pe", lambda e, jj=jj, k=k, kidx=kidx, n=n: e.matmul(
                                PS[4 + jj][:, 0:n], lhsT=Dg[:, jj, kidx, :], rhs=Ub[jj][:, k:k + n],
                                start=(kidx == 0), stop=(kidx == NPE - 1)),
                                reads=[("Dg", jj, kidx), ("Ub", jj)], writes=[("ps", 4 + jj)], sig=(kidx == NPE - 1))
                    for jj in range(2):
                        j = 2 * jp + jj
                        S.op("dve", lambda e, jj=jj, j=j, n=n: e.tensor_tensor(
                            out=V[:, j, 0:n], in0=V[:, j, 0:n], in1=PS[4 + jj][:, 0:n], op=ALU.add),
                            reads=[("V", j), ("ps", 4 + jj)], writes=[("V", j)])
                while pend_d:
                    pend_d.pop(0)()
                for j in range(NCH):
                    S.op("pe", lambda e, j=j, n=n: e.matmul(PS[4][:, 0:n], lhsT=onesf[:], rhs=V[:, j, 0:n],
                                                           start=(j == 0), stop=(j == NCH - 1)),
                         reads=[("V", j), "onesf"], writes=[("ps", 4)], sig=True)
                    S.op("act", lambda e, j=j, n=n: e.activation(out=sqv[0][:, 0:n], in_=V[:, j, 0:n], func=AF.Square),
                         reads=[("V", j)], writes=[("sqv", 0)])
                    S.op("pe", lambda e, j=j, n=n: e.matmul(PS[5][:, 0:n], lhsT=onesf[:], rhs=sqv[0][:, 0:n],
                                                           start=(j == 0), stop=(j == NCH - 1)),
                         reads=[("sqv", 0), "onesf"], writes=[("ps", 5)], sig=True)
                S.op("act", lambda e, n=n: e.activation(out=varb[:, 0:n], in_=PS[4][:, 0:n], func=AF.Square),
                     reads=[("ps", 4)], writes=["varb"])
                S.op("dve", lambda e, n=n: e.tensor_tensor(out=varb[:, 0:n], in0=PS[5][:, 0:n], in1=varb[:, 0:n],
                                                           op=ALU.subtract), reads=[("ps", 5), "varb"], writes=["varb"])
                S.op("act", lambda e, n=n: e.activation(out=varb[:, 0:n], in_=varb[:, 0:n], func=AF.Sqrt,
                                                        bias=epsb[:, 0:1], scale=1.0),
                     reads=["varb", "epsb"], writes=["varb"])
                S.op("dve", lambda e, n=n: e.reciprocal(out=varb[:, 0:n], in_=varb[:, 0:n]),
                     reads=["varb"], writes=["varb"])
                for j in range(NCH):
                    S.op("dve", lambda e, j=j, n=n: e.tensor_tensor(
                        out=V[:, j, 0:n], in0=V[:, j, 0:n], in1=PS[4][:, 0:n], op=ALU.subtract),
                        reads=[("V", j), ("ps", 4)], writes=[("V", j)])
                    S.op("dve", lambda e, j=j, n=n: e.tensor_tensor(
                        out=V[:, j, 0:n], in0=V[:, j, 0:n], in1=varb[:, 0:n], op=ALU.mult),
                        reads=[("V", j), "varb"], writes=[("V", j)])
                    S.op("act", lambda e, j=j, n=n: e.activation(
                        out=HN[:, j, 0:n], in_=V[:, j, 0:n], func=AF.Silu,
                        bias=P("ln_b", j, j + 1), scale=P("ln_g", j, j + 1)),
                        reads=[("V", j), "prm"], writes=[("HN", j)])
                def stage_d(a=a, n=n, w=w, xb=xb, xa=xa):
                    for d in range(NCH):
                        s = slots[d // 2]
                        v2 = Wr[:, s, 4096:6144].rearrange("p (k n) -> p k n", k=NCH)
                        yb = 6 + d % 2
                        for kk in range(NCH):
                            S.op("pe", lambda e, v2=v2, kk=kk, d=d, yb=yb, n=n: e.matmul(
                                PS[yb][:, 0:n], lhsT=v2[:, kk, (d % 2) * 128:(d % 2 + 1) * 128], rhs=HN[:, kk, 0:n],
                                start=(kk == 0), stop=(kk == NCH - 1)),
                                reads=[("W", s), ("HN", kk)], writes=[("ps", yb)], sig=(kk == NCH - 1))
                        S.op("act", lambda e, d=d, n=n, w=w, yb=yb: e.activation(
                            out=sqv[0][:, 0:n], in_=PS[yb][:, 0:n], func=AF.Identity,
                            bias=tabGb[:, d, w:w + 1], scale=tabG[:, 1, d, w:w + 1]),
                            reads=[("ps", yb), ("tabG", 1), "tabGb"], writes=[("sqv", 0)])
                        S.op("dve", lambda e, d=d, xb=xb, xa=xa, n=n: e.tensor_tensor(
                            out=xb[:, d, xa:xa + n], in0=xb[:, d, xa:xa + n], in1=sqv[0][:, 0:n], op=ALU.add),
                            reads=[("sqv", 0)] + xk(w, d, xa, n), writes=xk(w, d, xa, n))
                pend_d.append(stage_d)
            while pend_d:
                pend_d.pop(0)()
            S.barrier()
            A.reset(m)

        def attn_phase(l, H):
            prepass(l, 1, TILES_A, H)
            m = A.mark()
            Qg = A.alloc([128, 2, OWN], BF16)
            KgA = A.alloc([128, TK + CTX], BF16)
            KgB = A.alloc([128, TK + CTX], BF16)
            Vpad = A.alloc([128, 19, 192], BF16)
            Og = [A.alloc([128, 2, 512], BF16) for _ in range(1)]
            PT = [A.alloc([128, 5, 2, 128] if ATT_PIPE else [128, 2, 5, 128], BF16) for _ in range(2)]
            cosb = A.alloc([128, 440], F32)
            sinb = A.alloc([128, 440], F32)
            t1 = A.alloc([128, 440], F32)
            rden = [A.alloc([128, 128], F32) for _ in range(2)]
            mask_lo = cst[:, 0:256].rearrange("p (h q) -> p h q", h=2)
            mask_hi = cst[:, 256:512].rearrange("p (h q) -> p h q", h=2)
            onespad = cst[:, 512:704]
            wa = watt.rearrange("(k p) n -> p k n", p=128)
            wvv = wv_d.rearrange("(k p) n -> p k n", p=128)
            wov = wo_d.rearrange("(j p) n -> p j n", p=128)
            S.op("pool", lambda e: e.memset(Vpad[:], 0.0), writes=["Vpad"])
            S.op("dve", lambda e: e.memset(KgA[64:128, :], 0.0), writes=[("Kg",)])
            S.op("dve", lambda e: e.memset(KgB[0:64, :], 0.0), writes=[("Kg",)])
            cnt = {"st": 0, "od": 0, "og": 0}
            for g in range(4):
                sA = slot_alloc()
                vA = Wr[:, sA, :].rearrange("p (k n) -> p k n", k=NCH)
                load_slot(sA, [lambda e, vA=vA, g=g: e.dma_start(out=vA, in_=wa[:, :, g * 768:(g + 1) * 768])])
                sB = slot_alloc()
                vBo = Wr[:, sB, 0:2048].rearrange("p (j n) -> p j n", j=2)
                vBv = Wr[:, sB, 2048:2560].rearrange("p (k n) -> p k n", k=NCH)
                load_slot(sB, [lambda e, vBo=vBo, g=g: e.dma_start(out=vBo, in_=wov[:, 2 * g:2 * g + 2, :]),
                               lambda e, vBv=vBv, g=g: e.dma_start(out=vBv, in_=wvv[:, :, 64 * g:64 * g + 64])])
                for ti in range(5):
                    a, n = 440 * ti, 440
                    S.dma("sp", [lambda e, a=a, n=n: e.dma_start(out=cosb[:, 0:n], in_=rope_d[0][:, a:a + n]),
                                 lambda e, a=a, n=n: e.dma_start(out=sinb[:, 0:n], in_=rope_d[1][:, a:a + n])],
                          "rp0", writes=["rope"])
                    items = [("q", 0), ("q", 1), ("k", 0)]
                    for ii, (kind, cc) in enumerate(items):
                        if kind == "q":
                            nn = max(0, min(a + n, OWN) - a)
                            base, bsw = cc * 128, 256 + cc * 128
                            dkey = ("Qg", cc)
                        else:
                            nn = max(0, min(a + n, TK) - a)
                            base, bsw = 512, 640
                            dkey = ("Kg",)
                        if nn == 0:
                            continue
                        pa, pb = 2 * (ii % 2), 2 * (ii % 2) + 1
                        for (pp, bb) in ((pa, base), (pb, bsw)):
                            for kk in range(NCH):
                                S.op("pe", lambda e, pp=pp, bb=bb, kk=kk, a=a, nn=nn, vA=vA: e.matmul(
                                    PS[pp][:, 0:nn], lhsT=vA[:, kk, bb:bb + 128], rhs=H[:, kk, a:a + nn],
                                    start=(kk == 0), stop=(kk == NCH - 1)),
                                    reads=[("W", sA)] + hk(kk, a, nn), writes=[("ps", pp)], sig=(kk == NCH - 1))
                        S.op("dve", lambda e, pa=pa, nn=nn: e.tensor_tensor(
                            out=PS[pa][:, 0:nn], in0=PS[pa][:, 0:nn], in1=cosb[:, 0:nn], op=ALU.mult),
                            reads=[("ps", pa), "rope"], writes=[("ps", pa)])
                        S.op("dve", lambda e, pb=pb, nn=nn: e.tensor_tensor(
                            out=t1[:, 0:nn], in0=PS[pb][:, 0:nn], in1=sinb[:, 0:nn], op=ALU.mult),
                            reads=[("ps", pb), "rope"], writes=["t1"])
                        if kind == "q":
                            S.op("dve", lambda e, pa=pa, nn=nn, cc=cc, a=a: e.tensor_tensor(
                                out=Qg[:, cc, a:a + nn], in0=PS[pa][:, 0:nn], in1=t1[:, 0:nn], op=ALU.add),
                                reads=[("ps", pa), "t1"], writes=[dkey])
                        else:
                            S.op("dve", lambda e, pa=pa, nn=nn, a=a: e.tensor_tensor(
                                out=KgA[0:64, a:a + nn], in0=PS[pa][0:64, 0:nn], in1=t1[0:64, 0:nn], op=ALU.add),
                                reads=[("ps", pa), "t1"], writes=[dkey])
                            S.op("dve", lambda e, pa=pa, nn=nn, a=a: e.tensor_tensor(
                                out=KgB[64:128, a:a + nn], in0=PS[pa][64:128, 0:nn], in1=t1[64:128, 0:nn], op=ALU.add),
                                reads=[("ps", pa), "t1"], writes=[dkey])
                for kk in range(NCH):
                    S.op("pe", lambda e, kk=kk, vA=vA: e.matmul(
                        PS[0][:, 0:CTX], lhsT=vA[:, kk, 512:640], rhs=H[:, kk, T:T + CTX],
                        start=(kk == 0), stop=(kk == NCH - 1)),
                        reads=[("W", sA)] + hk(kk, T, CTX), writes=[("ps", 0)], sig=(kk == NCH - 1))
                S.op("act", lambda e: e.activation(out=KgA[0:64, TK:TK + CTX], in_=PS[0][0:64, 0:CTX], func=AF.Identity),
                     reads=[("ps", 0)], writes=[("Kg",)])
                S.op("act", lambda e: e.activation(out=KgB[64:128, TK:TK + CTX], in_=PS[0][64:128, 0:CTX], func=AF.Identity),
                     reads=[("ps", 0)], writes=[("Kg",)])
                for b0 in range(0, 19, 8):
                    nb = min(8, 19 - b0)
                    vb = 6 + (b0 // 8) % 2
                    psv = PS[vb].rearrange("p (b d) -> p b d", d=64)
                    for bi in range(nb):
                        blk = b0 + bi
                        c0 = 128 * blk if blk < 17 else T + 128 * (blk - 17)
                        for kk in range(NCH):
                            S.op("pe", lambda e, psv=psv, bi=bi, kk=kk, c0=c0, vBv=vBv: e.matmul(
                                psv[:, bi, :], lhsT=H[:, kk, c0:c0 + 128], rhs=vBv[:, kk, :],
                                start=(kk == 0), stop=(kk == NCH - 1)),
                                reads=[("W", sB)] + hk(kk, c0, 128), writes=[("ps", vb)],
                                sig=(kk == NCH - 1 and bi == nb - 1))
                    S.op("act", lambda e, psv=psv, b0=b0, nb=nb: e.activation(
                        out=Vpad[:, b0:b0 + nb, 64:128], in_=psv[:, 0:nb, :], func=AF.Identity),
                        reads=[("ps", vb)], writes=["Vpad"])
                its = [(qt, qb, cc) for qt in range(4) for qb in range(4) for cc in range(2)]
                info = {}

                def stage_s(it):
                    qt, qb, cc = its[it]
                    i = 4 * qt + qb
                    sb = cnt["st"] % 2
                    cnt["st"] += 1
                    KBM = ATT_PIPE
                    if KBM:
                        stv = PSALL[:, sb * 1536:sb * 1536 + 1280].rearrange("p (k h q) -> p k h q", k=5, h=2)
                    else:
                        stv = PSALL[:, sb * 1536:sb * 1536 + 1280].rearrange("p (h k q) -> p h k q", h=2, k=5)
                    stkeys = [("ps", sb * 3 + z) for z in range(3)]
                    k0 = 1 if i == 0 else 0
                    srcs = []
                    for kbi in range(k0, 5):
                        if kbi < 3:
                            blk = i - 1 + kbi
                            srcs.append((kbi, 128 * blk, blk))
                        else:
                            srcs.append((kbi, TK + 128 * (kbi - 3), 17 + kbi - 3))
                    nmm = 2 * len(srcs)
                    PEM = (ATT_MASK == "pe")
                    z = 0
                    for h in (1, 0):
                        for (kbi, kc, vbk) in srcs:
                            z += 1
                            msk = PEM and kbi in (0, 2)
                            S.op("pe", lambda e, stv=stv, h=h, kbi=kbi, kc=kc, cc=cc, i=i, KBM=KBM, msk=msk: e.matmul(
                                stv[:, kbi, h, :] if KBM else stv[:, h, kbi, :],
                                lhsT=(KgA if h == 0 else KgB)[:, kc:kc + 128],
                                rhs=Qg[:, cc, 128 * i:128 * i + 128], start=True, stop=(not msk)),
                                reads=[("Kg",), ("Qg", cc)], writes=stkeys, sig=(z == nmm and not msk))
                            if msk:
                                mb = cst[:, 832:960] if kbi == 0 else cst[:, 960:1088]
                                S.op("pe", lambda e, stv=stv, h=h, kbi=kbi, KBM=KBM, mb=mb: e.matmul(
                                    stv[:, kbi, h, :] if KBM else stv[:, h, kbi, :],
                                    lhsT=cst[:, 704:832], rhs=mb, start=False, stop=True),
                                    reads=["cst"], writes=stkeys, sig=(z == nmm))
                    pb = sb
                    if KBM:
                        for (ka, kb_) in ((k0, 5),):
                            S.op("act", lambda e, stv=stv, pb=pb, ka=ka, kb_=kb_: e.activation(
                                out=PT[pb][:, ka:kb_, :, :], in_=stv[:, ka:kb_, :, :], func=AF.Exp, scale=0.125),
                                reads=stkeys, writes=[("PT", pb)])
                        if i > 0 and not PEM:
                            S.op(ATT_MASK, lambda e, pb=pb: e.tensor_tensor(
                                out=PT[pb][:, 0, :, :], in0=PT[pb][:, 0, :, :], in1=mask_lo, op=ALU.mult),
                                reads=[("PT", pb), "cst"], writes=[("PT", pb)])
                        if not PEM:
                            S.op(ATT_MASK, lambda e, pb=pb: e.tensor_tensor(
                                out=PT[pb][:, 2, :, :], in0=PT[pb][:, 2, :, :], in1=mask_hi, op=ALU.mult),
                                reads=[("PT", pb), "cst"], writes=[("PT", pb)])
                    else:
                        S.op("act", lambda e, stv=stv, pb=pb, k0=k0: e.activation(
                            out=PT[pb][:, :, k0:5, :], in_=stv[:, :, k0:5, :], func=AF.Exp, scale=0.125),
                            reads=stkeys, writes=[("PT", pb)])
                        if i > 0 and not PEM:
                            S.op(ATT_MASK, lambda e, pb=pb: e.tensor_tensor(
                                out=PT[pb][:, :, 0, :], in0=PT[pb][:, :, 0, :], in1=mask_lo, op=ALU.mult),
                                reads=[("PT", pb), "cst"], writes=[("PT", pb)])
                        if not PEM:
                            S.op(ATT_MASK, lambda e, pb=pb: e.tensor_tensor(
                                out=PT[pb][:, :, 2, :], in0=PT[pb][:, :, 2, :], in1=mask_hi, op=ALU.mult),
                                reads=[("PT", pb), "cst"], writes=[("PT", pb)])
                    info[it] = (pb, srcs, nmm)

                def stage_p(it):
                    qt, qb, cc = its[it]
                    pb, srcs, nmm = info.pop(it)
                    ob = 0
                    odb = cnt["od"] % 2
                    cnt["od"] += 1
                    KBM = ATT_PIPE
                    if KBM:
                        o_ps = PS[6 + odb][:, 0:128]
                        d_ps = PS[6 + odb][:, 128:256]
                        odk = [("ps", 6 + odb)]
                    else:
                        o_ps = PS[6][:, (2 * odb) * 128:(2 * odb + 1) * 128]
                        d_ps = PS[6][:, (2 * odb + 1) * 128:(2 * odb + 2) * 128]
                        odk = []
                    for (tgt, is_den) in ((o_ps, False), (d_ps, True)):
                        z = 0
                        for h in range(2):
                            c_lo = 64 if h == 0 else 0
                            for (kbi, kc, vbk) in srcs:
                                z += 1
                                if is_den:
                                    lh = onespad[:, c_lo:c_lo + 128]
                                else:
                                    lh = Vpad[:, vbk, c_lo:c_lo + 128]
                                S.op("pe", lambda e, tgt=tgt, lh=lh, pb=pb, h=h, kbi=kbi, z=z, nmm=nmm, KBM=KBM: e.matmul(
                                    tgt, lhsT=lh, rhs=(PT[pb][:, kbi, h, :] if KBM else PT[pb][:, h, kbi, :]),
                                    start=(z == 1), stop=(z == nmm)),
                                    reads=[("PT", pb), "Vpad", "cst"], writes=[("ps6", odb, is_den)] + odk,
                                    sig=(z == nmm))
                    S.op("dve", lambda e, d_ps=d_ps, odb=odb, cc=cc, g=g: e.tensor_scalar(
                        out=rden[odb][:], in0=d_ps, scalar1=esink[:, 2 * g + cc:2 * g + cc + 1], scalar2=None,
                        op0=ALU.add), reads=[("ps6", odb, True), "esink"] + odk, writes=[("rden", odb)])
                    if ATT_DIV:
                        S.op("dve", lambda e, o_ps=o_ps, odb=odb, ob=ob, cc=cc, qb=qb: e.tensor_tensor(
                            out=Og[ob][:, cc, qb * 128:(qb + 1) * 128], in0=o_ps, in1=rden[odb][:], op=ALU.divide),
                            reads=[("ps6", odb, False), ("rden", odb)] + odk, writes=[("Og", ob, cc)])
                    else:
                        S.op("dve", lambda e, odb=odb: e.reciprocal(out=rden[odb][:], in_=rden[odb][:]),
                             reads=[("rden", odb)], writes=[("rden", odb)])
                        S.op("dve", lambda e, o_ps=o_ps, odb=odb, ob=ob, cc=cc, qb=qb: e.tensor_tensor(
                            out=Og[ob][:, cc, qb * 128:(qb + 1) * 128], in0=o_ps, in1=rden[odb][:], op=ALU.mult),
                            reads=[("ps6", odb, False), ("rden", odb)] + odk, writes=[("Og", ob, cc)])
                    if qb == 3 and cc == 1 and not ATT_EXP:
                        for d in range(NCH):
                            if KBM:
                                for hf in range(2):
                                    ybk = 2 + 3 * hf
                                    yv = PSALL[:, ybk * 512 + 256:ybk * 512 + 512]
                                    for c2 in range(2):
                                        S.op("pe", lambda e, d=d, c2=c2, ob=ob, vBo=vBo, yv=yv, hf=hf: e.matmul(
                                            yv, lhsT=vBo[:, c2, d * 128:(d + 1) * 128],
                                            rhs=Og[ob][:, c2, hf * 256:(hf + 1) * 256],
                                            start=(c2 == 0), stop=(c2 == 1)),
                                            reads=[("W", sB), ("Og", ob, c2)], writes=[("ps", ybk)], sig=(c2 == 1))
                                    x0 = qt * 512 + hf * 256
                                    S.op("dve", lambda e, d=d, x0=x0, yv=yv: e.scalar_tensor_tensor(
                                        out=X[:, d, x0:x0 + 256], in0=yv, scalar=tabG[:, 1, d, 0:1],
                                        in1=X[:, d, x0:x0 + 256], op0=ALU.mult, op1=ALU.add),
                                        reads=[("ps", ybk), ("tabG", 1)] + xk(0, d, x0, 256),
                                        writes=xk(0, d, x0, 256))
                                continue
                            yb = 7
                            for c2 in range(2):
                                S.op("pe", lambda e, d=d, c2=c2, ob=ob, vBo=vBo, yb=yb: e.matmul(
                                    PS[yb][:, 0:512], lhsT=vBo[:, c2, d * 128:(d + 1) * 128], rhs=Og[ob][:, c2, :],
                                    start=(c2 == 0), stop=(c2 == 1)),
                                    reads=[("W", sB), ("Og", ob, c2)], writes=[("ps", yb)], sig=(c2 == 1))
                            S.op("dve", lambda e, d=d, qt=qt, yb=yb: e.scalar_tensor_tensor(
                                out=X[:, d, qt * 512:(qt + 1) * 512], in0=PS[yb][:, 0:512], scalar=tabG[:, 1, d, 0:1],
                                in1=X[:, d, qt * 512:(qt + 1) * 512], op0=ALU.mult, op1=ALU.add),
                                reads=[("ps", yb), ("tabG", 1)] + xk(0, d, qt * 512, 512),
                                writes=xk(0, d, qt * 512, 512))

                if ATT_PIPE:
                    stage_s(0)
                    for it in range(len(its)):
                        if it + 1 < len(its):
                            stage_s(it + 1)
                        stage_p(it)
                else:
                    for it in range(len(its)):
                        stage_s(it)
                        stage_p(it)
            S.barrier()
            A.reset(m)

        H = A.alloc([128, NCH, TT], BF16)
        NOT = lambda *names: stop_after not in names
        if ONLY_ATTN:
            m0 = A.mark()
            ad0 = [A.alloc([128, NCH, 256], BF16) for _ in range(2)]
            ada_run(1, list(range(36)), ad0)
            for jj_ in range(3):
                ada_tables(1, jj_)
            S.barrier()
            A.reset(m0)
            attn_phase(1, H)
            stop_after = "x_ffn1_0"
        else:
            m0 = A.mark()
            ad0 = [A.alloc([128, NCH, 256], BF16) for _ in range(2)]
            ada_run(0, list(range(12)), ad0)
            ada_tables(0, 0)
            S.barrier()
            A.reset(m0)
            ffn(0, 0, TILES_A, H, ada_job=(0, list(range(12, 36))))
            ada_tables(0, 1)
            ada_tables(0, 2)
        if NOT("x_ffn1_0"):
            conv_phase(0, H)
        if NOT("x_ffn1_0", "x_mix_0"):
            ffn(0, 1, TILES_A, H, ada_job=(1, list(range(36))))
        if NOT("x_ffn1_0", "x_mix_0", "x_out_0"):
            for jj_ in range(3):
                ada_tables(1, jj_)
            ffn(1, 0, TILES_A, H)
        if NOT("x_ffn1_0", "x_mix_0", "x_out_0", "x_ffn1_1"):
            attn_phase(1, H)
        if NOT("x_ffn1_0", "x_mix_0", "x_out_0", "x_ffn1_1", "x_mix_1"):
            ffn(1, 1, TILES_OWN, H)

        def final_out(do_norm):
            S.barrier()
            m = A.mark()
            A.reset(0)
            stg = [A.alloc([128, NCH, 512], F32) for _ in range(2)]
            sq = [A.alloc([128, 512], F32) for _ in range(3)]
            rstd = [A.alloc([128, 512], F32) for _ in range(2)]
            rsq = [A.alloc([128, 512], F32) for _ in range(2)]
            oT_v = outT.rearrange("(c p) t -> p c t", p=128)
            toks = []
            n_sq = 0
            for ti, (a, n, w) in enumerate(TILES_OWN):
                b = ti % 2
                if do_norm:
                    msb = 6 + b
                    ms = PS[msb][:, 0:n]
                    for c in range(NCH):
                        q = n_sq % 3
                        n_sq += 1
                        S.op("act", lambda e, q=q, c=c, a=a, n=n: e.activation(
                            out=sq[q][:, 0:n], in_=X[:, c, a:a + n], func=AF.Square),
                            reads=xk(0, c, a, n), writes=[("sq", q)])
                        S.op("pe", lambda e, q=q, c=c, ms=ms, n=n: e.matmul(
                            ms, lhsT=onesf[:], rhs=sq[q][:, 0:n], start=(c == 0), stop=(c == NCH - 1)),
                            reads=[("sq", q), "onesf"], writes=[("ps", msb)], sig=True)
                    S.op("act", lambda e, b=b, ms=ms, n=n: e.activation(
                        out=rsq[b][:, 0:n], in_=ms, func=AF.Sqrt, bias=epsb[:, 0:1], scale=1.0),
                        reads=[("ps", msb), "epsb"], writes=[("rsq", b)])
                    S.op("dve", lambda e, b=b, n=n: e.reciprocal(out=rstd[b][:, 0:n], in_=rsq[b][:, 0:n]),
                         reads=[("rsq", b)], writes=[("rstd", b)])
                    for c in range(NCH):
                        S.op("dve", lambda e, c=c, a=a, n=n, b=b: e.scalar_tensor_tensor(
                            out=stg[b][:, c, 0:n], in0=X[:, c, a:a + n], scalar=P("final_g", c, c + 1),
                            in1=rstd[b][:, 0:n], op0=ALU.mult, op1=ALU.mult),
                            reads=xk(0, c, a, n) + [("rstd", b), "prm"], writes=[("stg", b)],
                            sig=True)
                else:
                    for c in range(NCH):
                        S.op("act", lambda e, c=c, a=a, n=n, b=b: e.activation(
                            out=stg[b][:, c, 0:n], in_=X[:, c, a:a + n], func=AF.Identity),
                            reads=xk(0, c, a, n), writes=[("stg", b)])
                toks.append(S.dma("sp", [lambda e, b=b, a=a, n=n: e.dma_start(
                    out=oT_v[:, :, a:a + n], in_=stg[b][:, :, 0:n])], f"st{b}", reads=[("stg", b)]))
            S.wait_tokens("sp", toks)
            A.reset(m)

        final_out(stop_after is None)

        S.finalize()
        with nc.Block() as block:
            @block.tensor
            def _(e):
                S.emit("pe", e)

            @block.scalar
            def _(e):
                S.emit("act", e)

            @block.vector
            def _(e):
                S.emit("dve", e)

            @block.gpsimd
            def _(e):
                S.emit("pool", e)

            @block.sync
            def _(e):
                S.emit("sp", e)
    return nc


def _partner_cols(n):
    j = np.arange(n)
    d = j % 64
    pd = np.where((d % 32) < 16, d + 16, d - 16)
    return j - d + pd


def prepare_inputs(inp):
    f = lambda a: np.ascontiguousarray(np.asarray(a, dtype=np.float32))
    x, c, ctx, c_ctx = f(inp["x"]), f(inp["c"]), f(inp["ctx"]), f(inp["c_ctx"])
    w_qkv = f(inp["attn_w_qkv"])[0]
    chunk = lambda v: np.ascontiguousarray(v.reshape(-1, 128).T)
    shared = {
        "ada_w": f(inp["ada_w"]), "ffn1_wi": f(inp["ffn1_wi"]), "ffn1_wo": f(inp["ffn1_wo"]),
        "ffn2_wi": f(inp["ffn2_wi"]), "ffn2_wo": f(inp["ffn2_wo"]),
        "pw1_w": f(inp["conv_pw1_w"])[0], "pw2_w": f(inp["conv_pw2_w"])[0],
        "w_v": np.ascontiguousarray(w_qkv[:, 1280:1536]), "w_o": f(inp["attn_w_o"])[0],
    }
    watt = np.zeros((D, 4 * 768), np.float32)
    for g in range(4):
        q = w_qkv[:, 256 * g:256 * g + 256]
        k = w_qkv[:, 1024 + 64 * g:1024 + 64 * g + 64]
        kd = np.concatenate([k, k], axis=1)
        o = 768 * g
        watt[:, o:o + 256] = q
        watt[:, o + 256:o + 512] = q[:, _partner_cols(256)]
        watt[:, o + 512:o + 640] = kd
        watt[:, o + 640:o + 768] = kd[:, _partner_cols(128)]
    shared["w_att"] = watt
    cst = np.zeros((128, NCST), np.float32)
    kk = np.arange(128)[:, None]
    qq = np.arange(128)[None, :]
    lo = (kk >= qq).astype(np.float32)
    hi = (kk <= qq).astype(np.float32)
    cst[:, 0:128] = lo
    cst[:, 128:256] = lo
    cst[:, 256:384] = hi
    cst[:, 384:512] = hi
    cst[:, 512 + 64:512 + 128] = 1.0
    cst[:, 704:832] = np.eye(128, dtype=np.float32)
    cst[:, 832:960] = (lo - 1.0) * 30000.0
    cst[:, 960:1088] = (hi - 1.0) * 30000.0
    shared["cst"] = cst
    p = np.arange(128)
    d = p % 64
    ax = d // 32
    hh = (d % 32) // 16
    fr = d % 16
    inv = (np.float32(10000.0) ** (-np.arange(16, dtype=np.float32) / np.float32(16))).astype(np.float32)
    in_maps = []
    for core in range(8):
        b, hf = core // 2, core % 2
        idx = np.arange(T) if hf == 0 else (SEQ - 1 - np.arange(T))
        m = dict(shared)
        m["xT"] = np.ascontiguousarray(x[b, idx, :].T)
        cb = ctx[b] if hf == 0 else ctx[b, ::-1]
        m["cT"] = np.ascontiguousarray(cb.T)
        prm = np.zeros((128, NPRM), np.float32)

        def put(name, arr):
            o, w = PRM[name]
            assert arr.shape == (128, w), (name, arr.shape)
            prm[:, o:o + w] = arr
        cc = np.stack([chunk(c[b]), chunk(c_ctx)], axis=2).reshape(128, 16)
        put("cc", cc)
        put("ada_b", np.concatenate([chunk(f(inp["ada_b"])[l]) for l in range(2)], axis=1))
        put("norm_g", np.concatenate([chunk(f(inp["norm_g"])[l, j]) for l in range(2) for j in range(3)], axis=1))
        put("pw1_b", chunk(f(inp["conv_pw1_b"])[0]))
        dw = f(inp["conv_dw_w"])[0]
        if hf == 1:
            dw = dw[::-1]
        put("dw_w", np.ascontiguousarray(dw.T.reshape(NCH, 128, CW).transpose(1, 0, 2)).reshape(128, NCH * CW))
        put("dw_b", chunk(f(inp["conv_dw_b"])[0]))
        put("ln_g", chunk(f(inp["conv_ln_g"])[0]))
        put("ln_b", chunk(f(inp["conv_ln_b"])[0]))
        put("pw2_b", chunk(f(inp["conv_pw2_b"])[0]))
        sk = f(inp["attn_sink"])[0]
        put("sink", np.stack([np.where(p >= 64, sk[2 * cch + 1], sk[2 * cch]) for cch in range(NCH)], axis=1).astype(np.float32))
        put("final_g", chunk(f(inp["final_g"])))
        m["prm"] = prm
        row = (idx // 64).astype(np.float32)
        col = (idx % 64).astype(np.float32)
        pos = np.where(ax[:, None] == 0, row[None, :], col[None, :]).astype(np.float32)
        ang = (pos * inv[fr][:, None]).astype(np.float32)
        cs = np.cos(ang).astype(np.float32)
        sn = np.sin(ang).astype(np.float32)
        sn = np.where(hh[:, None] == 0, -sn, sn).astype(np.float32)
        m["rope"] = np.ascontiguousarray(np.stack([cs, sn], axis=0))
        in_maps.append(m)
    return in_maps


def assemble(results):
    out = np.zeros((4, SEQ, D), np.float32)
    for core in range(8):
        b, hf = core // 2, core % 2
        idx = np.arange(OWN) if hf == 0 else (SEQ - 1 - np.arange(OWN))
        out[b, idx, :] = np.asarray(results[core]["outT"]).T
    return out


def kernel(**inputs):
    nc = build_program()
    in_maps = prepare_inputs(inputs)
    res = run_bass_kernel_spmd(nc, in_maps, core_ids=list(range(8)))
    return assemble(res.results)
```
